# Optimizing a Trainium2 kernel written in Bass

```python
import jax, jax.numpy as jnp
from jax import lax
import numpy as np

D_MODEL = 2048
BATCH = 4
SEQ = 2048
DEPTH = 4
DEC_BATCH = 16
DEC_SEQ = 64
PAST_LEN = 4096

CHUNK = 64

CONV_DIM = D_MODEL // 4
CONV_HEADS = 4
CONV_WIDTH = 31
POOL_DIM = D_MODEL // 4
POOL_WINDOWS = (2, 4, 8, 16)
POOL_GROUPS = len(POOL_WINDOWS)
POOL_GROUP_DIM = POOL_DIM // POOL_GROUPS
POOL_MAX = max(POOL_WINDOWS)
RWKV_DIM = D_MODEL // 2
HEAD_SIZE = 64
RWKV_HEADS = RWKV_DIM // HEAD_SIZE
DECAY_LORA = 64
AAA_LORA = 64
GATE_LORA = 64
RWKV_PROJ = 3 * RWKV_DIM + DECAY_LORA + AAA_LORA + GATE_LORA
MIX_DIM = CONV_DIM + POOL_DIM + RWKV_DIM
IN_PROJ = 2 * CONV_DIM + POOL_DIM + RWKV_PROJ
D_FF = -(-(8 * D_MODEL) // (3 * 256)) * 256

RMS_EPS = 1e-6
LN_EPS = 1e-5
GN_EPS = 64e-5

kernel_name = 'hybrid_conv_pool_rwkv7_stream_step'


def rms_norm(x, g):
    xf = x.astype(jnp.float32)
    y = xf * lax.rsqrt(jnp.mean(xf * xf, axis=-1, keepdims=True) + RMS_EPS)
    return (y * g.astype(jnp.float32)).astype(x.dtype)


def conv_mixer(z, hist, conv_w, conv_b, ln_g, ln_b):
    val, gate = jnp.split(z, 2, axis=-1)
    u = val * jax.nn.sigmoid(gate)
    u_pad = jnp.concatenate([hist.astype(u.dtype), u], axis=1)
    h = lax.conv_general_dilated(u_pad, conv_w[:, None, :].astype(u.dtype), window_strides=(1,),
                                 padding='VALID', dimension_numbers=('NWC', 'WIO', 'NWC'),
                                 feature_group_count=CONV_DIM) + conv_b
    hf = h.astype(jnp.float32)
    mu = jnp.mean(hf, axis=-1, keepdims=True)
    var = jnp.mean(jnp.square(hf - mu), axis=-1, keepdims=True)
    hn = (hf - mu) * lax.rsqrt(var + LN_EPS) * ln_g.astype(jnp.float32) + ln_b.astype(jnp.float32)
    out = jax.nn.silu(hn).astype(z.dtype)
    return out, u_pad[:, -(CONV_WIDTH - 1):]


def pool_mixer(p, hist, start_pos, pool_w, pool_scale):
    B, L, _ = p.shape
    p_pad = jnp.concatenate([hist.astype(p.dtype), p], axis=1)
    pf = p_pad.astype(jnp.float32)
    csum = jnp.concatenate([jnp.zeros_like(pf[:, :1]), jnp.cumsum(pf, axis=1)], axis=1)
    end = csum[:, POOL_MAX:]
    pos = start_pos + jnp.arange(L)
    means = []
    for gi, w in enumerate(POOL_WINDOWS):
        sl = slice(gi * POOL_GROUP_DIM, (gi + 1) * POOL_GROUP_DIM)
        begin = csum[:, POOL_MAX - w:POOL_MAX - w + L, sl]
        cnt = jnp.minimum(w, pos + 1).astype(jnp.float32)[None, :, None]
        means.append((end[..., sl] - begin) / cnt)
    d = (jnp.concatenate(means, axis=-1) - pf[:, POOL_MAX - 1:]).astype(p.dtype)
    d = jnp.einsum('blgc,gcd->blgd', d.reshape(B, L, POOL_GROUPS, POOL_GROUP_DIM), pool_w)
    out = d.reshape(B, L, POOL_DIM) * pool_scale
    return out, p_pad[:, -(POOL_MAX - 1):]


def rwkv_mixer(q, shift_prev, wkv_state, mu, w0, w_up, a0, a_up, g_up, k_k, k_a, r_k, gn_g, gn_b):
    B, L, _ = q.shape
    H, N = RWKV_HEADS, HEAD_SIZE
    f32 = jnp.float32
    q_prev = jnp.concatenate([shift_prev.astype(q.dtype), q[:, :-1]], axis=1)
    qs = q + (q_prev - q) * mu
    c0 = 3 * RWKV_DIM
    r, k, v, w_lo, a_lo, g_lo = jnp.split(
        qs, [RWKV_DIM, 2 * RWKV_DIM, c0, c0 + DECAY_LORA, c0 + DECAY_LORA + AAA_LORA], axis=-1)
    w = -jax.nn.softplus(-(w0 + jnp.tanh(w_lo) @ w_up).astype(f32)) - 0.5
    decay = jnp.exp(-jnp.exp(w))
    a = jax.nn.sigmoid((a0 + a_lo @ a_up).astype(f32))
    g = jax.nn.sigmoid(g_lo) @ g_up
    hd = lambda t: t.astype(f32).reshape(B, L, H, N)
    r, k, v, decay, a = hd(r), hd(k), hd(v), hd(decay), hd(a)
    kk = k * k_k.astype(f32).reshape(H, N)
    kk = kk * lax.rsqrt(jnp.maximum(jnp.sum(kk * kk, axis=-1, keepdims=True), 1e-24))
    k = k * (1.0 + (a - 1.0) * k_a.astype(f32).reshape(H, N))

    def step(S, inp):
        r_t, w_t, k_t, v_t, kk_t, a_t = inp
        sa = jnp.einsum('bhij,bhj->bhi', S, -kk_t)
        S = (S * w_t[:, :, None, :] + sa[..., None] * (kk_t * a_t)[:, :, None, :]
             + v_t[..., None] * k_t[:, :, None, :])
        return S, jnp.einsum('bhij,bhj->bhi', S, r_t)

    xs = tuple(jnp.moveaxis(t, 1, 0) for t in (r, decay, k, v, kk, a))
    S_final, y = lax.scan(step, wkv_state.astype(f32), xs)
    y = jnp.moveaxis(y, 0, 1)
    ym = jnp.mean(y, axis=-1, keepdims=True)
    yv = jnp.mean(jnp.square(y - ym), axis=-1, keepdims=True)
    yn = (y - ym) * lax.rsqrt(yv + GN_EPS) * gn_g.astype(f32).reshape(H, N) + gn_b.astype(f32).reshape(H, N)
    bonus = jnp.sum(r * k * r_k.astype(f32), axis=-1, keepdims=True) * v
    out = (yn + bonus).reshape(B, L, RWKV_DIM).astype(q.dtype) * g
    return out, q[:, -1:], S_final


def trunk(x, cache_conv, cache_pool, state_shift, state_wkv, start_pos,
          norm_mix, w_in, conv_w, conv_b, conv_ln_g, conv_ln_b, pool_w, pool_scale,
          shift_mu, decay_w0, decay_up, iclr_a0, iclr_up, gate_up, k_k, k_a, r_k, gn_g, gn_b,
          w_out, norm_ffn, ffn_gate, ffn_up, ffn_down, norm_final):
    conv_list, pool_list, shift_list, wkv_list = [], [], [], []
    for l in range(DEPTH):
        h = rms_norm(x, norm_mix[l])
        z = h @ w_in[l]
        z_conv, z_pool, z_rwkv = jnp.split(z, [2 * CONV_DIM, 2 * CONV_DIM + POOL_DIM], axis=-1)
        o_conv, c_new = conv_mixer(z_conv, cache_conv[l], conv_w[l], conv_b[l], conv_ln_g[l], conv_ln_b[l])
        o_pool, p_new = pool_mixer(z_pool, cache_pool[l], start_pos, pool_w[l], pool_scale[l])
        o_rwkv, s_new, S_new = rwkv_mixer(z_rwkv, state_shift[l], state_wkv[l], shift_mu[l], decay_w0[l],
                                          decay_up[l], iclr_a0[l], iclr_up[l], gate_up[l], k_k[l], k_a[l],
                                          r_k[l], gn_g[l], gn_b[l])
        x = x + jnp.concatenate([o_conv, o_pool, o_rwkv], axis=-1) @ w_out[l]
        h = rms_norm(x, norm_ffn[l])
        x = x + (jax.nn.silu(h @ ffn_gate[l]) * (h @ ffn_up[l])) @ ffn_down[l]
        conv_list.append(c_new)
        pool_list.append(p_new)
        shift_list.append(s_new)
        wkv_list.append(S_new)
    y = rms_norm(x, norm_final)
    return (y, jnp.stack(conv_list), jnp.stack(pool_list), jnp.stack(shift_list), jnp.stack(wkv_list))


def setup_inputs(seed: int = 0) -> dict:
    key = jax.random.key(seed)
    ks = iter(jax.random.split(key, 40))
    nrm = lambda shape, s: jax.random.normal(next(ks), shape, jnp.float32) * s
    uni = lambda shape, lo, hi: jax.random.uniform(next(ks), shape, jnp.float32, lo, hi)
    H, N = RWKV_HEADS, HEAD_SIZE
    return {
        'x_prompt': nrm((BATCH, SEQ, D_MODEL), 1.0),
        'x_sample': nrm((DEC_BATCH, DEC_SEQ, D_MODEL), 1.0),
        'cache_conv': nrm((DEPTH, DEC_BATCH, CONV_WIDTH - 1, CONV_DIM), 0.5),
        'cache_pool': nrm((DEPTH, DEC_BATCH, POOL_MAX - 1, POOL_DIM), 1.0),
        'state_shift': nrm((DEPTH, DEC_BATCH, 1, RWKV_PROJ), 1.0),
        'state_wkv': nrm((DEPTH, DEC_BATCH, H, N, N), 0.5),
        'norm_mix': 1.0 + nrm((DEPTH, D_MODEL), 0.05),
        'w_in': nrm((DEPTH, D_MODEL, IN_PROJ), D_MODEL ** -0.5),
        'conv_w': nrm((DEPTH, CONV_WIDTH, CONV_DIM), CONV_WIDTH ** -0.5),
        'conv_b': nrm((DEPTH, CONV_DIM), 0.02),
        'conv_ln_g': 1.0 + nrm((DEPTH, CONV_DIM), 0.05),
        'conv_ln_b': nrm((DEPTH, CONV_DIM), 0.02),
        'pool_w': nrm((DEPTH, POOL_GROUPS, POOL_GROUP_DIM, POOL_GROUP_DIM), POOL_GROUP_DIM ** -0.5),
        'pool_scale': uni((DEPTH, POOL_DIM), 0.5, 1.5),
        'shift_mu': uni((DEPTH, RWKV_PROJ), 0.0, 1.0),
        'decay_w0': uni((DEPTH, RWKV_DIM), -5.0, 1.0),
        'decay_up': nrm((DEPTH, DECAY_LORA, RWKV_DIM), 0.1),
        'iclr_a0': nrm((DEPTH, RWKV_DIM), 0.1),
        'iclr_up': nrm((DEPTH, AAA_LORA, RWKV_DIM), AAA_LORA ** -0.5),
        'gate_up': nrm((DEPTH, GATE_LORA, RWKV_DIM), GATE_LORA ** -0.5),
        'k_k': 0.85 + nrm((DEPTH, RWKV_DIM), 0.05),
        'k_a': 1.0 + nrm((DEPTH, RWKV_DIM), 0.05),
        'r_k': nrm((DEPTH, H, N), 0.1),
        'gn_g': 1.0 + nrm((DEPTH, RWKV_DIM), 0.05),
        'gn_b': nrm((DEPTH, RWKV_DIM), 0.02),
        'w_out': nrm((DEPTH, MIX_DIM, D_MODEL), MIX_DIM ** -0.5),
        'norm_ffn': 1.0 + nrm((DEPTH, D_MODEL), 0.05),
        'ffn_gate': nrm((DEPTH, D_MODEL, D_FF), D_MODEL ** -0.5),
        'ffn_up': nrm((DEPTH, D_MODEL, D_FF), D_MODEL ** -0.5),
        'ffn_down': nrm((DEPTH, D_FF, D_MODEL), D_FF ** -0.5),
        'norm_final': 1.0 + nrm((D_MODEL,), 0.05),
    }


def reference(x_prompt, x_sample, cache_conv, cache_pool, state_shift, state_wkv,
              norm_mix, w_in, conv_w, conv_b, conv_ln_g, conv_ln_b, pool_w, pool_scale,
              shift_mu, decay_w0, decay_up, iclr_a0, iclr_up, gate_up, k_k, k_a, r_k, gn_g, gn_b,
              w_out, norm_ffn, ffn_gate, ffn_up, ffn_down, norm_final):
    weights = (norm_mix, w_in, conv_w, conv_b, conv_ln_g, conv_ln_b, pool_w, pool_scale,
               shift_mu, decay_w0, decay_up, iclr_a0, iclr_up, gate_up, k_k, k_a, r_k, gn_g, gn_b,
               w_out, norm_ffn, ffn_gate, ffn_up, ffn_down, norm_final)
    bp = x_prompt.shape[0]
    dt = x_prompt.dtype
    zc = jnp.zeros((DEPTH, bp, CONV_WIDTH - 1, CONV_DIM), dt)
    zp = jnp.zeros((DEPTH, bp, POOL_MAX - 1, POOL_DIM), dt)
    zs = jnp.zeros((DEPTH, bp, 1, RWKV_PROJ), dt)
    zw = jnp.zeros((DEPTH, bp, RWKV_HEADS, HEAD_SIZE, HEAD_SIZE), jnp.float32)
    y_prompt, p_conv, p_pool, p_shift, p_wkv = trunk(x_prompt, zc, zp, zs, zw, 0, *weights)
    y_sample, s_conv, s_pool, s_shift, s_wkv = trunk(x_sample, cache_conv, cache_pool, state_shift,
                                                     state_wkv, PAST_LEN, *weights)
    return (y_prompt, y_sample, p_conv, p_pool, p_shift, p_wkv, s_conv, s_pool, s_shift, s_wkv)
```

```python
import numpy as np
import concourse.bass as bass
import concourse.mybir as mybir
from concourse.bass_utils import run_bass_kernel_spmd

F32 = mybir.dt.float32
BF16 = mybir.dt.bfloat16
AF = mybir.ActivationFunctionType
ALU = mybir.AluOpType

D = 2048
KC = 16
DFF = 5632
FC = 44
HEADS = 16
PAIRS = 8
NQ = 26
DEPTH = 4
SEQ = 2048
SLEN = 64
RMS_EPS = 1e-6
LN_EPS = 1e-5
GN_EPS = 64e-5
LW_SCALE = -float(np.exp(-0.5))
POOL_WINDOWS = (2, 4, 8, 16)

NBLK = 2 + 38 + 16 + 88 + 48
SLOT = 2048
NSLOT = 6

VO = {}
_o = 0
for _n, _w in (("nm", 16), ("nf", 16), ("cb", 4), ("cw", 124), ("lg", 4), ("lb", 4), ("psc", 4),
               ("mu", NQ), ("w0", 8), ("a0", 8), ("kk", 8), ("ka", 8), ("rk", 8), ("gg", 8), ("gb", 8)):
    VO[_n] = _o
    _o += _w
VL = _o
NVEC = DEPTH * VL + 16


class Sched:
    def __init__(self, nc):
        self.nc = nc
        self.eng = {"pe": nc.tensor, "act": nc.scalar, "dve": nc.vector, "pool": nc.gpsimd, "sp": nc.sync}
        self.semh = {}
        self.cnt = {}
        for e in self.eng:
            self.semh[e] = nc.alloc_semaphore("sem_" + e)
            self.cnt[e] = 0
        self.waited = {e: {} for e in self.eng}
        self.lastw = {}
        self.readers = {}
        self.dma_sems = []
        self.dma_rr = 0
        self.ninst = 0
        self.dead = False

    def new_dma_sem(self, name):
        self.semh[name] = self.nc.alloc_semaphore("sem_" + name)
        self.cnt[name] = 0
        return name

    def _deps(self, r, w):
        need = {}
        for k in r:
            t = self.lastw.get(k)
            if t is not None:
                need[t[0]] = max(need.get(t[0], 0), t[1])
        for k in w:
            t = self.lastw.get(k)
            if t is not None:
                need[t[0]] = max(need.get(t[0], 0), t[1])
            for t in self.readers.get(k, ()):
                need[t[0]] = max(need.get(t[0], 0), t[1])
        return need

    def _wait(self, e, need, skip_self=False):
        wd = self.waited[e]
        for s, v in need.items():
            if skip_self and s == e:
                continue
            if wd.get(s, 0) < v:
                self.eng[e].wait_ge(self.semh[s], v)
                wd[s] = v
                self.ninst += 1

    def _commit(self, tok, r, w):
        for k in r:
            lst = self.readers.setdefault(k, [])
            lst[:] = [t for t in lst if t[0] != tok[0]]
            lst.append(tok)
        for k in w:
            self.lastw[k] = tok
            self.readers[k] = []

    def op(self, e, fn, r=(), w=()):
        if self.dead:
            return None
        need = self._deps(r, w)
        self._wait(e, need, skip_self=(e == "pe"))
        inst = fn(self.eng[e])
        self.cnt[e] += 1
        inst.then_inc(self.semh[e], 1)
        self.ninst += 1
        tok = (e, self.cnt[e])
        self._commit(tok, r, w)
        return tok

    def pe_group(self, fns, r=(), w=(), pe_sync=False):
        if self.dead:
            return None
        need = self._deps(r, w)
        self._wait("pe", need, skip_self=not pe_sync)
        inst = None
        for fn in fns:
            inst = fn(self.eng["pe"])
            self.ninst += 1
        self.cnt["pe"] += 1
        inst.then_inc(self.semh["pe"], 1)
        tok = ("pe", self.cnt["pe"])
        self._commit(tok, r, w)
        return tok

    def dma(self, q, out, in_, r=(), w=(), sem=None):
        if self.dead:
            return None
        if sem is None:
            if len(self.dma_sems) < 24:
                sem = self.new_dma_sem("d%d" % len(self.dma_sems))
                self.dma_sems.append(sem)
            else:
                sem = self.dma_sems[self.dma_rr % len(self.dma_sems)]
                self.dma_rr += 1
        need = self._deps(r, w)
        if self.cnt[sem] > 0:
            need[sem] = max(need.get(sem, 0), self.cnt[sem])
        self._wait(q, need)
        self.eng[q].dma_start(out=out, in_=in_).then_inc(self.semh[sem], 16)
        self.cnt[sem] += 16
        self.ninst += 1
        tok = (sem, self.cnt[sem])
        self._commit(tok, r, w)
        return tok

    def barrier(self, engines=("pe", "act", "dve", "pool")):
        if self.dead:
            return
        for e in engines:
            need = {}
            for o in engines:
                if o != e and self.cnt[o] > 0:
                    need[o] = self.cnt[o]
            self._wait(e, need)

    def finish(self, e="sp"):
        need = {}
        for s, c in self.cnt.items():
            if c > 0 and s != e:
                need[s] = c
        self._wait(e, need)


class Arena:
    def __init__(self, nc, nbytes, name="arena"):
        assert nbytes % 4 == 0
        self.t = nc.alloc_sbuf_tensor(name, [128, nbytes // 4], F32)
        self.off = 0
        self.cap = nbytes

    def alloc(self, shape, dt, at=None, name=None):
        esz = 4 if dt == F32 else 2
        n = 1
        for s in shape[1:]:
            n *= s
        nb = (n * esz + 31) // 32 * 32
        if at is None:
            at = self.off
            self.off += nb
            assert self.off <= self.cap, ("arena overflow", self.off, self.cap)
        if not hasattr(self, "reg"):
            self.reg = {}
        self.reg[name if name is not None else "anon%d" % len(self.reg)] = (at, list(shape), "f32" if dt == F32 else "bf16")
        v = self.t[:, at // 4:(at + nb) // 4]
        if dt != F32:
            v = v.bitcast(dt)
        v = v[:, 0:n]
        if len(shape) == 3:
            v = v.rearrange("p (a b) -> p a b", b=shape[2])
        elif len(shape) == 4:
            v = v.rearrange("p (a b c) -> p a b c", b=shape[2], c=shape[3])
        return v


def build_program(layers=DEPTH, npg=4, dbg=None, with_s=True):
    nc = bass.Bass("TRN2", target_bir_lowering=False)
    S = Sched(nc)
    nseq_tok = 512 * npg

    xp = nc.dram_tensor("xp", [nseq_tok, D], F32, kind="ExternalInput").ap()
    xs = nc.dram_tensor("xs", [2 * SLEN, D], F32, kind="ExternalInput").ap()
    cconv = nc.dram_tensor("cconv", [DEPTH, 2, 30, 512], F32, kind="ExternalInput").ap()
    cpool = nc.dram_tensor("cpool", [DEPTH, 2, 15, 512], F32, kind="ExternalInput").ap()
    cshift = nc.dram_tensor("cshift", [DEPTH, 2, NQ, 128], F32, kind="ExternalInput").ap()
    cwkv = nc.dram_tensor("cwkv", [DEPTH, 2, HEADS, 64, 64], F32, kind="ExternalInput").ap()
    wblk = nc.dram_tensor("wblk", [layers, NBLK, 128, SLOT], F32, kind="ExternalInput").ap()
    vecs_d = nc.dram_tensor("vecs", [128, NVEC], F32, kind="ExternalInput").ap()
    yp = nc.dram_tensor("yp", [nseq_tok, D], F32, kind="ExternalOutput").ap()
    ys = nc.dram_tensor("ys", [2 * SLEN, D], F32, kind="ExternalOutput").ap()
    nconv = nc.dram_tensor("nconv", [DEPTH, 3, 30, 512], F32, kind="ExternalOutput").ap()
    npool = nc.dram_tensor("npool", [DEPTH, 3, 15, 512], F32, kind="ExternalOutput").ap()
    nshift = nc.dram_tensor("nshift", [DEPTH, 3, NQ, 128], F32, kind="ExternalOutput").ap()
    nwkv = nc.dram_tensor("nwkv", [DEPTH, 3, HEADS, 64, 64], F32, kind="ExternalOutput").ap()

    A = Arena(nc, 212736)
    vecs = A.alloc([128, NVEC], F32)
    omu = A.alloc([128, DEPTH, NQ], F32)
    ident_f = A.alloc([128, 128], F32)
    ident_b = A.alloc([128, 128], BF16)
    ones_b = A.alloc([128, 128], BF16)
    bones_b = A.alloc([128, 128], BF16)
    bones_f = A.alloc([128, 128], F32)
    onesD_f = A.alloc([128, 128], F32)
    m_su = A.alloc([128, 128], F32)
    m_ui = A.alloc([128, 128], F32)
    m_sl = A.alloc([128, 128], F32)
    cmask = {64: A.alloc([128, 64], F32), 128: A.alloc([128, 512], F32)}
    invc_first = A.alloc([128, 4, 16], F32)
    st = {}
    for l in range(DEPTH):
        st[("P", l)] = dict(u=A.alloc([128, 4, 30], F32), p=A.alloc([128, 4, 15], F32),
                            q=A.alloc([128, NQ], F32), S=A.alloc([128, PAIRS, 64], F32))
    for sq in ("S0", "S1"):
        d_ = dict(u=A.alloc([128, 4, 30], F32), p=A.alloc([128, 4, 15], F32),
                  q=A.alloc([128, NQ], F32), S=A.alloc([128, PAIRS, 64], F32))
        for l in range(DEPTH):
            st[(sq, l)] = d_
    xT = A.alloc([128, KC, 512], F32, name='xT')
    hT = A.alloc([128, KC, 512], BF16, name='hT')
    cat = A.alloc([128, KC, 512], BF16, name='cat')
    wring = A.alloc([128, NSLOT, SLOT], BF16)
    wsmall = A.alloc([128, SLOT], BF16)
    wpool = A.alloc([128, 512], BF16)
    rstd = A.alloc([128, 512], F32)
    sqb = A.alloc([128, 512], BF16)
    base_off = A.off

    def mixer_bufs(Wm):
        b = {}
        o0 = A.off
        b["ubuf"] = A.alloc([128, 4, 30 + Wm], F32, name="mb%d_ubuf" % Wm)
        b["ubf"] = A.alloc([128, 4, 30 + Wm], BF16)
        b["hconv"] = A.alloc([128, 4, Wm], F32)
        o1 = A.off
        b["pbuf"] = A.alloc([128, 4, 15 + Wm], F32, at=o0)
        b["dpool"] = A.alloc([128, 4, Wm], BF16, at=o0 + (4 * (15 + Wm) * 4 + 31) // 32 * 32)
        assert o0 + (4 * (15 + Wm) * 4 + 31) // 32 * 32 + 4 * Wm * 2 <= o1
        for n in ("t0", "t1", "t2", "t3", "t4", "t5", "t6", "t7", "t8", "t9", "t10", "t11"):
            b["off_" + n] = A.off
            b[n] = A.alloc([128, Wm + 16], F32, name="mb%d_%s" % (Wm, n))
        for n in ("b0", "b1", "b2", "b3", "b4", "b5"):
            b[n] = A.alloc([128, Wm], BF16, name="mb%d_%s" % (Wm, n))
        b["AR"] = A.alloc([128, 2 * Wm], BF16, name="mb%d_AR" % Wm)
        b["lora"] = A.alloc([128, 2, Wm], BF16, name="mb%d_lora" % Wm)
        b["gl"] = A.alloc([128, Wm], F32)
        return b
    mb = [mixer_bufs(512), mixer_bufs(64)]
    diag = A.alloc([128, 31, 128], BF16, at=mb[0]['off_t8'])
    tm = {}
    for h in range(2):
        tm[("S1m", h)] = A.alloc([128, 2, 128], BF16, name="tm_S1m%d" % h)
        tm[("S2m", h)] = A.alloc([128, 2, 128], BF16, name="tm_S2m%d" % h)
    for n in ("L0", "L1", "N0", "N1", "X0", "X1"):
        tm[n] = A.alloc([128, 128], BF16, name="tm_" + n)
    tm["KBV"] = A.alloc([128, 3, 128], BF16, name="tm_KBV")
    tm["Psb"] = A.alloc([128, 64], BF16, name="tm_Psb")
    tm["Usb"] = A.alloc([128, 64], BF16, name="tm_Usb")
    tm["STb"] = A.alloc([128, 64], BF16, name="tm_STb")
    tm["gC"] = A.alloc([128, 8], F32, name="tm_gC")
    mc = {64: A.alloc([128, 2, 64], F32), 128: A.alloc([128, 2, 128], F32)}
    stage = mb[0]["hconv"].rearrange("p a b -> p (a b)")
    stage2 = A.alloc([128, 1024], F32)
    stage3 = A.alloc([128, 640], F32)
    mix_end = A.off
    act = A.alloc([128, FC, 512], BF16, at=base_off)
    assert base_off + FC * 512 * 2 <= A.cap
    A.off = max(mix_end, base_off + FC * 512 * 2)
    ftmp = [A.alloc([128, 512], F32), A.alloc([128, 512], F32)]
    print("SBUF used", A.off, "of", A.cap)

    psb = [nc.alloc_psum_tensor("ps%d" % i, [128, 512], F32) for i in range(8)]
    bank_rr = {"big": 0, "small": 0}

    def bank(pool):
        if pool == "big":
            i = bank_rr["big"] % 3
            bank_rr["big"] += 1
            return i
        if pool == "y":
            return 3
        if pool == "state":
            return 7
        i = 4 + bank_rr["small"] % 3
        bank_rr["small"] += 1
        return i

    psap = [t[:, :] for t in psb]

    def PS(i):
        return psap[i]

    wsem = [S.new_dma_sem("w%d" % i) for i in range(NSLOT)]
    wsem_small = S.new_dma_sem("wsm")
    wq = {"next": 0, "issued": 0, "order": []}

    def w_issue_upto(n):
        while wq["issued"] < min(n, len(wq["order"])):
            i = wq["issued"]
            (l, b, ncols) = wq["order"][i]
            slot = i % NSLOT
            S.dma("pool", wring[:, slot, 0:ncols], wblk[l, b, :, 0:ncols], w=[("w", slot)], sem=wsem[slot])
            wq["issued"] += 1

    def w_next():
        i = wq["next"]
        wq["next"] += 1
        w_issue_upto(i + NSLOT - 1)
        return i % NSLOT

    def blk_in(cc):
        return 2 + cc
    def blk_out(n):
        return 2 + 38 + n
    def blk_gate(f):
        return 2 + 38 + 16 + 2 * f
    def blk_up(f):
        return 2 + 38 + 16 + 2 * f + 1
    def blk_down(n, j):
        return 2 + 38 + 16 + 88 + 3 * n + j
    IN_ORDER = [4, 0, 5, 1, 6, 2, 7, 3, 8, 9, 10, 11, 36, 37]
    for p_ in range(PAIRS):
        IN_ORDER += [12 + p_, 20 + p_, 28 + p_]

    def layer_order(l):
        o = [(l, blk_in(cc), SLOT) for cc in IN_ORDER]
        o += [(l, blk_out(n), SLOT) for n in range(16)]
        for f in range(FC):
            o += [(l, blk_gate(f), SLOT), (l, blk_up(f), SLOT)]
        for n in range(16):
            o += [(l, blk_down(n, 0), SLOT), (l, blk_down(n, 1), SLOT), (l, blk_down(n, 2), 12 * 128)]
        return o

    def pool_op(fn, r=(), w=()):
        return S.op("pool", fn, r, w)

    S.dma("sp", vecs, vecs_d[:, :], w=["vecs"])
    pool_op(lambda e: e.memset(ident_f, 1.0), w=["c_if"])
    pool_op(lambda e: e.affine_select(out=ident_f, in_=ident_f, pattern=[[-1, 128]], compare_op=ALU.is_equal,
                                      fill=0.0, base=0, channel_multiplier=1), r=["c_if"], w=["c_if"])
    S.op("dve", lambda e: e.tensor_copy(out=ident_b, in_=ident_f), r=["c_if"], w=["c_ib"])
    S.op("dve", lambda e: e.memset(ones_b, 1.0), w=["c_ones"])
    S.op("dve", lambda e: e.memset(onesD_f, 1.0 / 512.0), w=["c_onesD"])
    S.op("dve", lambda e: e.memset(bones_b, 0.0), w=["c_bones"])
    S.op("dve", lambda e: e.memset(bones_b[0:64, 0:64], 1.0), w=["c_bones"])
    S.op("dve", lambda e: e.memset(bones_b[64:128, 64:128], 1.0), w=["c_bones"])
    S.op("dve", lambda e: e.memset(bones_f, 0.0), w=["c_bonesf"])
    S.op("dve", lambda e: e.memset(bones_f[0:64, 0:64], 1.0 / 64.0), w=["c_bonesf"])
    S.op("dve", lambda e: e.memset(bones_f[64:128, 64:128], 1.0 / 64.0), w=["c_bonesf"])
    for (m, base, cm, step) in ((m_su, -1, -1, 1), (m_ui, 0, -1, 1), (m_sl, -1, 1, -1)):
        pool_op(lambda e, m=m: e.memset(m, 1.0), w=["c_masks"])
        pool_op(lambda e, m=m, base=base, cm=cm, step=step: e.affine_select(
            out=m, in_=m, pattern=[[step, 128]], compare_op=ALU.is_ge, fill=0.0, base=base,
            channel_multiplier=cm), r=["c_masks"], w=["c_masks"])
    for C in (64, 128):
        S.op("dve", lambda e, C=C: e.tensor_copy(out=mc[C][:, 0, :], in_=m_su[:, 0:C]), r=["c_masks"], w=["c_mc"])
        S.op("dve", lambda e, C=C: e.tensor_copy(out=mc[C][:, 1, :], in_=m_ui[:, 0:C]), r=["c_masks"], w=["c_mc"])
        S.op("dve", lambda e, C=C: e.memset(cmask[C], 1.0), w=["c_cmask"])
        S.op("dve", lambda e, C=C: e.memset(cmask[C].rearrange("p (a b) -> p a b", b=C)[:, :, 0:1], 0.0),
             r=["c_cmask"], w=["c_cmask"])
    pool_op(lambda e: e.iota(out=invc_first[:, 0, :], pattern=[[1, 16]], base=1, channel_multiplier=0,
                             allow_small_or_imprecise_dtypes=True), w=["c_invc"])
    for gi, wdw in enumerate(POOL_WINDOWS):
        if gi > 0:
            S.op("dve", lambda e, gi=gi: e.tensor_copy(out=invc_first[:, gi, :], in_=invc_first[:, 0, :]),
                 r=["c_invc"], w=["c_invc%d" % gi])
    for gi, wdw in enumerate(POOL_WINDOWS):
        S.op("dve", lambda e, gi=gi, wdw=wdw: e.tensor_scalar(out=invc_first[:, gi, :], in0=invc_first[:, gi, :],
                                                              scalar1=float(wdw), scalar2=None, op0=ALU.min),
             r=["c_invc", "c_invc%d" % gi], w=["c_invc%d" % gi] + (["c_invc"] if gi == 0 else []))
        S.op("dve", lambda e, gi=gi: e.reciprocal(out=invc_first[:, gi, :], in_=invc_first[:, gi, :]),
             r=["c_invc%d" % gi], w=["c_invc%d" % gi] + (["c_invc"] if gi == 0 else []))
    for l in range(DEPTH):
        o = l * VL + VO["mu"]
        S.op("dve", lambda e, l=l, o=o: e.tensor_scalar(out=omu[:, l, :], in0=vecs[:, o:o + NQ], scalar1=-1.0,
                                                        scalar2=1.0, op0=ALU.mult, op1=ALU.add),
             r=["vecs"], w=["omu"])
    CONST_KEYS = ["vecs", "omu", "c_if", "c_ib", "c_ones", "c_onesD", "c_bones", "c_bonesf", "c_masks",
                  "c_cmask", "c_onesrow", "c_invc", "c_invc1", "c_invc2", "c_invc3"]
    S.barrier()

    def V(l, name, c0=0, n=1):
        o = l * VL + VO[name] + c0
        return vecs[:, o:o + n]

    def rmsnorm_to(dst_fn, gw, gcol, key_out, l_vec_off, dst_is_bf=True):
        b = bank("small")
        fns = []
        for k in range(KC):
            S.op("act", lambda e, k=k: e.activation(out=sqb[:, 0:gw], in_=xT[:, k, 0:gw], func=AF.Square),
                 r=["xT"], w=["sqb"])
            S.pe_group([lambda e, k=k: e.matmul(PS(b)[:, 0:gw], lhsT=ones_b, rhs=sqb[:, 0:gw],
                                                 start=(k == 0), stop=(k == KC - 1))],
                       r=["sqb"], w=[("ps", b)])
        S.op("act", lambda e: e.activation(out=rstd[:, 0:gw], in_=PS(b)[:, 0:gw], func=AF.Sqrt,
                                           bias=RMS_EPS, scale=1.0 / D), r=[("ps", b)], w=["rstd"])
        S.op("dve", lambda e: e.reciprocal(out=rstd[:, 0:gw], in_=rstd[:, 0:gw]), r=["rstd"], w=["rstd"])
        for k in range(KC):
            S.op("dve", lambda e, k=k: e.scalar_tensor_tensor(
                out=dst_fn(k), in0=xT[:, k, 0:gw], scalar=vecs[:, l_vec_off + k:l_vec_off + k + 1],
                in1=rstd[:, 0:gw], op0=ALU.mult, op1=ALU.mult), r=["xT", "rstd"], w=[key_out])

    def proj(slot, src, gw, key_src, kchunks=KC, b=None):
        if b is None:
            b = bank("big")
        wv = wring[:, slot, :].rearrange("p (k n) -> p k n", n=128)
        fns = [lambda e, k=k: e.matmul(PS(b)[:, 0:gw], lhsT=wv[:, k, :], rhs=src[:, k, 0:gw],
                                       start=(k == 0), stop=(k == kchunks - 1)) for k in range(kchunks)]
        S.pe_group(fns, r=[("w", slot), key_src], w=[("ps", b)])
        return b

    def shifted_from_psum(l, bq, qc, p, dst, key_dst, scratch, key_scr):
        W, off, bi = p["W"], p["off"], p["bi"]
        sk = (p["seq"], l)
        sd = st[sk]
        S.op("act", lambda e: e.activation(out=scratch[:, 0:W], in_=PS(bq)[:, off:off + W], func=AF.Identity,
                                           scale=omu[:, l, qc:qc + 1]), r=[("ps", bq)], w=[key_scr])
        S.op("dve", lambda e: e.scalar_tensor_tensor(
            out=dst[:, 1:W], in0=PS(bq)[:, off:off + W - 1], scalar=V(l, "mu", qc), in1=scratch[:, 1:W],
            op0=ALU.mult, op1=ALU.add), r=[("ps", bq), key_scr], w=[key_dst])
        S.op("dve", lambda e: e.scalar_tensor_tensor(
            out=dst[:, 0:1], in0=sd["q"][:, qc:qc + 1], scalar=V(l, "mu", qc), in1=scratch[:, 0:1],
            op0=ALU.mult, op1=ALU.add), r=[("st_q",) + sk, key_scr], w=[key_dst])
        S.op("act", lambda e: e.activation(out=sd["q"][:, qc:qc + 1], in_=PS(bq)[:, off + W - 1:off + W],
                                           func=AF.Copy), r=[("ps", bq), key_dst], w=[("st_q",) + sk])

    def wkv_pair(l, pr, p, br, bk, bv_):
        B = mb[p["bi"]]
        W, off, bi, C = p["W"], p["off"], p["bi"], p["C"]
        nch = W // C
        sk = (p["seq"], l)
        sd = st[sk]
        T = lambda n: B[n][:, 0:W]
        K = lambda n: (n, bi)
        c3 = lambda ap: ap.rearrange("p (a c) -> p a c", c=C)
        shifted_from_psum(l, br, pr, p, B["t1"], K("t1"), B["t0"], K("t0"))
        shifted_from_psum(l, bk, 8 + pr, p, B["t2"], K("t2"), B["t0"], K("t0"))
        shifted_from_psum(l, bv_, 16 + pr, p, B["t3"], K("t3"), B["t0"], K("t0"))
        r_, k_, v_ = T("t1"), T("t2"), T("t3")
        bw = bank("small")
        S.pe_group([lambda e: e.matmul(PS(bw)[:, 0:W], lhsT=wsmall[0:64, pr * 128:(pr + 1) * 128],
                                       rhs=B["lora"][0:64, 0, 0:W], start=True, stop=True)],
                   r=["wsmall", K("lora")], w=[("ps", bw)])
        lw = T("t4")
        S.op("act", lambda e: e.activation(out=lw, in_=PS(bw)[:, 0:W], func=AF.Sigmoid,
                                           bias=V(l, "w0", pr), scale=1.0), r=[("ps", bw)], w=[K("t4")])
        ba = bank("small")
        S.pe_group([lambda e: e.matmul(PS(ba)[:, 0:W], lhsT=wsmall[64:128, pr * 128:(pr + 1) * 128],
                                       rhs=B["lora"][64:128, 0, 0:W], start=True, stop=True)],
                   r=["wsmall", K("lora")], w=[("ps", ba)])
        a_ = T("t5")
        S.op("act", lambda e: e.activation(out=a_, in_=PS(ba)[:, 0:W], func=AF.Sigmoid,
                                           bias=V(l, "a0", pr), scale=1.0), r=[("ps", ba)], w=[K("t5")])
        bgp = bank("small")
        S.pe_group([lambda e: e.matmul(PS(bgp)[:, 0:W], lhsT=wsmall[0:64, 1024 + pr * 128:1024 + (pr + 1) * 128],
                                       rhs=B["lora"][0:64, 1, 0:W], start=True, stop=True)],
                   r=["wsmall", K("lora")], w=[("ps", bgp)])
        g_ = T("t6")
        S.op("act", lambda e: e.activation(out=g_, in_=PS(bgp)[:, 0:W], func=AF.Copy), r=[("ps", bgp)], w=[K("t6")])
        kk = T("t7")
        S.op("dve", lambda e: e.tensor_scalar(out=kk, in0=k_, scalar1=V(l, "kk", pr), scalar2=None, op0=ALU.mult),
             r=[K("t2")], w=[K("t7")])
        S.op("act", lambda e: e.activation(out=T("b0"), in_=kk, func=AF.Square), r=[K("t7")], w=[K("b0")])
        bs = bank("small")
        S.pe_group([lambda e: e.matmul(PS(bs)[:, 0:W], lhsT=bones_b, rhs=T("b0"), start=True, stop=True)],
                   r=[K("b0")], w=[("ps", bs)])
        nrm = T("t8")
        S.op("dve", lambda e: e.tensor_scalar(out=nrm, in0=PS(bs)[:, 0:W], scalar1=1e-24, scalar2=None, op0=ALU.max),
             r=[("ps", bs)], w=[K("t8")])
        S.op("act", lambda e: e.activation(out=nrm, in_=nrm, func=AF.Sqrt), r=[K("t8")], w=[K("t8")])
        S.op("dve", lambda e: e.reciprocal(out=nrm, in_=nrm), r=[K("t8")], w=[K("t8")])
        S.op("dve", lambda e: e.tensor_tensor(out=kk, in0=kk, in1=nrm, op=ALU.mult), r=[K("t7"), K("t8")], w=[K("t7")])
        bvec = T("t8")
        S.op("dve", lambda e: e.tensor_tensor(out=bvec, in0=kk, in1=a_, op=ALU.mult), r=[K("t7"), K("t5")], w=[K("t8")])
        kp = T("t9")
        S.op("dve", lambda e: e.tensor_scalar(out=kp, in0=a_, scalar1=-1.0, scalar2=V(l, "ka", pr), op0=ALU.add,
                                              op1=ALU.mult), r=[K("t5")], w=[K("t9")])
        S.op("dve", lambda e: e.scalar_tensor_tensor(out=kp, in0=kp, scalar=1.0, in1=k_, op0=ALU.add, op1=ALU.mult),
             r=[K("t9"), K("t2")], w=[K("t9")])
        S.op("dve", lambda e: e.scalar_tensor_tensor(out=T("b0"), in0=r_, scalar=V(l, "rk", pr), in1=kp, op0=ALU.mult,
                                                     op1=ALU.mult), r=[K("t1"), K("t9")], w=[K("b0")])
        bb = bank("small")
        S.pe_group([lambda e: e.matmul(PS(bb)[:, 0:W], lhsT=bones_b, rhs=T("b0"), start=True, stop=True)],
                   r=[K("b0")], w=[("ps", bb)])
        bonus = T("t10")
        S.op("dve", lambda e: e.tensor_tensor(out=bonus, in0=PS(bb)[:, 0:W], in1=v_, op=ALU.mult),
             r=[("ps", bb), K("t3")], w=[K("t10")])
        S.op("dve", lambda e: e.tensor_scalar(out=lw, in0=lw, scalar1=LW_SCALE, scalar2=None, op0=ALU.mult),
             r=[K("t4")], w=[K("t4")])
        cl = T("t11")
        S.op("dve", lambda e: e.tensor_tensor_scan(out=cl, data0=cmask[C][:, 0:W], data1=lw, initial=0.0,
                                                   op0=ALU.mult, op1=ALU.add), r=[K("t4")], w=[K("t11")])
        gC = tm["gC"]
        S.op("act", lambda e: e.activation(out=gC[:, 0:nch], in_=c3(cl)[:, :, C - 1], func=AF.Exp),
             r=[K("t11")], w=["gC"])
        e_pos = T("t0")
        S.op("act", lambda e: e.activation(out=e_pos, in_=cl, func=AF.Exp), r=[K("t11")], w=[K("t0")])
        AR = B["AR"][:, 0:2 * W].rearrange("p (a two c) -> p a two c", two=2, c=C)
        S.op("dve", lambda e: e.tensor_tensor(out=AR[:, :, 1, :], in0=c3(r_), in1=c3(e_pos), op=ALU.mult),
             r=[K("t1"), K("t0")], w=[K("AR")])
        S.op("dve", lambda e: e.tensor_tensor(out=lw, in0=cl, in1=lw, op=ALU.subtract), r=[K("t11"), K("t4")],
             w=[K("t4")])
        S.op("act", lambda e: e.activation(out=lw, in_=lw, func=AF.Exp), r=[K("t4")], w=[K("t4")])
        S.op("dve", lambda e: e.scalar_tensor_tensor(out=AR[:, :, 0, :], in0=c3(kk), scalar=-1.0, in1=c3(lw),
                                                     op0=ALU.mult, op1=ALU.mult), r=[K("t7"), K("t4")], w=[K("AR")])
        e_neg = T("t0")
        S.op("act", lambda e: e.activation(out=e_neg, in_=cl, func=AF.Exp, scale=-1.0), r=[K("t11"), K("AR")],
             w=[K("t0")])
        kt, bt, kh, bh, vb = T("b1"), T("b2"), T("b3"), T("b4"), T("b5")
        S.op("dve", lambda e: e.tensor_tensor(out=kp, in0=kp, in1=e_neg, op=ALU.mult), r=[K("t9"), K("t0")], w=[K("t9")])
        S.op("act", lambda e: e.activation(out=kt, in_=kp, func=AF.Copy), r=[K("t9")], w=[K("b1")])
        S.op("dve", lambda e: e.tensor_tensor(out=bvec, in0=bvec, in1=e_neg, op=ALU.mult), r=[K("t8"), K("t0")],
             w=[K("t8")])
        S.op("act", lambda e: e.activation(out=bt, in_=bvec, func=AF.Copy), r=[K("t8")], w=[K("b2")])
        for ch in range(nch):
            cs = slice(ch * C, (ch + 1) * C)
            S.op("dve", lambda e, cs=cs, ch=ch: e.tensor_scalar(out=kh[:, cs], in0=kp[:, cs], scalar1=gC[:, ch:ch + 1],
                                                                scalar2=None, op0=ALU.mult), r=[K("t9"), "gC"], w=[K("b3")])
            S.op("dve", lambda e, cs=cs, ch=ch: e.tensor_scalar(out=bh[:, cs], in0=bvec[:, cs], scalar1=gC[:, ch:ch + 1],
                                                                scalar2=None, op0=ALU.mult), r=[K("t8"), "gC"], w=[K("b4")])
        S.op("act", lambda e: e.activation(out=vb, in_=v_, func=AF.Copy), r=[K("t3")], w=[K("b5")])

        by = bank("y")
        STf = sd["S"][:, pr, :]
        STb = tm["STb"]
        S.op("act", lambda e: e.activation(out=STb, in_=STf, func=AF.Copy), r=[("st_S",) + sk], w=["STb"])
        nupd = {64: 5, 128: 6}[C]
        for ch in range(nch):
            cs = slice(ch * C, (ch + 1) * C)
            ARc = AR[:, ch, :, :]
            ARf = B["AR"][:, ch * 2 * C:(ch + 1) * 2 * C]
            btp = bank("small")
            ptv = PS(btp)[:, 0:192].bitcast(BF16).rearrange("p (a c) -> p a c", c=128)
            S.pe_group([lambda e, src=src, i=i: e.transpose(out=ptv[0:C, i, :], in_=src[:, cs], identity=ident_b)
                        for i, src in enumerate((kh, bh, vb))], r=[K("b3"), K("b4"), K("b5")], w=[("ps", btp)])
            S.op("act", lambda e: e.activation(out=tm["KBV"][0:C], in_=ptv[0:C], func=AF.Copy),
                 r=[("ps", btp)], w=["KBV"])
            KH, BH, VT = tm["KBV"][0:C, 0, :], tm["KBV"][0:C, 1, :], tm["KBV"][0:C, 2, :]
            bS = bank("state")
            for h in range(2):
                hs = slice(h * 64, h * 64 + 64)
                S1m = tm[("S1m", h)][0:C, :, 0:C]
                S2m = tm[("S2m", h)][0:C, :, 0:C]
                b1 = bank("small")
                S.pe_group([lambda e: e.matmul(PS(b1)[0:C, 0:2 * C], lhsT=kt[hs, cs], rhs=ARf[hs, :], start=True, stop=True)],
                           r=[K("b1"), K("AR")], w=[("ps", b1)])
                S.op("dve", lambda e: e.tensor_tensor(out=S1m, in0=PS(b1)[0:C, 0:2 * C].rearrange("p (a c) -> p a c", c=C),
                                                      in1=mc[C][0:C], op=ALU.mult), r=[("ps", b1)], w=[("S1m", h)])
                b2 = bank("small")
                S.pe_group([lambda e: e.matmul(PS(b2)[0:C, 0:2 * C], lhsT=bt[hs, cs], rhs=ARf[hs, :], start=True, stop=True)],
                           r=[K("b2"), K("AR")], w=[("ps", b2)])
                S.op("dve", lambda e: e.tensor_tensor(out=S2m, in0=PS(b2)[0:C, 0:2 * C].rearrange("p (a c) -> p a c", c=C),
                                                      in1=mc[C][0:C], op=ALU.mult), r=[("ps", b2)], w=[("S2m", h)])
                b3 = bank("small")
                S.pe_group([lambda e: e.matmul(PS(b3)[0:C, 0:C], lhsT=ARf[hs, 0:C], rhs=bt[hs, cs], start=True, stop=True)],
                           r=[K("b2"), K("AR")], w=[("ps", b3)])
                Lc, Nc, Xc = tm["L0"][0:C, 0:C], S2m[:, 0, :], tm["X0"][0:C, 0:C]
                lkey, nkey, xkey = "L0", ("S2m", h), "X0"
                S.op("dve", lambda e, Lc=Lc: e.tensor_tensor(out=Lc, in0=PS(b3)[0:C, 0:C], in1=m_sl[0:C, 0:C], op=ALU.mult),
                     r=[("ps", b3)], w=[lkey])
                S.op("pool", lambda e, Xc=Xc, Nc=Nc: e.tensor_tensor(out=Xc, in0=Nc, in1=ident_b[0:C, 0:C], op=ALU.add),
                     r=[nkey], w=[xkey])
                for u in range(nupd):
                    L2k, N2k, X2k = "L%d" % ((u + 1) % 2), "N%d" % ((u + 1) % 2), "X%d" % ((u + 1) % 2)
                    L2, N2, X2 = tm[L2k][0:C, 0:C], tm[N2k][0:C, 0:C], tm[X2k][0:C, 0:C]
                    bl = bank("small")
                    S.pe_group([lambda e, Nc=Nc, Lc=Lc, bl=bl: e.matmul(PS(bl)[0:C, 0:C], lhsT=Nc, rhs=Lc, start=True, stop=True)],
                               r=[lkey, nkey], w=[("ps", bl)])
                    S.op("act", lambda e, L2=L2, bl=bl: e.activation(out=L2, in_=PS(bl)[0:C, 0:C], func=AF.Copy),
                         r=[("ps", bl)], w=[L2k])
                    if u < nupd - 1:
                        bn = bank("small")
                        S.pe_group([lambda e, Nc=Nc, Lc=Lc, bn=bn: e.matmul(PS(bn)[0:C, 0:C], lhsT=Lc, rhs=Nc, start=True, stop=True)],
                                   r=[lkey, nkey], w=[("ps", bn)])
                        S.op("act", lambda e, N2=N2, bn=bn: e.activation(out=N2, in_=PS(bn)[0:C, 0:C], func=AF.Copy),
                             r=[("ps", bn)], w=[N2k])
                    bx = bank("small")
                    S.pe_group([lambda e, L2=L2, Xc=Xc, bx=bx: e.matmul(PS(bx)[0:C, 0:C], lhsT=L2, rhs=Xc, start=True, stop=True)],
                               r=[L2k, xkey], w=[("ps", bx)])
                    S.op("dve", lambda e, X2=X2, Xc=Xc, bx=bx: e.tensor_tensor(out=X2, in0=PS(bx)[0:C, 0:C], in1=Xc, op=ALU.add),
                         r=[("ps", bx), xkey], w=[X2k])
                    Lc, Nc, Xc = L2, N2, X2
                    lkey, nkey, xkey = L2k, N2k, X2k
                bP = bank("small")
                rowsplit = (C == 64 and h == 1)
                if not rowsplit:
                    S.pe_group([lambda e: e.matmul(PS(bP)[0:C, 0:64], lhsT=ARf[hs, 0:C], rhs=STb[hs, :], start=True, stop=False),
                                lambda e: e.matmul(PS(bP)[0:C, 0:64], lhsT=S1m[:, 0, :], rhs=VT[:, hs], start=False, stop=True)],
                               r=[K("AR"), "STb", ("S1m", h), "KBV"], w=[("ps", bP)])
                else:
                    S.pe_group([lambda e: e.matmul(PS(bP)[0:C, 0:64], lhsT=ARf[hs, 0:C], rhs=STb[hs, :], start=True, stop=False)],
                               r=[K("AR"), "STb"], w=[("ps", bP)], pe_sync=True)
                    S.pe_group([lambda e: e.matmul(PS(bP)[0:C, 0:64], lhsT=S1m[:, 0, :], rhs=VT[:, hs], start=False, stop=True)],
                               r=[("S1m", h), "KBV"], w=[("ps", bP)], pe_sync=True)
                Psb, Usb = tm["Psb"][0:C, :], tm["Usb"][0:C, :]
                S.op("act", lambda e: e.activation(out=Psb, in_=PS(bP)[0:C, 0:64], func=AF.Copy), r=[("ps", bP)], w=["Psb"])
                bU = bank("small")
                S.pe_group([lambda e, Xc=Xc: e.matmul(PS(bU)[0:C, 0:64], lhsT=Xc, rhs=Psb, start=True, stop=True)],
                           r=[xkey, "Psb"], w=[("ps", bU)])
                S.op("act", lambda e: e.activation(out=Usb, in_=PS(bU)[0:C, 0:64], func=AF.Copy), r=[("ps", bU)], w=["Usb"])
                S.pe_group([lambda e: e.matmul(PS(bS)[hs, 0:64], lhsT=BH[:, hs], rhs=Usb, start=True, stop=False),
                            lambda e: e.matmul(PS(bS)[hs, 0:64], lhsT=KH[:, hs], rhs=VT[:, hs], start=False, stop=True)],
                           r=["KBV", "Usb"], w=[("ps", bS)], pe_sync=(C == 64))
                if not rowsplit:
                    S.pe_group([lambda e: e.matmul(PS(by)[hs, cs], lhsT=STb[hs, :], rhs=ARf[hs, C:2 * C], start=True, stop=False),
                                lambda e: e.matmul(PS(by)[hs, cs], lhsT=Usb, rhs=S2m[:, 1, :], start=False, stop=False),
                                lambda e: e.matmul(PS(by)[hs, cs], lhsT=VT[:, hs], rhs=S1m[:, 1, :], start=False, stop=True)],
                               r=["STb", K("AR"), "Usb", ("S2m", h), ("S1m", h), "KBV"], w=[("ps", by)], pe_sync=(C == 64))
                else:
                    S.pe_group([lambda e: e.matmul(PS(by)[hs, cs], lhsT=STb[hs, :], rhs=ARf[hs, C:2 * C], start=True, stop=False)],
                               r=["STb", K("AR")], w=[("ps", by)], pe_sync=True)
                    S.pe_group([lambda e: e.matmul(PS(by)[hs, cs], lhsT=Usb, rhs=S2m[:, 1, :], start=False, stop=False),
                                lambda e: e.matmul(PS(by)[hs, cs], lhsT=VT[:, hs], rhs=S1m[:, 1, :], start=False, stop=True)],
                               r=["Usb", ("S2m", h), ("S1m", h), "KBV"], w=[("ps", by)], pe_sync=True)
            S.op("dve", lambda e, ch=ch: e.scalar_tensor_tensor(out=STf, in0=STf, scalar=gC[:, ch:ch + 1], in1=PS(bS)[:, 0:64],
                                                                op0=ALU.mult, op1=ALU.add),
                 r=[("ps", bS), ("st_S",) + sk, "gC"], w=[("st_S",) + sk])
            S.op("act", lambda e: e.activation(out=STb, in_=STf, func=AF.Copy), r=[("st_S",) + sk], w=["STb"])

        y = T("t7")
        S.op("act", lambda e: e.activation(out=y, in_=PS(by)[:, 0:W], func=AF.Copy), r=[("ps", by)], w=[K("t7")])
        bm = bank("small")
        S.pe_group([lambda e: e.matmul(PS(bm)[:, 0:W], lhsT=bones_f, rhs=y, start=True, stop=True)],
                   r=[K("t7")], w=[("ps", bm)])
        S.op("dve", lambda e: e.tensor_tensor(out=y, in0=y, in1=PS(bm)[:, 0:W], op=ALU.subtract),
             r=[("ps", bm), K("t7")], w=[K("t7")])
        sq = T("t8")
        S.op("act", lambda e: e.activation(out=sq, in_=y, func=AF.Square), r=[K("t7")], w=[K("t8")])
        bv2 = bank("small")
        S.pe_group([lambda e: e.matmul(PS(bv2)[:, 0:W], lhsT=bones_f, rhs=sq, start=True, stop=True)],
                   r=[K("t8")], w=[("ps", bv2)])
        rs = T("t9")
        S.op("act", lambda e: e.activation(out=rs, in_=PS(bv2)[:, 0:W], func=AF.Sqrt, bias=GN_EPS, scale=1.0),
             r=[("ps", bv2)], w=[K("t9")])
        S.op("dve", lambda e: e.reciprocal(out=rs, in_=rs), r=[K("t9")], w=[K("t9")])
        S.op("dve", lambda e: e.tensor_tensor(out=y, in0=y, in1=rs, op=ALU.mult), r=[K("t7"), K("t9")], w=[K("t7")])
        S.op("dve", lambda e: e.tensor_scalar(out=y, in0=y, scalar1=V(l, "gg", pr), scalar2=V(l, "gb", pr),
                                              op0=ALU.mult, op1=ALU.add), r=[K("t7")], w=[K("t7")])
        S.op("dve", lambda e: e.tensor_tensor(out=y, in0=y, in1=bonus, op=ALU.add), r=[K("t7"), K("t10")], w=[K("t7")])
        S.op("dve", lambda e: e.tensor_tensor(out=cat[:, 8 + pr, off:off + W], in0=y, in1=g_, op=ALU.mult),
             r=[K("t7"), K("t6")], w=["cat"])

    groups = []
    for g in range(npg):
        groups.append(dict(gw=512, parts=[dict(seq="P", off=0, W=512, C=128, first=(g == 0), last=(g == npg - 1),
                                               bi=0)],
                           src=xp[g * 512:(g + 1) * 512, :], dst=yp[g * 512:(g + 1) * 512, :]))
    if with_s:
      groups.append(dict(gw=128, parts=[dict(seq="S0", off=0, W=64, C=64, first=False, last=True, bi=0, sidx=0),
                                      dict(seq="S1", off=64, W=64, C=64, first=False, last=True, bi=1, sidx=1)],
                       src=xs[:, :], dst=ys[:, :]))
    for g in groups:
        for l in range(layers):
            wq["order"] += layer_order(l)

    dbg_out = {}

    def chk(name):
        if dbg == name:
            S.dead = True

    for gi_, G in enumerate(groups):
        gw = G["gw"]
        parts = G["parts"]
        ntb = gw // 128
        for tb in range(ntb):
            S.dma("sp", stage, G["src"][tb * 128:(tb + 1) * 128, :], w=["stage"])
            for k4 in range(4):
                b = bank("small")
                S.pe_group([lambda e, k=k: e.transpose(out=PS(b)[:, (k % 4) * 128:(k % 4 + 1) * 128],
                                                        in_=stage[:, k * 128:(k + 1) * 128], identity=ident_f)
                            for k in range(k4 * 4, k4 * 4 + 4)], r=["stage"], w=[("ps", b)])
                S.op("act" if k4 % 2 else "dve",
                     (lambda e, k4=k4, tb=tb, b=b: e.activation(
                         out=xT[:, k4 * 4:k4 * 4 + 4, tb * 128:(tb + 1) * 128],
                         in_=PS(b).rearrange("p (a c) -> p a c", c=128), func=AF.Copy)) if k4 % 2 else
                     (lambda e, k4=k4, tb=tb, b=b: e.tensor_copy(
                         out=xT[:, k4 * 4:k4 * 4 + 4, tb * 128:(tb + 1) * 128],
                         in_=PS(b).rearrange("p (a c) -> p a c", c=128))),
                     r=[("ps", b)], w=["xT"])

        for l in range(layers):
            LV = l * VL
            chk("A")
            S.dma("pool", wsmall, wblk[l, 0, :, :], w=["wsmall"], sem=wsem_small)
            S.dma("pool", wpool, wblk[l, 1, :, 0:512], w=["wpool"], sem=wsem_small)
            for p in parts:
                if p["seq"] == "P":
                    if p["first"]:
                        sd = st[("P", l)]
                        S.op("dve", lambda e, sd=sd: e.memset(sd["u"], 0.0), w=[("st_u", "P", l)])
                        S.op("dve", lambda e, sd=sd: e.memset(sd["p"], 0.0), w=[("st_p", "P", l)])
                        S.op("dve", lambda e, sd=sd: e.memset(sd["q"], 0.0), w=[("st_q", "P", l)])
                        S.op("dve", lambda e, sd=sd: e.memset(sd["S"], 0.0), w=[("st_S", "P", l)])
                    continue
                sq, si = p["seq"], p["sidx"]
                sd = st[(sq, l)]
                S.dma("sp", stage2[0:30, 0:512], cconv[l, si, :, :], w=["stage2"])
                b = bank("small")
                S.pe_group([lambda e, c=c: e.transpose(out=PS(b)[:, c * 32:c * 32 + 30],
                                                        in_=stage2[0:30, c * 128:(c + 1) * 128],
                                                        identity=ident_f[0:30, 0:30]) for c in range(4)],
                           r=["stage2"], w=[("ps", b)])
                S.op("dve", lambda e, sd=sd, b=b: e.tensor_copy(
                    out=sd["u"], in_=PS(b)[:, 0:128].rearrange("p (a c) -> p a c", c=32)[:, :, 0:30]),
                    r=[("ps", b)], w=[("st_u", sq, l)])
                S.dma("sp", stage2[0:15, 0:512], cpool[l, si, :, :], w=["stage2"])
                b = bank("small")
                S.pe_group([lambda e, c=c: e.transpose(out=PS(b)[:, c * 16:c * 16 + 15],
                                                        in_=stage2[0:15, c * 128:(c + 1) * 128],
                                                        identity=ident_f[0:15, 0:15]) for c in range(4)],
                           r=["stage2"], w=[("ps", b)])
                S.op("dve", lambda e, sd=sd, b=b: e.tensor_copy(
                    out=sd["p"], in_=PS(b)[:, 0:64].rearrange("p (a c) -> p a c", c=16)[:, :, 0:15]),
                    r=[("ps", b)], w=[("st_p", sq, l)])
                S.dma("sp", stage2[0:NQ, 0:128], cshift[l, si, :, :], w=["stage2"])
                b = bank("small")
                S.pe_group([lambda e: e.transpose(out=PS(b)[:, 0:NQ], in_=stage2[0:NQ, 0:128],
                                                  identity=ident_f[0:NQ, 0:NQ])], r=["stage2"], w=[("ps", b)])
                S.op("dve", lambda e, sd=sd, b=b: e.tensor_copy(out=sd["q"], in_=PS(b)[:, 0:NQ]),
                     r=[("ps", b)], w=[("st_q", sq, l)])
                S.dma("sp", stage2[0:64, :].rearrange("p (h j) -> p h j", j=64),
                      cwkv[l, si].rearrange("h i j -> i h j"), w=["stage2"])
                for half in range(2):
                    b = bank("small")
                    S.pe_group([lambda e, pr=pr: e.transpose(
                        out=PS(b)[:, (pr % 4) * 64:(pr % 4) * 64 + 64], in_=stage2[0:64, pr * 128:(pr + 1) * 128],
                        identity=ident_f[0:64, 0:64]) for pr in range(half * 4, half * 4 + 4)],
                        r=["stage2"], w=[("ps", b)])
                    S.op("dve", lambda e, sd=sd, b=b, half=half: e.tensor_copy(
                        out=sd["S"][:, half * 4:half * 4 + 4, :],
                        in_=PS(b)[:, 0:256].rearrange("p (a c) -> p a c", c=64)),
                        r=[("ps", b)], w=[("st_S", sq, l)])

            chk("A2")
            rmsnorm_to(lambda k: hT[:, k, 0:gw], gw, 0, "hT", LV + VO["nm"])
            chk("B")

            for c in range(4):
                bg = proj(w_next(), hT, gw, "hT")
                bv = proj(w_next(), hT, gw, "hT")
                for p in parts:
                    B = mb[p["bi"]]
                    W, off = p["W"], p["off"]
                    sk = (p["seq"], l)
                    t0 = B["t0"][:, 0:W]
                    if c == 0:
                        S.op("dve", lambda e, B=B, p=p: e.tensor_copy(out=B["ubuf"][:, :, 0:30],
                                                                     in_=st[(p["seq"], l)]["u"]),
                             r=[("st_u",) + sk], w=[("ubuf", p["bi"])])
                    S.op("act", lambda e, t0=t0, off=off, W=W, bg=bg: e.activation(
                        out=t0, in_=PS(bg)[:, off:off + W], func=AF.Sigmoid), r=[("ps", bg)], w=[("t0", p["bi"])])
                    S.op("dve", lambda e, B=B, t0=t0, off=off, W=W, bv=bv, c=c: e.tensor_tensor(
                        out=B["ubuf"][:, c, 30:30 + W], in0=PS(bv)[:, off:off + W], in1=t0, op=ALU.mult),
                        r=[("ps", bv), ("t0", p["bi"])], w=[("ubuf", p["bi"])])
            for p in parts:
                B = mb[p["bi"]]
                W, off, bi = p["W"], p["off"], p["bi"]
                sk = (p["seq"], l)
                S.op("act", lambda e, B=B, W=W: e.activation(out=B["ubf"][:, :, 0:30 + W], in_=B["ubuf"][:, :, 0:30 + W],
                                                            func=AF.Copy), r=[("ubuf", bi)], w=[("ubf", bi)])
                S.op("dve", lambda e, B=B, W=W, p=p: e.tensor_copy(out=st[(p["seq"], l)]["u"], in_=B["ubuf"][:, :, W:W + 30]),
                     r=[("ubuf", bi)], w=[("st_u",) + sk])
                pass
            for c in range(4):
                for j in range(31):
                    S.op("pool", lambda e, c=c, j=j: e.tensor_scalar(
                        out=diag[:, j, :], in0=ident_b, scalar1=V(l, "cw", c * 31 + j), scalar2=None, op0=ALU.mult),
                        r=[], w=[("diag", j)])
                for p in parts:
                    B = mb[p["bi"]]
                    W, off, bi = p["W"], p["off"], p["bi"]
                    b = bank("small")
                    S.pe_group([lambda e, j=j, c=c, B=B, W=W, b=b: e.matmul(
                        PS(b)[:, 0:W], lhsT=diag[:, j, :], rhs=B["ubf"][:, c, j:j + W],
                        start=(j == 0), stop=(j == 30)) for j in range(31)],
                        r=[("diag", j) for j in range(31)] + [("ubf", bi)], w=[("ps", b)])
                    S.op("act", lambda e, B=B, W=W, b=b, c=c: e.activation(
                        out=B["hconv"][:, c, 0:W], in_=PS(b)[:, 0:W], func=AF.Identity,
                        bias=V(l, "cb", c), scale=1.0), r=[("ps", b)], w=[("hconv", bi)])
            for p in parts:
                B = mb[p["bi"]]
                W, off, bi = p["W"], p["off"], p["bi"]
                sk = (p["seq"], l)
                bm = bank("small")
                S.pe_group([lambda e, c=c, B=B, W=W: e.matmul(PS(bm)[:, 0:W], lhsT=onesD_f, rhs=B["hconv"][:, c, 0:W],
                                                              start=(c == 0), stop=(c == 3)) for c in range(4)],
                           r=[("hconv", bi)], w=[("ps", bm)])
                mean = B["t1"][:, 0:W]
                S.op("act", lambda e, mean=mean, W=W: e.activation(out=mean, in_=PS(bm)[:, 0:W], func=AF.Copy),
                     r=[("ps", bm)], w=[("t1", bi)])
                for c in range(4):
                    S.op("dve", lambda e, c=c, B=B, W=W, mean=mean: e.tensor_tensor(
                        out=B["hconv"][:, c, 0:W], in0=B["hconv"][:, c, 0:W], in1=mean, op=ALU.subtract),
                        r=[("hconv", bi), ("t1", bi)], w=[("hconv", bi)])
                bvv = bank("small")
                for c in range(4):
                    S.op("act", lambda e, c=c, B=B, W=W: e.activation(out=B["t2"][:, 0:W], in_=B["hconv"][:, c, 0:W],
                                                                      func=AF.Square),
                         r=[("hconv", bi)], w=[("t2", bi)])
                    S.pe_group([lambda e, c=c, B=B, W=W: e.matmul(PS(bvv)[:, 0:W], lhsT=onesD_f, rhs=B["t2"][:, 0:W],
                                                                  start=(c == 0), stop=(c == 3))],
                               r=[("t2", bi)], w=[("ps", bvv)])
                rs = B["t3"][:, 0:W]
                S.op("act", lambda e, rs=rs, W=W: e.activation(out=rs, in_=PS(bvv)[:, 0:W], func=AF.Sqrt,
                                                              bias=LN_EPS, scale=1.0), r=[("ps", bvv)], w=[("t3", bi)])
                S.op("dve", lambda e, rs=rs: e.reciprocal(out=rs, in_=rs), r=[("t3", bi)], w=[("t3", bi)])
                for c in range(4):
                    S.op("dve", lambda e, c=c, B=B, W=W, rs=rs: e.tensor_tensor(
                        out=B["hconv"][:, c, 0:W], in0=B["hconv"][:, c, 0:W], in1=rs, op=ALU.mult),
                        r=[("hconv", bi), ("t3", bi)], w=[("hconv", bi)])
                    S.op("act", lambda e, c=c, B=B, W=W, off=off: e.activation(
                        out=cat[:, c, off:off + W], in_=B["hconv"][:, c, 0:W], func=AF.Silu,
                        bias=V(l, "lb", c), scale=V(l, "lg", c)), r=[("hconv", bi)], w=["cat"])

            chk("C")
            S.barrier()
            for c in range(4):
                bp = proj(w_next(), hT, gw, "hT")
                for p in parts:
                    B = mb[p["bi"]]
                    W, off, bi = p["W"], p["off"], p["bi"]
                    sk = (p["seq"], l)
                    if c == 0:
                        S.op("dve", lambda e, B=B, p=p: e.tensor_copy(out=B["pbuf"][:, :, 0:15],
                                                                     in_=st[(p["seq"], l)]["p"]),
                             r=[("st_p",) + sk], w=[("pbuf", bi)])
                    S.op("act", lambda e, B=B, W=W, off=off, bp=bp, c=c: e.activation(
                        out=B["pbuf"][:, c, 15:15 + W], in_=PS(bp)[:, off:off + W], func=AF.Copy),
                        r=[("ps", bp)], w=[("pbuf", bi)])
            for p in parts:
                B = mb[p["bi"]]
                W, off, bi = p["W"], p["off"], p["bi"]
                sk = (p["seq"], l)
                S.op("dve", lambda e, B=B, W=W, p=p: e.tensor_copy(out=st[(p["seq"], l)]["p"], in_=B["pbuf"][:, :, W:W + 15]),
                     r=[("pbuf", bi)], w=[("st_p",) + sk])
                for c, wdw in enumerate(POOL_WINDOWS):
                    src = B["pbuf"][:, c, :]
                    lo = 15
                    span = 1
                    ta, tb_ = B["t4"], B["t5"]
                    cur, cur_lo = src, 0
                    nsteps = {2: 1, 4: 2, 8: 3, 16: 4}[wdw]
                    for s_ in range(nsteps):
                        dst = ta if s_ % 2 == 0 else tb_
                        new_lo = cur_lo + span
                        n = 15 + W - new_lo
                        S.op("dve", lambda e, dst=dst, cur=cur, new_lo=new_lo, span=span, n=n: e.tensor_tensor(
                            out=dst[:, new_lo:new_lo + n], in0=cur[:, new_lo:new_lo + n],
                            in1=cur[:, new_lo - span:new_lo - span + n], op=ALU.add),
                            r=[("pbuf", bi), ("t4", bi), ("t5", bi)], w=[("t4" if s_ % 2 == 0 else "t5", bi)])
                        cur, cur_lo = dst, new_lo
                        span *= 2
                    S.op("dve", lambda e, cur=cur, W=W, c=c, B=B, wdw=wdw: e.scalar_tensor_tensor(
                        out=B["dpool"][:, c, 0:W], in0=cur[:, 15:15 + W], scalar=1.0 / wdw,
                        in1=B["pbuf"][:, c, 15:15 + W], op0=ALU.mult, op1=ALU.subtract),
                        r=[("t4", bi), ("t5", bi), ("pbuf", bi)], w=[("dpool", bi)])
                    if p["first"]:
                        S.op("dve", lambda e, cur=cur, c=c: e.tensor_tensor(
                            out=cur[:, 15:31], in0=cur[:, 15:31], in1=invc_first[:, c, :], op=ALU.mult),
                            r=[("t4", bi), ("t5", bi), ("dpool", bi)], w=[("t4", bi), ("t5", bi)])
                        S.op("dve", lambda e, cur=cur, c=c, B=B: e.tensor_tensor(
                            out=B["dpool"][:, c, 0:16], in0=cur[:, 15:31], in1=B["pbuf"][:, c, 15:31],
                            op=ALU.subtract), r=[("t4", bi), ("t5", bi), ("pbuf", bi)], w=[("dpool", bi)])
                    b = bank("small")
                    S.pe_group([lambda e, c=c, B=B, W=W, b=b: e.matmul(PS(b)[:, 0:W], lhsT=wpool[:, c * 128:(c + 1) * 128],
                                                                        rhs=B["dpool"][:, c, 0:W], start=True, stop=True)],
                               r=["wpool", ("dpool", bi)], w=[("ps", b)])
                    S.op("act", lambda e, c=c, W=W, off=off, b=b: e.activation(
                        out=cat[:, 4 + c, off:off + W], in_=PS(b)[:, 0:W], func=AF.Identity, scale=V(l, "psc", c)),
                        r=[("ps", b)], w=["cat"])


            chk("D")
            b24 = proj(w_next(), hT, gw, "hT")
            b25 = proj(w_next(), hT, gw, "hT")
            for p in parts:
                B = mb[p["bi"]]
                W, bi = p["W"], p["bi"]
                gl = B["gl"]
                shifted_from_psum(l, b24, 24, p, gl, ("gl", bi), B["t0"], ("t0", bi))
                S.op("act", lambda e, B=B, W=W, gl=gl: e.activation(out=B["lora"][0:64, 0, 0:W], in_=gl[0:64, 0:W],
                                                                    func=AF.Tanh), r=[("gl", bi)], w=[("lora", bi)])
                S.op("act", lambda e, B=B, W=W, gl=gl: e.activation(out=B["lora"][64:128, 0, 0:W], in_=gl[64:128, 0:W],
                                                                    func=AF.Copy), r=[("gl", bi)], w=[("lora", bi)])
                shifted_from_psum(l, b25, 25, p, gl, ("gl", bi), B["t0"], ("t0", bi))
                S.op("act", lambda e, B=B, W=W, gl=gl: e.activation(out=B["lora"][0:64, 1, 0:W], in_=gl[0:64, 0:W],
                                                                    func=AF.Sigmoid), r=[("gl", bi)], w=[("lora", bi)])

            chk("E")
            for pr in range(PAIRS):
                if pr == 1:
                    chk("F")
                br = proj(w_next(), hT, gw, "hT")
                bk = proj(w_next(), hT, gw, "hT")
                bv_ = proj(w_next(), hT, gw, "hT")
                for p in parts:
                    wkv_pair(l, pr, p, br, bk, bv_)

            chk("G")
            for n in range(16):
                bo = proj(w_next(), cat, gw, "cat")
                S.op("dve", lambda e, n=n, bo=bo: e.tensor_tensor(out=xT[:, n, 0:gw], in0=PS(bo)[:, 0:gw],
                                                                  in1=xT[:, n, 0:gw], op=ALU.add),
                     r=[("ps", bo), "xT"], w=["xT"])

            chk("H")
            rmsnorm_to(lambda k: hT[:, k, 0:gw], gw, 0, "hT", LV + VO["nf"])
            S.barrier()
            for f in range(FC if dbg != "outproj" else 0):
                bg = proj(w_next(), hT, gw, "hT")
                bu = proj(w_next(), hT, gw, "hT")
                ft = ftmp[f % 2]
                S.op("act", lambda e, ft=ft, bg=bg: e.activation(out=ft[:, 0:gw], in_=PS(bg)[:, 0:gw], func=AF.Silu),
                     r=[("ps", bg)], w=[("ftmp", f % 2)])
                S.op("dve", lambda e, ft=ft, bu=bu, f=f: e.tensor_tensor(out=act[:, f, 0:gw], in0=PS(bu)[:, 0:gw],
                                                                         in1=ft[:, 0:gw], op=ALU.mult),
                     r=[("ps", bu), ("ftmp", f % 2)], w=[("act", f)])
            for n in range(16 if dbg != "outproj" else 0):
                bd = bank("big")
                for j, nk in enumerate((16, 16, 12)):
                    sl = w_next()
                    wv = wring[:, sl, :].rearrange("p (k n) -> p k n", n=128)
                    fns = []
                    for k in range(nk):
                        f = j * 16 + k
                        fns.append(lambda e, wv=wv, k=k, f=f: e.matmul(PS(bd)[:, 0:gw], lhsT=wv[:, k, :],
                                                                        rhs=act[:, f, 0:gw], start=(f == 0),
                                                                        stop=(f == FC - 1)))
                    S.pe_group(fns, r=[("w", sl)] + [("act", f) for f in range(j * 16, j * 16 + nk)],
                               w=[("ps", bd)])
                S.op("dve", lambda e, n=n, bd=bd: e.tensor_tensor(out=xT[:, n, 0:gw], in0=PS(bd)[:, 0:gw],
                                                                  in1=xT[:, n, 0:gw], op=ALU.add),
                     r=[("ps", bd), "xT"], w=["xT"])
            S.barrier()

            chk("I")
            for p in parts:
                if not p["last"]:
                    continue
                sk = (p["seq"], l)
                sd = st[sk]
                oi = {"P": 0, "S0": 1, "S1": 2}[p["seq"]]
                b = bank("small")
                S.pe_group([lambda e, c=c: e.transpose(out=PS(b)[0:30, c * 128:(c + 1) * 128], in_=sd["u"][:, c, :],
                                                        identity=ident_f) for c in range(4)],
                           r=[("st_u",) + sk], w=[("ps", b)])
                S.op("act", lambda e, b=b: e.activation(out=stage2[0:30, 0:512], in_=PS(b)[0:30, 0:512], func=AF.Copy),
                     r=[("ps", b)], w=["stage2"])
                S.dma("sp", nconv[l, oi, :, :], stage2[0:30, 0:512], r=["stage2"], w=[("o_conv", l, oi)])
                b = bank("small")
                S.pe_group([lambda e, c=c: e.transpose(out=PS(b)[0:15, c * 128:(c + 1) * 128], in_=sd["p"][:, c, :],
                                                        identity=ident_f) for c in range(4)],
                           r=[("st_p",) + sk], w=[("ps", b)])
                S.op("act", lambda e, b=b: e.activation(out=stage2[0:15, 512:1024], in_=PS(b)[0:15, 0:512], func=AF.Copy),
                     r=[("ps", b)], w=["stage2"])
                S.dma("sp", npool[l, oi, :, :], stage2[0:15, 512:1024], r=["stage2"], w=[("o_pool", l, oi)])
                b = bank("small")
                S.pe_group([lambda e: e.transpose(out=PS(b)[0:NQ, 0:128], in_=sd["q"], identity=ident_f)],
                           r=[("st_q",) + sk], w=[("ps", b)])
                S.op("act", lambda e, b=b: e.activation(out=stage3[0:NQ, 512:640], in_=PS(b)[0:NQ, 0:128], func=AF.Copy),
                     r=[("ps", b)], w=["stage3"])
                S.dma("sp", nshift[l, oi, :, :], stage3[0:NQ, 512:640], r=["stage3"], w=[("o_shift", l, oi)])
                for half in range(2):
                    b = bank("small")
                    S.pe_group([lambda e, pr=pr: e.transpose(out=PS(b)[0:64, (pr % 4) * 128:(pr % 4 + 1) * 128],
                                                              in_=sd["S"][:, pr, :], identity=ident_f)
                                for pr in range(half * 4, half * 4 + 4)], r=[("st_S",) + sk], w=[("ps", b)])
                    S.op("act", lambda e, b=b: e.activation(out=stage3[0:64, 0:512], in_=PS(b)[0:64, 0:512],
                                                            func=AF.Copy), r=[("ps", b)], w=["stage3"])
                    S.dma("sp", nwkv[l, oi, half * 8:half * 8 + 8].rearrange("h i j -> i h j"),
                          stage3[0:64, 0:512].rearrange("p (h j) -> p h j", j=64), r=["stage3"],
                          w=[("o_wkv", l, oi, half)])

        S.dead = False
        if dbg is None:
            rmsnorm_to(lambda k: xT[:, k, 0:gw], gw, 0, "xT", DEPTH * VL)
        for tb in range(ntb):
            for k4 in range(4):
                b = bank("small")
                S.pe_group([lambda e, k=k: e.transpose(out=PS(b)[:, (k % 4) * 128:(k % 4 + 1) * 128],
                                                        in_=xT[:, k, tb * 128:(tb + 1) * 128], identity=ident_f)
                            for k in range(k4 * 4, k4 * 4 + 4)], r=["xT"], w=[("ps", b)])
                S.op("act" if k4 % 2 else "dve",
                     (lambda e, k4=k4, b=b: e.activation(out=stage[:, k4 * 512:(k4 + 1) * 512], in_=PS(b), func=AF.Copy))
                     if k4 % 2 else
                     (lambda e, k4=k4, b=b: e.tensor_copy(out=stage[:, k4 * 512:(k4 + 1) * 512], in_=PS(b))),
                     r=[("ps", b)], w=["stage"])
            S.dma("sp", G["dst"][tb * 128:(tb + 1) * 128, :], stage, r=["stage"], w=[("o_y", gi_, tb)])

    S.finish("sp")
    print("instructions emitted:", S.ninst)
    nc._arena_reg = A.reg
    return nc


def _colize(v):
    v = np.asarray(v, np.float32).reshape(-1)
    n = (v.size + 127) // 128
    out = np.zeros((n * 128,), np.float32)
    out[:v.size] = v
    return out.reshape(n, 128).T


def _prep_shared(inp):
    wblk = np.zeros((DEPTH, NBLK, 128, SLOT), np.float32)
    vecs = np.zeros((128, NVEC), np.float32)
    for l in range(DEPTH):
        wblk[l, 0, 0:64, 0:1024] = inp["decay_up"][l]
        wblk[l, 0, 64:128, 0:1024] = inp["iclr_up"][l]
        wblk[l, 0, 0:64, 1024:2048] = inp["gate_up"][l]
        wblk[l, 1, :, 0:512] = np.asarray(inp["pool_w"][l]).transpose(1, 0, 2).reshape(128, 512)
        win = np.zeros((D, 38 * 128), np.float32)
        win[:, :4800] = inp["w_in"][l]
        wblk[l, 2:40] = win.reshape(16, 128, 38, 128).transpose(2, 1, 0, 3).reshape(38, 128, SLOT)
        wblk[l, 40:56] = np.asarray(inp["w_out"][l]).reshape(16, 128, 16, 128).transpose(2, 1, 0, 3).reshape(16, 128, SLOT)
        g = np.asarray(inp["ffn_gate"][l]).reshape(16, 128, FC, 128).transpose(2, 1, 0, 3).reshape(FC, 128, SLOT)
        u = np.asarray(inp["ffn_up"][l]).reshape(16, 128, FC, 128).transpose(2, 1, 0, 3).reshape(FC, 128, SLOT)
        wblk[l, 56:144:2] = g
        wblk[l, 57:144:2] = u
        dn = np.zeros((48, 128, 16, 128), np.float32)
        dn[:FC] = np.asarray(inp["ffn_down"][l]).reshape(FC, 128, 16, 128)
        dn = dn.reshape(3, 16, 128, 16, 128).transpose(3, 0, 2, 1, 4).reshape(16, 3, 128, SLOT)
        wblk[l, 144:192] = dn.reshape(48, 128, SLOT)
        o = l * VL
        vecs[:, o + VO["nm"]:o + VO["nm"] + 16] = _colize(inp["norm_mix"][l])
        vecs[:, o + VO["nf"]:o + VO["nf"] + 16] = _colize(inp["norm_ffn"][l])
        vecs[:, o + VO["cb"]:o + VO["cb"] + 4] = _colize(inp["conv_b"][l])
        cw = np.asarray(inp["conv_w"][l])
        vecs[:, o + VO["cw"]:o + VO["cw"] + 124] = cw.reshape(31, 4, 128).transpose(2, 1, 0).reshape(128, 124)
        vecs[:, o + VO["lg"]:o + VO["lg"] + 4] = _colize(inp["conv_ln_g"][l])
        vecs[:, o + VO["lb"]:o + VO["lb"] + 4] = _colize(inp["conv_ln_b"][l])
        vecs[:, o + VO["psc"]:o + VO["psc"] + 4] = _colize(inp["pool_scale"][l])
        vecs[:, o + VO["mu"]:o + VO["mu"] + NQ] = _colize(inp["shift_mu"][l])
        vecs[:, o + VO["w0"]:o + VO["w0"] + 8] = _colize(inp["decay_w0"][l])
        vecs[:, o + VO["a0"]:o + VO["a0"] + 8] = _colize(inp["iclr_a0"][l])
        vecs[:, o + VO["kk"]:o + VO["kk"] + 8] = _colize(inp["k_k"][l])
        vecs[:, o + VO["ka"]:o + VO["ka"] + 8] = _colize(inp["k_a"][l])
        vecs[:, o + VO["rk"]:o + VO["rk"] + 8] = _colize(inp["r_k"][l])
        vecs[:, o + VO["gg"]:o + VO["gg"] + 8] = _colize(inp["gn_g"][l])
        vecs[:, o + VO["gb"]:o + VO["gb"] + 8] = _colize(inp["gn_b"][l])
    vecs[:, DEPTH * VL:DEPTH * VL + 16] = _colize(inp["norm_final"])
    return wblk, vecs


def _core_inputs(inp, c, shared, nseq_tok=SEQ):
    wblk, vecs = shared
    sh = np.zeros((DEPTH, 2, NQ * 128), np.float32)
    sh[:, :, :3264] = np.asarray(inp["state_shift"])[:, 2 * c:2 * c + 2, 0, :]
    return {
        "xp": np.ascontiguousarray(np.asarray(inp["x_prompt"])[c % 4, :nseq_tok]),
        "xs": np.ascontiguousarray(np.asarray(inp["x_sample"])[2 * c:2 * c + 2].reshape(2 * SLEN, D)),
        "cconv": np.ascontiguousarray(np.asarray(inp["cache_conv"])[:, 2 * c:2 * c + 2]),
        "cpool": np.ascontiguousarray(np.asarray(inp["cache_pool"])[:, 2 * c:2 * c + 2]),
        "cshift": sh.reshape(DEPTH, 2, NQ, 128),
        "cwkv": np.ascontiguousarray(np.asarray(inp["state_wkv"])[:, 2 * c:2 * c + 2]),
        "wblk": wblk,
        "vecs": vecs,
    }


_NC_CACHE = {}


def kernel(**inp):
    inp = {k: np.asarray(v) for k, v in inp.items()}
    shared = _prep_shared(inp)
    if "nc" not in _NC_CACHE:
        _NC_CACHE["nc"] = build_program()
    nc = _NC_CACHE["nc"]
    in_maps = [_core_inputs(inp, c, shared) for c in range(8)]
    res = run_bass_kernel_spmd(nc, in_maps, core_ids=list(range(8)))
    R = res.results
    y_prompt = np.stack([R[c]["yp"] for c in range(4)]).astype(np.float32)
    y_sample = np.concatenate([R[c]["ys"].reshape(2, SLEN, D) for c in range(8)]).astype(np.float32)

    def gather(name, tailshape, fix=None):
        pr = np.stack([R[c][name][:, 0] for c in range(4)], axis=1)
        sm = np.concatenate([R[c][name][:, 1:3] for c in range(8)], axis=1)
        if fix is not None:
            pr, sm = fix(pr), fix(sm)
        return pr.astype(np.float32), sm.astype(np.float32)

    p_conv, s_conv = gather("nconv", None)
    p_pool, s_pool = gather("npool", None)
    fixs = lambda a: a.reshape(a.shape[0], a.shape[1], 1, NQ * 128)[..., :3264]
    p_shift, s_shift = gather("nshift", None, fixs)
    p_wkv, s_wkv = gather("nwkv", None)
    return (y_prompt, y_sample, p_conv, p_pool, p_shift, p_wkv, s_conv, s_pool, s_shift, s_wkv)
```

```python
import numpy as np
import concourse.bass as bass
import concourse.mybir as mybir
from concourse.bass_utils import run_bass_kernel_spmd

F32 = mybir.dt.float32
BF16 = mybir.dt.bfloat16
AF = mybir.ActivationFunctionType
ALU = mybir.AluOpType

D = 2048
KC = 16
DFF = 5632
FC = 44
HEADS = 16
PAIRS = 8
NQ = 26
DEPTH = 4
SEQ = 2048
SLEN = 64
RMS_EPS = 1e-6
LN_EPS = 1e-5
GN_EPS = 64e-5
LW_SCALE = -float(np.exp(-0.5))
POOL_WINDOWS = (2, 4, 8, 16)

NBLK = 2 + 38 + 16 + 88 + 48
SLOT = 2048
NSLOT = 6

VO = {}
_o = 0
for _n, _w in (("nm", 16), ("nf", 16), ("cb", 4), ("cw", 124), ("lg", 4), ("lb", 4), ("psc", 4),
               ("mu", NQ), ("w0", 8), ("a0", 8), ("kk", 8), ("ka", 8), ("rk", 8), ("gg", 8), ("gb", 8)):
    VO[_n] = _o
    _o += _w
VL = _o
NVEC = DEPTH * VL + 16


class Sched:
    def __init__(self, nc):
        self.nc = nc
        self.eng = {"pe": nc.tensor, "act": nc.scalar, "dve": nc.vector, "pool": nc.gpsimd, "sp": nc.sync}
        self.semh = {}
        self.cnt = {}
        for e in self.eng:
            self.semh[e] = nc.alloc_semaphore("sem_" + e)
            self.cnt[e] = 0
        self.waited = {e: {} for e in self.eng}
        self.lastw = {}
        self.readers = {}
        self.dma_sems = []
        self.dma_rr = 0
        self.ninst = 0
        self.dead = False

    def new_dma_sem(self, name):
        self.semh[name] = self.nc.alloc_semaphore("sem_" + name)
        self.cnt[name] = 0
        return name

    def _deps(self, r, w):
        need = {}
        for k in r:
            t = self.lastw.get(k)
            if t is not None:
                need[t[0]] = max(need.get(t[0], 0), t[1])
        for k in w:
            t = self.lastw.get(k)
            if t is not None:
                need[t[0]] = max(need.get(t[0], 0), t[1])
            for t in self.readers.get(k, ()):
                need[t[0]] = max(need.get(t[0], 0), t[1])
        return need

    def _wait(self, e, need, skip_self=False):
        wd = self.waited[e]
        for s, v in need.items():
            if skip_self and s == e:
                continue
            if wd.get(s, 0) < v:
                self.eng[e].wait_ge(self.semh[s], v)
                wd[s] = v
                self.ninst += 1

    def _commit(self, tok, r, w):
        for k in r:
            lst = self.readers.setdefault(k, [])
            lst[:] = [t for t in lst if t[0] != tok[0]]
            lst.append(tok)
        for k in w:
            self.lastw[k] = tok
            self.readers[k] = []

    def op(self, e, fn, r=(), w=()):
        if self.dead:
            return None
        need = self._deps(r, w)
        self._wait(e, need, skip_self=(e == "pe"))
        inst = fn(self.eng[e])
        self.cnt[e] += 1
        inst.then_inc(self.semh[e], 1)
        self.ninst += 1
        tok = (e, self.cnt[e])
        self._commit(tok, r, w)
        return tok

    def pe_group(self, fns, r=(), w=(), pe_sync=False):
        if self.dead:
            return None
        need = self._deps(r, w)
        self._wait("pe", need, skip_self=not pe_sync)
        inst = None
        for fn in fns:
            inst = fn(self.eng["pe"])
            self.ninst += 1
        self.cnt["pe"] += 1
        inst.then_inc(self.semh["pe"], 1)
        tok = ("pe", self.cnt["pe"])
        self._commit(tok, r, w)
        return tok

    def dma(self, q, out, in_, r=(), w=(), sem=None):
        if self.dead:
            return None
        if sem is None:
            if len(self.dma_sems) < 24:
                sem = self.new_dma_sem("d%d" % len(self.dma_sems))
                self.dma_sems.append(sem)
            else:
                sem = self.dma_sems[self.dma_rr % len(self.dma_sems)]
                self.dma_rr += 1
        need = self._deps(r, w)
        if self.cnt[sem] > 0:
            need[sem] = max(need.get(sem, 0), self.cnt[sem])
        self._wait(q, need)
        self.eng[q].dma_start(out=out, in_=in_).then_inc(self.semh[sem], 16)
        self.cnt[sem] += 16
        self.ninst += 1
        tok = (sem, self.cnt[sem])
        self._commit(tok, r, w)
        return tok

    def barrier(self, engines=("pe", "act", "dve", "pool")):
        if self.dead:
            return
        for e in engines:
            need = {}
            for o in engines:
                if o != e and self.cnt[o] > 0:
                    need[o] = self.cnt[o]
            self._wait(e, need)

    def finish(self, e="sp"):
        need = {}
        for s, c in self.cnt.items():
            if c > 0 and s != e:
                need[s] = c
        self._wait(e, need)


class Arena:
    def __init__(self, nc, nbytes, name="arena"):
        assert nbytes % 4 == 0
        self.t = nc.alloc_sbuf_tensor(name, [128, nbytes // 4], F32)
        self.off = 0
        self.cap = nbytes

    def alloc(self, shape, dt, at=None, name=None):
        esz = 4 if dt == F32 else 2
        n = 1
        for s in shape[1:]:
            n *= s
        nb = (n * esz + 31) // 32 * 32
        if at is None:
            at = self.off
            self.off += nb
            assert self.off <= self.cap, ("arena overflow", self.off, self.cap)
        if not hasattr(self, "reg"):
            self.reg = {}
        self.reg[name if name is not None else "anon%d" % len(self.reg)] = (at, list(shape), "f32" if dt == F32 else "bf16")
        v = self.t[:, at // 4:(at + nb) // 4]
        if dt != F32:
            v = v.bitcast(dt)
        v = v[:, 0:n]
        if len(shape) == 3:
            v = v.rearrange("p (a b) -> p a b", b=shape[2])
        elif len(shape) == 4:
            v = v.rearrange("p (a b c) -> p a b c", b=shape[2], c=shape[3])
        return v


def build_program(layers=DEPTH, npg=4, dbg=None, with_s=True):
    nc = bass.Bass("TRN2", target_bir_lowering=False)
    S = Sched(nc)
    nseq_tok = 512 * npg

    xp = nc.dram_tensor("xp", [nseq_tok, D], F32, kind="ExternalInput").ap()
    xs = nc.dram_tensor("xs", [2 * SLEN, D], F32, kind="ExternalInput").ap()
    cconv = nc.dram_tensor("cconv", [DEPTH, 2, 30, 512], F32, kind="ExternalInput").ap()
    cpool = nc.dram_tensor("cpool", [DEPTH, 2, 15, 512], F32, kind="ExternalInput").ap()
    cshift = nc.dram_tensor("cshift", [DEPTH, 2, NQ, 128], F32, kind="ExternalInput").ap()
    cwkv = nc.dram_tensor("cwkv", [DEPTH, 2, HEADS, 64, 64], F32, kind="ExternalInput").ap()
    wblk = nc.dram_tensor("wblk", [layers, NBLK, 128, SLOT], F32, kind="ExternalInput").ap()
    vecs_d = nc.dram_tensor("vecs", [128, NVEC], F32, kind="ExternalInput").ap()
    yp = nc.dram_tensor("yp", [nseq_tok, D], F32, kind="ExternalOutput").ap()
    ys = nc.dram_tensor("ys", [2 * SLEN, D], F32, kind="ExternalOutput").ap()
    nconv = nc.dram_tensor("nconv", [DEPTH, 3, 30, 512], F32, kind="ExternalOutput").ap()
    npool = nc.dram_tensor("npool", [DEPTH, 3, 15, 512], F32, kind="ExternalOutput").ap()
    nshift = nc.dram_tensor("nshift", [DEPTH, 3, NQ, 128], F32, kind="ExternalOutput").ap()
    nwkv = nc.dram_tensor("nwkv", [DEPTH, 3, HEADS, 64, 64], F32, kind="ExternalOutput").ap()

    A = Arena(nc, 212736)
    vecs = A.alloc([128, NVEC], F32)
    omu = A.alloc([128, DEPTH, NQ], F32)
    ident_f = A.alloc([128, 128], F32)
    ident_b = A.alloc([128, 128], BF16)
    ones_b = A.alloc([128, 128], BF16)
    bones_b = A.alloc([128, 128], BF16)
    bones_f = A.alloc([128, 128], F32)
    onesD_f = A.alloc([128, 128], F32)
    m_su = A.alloc([128, 128], F32)
    m_ui = A.alloc([128, 128], F32)
    m_sl = A.alloc([128, 128], F32)
    cmask = {64: A.alloc([128, 64], BF16), 128: A.alloc([128, 512], BF16)}
    invc_first = A.alloc([128, 4, 16], F32)
    st = {}
    for l in range(DEPTH):
        st[("P", l)] = dict(u=A.alloc([128, 4, 30], F32), p=A.alloc([128, 4, 15], F32),
                            q=A.alloc([128, NQ], F32), S=A.alloc([128, PAIRS, 64], F32))
    for sq in ("S0", "S1"):
        d_ = dict(u=A.alloc([128, 4, 30], F32), p=A.alloc([128, 4, 15], F32),
                  q=A.alloc([128, NQ], F32), S=A.alloc([128, PAIRS, 64], F32))
        for l in range(DEPTH):
            st[(sq, l)] = d_
    xT = A.alloc([128, KC, 512], F32, name='xT')
    hT = A.alloc([128, KC, 512], BF16, name='hT')
    cat = A.alloc([128, KC, 512], BF16, name='cat')
    wring = A.alloc([128, NSLOT, SLOT], BF16)
    wsmall = A.alloc([128, SLOT], BF16)
    wpool = A.alloc([128, 512], BF16)
    rstd = A.alloc([128, 512], F32)
    sqb = A.alloc([128, 512], BF16)
    base_off = A.off

    def mixer_bufs(Wm):
        b = {}
        o0 = A.off
        b["ubuf"] = A.alloc([128, 4, 30 + Wm], F32, name="mb%d_ubuf" % Wm)
        b["ubf"] = A.alloc([128, 4, 30 + Wm], BF16)
        b["hconv"] = A.alloc([128, 4, Wm], F32)
        o1 = A.off
        b["pbuf"] = A.alloc([128, 4, 15 + Wm], F32, at=o0)
        b["dpool"] = A.alloc([128, 4, Wm], BF16, at=o0 + (4 * (15 + Wm) * 4 + 31) // 32 * 32)
        assert o0 + (4 * (15 + Wm) * 4 + 31) // 32 * 32 + 4 * Wm * 2 <= o1
        for n in ("t0", "t1", "t2", "t3", "t4", "t5", "t6", "t7", "t8", "t9", "t10", "t11"):
            b["off_" + n] = A.off
            b[n] = A.alloc([128, Wm + 16], F32, name="mb%d_%s" % (Wm, n))
        for n in ("b0", "b1", "b2", "b3", "b4", "b5"):
            b[n] = A.alloc([128, Wm], BF16, name="mb%d_%s" % (Wm, n))
        b["AR"] = A.alloc([128, 2 * Wm], BF16, name="mb%d_AR" % Wm)
        b["lora"] = A.alloc([128, 2, Wm], BF16, name="mb%d_lora" % Wm)
        b["gl"] = b["t1"]
        return b
    mb = [mixer_bufs(512), mixer_bufs(64)]
    diag = A.alloc([128, 31, 128], BF16, at=mb[0]['off_t8'])
    tm = {}
    for h in range(2):
        tm[("S1m", h)] = A.alloc([128, 4, 2, 128], BF16, name="tm_S1m%d" % h)
        tm[("S2m", h)] = A.alloc([128, 4, 2, 128], BF16, name="tm_S2m%d" % h)
        tm[("L", h)] = A.alloc([128, 4, 128], BF16, name="tm_L%d" % h)
        tm[("X", h)] = A.alloc([128, 4, 128], BF16, name="tm_X%d" % h)
    tm["KBV"] = A.alloc([128, 4, 3, 128], BF16, name="tm_KBV")
    tm["Psb"] = A.alloc([128, 2, 64], BF16, name="tm_Psb")
    tm["Usb"] = A.alloc([128, 2, 64], BF16, name="tm_Usb")
    tm["STb"] = A.alloc([128, 64], BF16, name="tm_STb")
    tm["gC"] = A.alloc([128, 8], F32, name="tm_gC")
    identb4 = A.alloc([128, 4, 128], BF16)
    mc = {64: A.alloc([128, 2, 64], F32), 128: A.alloc([128, 2, 128], F32)}
    stage = mb[0]["hconv"].rearrange("p a b -> p (a b)")
    stage2 = stage[:, 0:1024]
    stage3 = stage[:, 1024:1664]
    mix_end = A.off
    act = A.alloc([128, FC, 512], BF16, at=base_off)
    assert base_off + FC * 512 * 2 <= A.cap
    A.off = max(mix_end, base_off + FC * 512 * 2 + 4096)
    ftmp = [A.alloc([128, 512], F32, at=base_off + FC * 512 * 2), A.alloc([128, 512], F32, at=base_off + FC * 512 * 2 + 2048)]
    print("SBUF used", A.off, "of", A.cap)

    psb = [nc.alloc_psum_tensor("ps%d" % i, [128, 512], F32) for i in range(8)]
    bank_rr = {"big": 0, "small": 0}

    def bank(pool):
        if pool == "big":
            i = bank_rr["big"] % 3
            bank_rr["big"] += 1
            return i
        if pool == "p1":
            i = (3, 4, 5, 6, 7)[bank_rr.setdefault("p1", 0) % 5]
            bank_rr["p1"] += 1
            return i
        if pool == "y":
            return 3
        if pool == "state":
            return 7
        i = 4 + bank_rr["small"] % 3
        bank_rr["small"] += 1
        return i

    psap = [t[:, :] for t in psb]

    def PS(i):
        return psap[i]

    wsem = [S.new_dma_sem("w%d" % i) for i in range(NSLOT)]
    wsem_small = S.new_dma_sem("wsm")
    wq = {"next": 0, "issued": 0, "order": []}

    def w_issue_upto(n):
        while wq["issued"] < min(n, len(wq["order"])):
            i = wq["issued"]
            (l, b, ncols) = wq["order"][i]
            slot = i % NSLOT
            S.dma("pool", wring[:, slot, 0:ncols], wblk[l, b, :, 0:ncols], w=[("w", slot)], sem=wsem[slot])
            wq["issued"] += 1

    def w_next():
        i = wq["next"]
        wq["next"] += 1
        w_issue_upto(i + NSLOT - 1)
        return i % NSLOT

    def blk_in(cc):
        return 2 + cc
    def blk_out(n):
        return 2 + 38 + n
    def blk_gate(f):
        return 2 + 38 + 16 + 2 * f
    def blk_up(f):
        return 2 + 38 + 16 + 2 * f + 1
    def blk_down(n, j):
        return 2 + 38 + 16 + 88 + 3 * n + j
    IN_ORDER = [4, 0, 5, 1, 6, 2, 7, 3, 8, 9, 10, 11, 36, 37]
    for p_ in range(PAIRS):
        IN_ORDER += [12 + p_, 20 + p_, 28 + p_]

    def layer_order(l):
        o = [(l, blk_in(cc), SLOT) for cc in IN_ORDER]
        o += [(l, blk_out(n), SLOT) for n in range(16)]
        for f in range(FC):
            o += [(l, blk_gate(f), SLOT), (l, blk_up(f), SLOT)]
        for n in range(16):
            o += [(l, blk_down(n, 0), SLOT), (l, blk_down(n, 1), SLOT), (l, blk_down(n, 2), 12 * 128)]
        return o

    def pool_op(fn, r=(), w=()):
        return S.op("pool", fn, r, w)

    S.dma("sp", vecs, vecs_d[:, :], w=["vecs"])
    pool_op(lambda e: e.memset(ident_f, 1.0), w=["c_if"])
    pool_op(lambda e: e.affine_select(out=ident_f, in_=ident_f, pattern=[[-1, 128]], compare_op=ALU.is_equal,
                                      fill=0.0, base=0, channel_multiplier=1), r=["c_if"], w=["c_if"])
    S.op("dve", lambda e: e.tensor_copy(out=ident_b, in_=ident_f), r=["c_if"], w=["c_ib"])
    for i4 in range(4):
        S.op("dve", lambda e, i4=i4: e.tensor_copy(out=identb4[:, i4, :], in_=ident_f), r=["c_if"], w=["c_ib4"])
    S.op("dve", lambda e: e.memset(ones_b, 1.0), w=["c_ones"])
    S.op("dve", lambda e: e.memset(onesD_f, 1.0 / 512.0), w=["c_onesD"])
    S.op("dve", lambda e: e.memset(bones_b, 0.0), w=["c_bones"])
    S.op("dve", lambda e: e.memset(bones_b[0:64, 0:64], 1.0), w=["c_bones"])
    S.op("dve", lambda e: e.memset(bones_b[64:128, 64:128], 1.0), w=["c_bones"])
    S.op("dve", lambda e: e.memset(bones_f, 0.0), w=["c_bonesf"])
    S.op("dve", lambda e: e.memset(bones_f[0:64, 0:64], 1.0 / 64.0), w=["c_bonesf"])
    S.op("dve", lambda e: e.memset(bones_f[64:128, 64:128], 1.0 / 64.0), w=["c_bonesf"])
    for (m, base, cm, step) in ((m_su, -1, -1, 1), (m_ui, 0, -1, 1), (m_sl, -1, 1, -1)):
        pool_op(lambda e, m=m: e.memset(m, 1.0), w=["c_masks"])
        pool_op(lambda e, m=m, base=base, cm=cm, step=step: e.affine_select(
            out=m, in_=m, pattern=[[step, 128]], compare_op=ALU.is_ge, fill=0.0, base=base,
            channel_multiplier=cm), r=["c_masks"], w=["c_masks"])
    for C in (64, 128):
        S.op("dve", lambda e, C=C: e.tensor_copy(out=mc[C][:, 0, :], in_=m_su[:, 0:C]), r=["c_masks"], w=["c_mc"])
        S.op("dve", lambda e, C=C: e.tensor_copy(out=mc[C][:, 1, :], in_=m_ui[:, 0:C]), r=["c_masks"], w=["c_mc"])
        S.op("dve", lambda e, C=C: e.memset(cmask[C], 1.0), w=["c_cmask"])
        S.op("dve", lambda e, C=C: e.memset(cmask[C].rearrange("p (a b) -> p a b", b=C)[:, :, 0:1], 0.0),
             r=["c_cmask"], w=["c_cmask"])
    pool_op(lambda e: e.iota(out=invc_first[:, 0, :], pattern=[[1, 16]], base=1, channel_multiplier=0,
                             allow_small_or_imprecise_dtypes=True), w=["c_invc"])
    for gi, wdw in enumerate(POOL_WINDOWS):
        if gi > 0:
            S.op("dve", lambda e, gi=gi: e.tensor_copy(out=invc_first[:, gi, :], in_=invc_first[:, 0, :]),
                 r=["c_invc"], w=["c_invc%d" % gi])
    for gi, wdw in enumerate(POOL_WINDOWS):
        S.op("dve", lambda e, gi=gi, wdw=wdw: e.tensor_scalar(out=invc_first[:, gi, :], in0=invc_first[:, gi, :],
                                                              scalar1=float(wdw), scalar2=None, op0=ALU.min),
             r=["c_invc", "c_invc%d" % gi], w=["c_invc%d" % gi] + (["c_invc"] if gi == 0 else []))
        S.op("dve", lambda e, gi=gi: e.reciprocal(out=invc_first[:, gi, :], in_=invc_first[:, gi, :]),
             r=["c_invc%d" % gi], w=["c_invc%d" % gi] + (["c_invc"] if gi == 0 else []))
    for l in range(DEPTH):
        o = l * VL + VO["mu"]
        S.op("dve", lambda e, l=l, o=o: e.tensor_scalar(out=omu[:, l, :], in0=vecs[:, o:o + NQ], scalar1=-1.0,
                                                        scalar2=1.0, op0=ALU.mult, op1=ALU.add),
             r=["vecs"], w=["omu"])
    CONST_KEYS = ["vecs", "omu", "c_if", "c_ib", "c_ones", "c_onesD", "c_bones", "c_bonesf", "c_masks",
                  "c_cmask", "c_onesrow", "c_invc", "c_invc1", "c_invc2", "c_invc3"]
    S.barrier()

    def V(l, name, c0=0, n=1):
        o = l * VL + VO[name] + c0
        return vecs[:, o:o + n]

    def rmsnorm_to(dst_fn, gw, gcol, key_out, l_vec_off, dst_is_bf=True):
        b = bank("small")
        fns = []
        for k in range(KC):
            S.op("act", lambda e, k=k: e.activation(out=sqb[:, 0:gw], in_=xT[:, k, 0:gw], func=AF.Square),
                 r=["xT"], w=["sqb"])
            S.pe_group([lambda e, k=k: e.matmul(PS(b)[:, 0:gw], lhsT=ones_b, rhs=sqb[:, 0:gw],
                                                 start=(k == 0), stop=(k == KC - 1))],
                       r=["sqb"], w=[("ps", b)])
        S.op("act", lambda e: e.activation(out=rstd[:, 0:gw], in_=PS(b)[:, 0:gw], func=AF.Sqrt,
                                           bias=RMS_EPS, scale=1.0 / D), r=[("ps", b)], w=["rstd"])
        S.op("dve", lambda e: e.reciprocal(out=rstd[:, 0:gw], in_=rstd[:, 0:gw]), r=["rstd"], w=["rstd"])
        for k in range(KC):
            S.op("dve", lambda e, k=k: e.scalar_tensor_tensor(
                out=dst_fn(k), in0=xT[:, k, 0:gw], scalar=vecs[:, l_vec_off + k:l_vec_off + k + 1],
                in1=rstd[:, 0:gw], op0=ALU.mult, op1=ALU.mult), r=["xT", "rstd"], w=[key_out])

    def proj(slot, src, gw, key_src, kchunks=KC, b=None):
        if b is None:
            b = bank("big")
        wv = wring[:, slot, :].rearrange("p (k n) -> p k n", n=128)
        fns = [lambda e, k=k: e.matmul(PS(b)[:, 0:gw], lhsT=wv[:, k, :], rhs=src[:, k, 0:gw],
                                       start=(k == 0), stop=(k == kchunks - 1)) for k in range(kchunks)]
        S.pe_group(fns, r=[("w", slot), key_src], w=[("ps", b)])
        return b

    def shifted_from_psum(l, bq, qc, p, dst, key_dst, scratch, key_scr):
        W, off, bi = p["W"], p["off"], p["bi"]
        sk = (p["seq"], l)
        sd = st[sk]
        S.op("act", lambda e: e.activation(out=scratch[:, 0:W], in_=PS(bq)[:, off:off + W], func=AF.Identity,
                                           scale=omu[:, l, qc:qc + 1]), r=[("ps", bq)], w=[key_scr])
        S.op("dve", lambda e: e.scalar_tensor_tensor(
            out=dst[:, 1:W], in0=PS(bq)[:, off:off + W - 1], scalar=V(l, "mu", qc), in1=scratch[:, 1:W],
            op0=ALU.mult, op1=ALU.add), r=[("ps", bq), key_scr], w=[key_dst])
        S.op("dve", lambda e: e.scalar_tensor_tensor(
            out=dst[:, 0:1], in0=sd["q"][:, qc:qc + 1], scalar=V(l, "mu", qc), in1=scratch[:, 0:1],
            op0=ALU.mult, op1=ALU.add), r=[("st_q",) + sk, key_scr], w=[key_dst])
        S.op("act", lambda e: e.activation(out=sd["q"][:, qc:qc + 1], in_=PS(bq)[:, off + W - 1:off + W],
                                           func=AF.Copy), r=[("ps", bq), key_dst], w=[("st_q",) + sk])

    def wkv_pair(l, pr, p, br, bk, bv_):
        B = mb[p["bi"]]
        W, off, bi, C = p["W"], p["off"], p["bi"], p["C"]
        nch = W // C
        sk = (p["seq"], l)
        sd = st[sk]
        T = lambda n: B[n][:, 0:W]
        K = lambda n: (n, bi)
        c3 = lambda ap: ap.rearrange("p (a c) -> p a c", c=C)
        shifted_from_psum(l, br, pr, p, B["t1"], K("t1"), B["t0"], K("t0"))
        shifted_from_psum(l, bk, 8 + pr, p, B["t2"], K("t2"), B["t0"], K("t0"))
        shifted_from_psum(l, bv_, 16 + pr, p, B["t3"], K("t3"), B["t0"], K("t0"))
        r_, k_, v_ = T("t1"), T("t2"), T("t3")
        bw = bank("small")
        S.pe_group([lambda e: e.matmul(PS(bw)[:, 0:W], lhsT=wsmall[0:64, pr * 128:(pr + 1) * 128],
                                       rhs=B["lora"][0:64, 0, 0:W], start=True, stop=True)],
                   r=["wsmall", K("lora")], w=[("ps", bw)])
        lw = T("t4")
        S.op("act", lambda e: e.activation(out=lw, in_=PS(bw)[:, 0:W], func=AF.Sigmoid,
                                           bias=V(l, "w0", pr), scale=1.0), r=[("ps", bw)], w=[K("t4")])
        ba = bank("small")
        S.pe_group([lambda e: e.matmul(PS(ba)[:, 0:W], lhsT=wsmall[64:128, pr * 128:(pr + 1) * 128],
                                       rhs=B["lora"][64:128, 0, 0:W], start=True, stop=True)],
                   r=["wsmall", K("lora")], w=[("ps", ba)])
        a_ = T("t5")
        S.op("act", lambda e: e.activation(out=a_, in_=PS(ba)[:, 0:W], func=AF.Sigmoid,
                                           bias=V(l, "a0", pr), scale=1.0), r=[("ps", ba)], w=[K("t5")])
        bgp = bank("small")
        S.pe_group([lambda e: e.matmul(PS(bgp)[:, 0:W], lhsT=wsmall[0:64, 1024 + pr * 128:1024 + (pr + 1) * 128],
                                       rhs=B["lora"][0:64, 1, 0:W], start=True, stop=True)],
                   r=["wsmall", K("lora")], w=[("ps", bgp)])
        g_ = T("t6")
        S.op("act", lambda e: e.activation(out=g_, in_=PS(bgp)[:, 0:W], func=AF.Copy), r=[("ps", bgp)], w=[K("t6")])
        kk = T("t7")
        S.op("dve", lambda e: e.tensor_scalar(out=kk, in0=k_, scalar1=V(l, "kk", pr), scalar2=None, op0=ALU.mult),
             r=[K("t2")], w=[K("t7")])
        S.op("act", lambda e: e.activation(out=T("b0"), in_=kk, func=AF.Square), r=[K("t7")], w=[K("b0")])
        bs = bank("small")
        S.pe_group([lambda e: e.matmul(PS(bs)[:, 0:W], lhsT=bones_b, rhs=T("b0"), start=True, stop=True)],
                   r=[K("b0")], w=[("ps", bs)])
        nrm = T("t8")
        S.op("dve", lambda e: e.tensor_scalar(out=nrm, in0=PS(bs)[:, 0:W], scalar1=1e-24, scalar2=None, op0=ALU.max),
             r=[("ps", bs)], w=[K("t8")])
        S.op("act", lambda e: e.activation(out=nrm, in_=nrm, func=AF.Sqrt), r=[K("t8")], w=[K("t8")])
        S.op("dve", lambda e: e.reciprocal(out=nrm, in_=nrm), r=[K("t8")], w=[K("t8")])
        S.op("dve", lambda e: e.tensor_tensor(out=kk, in0=kk, in1=nrm, op=ALU.mult), r=[K("t7"), K("t8")], w=[K("t7")])
        bvec = T("t8")
        S.op("dve", lambda e: e.tensor_tensor(out=bvec, in0=kk, in1=a_, op=ALU.mult), r=[K("t7"), K("t5")], w=[K("t8")])
        kp = T("t9")
        S.op("dve", lambda e: e.tensor_scalar(out=kp, in0=a_, scalar1=-1.0, scalar2=V(l, "ka", pr), op0=ALU.add,
                                              op1=ALU.mult), r=[K("t5")], w=[K("t9")])
        S.op("dve", lambda e: e.scalar_tensor_tensor(out=kp, in0=kp, scalar=1.0, in1=k_, op0=ALU.add, op1=ALU.mult),
             r=[K("t9"), K("t2")], w=[K("t9")])
        S.op("dve", lambda e: e.scalar_tensor_tensor(out=T("b0"), in0=r_, scalar=V(l, "rk", pr), in1=kp, op0=ALU.mult,
                                                     op1=ALU.mult), r=[K("t1"), K("t9")], w=[K("b0")])
        bb = bank("small")
        S.pe_group([lambda e: e.matmul(PS(bb)[:, 0:W], lhsT=bones_b, rhs=T("b0"), start=True, stop=True)],
                   r=[K("b0")], w=[("ps", bb)])
        bonus = T("t10")
        S.op("dve", lambda e: e.tensor_tensor(out=bonus, in0=PS(bb)[:, 0:W], in1=v_, op=ALU.mult),
             r=[("ps", bb), K("t3")], w=[K("t10")])
        S.op("dve", lambda e: e.tensor_scalar(out=lw, in0=lw, scalar1=LW_SCALE, scalar2=None, op0=ALU.mult),
             r=[K("t4")], w=[K("t4")])
        cl = T("t11")
        S.op("dve", lambda e: e.tensor_tensor_scan(out=cl, data0=cmask[C][:, 0:W], data1=lw, initial=0.0,
                                                   op0=ALU.mult, op1=ALU.add), r=[K("t4")], w=[K("t11")])
        gC = tm["gC"]
        S.op("act", lambda e: e.activation(out=gC[:, 0:nch], in_=c3(cl)[:, :, C - 1], func=AF.Exp),
             r=[K("t11")], w=["gC"])
        e_pos = T("t0")
        S.op("act", lambda e: e.activation(out=e_pos, in_=cl, func=AF.Exp), r=[K("t11")], w=[K("t0")])
        AR = B["AR"][:, 0:2 * W].rearrange("p (a two c) -> p a two c", two=2, c=C)
        S.op("dve", lambda e: e.tensor_tensor(out=AR[:, :, 1, :], in0=c3(r_), in1=c3(e_pos), op=ALU.mult),
             r=[K("t1"), K("t0")], w=[K("AR")])
        S.op("dve", lambda e: e.tensor_tensor(out=lw, in0=cl, in1=lw, op=ALU.subtract), r=[K("t11"), K("t4")],
             w=[K("t4")])
        S.op("act", lambda e: e.activation(out=lw, in_=lw, func=AF.Exp), r=[K("t4")], w=[K("t4")])
        S.op("dve", lambda e: e.scalar_tensor_tensor(out=AR[:, :, 0, :], in0=c3(kk), scalar=-1.0, in1=c3(lw),
                                                     op0=ALU.mult, op1=ALU.mult), r=[K("t7"), K("t4")], w=[K("AR")])
        e_neg = T("t0")
        S.op("act", lambda e: e.activation(out=e_neg, in_=cl, func=AF.Exp, scale=-1.0), r=[K("t11"), K("AR")],
             w=[K("t0")])
        kt, bt, kh, bh, vb = T("b1"), T("b2"), T("b3"), T("b4"), T("b5")
        S.op("dve", lambda e: e.tensor_tensor(out=kp, in0=kp, in1=e_neg, op=ALU.mult), r=[K("t9"), K("t0")], w=[K("t9")])
        S.op("act", lambda e: e.activation(out=kt, in_=kp, func=AF.Copy), r=[K("t9")], w=[K("b1")])
        S.op("dve", lambda e: e.tensor_tensor(out=bvec, in0=bvec, in1=e_neg, op=ALU.mult), r=[K("t8"), K("t0")],
             w=[K("t8")])
        S.op("act", lambda e: e.activation(out=bt, in_=bvec, func=AF.Copy), r=[K("t8")], w=[K("b2")])
        for ch in range(nch):
            cs = slice(ch * C, (ch + 1) * C)
            S.op("dve", lambda e, cs=cs, ch=ch: e.tensor_scalar(out=kh[:, cs], in0=kp[:, cs], scalar1=gC[:, ch:ch + 1],
                                                                scalar2=None, op0=ALU.mult), r=[K("t9"), "gC"], w=[K("b3")])
            S.op("dve", lambda e, cs=cs, ch=ch: e.tensor_scalar(out=bh[:, cs], in0=bvec[:, cs], scalar1=gC[:, ch:ch + 1],
                                                                scalar2=None, op0=ALU.mult), r=[K("t8"), "gC"], w=[K("b4")])
        S.op("act", lambda e: e.activation(out=vb, in_=v_, func=AF.Copy), r=[K("t3")], w=[K("b5")])

        by = bank("y")
        STf = sd["S"][:, pr, :]
        STb = tm["STb"]
        nupd = {64: 5, 128: 6}[C]
        HS = [slice(0, 64), slice(64, 128)]
        KBV = tm["KBV"]
        ARc = lambda ch: B["AR"][:, ch * 2 * C:(ch + 1) * 2 * C]
        CS = lambda ch: slice(ch * C, (ch + 1) * C)
        for ch in range(nch):
            btp = bank("p1")
            ptv = PS(btp)[:, 0:192].bitcast(BF16).rearrange("p (a c) -> p a c", c=128)
            S.pe_group([lambda e, src_=src_, i=i, ch=ch: e.transpose(out=ptv[0:C, i, :], in_=src_[:, CS(ch)], identity=ident_b)
                        for i, src_ in enumerate((kh, bh, vb))], r=[K("b3"), K("b4"), K("b5")], w=[("ps", btp)])
            S.op("act", lambda e, ch=ch: e.activation(out=KBV[0:C, ch], in_=ptv[0:C], func=AF.Copy),
                 r=[("ps", btp)], w=["KBV"])
        S1 = [tm[("S1m", h)] for h in range(2)]
        S2 = [tm[("S2m", h)] for h in range(2)]
        Lt = [tm[("L", h)] for h in range(2)]
        Xt = [tm[("X", h)] for h in range(2)]
        for c0 in range(0, nch, 2):
            cn = min(2, nch - c0)
            for h in range(2):
                hs = HS[h]
                for (lhs, dst, key, eng) in ((kt, S1[h], ("S1m", h), "dve"), (bt, S2[h], ("S2m", h), "dve")):
                    b1 = bank("p1")
                    S.pe_group([lambda e, ch=ch, j=j, lhs=lhs, b1=b1: e.matmul(PS(b1)[0:C, j * 2 * C:(j + 1) * 2 * C], lhsT=lhs[hs, CS(ch)],
                                                                               rhs=ARc(ch)[hs, :], start=True, stop=True)
                                for j, ch in enumerate(range(c0, c0 + cn))],
                               r=[K("b1"), K("b2"), K("AR")], w=[("ps", b1)])
                    for j, ch in enumerate(range(c0, c0 + cn)):
                        S.op(eng, lambda e, ch=ch, j=j, dst=dst, b1=b1: e.tensor_tensor(
                            out=dst[0:C, ch, :, 0:C], in0=PS(b1)[0:C, j * 2 * C:(j + 1) * 2 * C].rearrange("p (a c) -> p a c", c=C),
                            in1=mc[C][0:C], op=ALU.mult), r=[("ps", b1)], w=[key])
        for h in range(2):
            hs = HS[h]
            b3 = bank("p1")
            S.pe_group([lambda e, ch=ch, b3=b3: e.matmul(PS(b3)[0:C, ch * C:(ch + 1) * C], lhsT=ARc(ch)[hs, 0:C], rhs=bt[hs, CS(ch)],
                                                          start=True, stop=True) for ch in range(nch)],
                       r=[K("b2"), K("AR")], w=[("ps", b3)])
            for ch in range(nch):
                S.op("dve", lambda e, ch=ch, b3=b3, h=h: e.tensor_tensor(out=Lt[h][0:C, ch, 0:C], in0=PS(b3)[0:C, ch * C:(ch + 1) * C],
                                                                         in1=m_sl[0:C, 0:C], op=ALU.mult),
                     r=[("ps", b3)], w=[("L", h)])
            S.op("pool", lambda e, h=h: e.tensor_tensor(out=Xt[h][0:C, 0:nch, 0:C], in0=S2[h][0:C, 0:nch, 0, 0:C],
                                                        in1=identb4[0:C, 0:nch, 0:C], op=ALU.add),
                 r=[("S2m", h)], w=[("X", h)])
        c3v = lambda ap: ap[0:C, 0:nch * C].rearrange("p (a c) -> p a c", c=C)
        for u in range(nupd):
            last = (u == nupd - 1)
            bl, bn = {}, {}
            for h in range(2):
                bl[h] = bank("p1")
                S.pe_group([lambda e, ch=ch, h=h: e.matmul(PS(bl[h])[0:C, ch * C:(ch + 1) * C], lhsT=S2[h][0:C, ch, 0, 0:C],
                                                          rhs=Lt[h][0:C, ch, 0:C], start=True, stop=True) for ch in range(nch)],
                           r=[("L", h), ("S2m", h)], w=[("ps", bl[h])])
                if not last:
                    bn[h] = bank("p1")
                    S.pe_group([lambda e, ch=ch, h=h: e.matmul(PS(bn[h])[0:C, ch * C:(ch + 1) * C], lhsT=Lt[h][0:C, ch, 0:C],
                                                              rhs=S2[h][0:C, ch, 0, 0:C], start=True, stop=True) for ch in range(nch)],
                               r=[("L", h), ("S2m", h)], w=[("ps", bn[h])])
            for h in range(2):
                S.op("act", lambda e, h=h: e.activation(out=Lt[h][0:C, 0:nch, 0:C], in_=c3v(PS(bl[h])), func=AF.Copy),
                     r=[("ps", bl[h])], w=[("L", h)])
                if not last:
                    S.op("dve", lambda e, h=h: e.tensor_copy(out=S2[h][0:C, 0:nch, 0, 0:C], in_=c3v(PS(bn[h]))),
                         r=[("ps", bn[h])], w=[("S2m", h)])
            bx = {}
            for h in range(2):
                bx[h] = bank("p1")
                S.pe_group([lambda e, ch=ch, h=h: e.matmul(PS(bx[h])[0:C, ch * C:(ch + 1) * C], lhsT=Lt[h][0:C, ch, 0:C],
                                                          rhs=Xt[h][0:C, ch, 0:C], start=True, stop=True) for ch in range(nch)],
                           r=[("L", h), ("X", h)], w=[("ps", bx[h])])
            for h in range(2):
                S.op("dve", lambda e, h=h: e.tensor_tensor(out=Xt[h][0:C, 0:nch, 0:C], in0=c3v(PS(bx[h])),
                                                           in1=Xt[h][0:C, 0:nch, 0:C], op=ALU.add),
                     r=[("ps", bx[h]), ("X", h)], w=[("X", h)])

        S.op("act", lambda e: e.activation(out=STb, in_=STf, func=AF.Copy), r=[("st_S",) + sk], w=["STb"])
        Psb, Usb = tm["Psb"], tm["Usb"]
        bS = bank("state")
        for ch in range(nch):
            cs = CS(ch)
            VT = KBV[0:C, ch, 2, :]
            KH = KBV[0:C, ch, 0, :]
            BH = KBV[0:C, ch, 1, :]
            bP = bank("small")
            for h in range(2):
                hs = HS[h]
                rowsplit = (C == 64 and h == 1)
                S.pe_group([lambda e: e.matmul(PS(bP)[0:C, h * 64:h * 64 + 64], lhsT=ARc(ch)[hs, 0:C], rhs=STb[hs, :], start=True, stop=False)],
                           r=[K("AR"), "STb"], w=[("ps", bP)], pe_sync=(C == 64))
                S.pe_group([lambda e: e.matmul(PS(bP)[0:C, h * 64:h * 64 + 64], lhsT=S1[h][0:C, ch, 0, 0:C], rhs=VT[:, hs], start=False, stop=True)],
                           r=[("S1m", h), "KBV"], w=[("ps", bP)], pe_sync=(C == 64))
            S.op("act", lambda e: e.activation(out=Psb[0:C], in_=PS(bP)[0:C, 0:128].rearrange("p (a c) -> p a c", c=64), func=AF.Copy),
                 r=[("ps", bP)], w=["Psb"])
            bU = bank("small")
            for h in range(2):
                S.pe_group([lambda e: e.matmul(PS(bU)[0:C, h * 64:h * 64 + 64], lhsT=Xt[h][0:C, ch, 0:C], rhs=Psb[0:C, h, :], start=True, stop=True)],
                           r=[("X", h), "Psb"], w=[("ps", bU)], pe_sync=(C == 64))
            S.op("dve", lambda e: e.tensor_copy(out=Usb[0:C], in_=PS(bU)[0:C, 0:128].rearrange("p (a c) -> p a c", c=64)),
                 r=[("ps", bU)], w=["Usb"])
            for h in range(2):
                hs = HS[h]
                S.pe_group([lambda e: e.matmul(PS(bS)[hs, 0:64], lhsT=BH[:, hs], rhs=Usb[0:C, h, :], start=True, stop=False),
                            lambda e: e.matmul(PS(bS)[hs, 0:64], lhsT=KH[:, hs], rhs=VT[:, hs], start=False, stop=True)],
                           r=["KBV", "Usb"], w=[("ps", bS)], pe_sync=(C == 64))
            for h in range(2):
                hs = HS[h]
                S.pe_group([lambda e: e.matmul(PS(by)[hs, cs], lhsT=STb[hs, :], rhs=ARc(ch)[hs, C:2 * C], start=True, stop=False)],
                           r=["STb", K("AR")], w=[("ps", by)], pe_sync=(C == 64))
                S.pe_group([lambda e: e.matmul(PS(by)[hs, cs], lhsT=Usb[0:C, h, :], rhs=S2[h][0:C, ch, 1, 0:C], start=False, stop=False),
                            lambda e: e.matmul(PS(by)[hs, cs], lhsT=VT[:, hs], rhs=S1[h][0:C, ch, 1, 0:C], start=False, stop=True)],
                           r=["Usb", ("S2m", h), ("S1m", h), "KBV"], w=[("ps", by)], pe_sync=(C == 64))
            S.op("dve", lambda e, ch=ch: e.scalar_tensor_tensor(out=STf, in0=STf, scalar=gC[:, ch:ch + 1], in1=PS(bS)[:, 0:64],
                                                                op0=ALU.mult, op1=ALU.add),
                 r=[("ps", bS), ("st_S",) + sk, "gC"], w=[("st_S",) + sk])
            S.op("act", lambda e: e.activation(out=STb, in_=STf, func=AF.Copy), r=[("st_S",) + sk], w=["STb"])

        y = T("t7")
        S.op("act", lambda e: e.activation(out=y, in_=PS(by)[:, 0:W], func=AF.Copy), r=[("ps", by)], w=[K("t7")])
        bm = bank("small")
        S.pe_group([lambda e: e.matmul(PS(bm)[:, 0:W], lhsT=bones_f, rhs=y, start=True, stop=True)],
                   r=[K("t7")], w=[("ps", bm)])
        S.op("dve", lambda e: e.tensor_tensor(out=y, in0=y, in1=PS(bm)[:, 0:W], op=ALU.subtract),
             r=[("ps", bm), K("t7")], w=[K("t7")])
        sq = T("t8")
        S.op("act", lambda e: e.activation(out=sq, in_=y, func=AF.Square), r=[K("t7")], w=[K("t8")])
        bv2 = bank("small")
        S.pe_group([lambda e: e.matmul(PS(bv2)[:, 0:W], lhsT=bones_f, rhs=sq, start=True, stop=True)],
                   r=[K("t8")], w=[("ps", bv2)])
        rs = T("t9")
        S.op("act", lambda e: e.activation(out=rs, in_=PS(bv2)[:, 0:W], func=AF.Sqrt, bias=GN_EPS, scale=1.0),
             r=[("ps", bv2)], w=[K("t9")])
        S.op("dve", lambda e: e.reciprocal(out=rs, in_=rs), r=[K("t9")], w=[K("t9")])
        S.op("dve", lambda e: e.tensor_tensor(out=y, in0=y, in1=rs, op=ALU.mult), r=[K("t7"), K("t9")], w=[K("t7")])
        S.op("dve", lambda e: e.tensor_scalar(out=y, in0=y, scalar1=V(l, "gg", pr), scalar2=V(l, "gb", pr),
                                              op0=ALU.mult, op1=ALU.add), r=[K("t7")], w=[K("t7")])
        S.op("dve", lambda e: e.tensor_tensor(out=y, in0=y, in1=bonus, op=ALU.add), r=[K("t7"), K("t10")], w=[K("t7")])
        S.op("dve", lambda e: e.tensor_tensor(out=cat[:, 8 + pr, off:off + W], in0=y, in1=g_, op=ALU.mult),
             r=[K("t7"), K("t6")], w=["cat"])

    groups = []
    for g in range(npg):
        groups.append(dict(gw=512, parts=[dict(seq="P", off=0, W=512, C=128, first=(g == 0), last=(g == npg - 1),
                                               bi=0)],
                           src=xp[g * 512:(g + 1) * 512, :], dst=yp[g * 512:(g + 1) * 512, :]))
    if with_s:
      groups.append(dict(gw=128, parts=[dict(seq="S0", off=0, W=64, C=64, first=False, last=True, bi=0, sidx=0),
                                      dict(seq="S1", off=64, W=64, C=64, first=False, last=True, bi=1, sidx=1)],
                       src=xs[:, :], dst=ys[:, :]))
    for g in groups:
        for l in range(layers):
            wq["order"] += layer_order(l)

    dbg_out = {}

    def chk(name):
        if dbg == name:
            S.dead = True

    for gi_, G in enumerate(groups):
        gw = G["gw"]
        parts = G["parts"]
        ntb = gw // 128
        for tb in range(ntb):
            S.dma("sp", stage, G["src"][tb * 128:(tb + 1) * 128, :], w=["stage"])
            for k4 in range(4):
                b = bank("small")
                S.pe_group([lambda e, k=k: e.transpose(out=PS(b)[:, (k % 4) * 128:(k % 4 + 1) * 128],
                                                        in_=stage[:, k * 128:(k + 1) * 128], identity=ident_f)
                            for k in range(k4 * 4, k4 * 4 + 4)], r=["stage"], w=[("ps", b)])
                S.op("act" if k4 % 2 else "dve",
                     (lambda e, k4=k4, tb=tb, b=b: e.activation(
                         out=xT[:, k4 * 4:k4 * 4 + 4, tb * 128:(tb + 1) * 128],
                         in_=PS(b).rearrange("p (a c) -> p a c", c=128), func=AF.Copy)) if k4 % 2 else
                     (lambda e, k4=k4, tb=tb, b=b: e.tensor_copy(
                         out=xT[:, k4 * 4:k4 * 4 + 4, tb * 128:(tb + 1) * 128],
                         in_=PS(b).rearrange("p (a c) -> p a c", c=128))),
                     r=[("ps", b)], w=["xT"])

        for l in range(layers):
            LV = l * VL
            chk("A")
            S.dma("pool", wsmall, wblk[l, 0, :, :], w=["wsmall"], sem=wsem_small)
            S.dma("pool", wpool, wblk[l, 1, :, 0:512], w=["wpool"], sem=wsem_small)
            for p in parts:
                if p["seq"] == "P":
                    if p["first"]:
                        sd = st[("P", l)]
                        S.op("dve", lambda e, sd=sd: e.memset(sd["u"], 0.0), w=[("st_u", "P", l)])
                        S.op("dve", lambda e, sd=sd: e.memset(sd["p"], 0.0), w=[("st_p", "P", l)])
                        S.op("dve", lambda e, sd=sd: e.memset(sd["q"], 0.0), w=[("st_q", "P", l)])
                        S.op("dve", lambda e, sd=sd: e.memset(sd["S"], 0.0), w=[("st_S", "P", l)])
                    continue
                sq, si = p["seq"], p["sidx"]
                sd = st[(sq, l)]
                S.dma("sp", stage2[0:30, 0:512], cconv[l, si, :, :], w=["stage"])
                b = bank("small")
                S.pe_group([lambda e, c=c: e.transpose(out=PS(b)[:, c * 32:c * 32 + 30],
                                                        in_=stage2[0:30, c * 128:(c + 1) * 128],
                                                        identity=ident_f[0:30, 0:30]) for c in range(4)],
                           r=["stage"], w=[("ps", b)])
                S.op("dve", lambda e, sd=sd, b=b: e.tensor_copy(
                    out=sd["u"], in_=PS(b)[:, 0:128].rearrange("p (a c) -> p a c", c=32)[:, :, 0:30]),
                    r=[("ps", b)], w=[("st_u", sq, l)])
                S.dma("sp", stage2[0:15, 0:512], cpool[l, si, :, :], w=["stage"])
                b = bank("small")
                S.pe_group([lambda e, c=c: e.transpose(out=PS(b)[:, c * 16:c * 16 + 15],
                                                        in_=stage2[0:15, c * 128:(c + 1) * 128],
                                                        identity=ident_f[0:15, 0:15]) for c in range(4)],
                           r=["stage"], w=[("ps", b)])
                S.op("dve", lambda e, sd=sd, b=b: e.tensor_copy(
                    out=sd["p"], in_=PS(b)[:, 0:64].rearrange("p (a c) -> p a c", c=16)[:, :, 0:15]),
                    r=[("ps", b)], w=[("st_p", sq, l)])
                S.dma("sp", stage2[0:NQ, 0:128], cshift[l, si, :, :], w=["stage"])
                b = bank("small")
                S.pe_group([lambda e: e.transpose(out=PS(b)[:, 0:NQ], in_=stage2[0:NQ, 0:128],
                                                  identity=ident_f[0:NQ, 0:NQ])], r=["stage"], w=[("ps", b)])
                S.op("dve", lambda e, sd=sd, b=b: e.tensor_copy(out=sd["q"], in_=PS(b)[:, 0:NQ]),
                     r=[("ps", b)], w=[("st_q", sq, l)])
                S.dma("sp", stage2[0:64, :].rearrange("p (h j) -> p h j", j=64),
                      cwkv[l, si].rearrange("h i j -> i h j"), w=["stage"])
                for half in range(2):
                    b = bank("small")
                    S.pe_group([lambda e, pr=pr: e.transpose(
                        out=PS(b)[:, (pr % 4) * 64:(pr % 4) * 64 + 64], in_=stage2[0:64, pr * 128:(pr + 1) * 128],
                        identity=ident_f[0:64, 0:64]) for pr in range(half * 4, half * 4 + 4)],
                        r=["stage"], w=[("ps", b)])
                    S.op("dve", lambda e, sd=sd, b=b, half=half: e.tensor_copy(
                        out=sd["S"][:, half * 4:half * 4 + 4, :],
                        in_=PS(b)[:, 0:256].rearrange("p (a c) -> p a c", c=64)),
                        r=[("ps", b)], w=[("st_S", sq, l)])

            chk("A2")
            rmsnorm_to(lambda k: hT[:, k, 0:gw], gw, 0, "hT", LV + VO["nm"])
            chk("B")

            for c in range(4):
                bg = proj(w_next(), hT, gw, "hT")
                bv = proj(w_next(), hT, gw, "hT")
                for p in parts:
                    B = mb[p["bi"]]
                    W, off = p["W"], p["off"]
                    sk = (p["seq"], l)
                    t0 = B["t0"][:, 0:W]
                    if c == 0:
                        S.op("dve", lambda e, B=B, p=p: e.tensor_copy(out=B["ubuf"][:, :, 0:30],
                                                                     in_=st[(p["seq"], l)]["u"]),
                             r=[("st_u",) + sk], w=[("ubuf", p["bi"])])
                    S.op("act", lambda e, t0=t0, off=off, W=W, bg=bg: e.activation(
                        out=t0, in_=PS(bg)[:, off:off + W], func=AF.Sigmoid), r=[("ps", bg)], w=[("t0", p["bi"])])
                    S.op("dve", lambda e, B=B, t0=t0, off=off, W=W, bv=bv, c=c: e.tensor_tensor(
                        out=B["ubuf"][:, c, 30:30 + W], in0=PS(bv)[:, off:off + W], in1=t0, op=ALU.mult),
                        r=[("ps", bv), ("t0", p["bi"])], w=[("ubuf", p["bi"])])
            for p in parts:
                B = mb[p["bi"]]
                W, off, bi = p["W"], p["off"], p["bi"]
                sk = (p["seq"], l)
                S.op("act", lambda e, B=B, W=W: e.activation(out=B["ubf"][:, :, 0:30 + W], in_=B["ubuf"][:, :, 0:30 + W],
                                                            func=AF.Copy), r=[("ubuf", bi)], w=[("ubf", bi)])
                S.op("dve", lambda e, B=B, W=W, p=p: e.tensor_copy(out=st[(p["seq"], l)]["u"], in_=B["ubuf"][:, :, W:W + 30]),
                     r=[("ubuf", bi)], w=[("st_u",) + sk])
                pass
            for c in range(4):
                for j in range(31):
                    if j % 2 == 0:
                        S.op("act", lambda e, c=c, j=j: e.activation(out=diag[:, j, :], in_=ident_b, func=AF.Identity,
                                                                      scale=V(l, "cw", c * 31 + j)), r=[], w=[("diag", j)])
                    else:
                        S.op("dve", lambda e, c=c, j=j: e.tensor_scalar(
                            out=diag[:, j, :], in0=ident_b, scalar1=V(l, "cw", c * 31 + j), scalar2=None, op0=ALU.mult),
                            r=[], w=[("diag", j)])
                for p in parts:
                    B = mb[p["bi"]]
                    W, off, bi = p["W"], p["off"], p["bi"]
                    b = bank("small")
                    S.pe_group([lambda e, j=j, c=c, B=B, W=W, b=b: e.matmul(
                        PS(b)[:, 0:W], lhsT=diag[:, j, :], rhs=B["ubf"][:, c, j:j + W],
                        start=(j == 0), stop=(j == 30)) for j in range(31)],
                        r=[("diag", j) for j in range(31)] + [("ubf", bi)], w=[("ps", b)])
                    S.op("act", lambda e, B=B, W=W, b=b, c=c: e.activation(
                        out=B["hconv"][:, c, 0:W], in_=PS(b)[:, 0:W], func=AF.Identity,
                        bias=V(l, "cb", c), scale=1.0), r=[("ps", b)], w=[("hconv", bi)] + (["stage"] if bi == 0 else []))
            for p in parts:
                B = mb[p["bi"]]
                W, off, bi = p["W"], p["off"], p["bi"]
                sk = (p["seq"], l)
                bm = bank("small")
                S.pe_group([lambda e, c=c, B=B, W=W: e.matmul(PS(bm)[:, 0:W], lhsT=onesD_f, rhs=B["hconv"][:, c, 0:W],
                                                              start=(c == 0), stop=(c == 3)) for c in range(4)],
                           r=[("hconv", bi)], w=[("ps", bm)])
                mean = B["t1"][:, 0:W]
                S.op("act", lambda e, mean=mean, W=W: e.activation(out=mean, in_=PS(bm)[:, 0:W], func=AF.Copy),
                     r=[("ps", bm)], w=[("t1", bi)])
                for c in range(4):
                    S.op("dve", lambda e, c=c, B=B, W=W, mean=mean: e.tensor_tensor(
                        out=B["hconv"][:, c, 0:W], in0=B["hconv"][:, c, 0:W], in1=mean, op=ALU.subtract),
                        r=[("hconv", bi), ("t1", bi)], w=[("hconv", bi)])
                bvv = bank("small")
                for c in range(4):
                    S.op("act", lambda e, c=c, B=B, W=W: e.activation(out=B["t2"][:, 0:W], in_=B["hconv"][:, c, 0:W],
                                                                      func=AF.Square),
                         r=[("hconv", bi)], w=[("t2", bi)])
                    S.pe_group([lambda e, c=c, B=B, W=W: e.matmul(PS(bvv)[:, 0:W], lhsT=onesD_f, rhs=B["t2"][:, 0:W],
                                                                  start=(c == 0), stop=(c == 3))],
                               r=[("t2", bi)], w=[("ps", bvv)])
                rs = B["t3"][:, 0:W]
                S.op("act", lambda e, rs=rs, W=W: e.activation(out=rs, in_=PS(bvv)[:, 0:W], func=AF.Sqrt,
                                                              bias=LN_EPS, scale=1.0), r=[("ps", bvv)], w=[("t3", bi)])
                S.op("dve", lambda e, rs=rs: e.reciprocal(out=rs, in_=rs), r=[("t3", bi)], w=[("t3", bi)])
                for c in range(4):
                    S.op("dve", lambda e, c=c, B=B, W=W, rs=rs: e.tensor_tensor(
                        out=B["hconv"][:, c, 0:W], in0=B["hconv"][:, c, 0:W], in1=rs, op=ALU.mult),
                        r=[("hconv", bi), ("t3", bi)], w=[("hconv", bi)])
                    S.op("act", lambda e, c=c, B=B, W=W, off=off: e.activation(
                        out=cat[:, c, off:off + W], in_=B["hconv"][:, c, 0:W], func=AF.Silu,
                        bias=V(l, "lb", c), scale=V(l, "lg", c)), r=[("hconv", bi)], w=["cat"])

            chk("C")
            S.barrier()
            for c in range(4):
                bp = proj(w_next(), hT, gw, "hT")
                for p in parts:
                    B = mb[p["bi"]]
                    W, off, bi = p["W"], p["off"], p["bi"]
                    sk = (p["seq"], l)
                    if c == 0:
                        S.op("dve", lambda e, B=B, p=p: e.tensor_copy(out=B["pbuf"][:, :, 0:15],
                                                                     in_=st[(p["seq"], l)]["p"]),
                             r=[("st_p",) + sk], w=[("pbuf", bi)])
                    S.op("act", lambda e, B=B, W=W, off=off, bp=bp, c=c: e.activation(
                        out=B["pbuf"][:, c, 15:15 + W], in_=PS(bp)[:, off:off + W], func=AF.Copy),
                        r=[("ps", bp)], w=[("pbuf", bi)])
            for p in parts:
                B = mb[p["bi"]]
                W, off, bi = p["W"], p["off"], p["bi"]
                sk = (p["seq"], l)
                S.op("dve", lambda e, B=B, W=W, p=p: e.tensor_copy(out=st[(p["seq"], l)]["p"], in_=B["pbuf"][:, :, W:W + 15]),
                     r=[("pbuf", bi)], w=[("st_p",) + sk])
                for c, wdw in enumerate(POOL_WINDOWS):
                    src = B["pbuf"][:, c, :]
                    lo = 15
                    span = 1
                    ta, tb_ = B["t4"], B["t5"]
                    cur, cur_lo = src, 0
                    nsteps = {2: 1, 4: 2, 8: 3, 16: 4}[wdw]
                    for s_ in range(nsteps):
                        dst = ta if s_ % 2 == 0 else tb_
                        new_lo = cur_lo + span
                        n = 15 + W - new_lo
                        S.op("dve", lambda e, dst=dst, cur=cur, new_lo=new_lo, span=span, n=n: e.tensor_tensor(
                            out=dst[:, new_lo:new_lo + n], in0=cur[:, new_lo:new_lo + n],
                            in1=cur[:, new_lo - span:new_lo - span + n], op=ALU.add),
                            r=[("pbuf", bi), ("t4", bi), ("t5", bi)], w=[("t4" if s_ % 2 == 0 else "t5", bi)])
                        cur, cur_lo = dst, new_lo
                        span *= 2
                    S.op("dve", lambda e, cur=cur, W=W, c=c, B=B, wdw=wdw: e.scalar_tensor_tensor(
                        out=B["dpool"][:, c, 0:W], in0=cur[:, 15:15 + W], scalar=1.0 / wdw,
                        in1=B["pbuf"][:, c, 15:15 + W], op0=ALU.mult, op1=ALU.subtract),
                        r=[("t4", bi), ("t5", bi), ("pbuf", bi)], w=[("dpool", bi)])
                    if p["first"]:
                        S.op("dve", lambda e, cur=cur, c=c: e.tensor_tensor(
                            out=cur[:, 15:31], in0=cur[:, 15:31], in1=invc_first[:, c, :], op=ALU.mult),
                            r=[("t4", bi), ("t5", bi), ("dpool", bi)], w=[("t4", bi), ("t5", bi)])
                        S.op("dve", lambda e, cur=cur, c=c, B=B: e.tensor_tensor(
                            out=B["dpool"][:, c, 0:16], in0=cur[:, 15:31], in1=B["pbuf"][:, c, 15:31],
                            op=ALU.subtract), r=[("t4", bi), ("t5", bi), ("pbuf", bi)], w=[("dpool", bi)])
                    b = bank("small")
                    S.pe_group([lambda e, c=c, B=B, W=W, b=b: e.matmul(PS(b)[:, 0:W], lhsT=wpool[:, c * 128:(c + 1) * 128],
                                                                        rhs=B["dpool"][:, c, 0:W], start=True, stop=True)],
                               r=["wpool", ("dpool", bi)], w=[("ps", b)])
                    S.op("act", lambda e, c=c, W=W, off=off, b=b: e.activation(
                        out=cat[:, 4 + c, off:off + W], in_=PS(b)[:, 0:W], func=AF.Identity, scale=V(l, "psc", c)),
                        r=[("ps", b)], w=["cat"])


            chk("D")
            b24 = proj(w_next(), hT, gw, "hT")
            b25 = proj(w_next(), hT, gw, "hT")
            for p in parts:
                B = mb[p["bi"]]
                W, bi = p["W"], p["bi"]
                gl = B["gl"]
                shifted_from_psum(l, b24, 24, p, gl, ("t1", bi), B["t0"], ("t0", bi))
                S.op("act", lambda e, B=B, W=W, gl=gl: e.activation(out=B["lora"][0:64, 0, 0:W], in_=gl[0:64, 0:W],
                                                                    func=AF.Tanh), r=[("t1", bi)], w=[("lora", bi)])
                S.op("act", lambda e, B=B, W=W, gl=gl: e.activation(out=B["lora"][64:128, 0, 0:W], in_=gl[64:128, 0:W],
                                                                    func=AF.Copy), r=[("t1", bi)], w=[("lora", bi)])
                shifted_from_psum(l, b25, 25, p, gl, ("t1", bi), B["t0"], ("t0", bi))
                S.op("act", lambda e, B=B, W=W, gl=gl: e.activation(out=B["lora"][0:64, 1, 0:W], in_=gl[0:64, 0:W],
                                                                    func=AF.Sigmoid), r=[("t1", bi)], w=[("lora", bi)])

            chk("E")
            for pr in range(PAIRS):
                if pr == 1:
                    chk("F")
                br = proj(w_next(), hT, gw, "hT")
                bk = proj(w_next(), hT, gw, "hT")
                bv_ = proj(w_next(), hT, gw, "hT")
                for p in parts:
                    wkv_pair(l, pr, p, br, bk, bv_)

            chk("G")
            for n in range(16):
                bo = proj(w_next(), cat, gw, "cat")
                S.op("dve", lambda e, n=n, bo=bo: e.tensor_tensor(out=xT[:, n, 0:gw], in0=PS(bo)[:, 0:gw],
                                                                  in1=xT[:, n, 0:gw], op=ALU.add),
                     r=[("ps", bo), "xT"], w=["xT"])

            chk("H")
            rmsnorm_to(lambda k: hT[:, k, 0:gw], gw, 0, "hT", LV + VO["nf"])
            S.barrier()
            for f in range(FC if dbg != "outproj" else 0):
                bg = proj(w_next(), hT, gw, "hT")
                bu = proj(w_next(), hT, gw, "hT")
                ft = ftmp[f % 2]
                S.op("act", lambda e, ft=ft, bg=bg: e.activation(out=ft[:, 0:gw], in_=PS(bg)[:, 0:gw], func=AF.Silu),
                     r=[("ps", bg)], w=[("ftmp", f % 2)])
                S.op("dve", lambda e, ft=ft, bu=bu, f=f: e.tensor_tensor(out=act[:, f, 0:gw], in0=PS(bu)[:, 0:gw],
                                                                         in1=ft[:, 0:gw], op=ALU.mult),
                     r=[("ps", bu), ("ftmp", f % 2)], w=[("act", f)])
            for n in range(16 if dbg != "outproj" else 0):
                bd = bank("big")
                for j, nk in enumerate((16, 16, 12)):
                    sl = w_next()
                    wv = wring[:, sl, :].rearrange("p (k n) -> p k n", n=128)
                    fns = []
                    for k in range(nk):
                        f = j * 16 + k
                        fns.append(lambda e, wv=wv, k=k, f=f: e.matmul(PS(bd)[:, 0:gw], lhsT=wv[:, k, :],
                                                                        rhs=act[:, f, 0:gw], start=(f == 0),
                                                                        stop=(f == FC - 1)))
                    S.pe_group(fns, r=[("w", sl)] + [("act", f) for f in range(j * 16, j * 16 + nk)],
                               w=[("ps", bd)])
                S.op("dve", lambda e, n=n, bd=bd: e.tensor_tensor(out=xT[:, n, 0:gw], in0=PS(bd)[:, 0:gw],
                                                                  in1=xT[:, n, 0:gw], op=ALU.add),
                     r=[("ps", bd), "xT"], w=["xT"])
            S.barrier()

            chk("I")
            for p in parts:
                if not p["last"]:
                    continue
                sk = (p["seq"], l)
                sd = st[sk]
                oi = {"P": 0, "S0": 1, "S1": 2}[p["seq"]]
                b = bank("small")
                S.pe_group([lambda e, c=c: e.transpose(out=PS(b)[0:30, c * 128:(c + 1) * 128], in_=sd["u"][:, c, :],
                                                        identity=ident_f) for c in range(4)],
                           r=[("st_u",) + sk], w=[("ps", b)])
                S.op("act", lambda e, b=b: e.activation(out=stage2[0:30, 0:512], in_=PS(b)[0:30, 0:512], func=AF.Copy),
                     r=[("ps", b)], w=["stage"])
                S.dma("sp", nconv[l, oi, :, :], stage2[0:30, 0:512], r=["stage"], w=[("o_conv", l, oi)])
                b = bank("small")
                S.pe_group([lambda e, c=c: e.transpose(out=PS(b)[0:15, c * 128:(c + 1) * 128], in_=sd["p"][:, c, :],
                                                        identity=ident_f) for c in range(4)],
                           r=[("st_p",) + sk], w=[("ps", b)])
                S.op("act", lambda e, b=b: e.activation(out=stage2[0:15, 512:1024], in_=PS(b)[0:15, 0:512], func=AF.Copy),
                     r=[("ps", b)], w=["stage"])
                S.dma("sp", npool[l, oi, :, :], stage2[0:15, 512:1024], r=["stage"], w=[("o_pool", l, oi)])
                b = bank("small")
                S.pe_group([lambda e: e.transpose(out=PS(b)[0:NQ, 0:128], in_=sd["q"], identity=ident_f)],
                           r=[("st_q",) + sk], w=[("ps", b)])
                S.op("act", lambda e, b=b: e.activation(out=stage3[0:NQ, 512:640], in_=PS(b)[0:NQ, 0:128], func=AF.Copy),
                     r=[("ps", b)], w=["stage"])
                S.dma("sp", nshift[l, oi, :, :], stage3[0:NQ, 512:640], r=["stage"], w=[("o_shift", l, oi)])
                for half in range(2):
                    b = bank("small")
                    S.pe_group([lambda e, pr=pr: e.transpose(out=PS(b)[0:64, (pr % 4) * 128:(pr % 4 + 1) * 128],
                                                              in_=sd["S"][:, pr, :], identity=ident_f)
                                for pr in range(half * 4, half * 4 + 4)], r=[("st_S",) + sk], w=[("ps", b)])
                    S.op("act", lambda e, b=b: e.activation(out=stage3[0:64, 0:512], in_=PS(b)[0:64, 0:512],
                                                            func=AF.Copy), r=[("ps", b)], w=["stage"])
                    S.dma("sp", nwkv[l, oi, half * 8:half * 8 + 8].rearrange("h i j -> i h j"),
                          stage3[0:64, 0:512].rearrange("p (h j) -> p h j", j=64), r=["stage"],
                          w=[("o_wkv", l, oi, half)])

        S.dead = False
        if dbg is None:
            rmsnorm_to(lambda k: xT[:, k, 0:gw], gw, 0, "xT", DEPTH * VL)
        for tb in range(ntb):
            for k4 in range(4):
                b = bank("small")
                S.pe_group([lambda e, k=k: e.transpose(out=PS(b)[:, (k % 4) * 128:(k % 4 + 1) * 128],
                                                        in_=xT[:, k, tb * 128:(tb + 1) * 128], identity=ident_f)
                            for k in range(k4 * 4, k4 * 4 + 4)], r=["xT"], w=[("ps", b)])
                S.op("act" if k4 % 2 else "dve",
                     (lambda e, k4=k4, b=b: e.activation(out=stage[:, k4 * 512:(k4 + 1) * 512], in_=PS(b), func=AF.Copy))
                     if k4 % 2 else
                     (lambda e, k4=k4, b=b: e.tensor_copy(out=stage[:, k4 * 512:(k4 + 1) * 512], in_=PS(b))),
                     r=[("ps", b)], w=["stage"])
            S.dma("sp", G["dst"][tb * 128:(tb + 1) * 128, :], stage, r=["stage"], w=[("o_y", gi_, tb)])

    S.finish("sp")
    print("instructions emitted:", S.ninst)
    nc._arena_reg = A.reg
    return nc


def _colize(v):
    v = np.asarray(v, np.float32).reshape(-1)
    n = (v.size + 127) // 128
    out = np.zeros((n * 128,), np.float32)
    out[:v.size] = v
    return out.reshape(n, 128).T


def _prep_shared(inp):
    wblk = np.zeros((DEPTH, NBLK, 128, SLOT), np.float32)
    vecs = np.zeros((128, NVEC), np.float32)
    for l in range(DEPTH):
        wblk[l, 0, 0:64, 0:1024] = inp["decay_up"][l]
        wblk[l, 0, 64:128, 0:1024] = inp["iclr_up"][l]
        wblk[l, 0, 0:64, 1024:2048] = inp["gate_up"][l]
        wblk[l, 1, :, 0:512] = np.asarray(inp["pool_w"][l]).transpose(1, 0, 2).reshape(128, 512)
        win = np.zeros((D, 38 * 128), np.float32)
        win[:, :4800] = inp["w_in"][l]
        wblk[l, 2:40] = win.reshape(16, 128, 38, 128).transpose(2, 1, 0, 3).reshape(38, 128, SLOT)
        wblk[l, 40:56] = np.asarray(inp["w_out"][l]).reshape(16, 128, 16, 128).transpose(2, 1, 0, 3).reshape(16, 128, SLOT)
        g = np.asarray(inp["ffn_gate"][l]).reshape(16, 128, FC, 128).transpose(2, 1, 0, 3).reshape(FC, 128, SLOT)
        u = np.asarray(inp["ffn_up"][l]).reshape(16, 128, FC, 128).transpose(2, 1, 0, 3).reshape(FC, 128, SLOT)
        wblk[l, 56:144:2] = g
        wblk[l, 57:144:2] = u
        dn = np.zeros((48, 128, 16, 128), np.float32)
        dn[:FC] = np.asarray(inp["ffn_down"][l]).reshape(FC, 128, 16, 128)
        dn = dn.reshape(3, 16, 128, 16, 128).transpose(3, 0, 2, 1, 4).reshape(16, 3, 128, SLOT)
        wblk[l, 144:192] = dn.reshape(48, 128, SLOT)
        o = l * VL
        vecs[:, o + VO["nm"]:o + VO["nm"] + 16] = _colize(inp["norm_mix"][l])
        vecs[:, o + VO["nf"]:o + VO["nf"] + 16] = _colize(inp["norm_ffn"][l])
        vecs[:, o + VO["cb"]:o + VO["cb"] + 4] = _colize(inp["conv_b"][l])
        cw = np.asarray(inp["conv_w"][l])
        vecs[:, o + VO["cw"]:o + VO["cw"] + 124] = cw.reshape(31, 4, 128).transpose(2, 1, 0).reshape(128, 124)
        vecs[:, o + VO["lg"]:o + VO["lg"] + 4] = _colize(inp["conv_ln_g"][l])
        vecs[:, o + VO["lb"]:o + VO["lb"] + 4] = _colize(inp["conv_ln_b"][l])
        vecs[:, o + VO["psc"]:o + VO["psc"] + 4] = _colize(inp["pool_scale"][l])
        vecs[:, o + VO["mu"]:o + VO["mu"] + NQ] = _colize(inp["shift_mu"][l])
        vecs[:, o + VO["w0"]:o + VO["w0"] + 8] = _colize(inp["decay_w0"][l])
        vecs[:, o + VO["a0"]:o + VO["a0"] + 8] = _colize(inp["iclr_a0"][l])
        vecs[:, o + VO["kk"]:o + VO["kk"] + 8] = _colize(inp["k_k"][l])
        vecs[:, o + VO["ka"]:o + VO["ka"] + 8] = _colize(inp["k_a"][l])
        vecs[:, o + VO["rk"]:o + VO["rk"] + 8] = _colize(inp["r_k"][l])
        vecs[:, o + VO["gg"]:o + VO["gg"] + 8] = _colize(inp["gn_g"][l])
        vecs[:, o + VO["gb"]:o + VO["gb"] + 8] = _colize(inp["gn_b"][l])
    vecs[:, DEPTH * VL:DEPTH * VL + 16] = _colize(inp["norm_final"])
    return wblk, vecs


def _core_inputs(inp, c, shared, nseq_tok=SEQ):
    wblk, vecs = shared
    sh = np.zeros((DEPTH, 2, NQ * 128), np.float32)
    sh[:, :, :3264] = np.asarray(inp["state_shift"])[:, 2 * c:2 * c + 2, 0, :]
    return {
        "xp": np.ascontiguousarray(np.asarray(inp["x_prompt"])[c % 4, :nseq_tok]),
        "xs": np.ascontiguousarray(np.asarray(inp["x_sample"])[2 * c:2 * c + 2].reshape(2 * SLEN, D)),
        "cconv": np.ascontiguousarray(np.asarray(inp["cache_conv"])[:, 2 * c:2 * c + 2]),
        "cpool": np.ascontiguousarray(np.asarray(inp["cache_pool"])[:, 2 * c:2 * c + 2]),
        "cshift": sh.reshape(DEPTH, 2, NQ, 128),
        "cwkv": np.ascontiguousarray(np.asarray(inp["state_wkv"])[:, 2 * c:2 * c + 2]),
        "wblk": wblk,
        "vecs": vecs,
    }


_NC_CACHE = {}


def kernel(**inp):
    inp = {k: np.asarray(v) for k, v in inp.items()}
    shared = _prep_shared(inp)
    if "nc" not in _NC_CACHE:
        _NC_CACHE["nc"] = build_program()
    nc = _NC_CACHE["nc"]
    in_maps = [_core_inputs(inp, c, shared) for c in range(8)]
    res = run_bass_kernel_spmd(nc, in_maps, core_ids=list(range(8)))
    R = res.results
    y_prompt = np.stack([R[c]["yp"] for c in range(4)]).astype(np.float32)
    y_sample = np.concatenate([R[c]["ys"].reshape(2, SLEN, D) for c in range(8)]).astype(np.float32)

    def gather(name, tailshape, fix=None):
        pr = np.stack([R[c][name][:, 0] for c in range(4)], axis=1)
        sm = np.concatenate([R[c][name][:, 1:3] for c in range(8)], axis=1)
        if fix is not None:
            pr, sm = fix(pr), fix(sm)
        return pr.astype(np.float32), sm.astype(np.float32)

    p_conv, s_conv = gather("nconv", None)
    p_pool, s_pool = gather("npool", None)
    fixs = lambda a: a.reshape(a.shape[0], a.shape[1], 1, NQ * 128)[..., :3264]
    p_shift, s_shift = gather("nshift", None, fixs)
    p_wkv, s_wkv = gather("nwkv", None)
    return (y_prompt, y_sample, p_conv, p_pool, p_shift, p_wkv, s_conv, s_pool, s_shift, s_wkv)
```

```python
import numpy as np
import concourse.bass as bass
import concourse.mybir as mybir
from concourse.bass_utils import run_bass_kernel_spmd

F32 = mybir.dt.float32
BF16 = mybir.dt.bfloat16
AF = mybir.ActivationFunctionType
ALU = mybir.AluOpType

D = 2048
KC = 16
DFF = 5632
FC = 44
HEADS = 16
PAIRS = 8
NQ = 26
DEPTH = 4
SEQ = 2048
SLEN = 64
RMS_EPS = 1e-6
LN_EPS = 1e-5
GN_EPS = 64e-5
LW_SCALE = -float(np.exp(-0.5))
POOL_WINDOWS = (2, 4, 8, 16)

NBLK = 2 + 38 + 16 + 88 + 48
SLOT = 2048
NSLOT = 6

VO = {}
_o = 0
for _n, _w in (("nm", 16), ("nf", 16), ("cb", 4), ("cw", 124), ("lg", 4), ("lb", 4), ("psc", 4),
               ("mu", NQ), ("w0", 8), ("a0", 8), ("kk", 8), ("ka", 8), ("rk", 8), ("gg", 8), ("gb", 8)):
    VO[_n] = _o
    _o += _w
VL = _o
NVEC = DEPTH * VL + 16


class _Dummy:
    def then_inc(self, *a, **k):
        return self


class _Rec:
    def __init__(self):
        self.calls = []

    def __getattr__(self, name):
        def f(*args, **kw):
            self.calls.append((name, args, kw))
            return _Dummy()
        return f


def _free_size(ap):
    try:
        n = 1
        for s in tuple(ap.shape)[1:]:
            n *= int(s)
        return n
    except Exception:
        return 256


class Sched:
    LAT_X = 0.45
    LAT_S = 0.25

    def __init__(self, nc, reorder=True):
        self.nc = nc
        self.reorder = reorder
        self.eng = {"pe": nc.tensor, "act": nc.scalar, "dve": nc.vector, "pool": nc.gpsimd, "sp": nc.sync}
        self.semh = {}
        self.cnt = {}
        for e in self.eng:
            self.semh[e] = nc.alloc_semaphore("sem_" + e)
            self.cnt[e] = 0
        self.waited = {e: {} for e in self.eng}
        self.lastw = {}
        self.readers = {}
        self.dma_sems = []
        self.dma_rr = 0
        self.ninst = 0
        self.dead = False
        self.pending = []

    def new_dma_sem(self, name):
        self.semh[name] = self.nc.alloc_semaphore("sem_" + name)
        self.cnt[name] = 0
        return name

    def op(self, e, fn, r=(), w=()):
        if self.dead:
            return
        rec = _Rec()
        fn(rec)
        self.pending.append(dict(kind="op", eng=e, calls=rec.calls, r=tuple(r), w=tuple(w), sync=False))

    def pe_group(self, fns, r=(), w=(), pe_sync=False):
        if self.dead:
            return
        rec = _Rec()
        for fn in fns:
            fn(rec)
        self.pending.append(dict(kind="op", eng="pe", calls=rec.calls, r=tuple(r), w=tuple(w), sync=pe_sync))

    def dma(self, q, out, in_, r=(), w=(), sem=None):
        if self.dead:
            return
        self.pending.append(dict(kind="dma", eng=q, out=out, in_=in_, r=tuple(r), w=tuple(w), sem=sem))

    def barrier(self, engines=("pe", "act", "dve", "pool")):
        if self.dead:
            return
        self.flush()
        for e in engines:
            need = {}
            for o in engines:
                if o != e and self.cnt[o] > 0:
                    need[o] = self.cnt[o]
            self._wait(e, need)

    def finish(self, e="sp"):
        self.flush()
        need = {}
        for s, c in self.cnt.items():
            if c > 0 and s != e:
                need[s] = c
        self._wait(e, need)

    def _cost(self, o):
        if o["kind"] == "dma":
            return 0.7
        e = o["eng"]
        if e == "pe":
            t = 0.0
            for (name, args, kw) in o["calls"]:
                if name == "matmul":
                    n = _free_size(kw.get("rhs"))
                    f = 4.0 if str(getattr(kw.get("rhs"), "dtype", "")) .endswith("float32") else 1.0
                    t += f * max(n, 64) / 2400.0 + 0.012
                else:
                    t += 0.06
            return t
        f = _free_size(o["calls"][0][2].get("out")) if o["calls"] else 64
        if e == "act":
            return 0.2 + f / 1200.0
        if e == "dve":
            return 0.08 + f / 960.0
        return 0.3 + f / 500.0

    def flush(self):
        ops = self.pending
        self.pending = []
        n = len(ops)
        if n == 0:
            return
        if not self.reorder or n < 3:
            for o in ops:
                self._emit(o)
            return
        lastw, readers = {}, {}
        preds = [set() for _ in range(n)]
        for i, o in enumerate(ops):
            for k in o["r"]:
                j = lastw.get(k)
                if j is not None:
                    preds[i].add(j)
            for k in o["w"]:
                j = lastw.get(k)
                if j is not None:
                    preds[i].add(j)
                for j in readers.get(k, ()):
                    preds[i].add(j)
            for k in o["r"]:
                readers.setdefault(k, []).append(i)
            for k in o["w"]:
                lastw[k] = i
                readers[k] = []
            preds[i].discard(i)
        succs = [[] for _ in range(n)]
        for i in range(n):
            for j in preds[i]:
                succs[j].append(i)
        cost = [self._cost(o) for o in ops]
        lat_out = [2.5 if o["kind"] == "dma" else 0.0 for o in ops]
        prio = [0.0] * n
        for i in range(n - 1, -1, -1):
            m = 0.0
            for s in succs[i]:
                m = max(m, prio[s] + self.LAT_X)
            prio[i] = cost[i] + lat_out[i] + m
        npred = [len(p) for p in preds]
        ready = [i for i in range(n) if npred[i] == 0]
        fin = [0.0] * n
        efree = {}
        order = []
        engs = [o["eng"] for o in ops]
        while ready:
            best, best_key = None, None
            for i in ready:
                e = engs[i]
                t = efree.get(e, 0.0)
                for j in preds[i]:
                    tj = fin[j] + lat_out[j] + (self.LAT_S if engs[j] == e else self.LAT_X)
                    if tj > t:
                        t = tj
                key = (t, -prio[i], i)
                if best_key is None or key < best_key:
                    best, best_key = i, key
            i = best
            ready.remove(i)
            t = best_key[0]
            fin[i] = t + cost[i]
            efree[engs[i]] = fin[i]
            order.append(i)
            for s in succs[i]:
                npred[s] -= 1
                if npred[s] == 0:
                    ready.append(s)
        assert len(order) == n
        for i in order:
            self._emit(ops[i])

    def _deps(self, r, w):
        need = {}
        for k in r:
            t = self.lastw.get(k)
            if t is not None:
                need[t[0]] = max(need.get(t[0], 0), t[1])
        for k in w:
            t = self.lastw.get(k)
            if t is not None:
                need[t[0]] = max(need.get(t[0], 0), t[1])
            for t in self.readers.get(k, ()):
                need[t[0]] = max(need.get(t[0], 0), t[1])
        return need

    def _wait(self, e, need, skip_self=False):
        wd = self.waited[e]
        for s, v in need.items():
            if skip_self and s == e:
                continue
            if wd.get(s, 0) < v:
                self.eng[e].wait_ge(self.semh[s], v)
                wd[s] = v
                self.ninst += 1

    def _commit(self, tok, r, w):
        for k in r:
            lst = self.readers.setdefault(k, [])
            lst[:] = [t for t in lst if t[0] != tok[0]]
            lst.append(tok)
        for k in w:
            self.lastw[k] = tok
            self.readers[k] = []

    def _emit(self, o):
        if o["kind"] == "dma":
            return self._emit_dma(o)
        e = o["eng"]
        need = self._deps(o["r"], o["w"])
        self._wait(e, need, skip_self=(e == "pe" and not o["sync"]))
        inst = None
        for (name, args, kw) in o["calls"]:
            inst = getattr(self.eng[e], name)(*args, **kw)
            self.ninst += 1
        self.cnt[e] += 1
        inst.then_inc(self.semh[e], 1)
        self._commit((e, self.cnt[e]), o["r"], o["w"])

    def _emit_dma(self, o):
        q, sem = o["eng"], o["sem"]
        if sem is None:
            if len(self.dma_sems) < 24:
                sem = self.new_dma_sem("d%d" % len(self.dma_sems))
                self.dma_sems.append(sem)
            else:
                sem = self.dma_sems[self.dma_rr % len(self.dma_sems)]
                self.dma_rr += 1
        need = self._deps(o["r"], o["w"])
        if self.cnt[sem] > 0:
            need[sem] = max(need.get(sem, 0), self.cnt[sem])
        self._wait(q, need)
        self.eng[q].dma_start(out=o["out"], in_=o["in_"]).then_inc(self.semh[sem], 16)
        self.cnt[sem] += 16
        self.ninst += 1
        self._commit((sem, self.cnt[sem]), o["r"], o["w"])


class Arena:
    def __init__(self, nc, nbytes, name="arena"):
        assert nbytes % 4 == 0
        self.t = nc.alloc_sbuf_tensor(name, [128, nbytes // 4], F32)
        self.off = 0
        self.cap = nbytes

    def alloc(self, shape, dt, at=None, name=None):
        esz = 4 if dt == F32 else 2
        n = 1
        for s in shape[1:]:
            n *= s
        nb = (n * esz + 31) // 32 * 32
        if at is None:
            at = self.off
            self.off += nb
            assert self.off <= self.cap, ("arena overflow", self.off, self.cap)
        if not hasattr(self, "reg"):
            self.reg = {}
        self.reg[name if name is not None else "anon%d" % len(self.reg)] = (at, list(shape), "f32" if dt == F32 else "bf16")
        v = self.t[:, at // 4:(at + nb) // 4]
        if dt != F32:
            v = v.bitcast(dt)
        v = v[:, 0:n]
        if len(shape) == 3:
            v = v.rearrange("p (a b) -> p a b", b=shape[2])
        elif len(shape) == 4:
            v = v.rearrange("p (a b c) -> p a b c", b=shape[2], c=shape[3])
        return v


def build_program(layers=DEPTH, npg=4, dbg=None, with_s=True, reorder=True):
    nc = bass.Bass("TRN2", target_bir_lowering=False)
    S = Sched(nc, reorder=reorder)
    nseq_tok = 512 * npg

    xp = nc.dram_tensor("xp", [nseq_tok, D], F32, kind="ExternalInput").ap()
    xs = nc.dram_tensor("xs", [2 * SLEN, D], F32, kind="ExternalInput").ap()
    cconv = nc.dram_tensor("cconv", [DEPTH, 2, 30, 512], F32, kind="ExternalInput").ap()
    cpool = nc.dram_tensor("cpool", [DEPTH, 2, 15, 512], F32, kind="ExternalInput").ap()
    cshift = nc.dram_tensor("cshift", [DEPTH, 2, NQ, 128], F32, kind="ExternalInput").ap()
    cwkv = nc.dram_tensor("cwkv", [DEPTH, 2, HEADS, 64, 64], F32, kind="ExternalInput").ap()
    wblk = nc.dram_tensor("wblk", [layers, NBLK, 128, SLOT], F32, kind="ExternalInput").ap()
    vecs_d = nc.dram_tensor("vecs", [128, NVEC], F32, kind="ExternalInput").ap()
    yp = nc.dram_tensor("yp", [nseq_tok, D], F32, kind="ExternalOutput").ap()
    ys = nc.dram_tensor("ys", [2 * SLEN, D], F32, kind="ExternalOutput").ap()
    nconv = nc.dram_tensor("nconv", [DEPTH, 3, 30, 512], F32, kind="ExternalOutput").ap()
    npool = nc.dram_tensor("npool", [DEPTH, 3, 15, 512], F32, kind="ExternalOutput").ap()
    nshift = nc.dram_tensor("nshift", [DEPTH, 3, NQ, 128], F32, kind="ExternalOutput").ap()
    nwkv = nc.dram_tensor("nwkv", [DEPTH, 3, HEADS, 64, 64], F32, kind="ExternalOutput").ap()

    A = Arena(nc, 212736)
    vecs = A.alloc([128, NVEC], F32)
    omu = A.alloc([128, DEPTH, NQ], F32)
    ident_f = A.alloc([128, 128], F32)
    ident_b = A.alloc([128, 128], BF16)
    ones_b = A.alloc([128, 128], BF16)
    bones_b = A.alloc([128, 128], BF16)
    bones_f = A.alloc([128, 128], F32)
    onesD_f = A.alloc([128, 128], F32)
    m_su = A.alloc([128, 128], F32)
    m_ui = A.alloc([128, 128], F32)
    m_sl = A.alloc([128, 128], F32)
    cmask = {64: A.alloc([128, 64], BF16), 128: A.alloc([128, 512], BF16)}
    invc_first = A.alloc([128, 4, 16], F32)
    st = {}
    for l in range(DEPTH):
        st[("P", l)] = dict(u=A.alloc([128, 4, 30], F32), p=A.alloc([128, 4, 15], F32),
                            q=A.alloc([128, NQ], F32), S=A.alloc([128, PAIRS, 64], F32))
    for sq in ("S0", "S1"):
        d_ = dict(u=A.alloc([128, 4, 30], F32), p=A.alloc([128, 4, 15], F32),
                  q=A.alloc([128, NQ], F32), S=A.alloc([128, PAIRS, 64], F32))
        for l in range(DEPTH):
            st[(sq, l)] = d_
    xT = A.alloc([128, KC, 512], F32, name='xT')
    hT = A.alloc([128, KC, 512], BF16, name='hT')
    cat = A.alloc([128, KC, 512], BF16, name='cat')
    wring = A.alloc([128, NSLOT, SLOT], BF16)
    wsmall = A.alloc([128, SLOT], BF16)
    wpool = A.alloc([128, 512], BF16)
    rstd = A.alloc([128, 512], F32)
    sqb = A.alloc([128, 512], BF16)
    base_off = A.off

    def mixer_bufs(Wm):
        b = {}
        o0 = A.off
        b["ubuf"] = A.alloc([128, 4, 30 + Wm], F32, name="mb%d_ubuf" % Wm)
        b["ubf"] = A.alloc([128, 4, 30 + Wm], BF16)
        b["hconv"] = A.alloc([128, 4, Wm], F32)
        o1 = A.off
        b["pbuf"] = A.alloc([128, 4, 15 + Wm], F32, at=o0)
        b["dpool"] = A.alloc([128, 4, Wm], BF16, at=o0 + (4 * (15 + Wm) * 4 + 31) // 32 * 32)
        assert o0 + (4 * (15 + Wm) * 4 + 31) // 32 * 32 + 4 * Wm * 2 <= o1
        for n in ("t0", "t1", "t2", "t3", "t4", "t5", "t6", "t7", "t8", "t9", "t10", "t11"):
            b["off_" + n] = A.off
            b[n] = A.alloc([128, Wm + 16], F32, name="mb%d_%s" % (Wm, n))
        for n in ("b0", "b1", "b2", "b3", "b4", "b5"):
            b[n] = A.alloc([128, Wm], BF16, name="mb%d_%s" % (Wm, n))
        b["AR"] = A.alloc([128, 2 * Wm], BF16, name="mb%d_AR" % Wm)
        b["lora"] = A.alloc([128, 2, Wm], BF16, name="mb%d_lora" % Wm)
        b["gl"] = b["t1"]
        return b
    mb = [mixer_bufs(512), mixer_bufs(64)]
    diag = A.alloc([128, 31, 128], BF16, at=mb[0]['off_t8'])
    tm = {}
    for h in range(2):
        tm[("S1m", h)] = A.alloc([128, 4, 2, 128], BF16, name="tm_S1m%d" % h)
        tm[("S2m", h)] = A.alloc([128, 4, 2, 128], BF16, name="tm_S2m%d" % h)
        tm[("L", h)] = A.alloc([128, 4, 128], BF16, name="tm_L%d" % h)
        tm[("X", h)] = A.alloc([128, 4, 128], BF16, name="tm_X%d" % h)
    tm["KBV"] = A.alloc([128, 4, 3, 128], BF16, name="tm_KBV")
    tm["Psb"] = A.alloc([128, 2, 64], BF16, name="tm_Psb")
    tm["Usb"] = A.alloc([128, 2, 64], BF16, name="tm_Usb")
    tm["STb"] = A.alloc([128, 64], BF16, name="tm_STb")
    tm["gC"] = A.alloc([128, 8], F32, name="tm_gC")
    identb4 = A.alloc([128, 4, 128], BF16)
    mc = {64: A.alloc([128, 2, 64], F32), 128: A.alloc([128, 2, 128], F32)}
    stage = mb[0]["hconv"].rearrange("p a b -> p (a b)")
    stage2 = stage[:, 0:1024]
    stage3 = stage[:, 1024:1664]
    mix_end = A.off
    act = A.alloc([128, FC, 512], BF16, at=base_off)
    assert base_off + FC * 512 * 2 <= A.cap
    A.off = max(mix_end, base_off + FC * 512 * 2 + 4096)
    ftmp = [A.alloc([128, 512], F32, at=base_off + FC * 512 * 2), A.alloc([128, 512], F32, at=base_off + FC * 512 * 2 + 2048)]
    print("SBUF used", A.off, "of", A.cap)

    psb = [nc.alloc_psum_tensor("ps%d" % i, [128, 512], F32) for i in range(8)]
    bank_rr = {"big": 0, "small": 0}

    def bank(pool):
        if pool == "big":
            i = bank_rr["big"] % 3
            bank_rr["big"] += 1
            return i
        if pool == "p1":
            i = (3, 4, 5, 6, 7)[bank_rr.setdefault("p1", 0) % 5]
            bank_rr["p1"] += 1
            return i
        if pool == "y":
            return 3
        if pool == "state":
            return 7
        i = 4 + bank_rr["small"] % 3
        bank_rr["small"] += 1
        return i

    psap = [t[:, :] for t in psb]

    def PS(i):
        return psap[i]

    wsem = [S.new_dma_sem("w%d" % i) for i in range(NSLOT)]
    wsem_small = S.new_dma_sem("wsm")
    wq = {"next": 0, "issued": 0, "order": []}

    def w_issue_upto(n):
        while wq["issued"] < min(n, len(wq["order"])):
            i = wq["issued"]
            (l, b, ncols) = wq["order"][i]
            slot = i % NSLOT
            S.dma("pool", wring[:, slot, 0:ncols], wblk[l, b, :, 0:ncols], w=[("w", slot)], sem=wsem[slot])
            wq["issued"] += 1

    def w_next():
        i = wq["next"]
        wq["next"] += 1
        w_issue_upto(i + NSLOT - 1)
        return i % NSLOT

    def blk_in(cc):
        return 2 + cc
    def blk_out(n):
        return 2 + 38 + n
    def blk_gate(f):
        return 2 + 38 + 16 + 2 * f
    def blk_up(f):
        return 2 + 38 + 16 + 2 * f + 1
    def blk_down(n, j):
        return 2 + 38 + 16 + 88 + 3 * n + j
    IN_ORDER = [4, 0, 5, 1, 6, 2, 7, 3, 8, 9, 10, 11, 36, 37]
    for p_ in range(PAIRS):
        IN_ORDER += [12 + p_, 20 + p_, 28 + p_]

    def layer_order(l):
        o = [(l, blk_in(cc), SLOT) for cc in IN_ORDER]
        o += [(l, blk_out(n), SLOT) for n in range(16)]
        for f in range(FC):
            o += [(l, blk_gate(f), SLOT), (l, blk_up(f), SLOT)]
        for n in range(16):
            o += [(l, blk_down(n, 0), SLOT), (l, blk_down(n, 1), SLOT), (l, blk_down(n, 2), 12 * 128)]
        return o

    def pool_op(fn, r=(), w=()):
        return S.op("pool", fn, r, w)

    S.dma("sp", vecs, vecs_d[:, :], w=["vecs"])
    pool_op(lambda e: e.memset(ident_f, 1.0), w=["c_if"])
    pool_op(lambda e: e.affine_select(out=ident_f, in_=ident_f, pattern=[[-1, 128]], compare_op=ALU.is_equal,
                                      fill=0.0, base=0, channel_multiplier=1), r=["c_if"], w=["c_if"])
    S.op("dve", lambda e: e.tensor_copy(out=ident_b, in_=ident_f), r=["c_if"], w=["c_ib"])
    for i4 in range(4):
        S.op("dve", lambda e, i4=i4: e.tensor_copy(out=identb4[:, i4, :], in_=ident_f), r=["c_if"], w=["c_ib4"])
    S.op("dve", lambda e: e.memset(ones_b, 1.0), w=["c_ones"])
    S.op("dve", lambda e: e.memset(onesD_f, 1.0 / 512.0), w=["c_onesD"])
    S.op("dve", lambda e: e.memset(bones_b, 0.0), w=["c_bones"])
    S.op("dve", lambda e: e.memset(bones_b[0:64, 0:64], 1.0), w=["c_bones"])
    S.op("dve", lambda e: e.memset(bones_b[64:128, 64:128], 1.0), w=["c_bones"])
    S.op("dve", lambda e: e.memset(bones_f, 0.0), w=["c_bonesf"])
    S.op("dve", lambda e: e.memset(bones_f[0:64, 0:64], 1.0 / 64.0), w=["c_bonesf"])
    S.op("dve", lambda e: e.memset(bones_f[64:128, 64:128], 1.0 / 64.0), w=["c_bonesf"])
    for (m, base, cm, step) in ((m_su, -1, -1, 1), (m_ui, 0, -1, 1), (m_sl, -1, 1, -1)):
        pool_op(lambda e, m=m: e.memset(m, 1.0), w=["c_masks"])
        pool_op(lambda e, m=m, base=base, cm=cm, step=step: e.affine_select(
            out=m, in_=m, pattern=[[step, 128]], compare_op=ALU.is_ge, fill=0.0, base=base,
            channel_multiplier=cm), r=["c_masks"], w=["c_masks"])
    for C in (64, 128):
        S.op("dve", lambda e, C=C: e.tensor_copy(out=mc[C][:, 0, :], in_=m_su[:, 0:C]), r=["c_masks"], w=["c_mc"])
        S.op("dve", lambda e, C=C: e.tensor_copy(out=mc[C][:, 1, :], in_=m_ui[:, 0:C]), r=["c_masks"], w=["c_mc"])
        S.op("dve", lambda e, C=C: e.memset(cmask[C], 1.0), w=["c_cmask"])
        S.op("dve", lambda e, C=C: e.memset(cmask[C].rearrange("p (a b) -> p a b", b=C)[:, :, 0:1], 0.0),
             r=["c_cmask"], w=["c_cmask"])
    pool_op(lambda e: e.iota(out=invc_first[:, 0, :], pattern=[[1, 16]], base=1, channel_multiplier=0,
                             allow_small_or_imprecise_dtypes=True), w=["c_invc"])
    for gi, wdw in enumerate(POOL_WINDOWS):
        if gi > 0:
            S.op("dve", lambda e, gi=gi: e.tensor_copy(out=invc_first[:, gi, :], in_=invc_first[:, 0, :]),
                 r=["c_invc"], w=["c_invc%d" % gi])
    for gi, wdw in enumerate(POOL_WINDOWS):
        S.op("dve", lambda e, gi=gi, wdw=wdw: e.tensor_scalar(out=invc_first[:, gi, :], in0=invc_first[:, gi, :],
                                                              scalar1=float(wdw), scalar2=None, op0=ALU.min),
             r=["c_invc", "c_invc%d" % gi], w=["c_invc%d" % gi] + (["c_invc"] if gi == 0 else []))
        S.op("dve", lambda e, gi=gi: e.reciprocal(out=invc_first[:, gi, :], in_=invc_first[:, gi, :]),
             r=["c_invc%d" % gi], w=["c_invc%d" % gi] + (["c_invc"] if gi == 0 else []))
    for l in range(DEPTH):
        o = l * VL + VO["mu"]
        S.op("dve", lambda e, l=l, o=o: e.tensor_scalar(out=omu[:, l, :], in0=vecs[:, o:o + NQ], scalar1=-1.0,
                                                        scalar2=1.0, op0=ALU.mult, op1=ALU.add),
             r=["vecs"], w=["omu"])
    CONST_KEYS = ["vecs", "omu", "c_if", "c_ib", "c_ones", "c_onesD", "c_bones", "c_bonesf", "c_masks",
                  "c_cmask", "c_onesrow", "c_invc", "c_invc1", "c_invc2", "c_invc3"]
    S.barrier()

    def V(l, name, c0=0, n=1):
        o = l * VL + VO[name] + c0
        return vecs[:, o:o + n]

    def rmsnorm_to(dst_fn, gw, gcol, key_out, l_vec_off, dst_is_bf=True):
        b = bank("small")
        fns = []
        for k in range(KC):
            S.op("act", lambda e, k=k: e.activation(out=sqb[:, 0:gw], in_=xT[:, k, 0:gw], func=AF.Square),
                 r=["xT"], w=["sqb"])
            S.pe_group([lambda e, k=k: e.matmul(PS(b)[:, 0:gw], lhsT=ones_b, rhs=sqb[:, 0:gw],
                                                 start=(k == 0), stop=(k == KC - 1))],
                       r=["sqb"], w=[("ps", b)])
        S.op("act", lambda e: e.activation(out=rstd[:, 0:gw], in_=PS(b)[:, 0:gw], func=AF.Sqrt,
                                           bias=RMS_EPS, scale=1.0 / D), r=[("ps", b)], w=["rstd"])
        S.op("dve", lambda e: e.reciprocal(out=rstd[:, 0:gw], in_=rstd[:, 0:gw]), r=["rstd"], w=["rstd"])
        for k in range(KC):
            S.op("dve", lambda e, k=k: e.scalar_tensor_tensor(
                out=dst_fn(k), in0=xT[:, k, 0:gw], scalar=vecs[:, l_vec_off + k:l_vec_off + k + 1],
                in1=rstd[:, 0:gw], op0=ALU.mult, op1=ALU.mult), r=["xT", "rstd"], w=[key_out])

    def proj(slot, src, gw, key_src, kchunks=KC, b=None):
        if b is None:
            b = bank("big")
        wv = wring[:, slot, :].rearrange("p (k n) -> p k n", n=128)
        fns = [lambda e, k=k: e.matmul(PS(b)[:, 0:gw], lhsT=wv[:, k, :], rhs=src[:, k, 0:gw],
                                       start=(k == 0), stop=(k == kchunks - 1)) for k in range(kchunks)]
        S.pe_group(fns, r=[("w", slot), key_src], w=[("ps", b)])
        return b

    def shifted_from_psum(l, bq, qc, p, dst, key_dst, scratch, key_scr):
        W, off, bi = p["W"], p["off"], p["bi"]
        sk = (p["seq"], l)
        sd = st[sk]
        S.op("act", lambda e: e.activation(out=scratch[:, 0:W], in_=PS(bq)[:, off:off + W], func=AF.Identity,
                                           scale=omu[:, l, qc:qc + 1]), r=[("ps", bq)], w=[key_scr])
        S.op("dve", lambda e: e.scalar_tensor_tensor(
            out=dst[:, 1:W], in0=PS(bq)[:, off:off + W - 1], scalar=V(l, "mu", qc), in1=scratch[:, 1:W],
            op0=ALU.mult, op1=ALU.add), r=[("ps", bq), key_scr], w=[key_dst])
        S.op("dve", lambda e: e.scalar_tensor_tensor(
            out=dst[:, 0:1], in0=sd["q"][:, qc:qc + 1], scalar=V(l, "mu", qc), in1=scratch[:, 0:1],
            op0=ALU.mult, op1=ALU.add), r=[("st_q",) + sk, key_scr], w=[key_dst])
        S.op("act", lambda e: e.activation(out=sd["q"][:, qc:qc + 1], in_=PS(bq)[:, off + W - 1:off + W],
                                           func=AF.Copy), r=[("ps", bq), key_dst], w=[("st_q",) + sk])

    def wkv_pair(l, pr, p, br, bk, bv_):
        B = mb[p["bi"]]
        W, off, bi, C = p["W"], p["off"], p["bi"], p["C"]
        nch = W // C
        sk = (p["seq"], l)
        sd = st[sk]
        T = lambda n: B[n][:, 0:W]
        K = lambda n: (n, bi)
        c3 = lambda ap: ap.rearrange("p (a c) -> p a c", c=C)
        shifted_from_psum(l, br, pr, p, B["t1"], K("t1"), B["t0"], K("t0"))
        shifted_from_psum(l, bk, 8 + pr, p, B["t2"], K("t2"), B["t0"], K("t0"))
        shifted_from_psum(l, bv_, 16 + pr, p, B["t3"], K("t3"), B["t0"], K("t0"))
        r_, k_, v_ = T("t1"), T("t2"), T("t3")
        bw = bank("small")
        S.pe_group([lambda e: e.matmul(PS(bw)[:, 0:W], lhsT=wsmall[0:64, pr * 128:(pr + 1) * 128],
                                       rhs=B["lora"][0:64, 0, 0:W], start=True, stop=True)],
                   r=["wsmall", K("lora")], w=[("ps", bw)])
        lw = T("t4")
        S.op("act", lambda e: e.activation(out=lw, in_=PS(bw)[:, 0:W], func=AF.Sigmoid,
                                           bias=V(l, "w0", pr), scale=1.0), r=[("ps", bw)], w=[K("t4")])
        ba = bank("small")
        S.pe_group([lambda e: e.matmul(PS(ba)[:, 0:W], lhsT=wsmall[64:128, pr * 128:(pr + 1) * 128],
                                       rhs=B["lora"][64:128, 0, 0:W], start=True, stop=True)],
                   r=["wsmall", K("lora")], w=[("ps", ba)])
        a_ = T("t5")
        S.op("act", lambda e: e.activation(out=a_, in_=PS(ba)[:, 0:W], func=AF.Sigmoid,
                                           bias=V(l, "a0", pr), scale=1.0), r=[("ps", ba)], w=[K("t5")])
        bgp = bank("small")
        S.pe_group([lambda e: e.matmul(PS(bgp)[:, 0:W], lhsT=wsmall[0:64, 1024 + pr * 128:1024 + (pr + 1) * 128],
                                       rhs=B["lora"][0:64, 1, 0:W], start=True, stop=True)],
                   r=["wsmall", K("lora")], w=[("ps", bgp)])
        g_ = T("t6")
        S.op("act", lambda e: e.activation(out=g_, in_=PS(bgp)[:, 0:W], func=AF.Copy), r=[("ps", bgp)], w=[K("t6")])
        kk = T("t7")
        S.op("dve", lambda e: e.tensor_scalar(out=kk, in0=k_, scalar1=V(l, "kk", pr), scalar2=None, op0=ALU.mult),
             r=[K("t2")], w=[K("t7")])
        S.op("act", lambda e: e.activation(out=T("b0"), in_=kk, func=AF.Square), r=[K("t7")], w=[K("b0")])
        bs = bank("small")
        S.pe_group([lambda e: e.matmul(PS(bs)[:, 0:W], lhsT=bones_b, rhs=T("b0"), start=True, stop=True)],
                   r=[K("b0")], w=[("ps", bs)])
        nrm = T("t8")
        S.op("dve", lambda e: e.tensor_scalar(out=nrm, in0=PS(bs)[:, 0:W], scalar1=1e-24, scalar2=None, op0=ALU.max),
             r=[("ps", bs)], w=[K("t8")])
        S.op("act", lambda e: e.activation(out=nrm, in_=nrm, func=AF.Sqrt), r=[K("t8")], w=[K("t8")])
        S.op("dve", lambda e: e.reciprocal(out=nrm, in_=nrm), r=[K("t8")], w=[K("t8")])
        S.op("dve", lambda e: e.tensor_tensor(out=kk, in0=kk, in1=nrm, op=ALU.mult), r=[K("t7"), K("t8")], w=[K("t7")])
        bvec = T("t8")
        S.op("dve", lambda e: e.tensor_tensor(out=bvec, in0=kk, in1=a_, op=ALU.mult), r=[K("t7"), K("t5")], w=[K("t8")])
        kp = T("t9")
        S.op("dve", lambda e: e.tensor_scalar(out=kp, in0=a_, scalar1=-1.0, scalar2=V(l, "ka", pr), op0=ALU.add,
                                              op1=ALU.mult), r=[K("t5")], w=[K("t9")])
        S.op("dve", lambda e: e.scalar_tensor_tensor(out=kp, in0=kp, scalar=1.0, in1=k_, op0=ALU.add, op1=ALU.mult),
             r=[K("t9"), K("t2")], w=[K("t9")])
        S.op("dve", lambda e: e.scalar_tensor_tensor(out=T("b0"), in0=r_, scalar=V(l, "rk", pr), in1=kp, op0=ALU.mult,
                                                     op1=ALU.mult), r=[K("t1"), K("t9")], w=[K("b0")])
        bb = bank("small")
        S.pe_group([lambda e: e.matmul(PS(bb)[:, 0:W], lhsT=bones_b, rhs=T("b0"), start=True, stop=True)],
                   r=[K("b0")], w=[("ps", bb)])
        bonus = T("t10")
        S.op("dve", lambda e: e.tensor_tensor(out=bonus, in0=PS(bb)[:, 0:W], in1=v_, op=ALU.mult),
             r=[("ps", bb), K("t3")], w=[K("t10")])
        S.op("dve", lambda e: e.tensor_scalar(out=lw, in0=lw, scalar1=LW_SCALE, scalar2=None, op0=ALU.mult),
             r=[K("t4")], w=[K("t4")])
        cl = T("t11")
        S.op("dve", lambda e: e.tensor_tensor_scan(out=cl, data0=cmask[C][:, 0:W], data1=lw, initial=0.0,
                                                   op0=ALU.mult, op1=ALU.add), r=[K("t4")], w=[K("t11")])
        gC = tm["gC"]
        S.op("act", lambda e: e.activation(out=gC[:, 0:nch], in_=c3(cl)[:, :, C - 1], func=AF.Exp),
             r=[K("t11")], w=["gC"])
        e_pos = T("t0")
        S.op("act", lambda e: e.activation(out=e_pos, in_=cl, func=AF.Exp), r=[K("t11")], w=[K("t0")])
        AR = B["AR"][:, 0:2 * W].rearrange("p (a two c) -> p a two c", two=2, c=C)
        S.op("dve", lambda e: e.tensor_tensor(out=AR[:, :, 1, :], in0=c3(r_), in1=c3(e_pos), op=ALU.mult),
             r=[K("t1"), K("t0")], w=[K("AR")])
        S.op("dve", lambda e: e.tensor_tensor(out=lw, in0=cl, in1=lw, op=ALU.subtract), r=[K("t11"), K("t4")],
             w=[K("t4")])
        S.op("act", lambda e: e.activation(out=lw, in_=lw, func=AF.Exp), r=[K("t4")], w=[K("t4")])
        S.op("dve", lambda e: e.scalar_tensor_tensor(out=AR[:, :, 0, :], in0=c3(kk), scalar=-1.0, in1=c3(lw),
                                                     op0=ALU.mult, op1=ALU.mult), r=[K("t7"), K("t4")], w=[K("AR")])
        e_neg = T("t0")
        S.op("act", lambda e: e.activation(out=e_neg, in_=cl, func=AF.Exp, scale=-1.0), r=[K("t11"), K("AR")],
             w=[K("t0")])
        kt, bt, kh, bh, vb = T("b1"), T("b2"), T("b3"), T("b4"), T("b5")
        S.op("dve", lambda e: e.tensor_tensor(out=kp, in0=kp, in1=e_neg, op=ALU.mult), r=[K("t9"), K("t0")], w=[K("t9")])
        S.op("act", lambda e: e.activation(out=kt, in_=kp, func=AF.Copy), r=[K("t9")], w=[K("b1")])
        S.op("dve", lambda e: e.tensor_tensor(out=bvec, in0=bvec, in1=e_neg, op=ALU.mult), r=[K("t8"), K("t0")],
             w=[K("t8")])
        S.op("act", lambda e: e.activation(out=bt, in_=bvec, func=AF.Copy), r=[K("t8")], w=[K("b2")])
        for ch in range(nch):
            cs = slice(ch * C, (ch + 1) * C)
            S.op("dve", lambda e, cs=cs, ch=ch: e.tensor_scalar(out=kh[:, cs], in0=kp[:, cs], scalar1=gC[:, ch:ch + 1],
                                                                scalar2=None, op0=ALU.mult), r=[K("t9"), "gC"], w=[K("b3")])
            S.op("dve", lambda e, cs=cs, ch=ch: e.tensor_scalar(out=bh[:, cs], in0=bvec[:, cs], scalar1=gC[:, ch:ch + 1],
                                                                scalar2=None, op0=ALU.mult), r=[K("t8"), "gC"], w=[K("b4")])
        S.op("act", lambda e: e.activation(out=vb, in_=v_, func=AF.Copy), r=[K("t3")], w=[K("b5")])

        by = bank("y")
        STf = sd["S"][:, pr, :]
        STb = tm["STb"]
        nupd = {64: 5, 128: 6}[C]
        HS = [slice(0, 64), slice(64, 128)]
        KBV = tm["KBV"]
        ARc = lambda ch: B["AR"][:, ch * 2 * C:(ch + 1) * 2 * C]
        CS = lambda ch: slice(ch * C, (ch + 1) * C)
        for ch in range(nch):
            btp = bank("p1")
            ptv = PS(btp)[:, 0:192].bitcast(BF16).rearrange("p (a c) -> p a c", c=128)
            S.pe_group([lambda e, src_=src_, i=i, ch=ch: e.transpose(out=ptv[0:C, i, :], in_=src_[:, CS(ch)], identity=ident_b)
                        for i, src_ in enumerate((kh, bh, vb))], r=[K("b3"), K("b4"), K("b5")], w=[("ps", btp)])
            S.op("act", lambda e, ch=ch: e.activation(out=KBV[0:C, ch], in_=ptv[0:C], func=AF.Copy),
                 r=[("ps", btp)], w=["KBV"])
        S1 = [tm[("S1m", h)] for h in range(2)]
        S2 = [tm[("S2m", h)] for h in range(2)]
        Lt = [tm[("L", h)] for h in range(2)]
        Xt = [tm[("X", h)] for h in range(2)]
        for c0 in range(0, nch, 2):
            cn = min(2, nch - c0)
            for h in range(2):
                hs = HS[h]
                for (lhs, dst, key, eng) in ((kt, S1[h], ("S1m", h), "dve"), (bt, S2[h], ("S2m", h), "dve")):
                    b1 = bank("p1")
                    S.pe_group([lambda e, ch=ch, j=j, lhs=lhs, b1=b1: e.matmul(PS(b1)[0:C, j * 2 * C:(j + 1) * 2 * C], lhsT=lhs[hs, CS(ch)],
                                                                               rhs=ARc(ch)[hs, :], start=True, stop=True)
                                for j, ch in enumerate(range(c0, c0 + cn))],
                               r=[K("b1"), K("b2"), K("AR")], w=[("ps", b1)])
                    for j, ch in enumerate(range(c0, c0 + cn)):
                        S.op(eng, lambda e, ch=ch, j=j, dst=dst, b1=b1: e.tensor_tensor(
                            out=dst[0:C, ch, :, 0:C], in0=PS(b1)[0:C, j * 2 * C:(j + 1) * 2 * C].rearrange("p (a c) -> p a c", c=C),
                            in1=mc[C][0:C], op=ALU.mult), r=[("ps", b1)], w=[key])
        for h in range(2):
            hs = HS[h]
            b3 = bank("p1")
            S.pe_group([lambda e, ch=ch, b3=b3: e.matmul(PS(b3)[0:C, ch * C:(ch + 1) * C], lhsT=ARc(ch)[hs, 0:C], rhs=bt[hs, CS(ch)],
                                                          start=True, stop=True) for ch in range(nch)],
                       r=[K("b2"), K("AR")], w=[("ps", b3)])
            for ch in range(nch):
                S.op("dve", lambda e, ch=ch, b3=b3, h=h: e.tensor_tensor(out=Lt[h][0:C, ch, 0:C], in0=PS(b3)[0:C, ch * C:(ch + 1) * C],
                                                                         in1=m_sl[0:C, 0:C], op=ALU.mult),
                     r=[("ps", b3)], w=[("L", h)])
            S.op("pool", lambda e, h=h: e.tensor_tensor(out=Xt[h][0:C, 0:nch, 0:C], in0=S2[h][0:C, 0:nch, 0, 0:C],
                                                        in1=identb4[0:C, 0:nch, 0:C], op=ALU.add),
                 r=[("S2m", h)], w=[("X", h)])
        c3v = lambda ap: ap[0:C, 0:nch * C].rearrange("p (a c) -> p a c", c=C)
        for u in range(nupd):
            last = (u == nupd - 1)
            bl, bn = {}, {}
            for h in range(2):
                bl[h] = bank("p1")
                S.pe_group([lambda e, ch=ch, h=h: e.matmul(PS(bl[h])[0:C, ch * C:(ch + 1) * C], lhsT=S2[h][0:C, ch, 0, 0:C],
                                                          rhs=Lt[h][0:C, ch, 0:C], start=True, stop=True) for ch in range(nch)],
                           r=[("L", h), ("S2m", h)], w=[("ps", bl[h])])
                if not last:
                    bn[h] = bank("p1")
                    S.pe_group([lambda e, ch=ch, h=h: e.matmul(PS(bn[h])[0:C, ch * C:(ch + 1) * C], lhsT=Lt[h][0:C, ch, 0:C],
                                                              rhs=S2[h][0:C, ch, 0, 0:C], start=True, stop=True) for ch in range(nch)],
                               r=[("L", h), ("S2m", h)], w=[("ps", bn[h])])
            for h in range(2):
                S.op("act", lambda e, h=h: e.activation(out=Lt[h][0:C, 0:nch, 0:C], in_=c3v(PS(bl[h])), func=AF.Copy),
                     r=[("ps", bl[h])], w=[("L", h)])
                if not last:
                    S.op("dve", lambda e, h=h: e.tensor_copy(out=S2[h][0:C, 0:nch, 0, 0:C], in_=c3v(PS(bn[h]))),
                         r=[("ps", bn[h])], w=[("S2m", h)])
            bx = {}
            for h in range(2):
                bx[h] = bank("p1")
                S.pe_group([lambda e, ch=ch, h=h: e.matmul(PS(bx[h])[0:C, ch * C:(ch + 1) * C], lhsT=Lt[h][0:C, ch, 0:C],
                                                          rhs=Xt[h][0:C, ch, 0:C], start=True, stop=True) for ch in range(nch)],
                           r=[("L", h), ("X", h)], w=[("ps", bx[h])])
            for h in range(2):
                S.op("dve", lambda e, h=h: e.tensor_tensor(out=Xt[h][0:C, 0:nch, 0:C], in0=c3v(PS(bx[h])),
                                                           in1=Xt[h][0:C, 0:nch, 0:C], op=ALU.add),
                     r=[("ps", bx[h]), ("X", h)], w=[("X", h)])

        S.op("act", lambda e: e.activation(out=STb, in_=STf, func=AF.Copy), r=[("st_S",) + sk], w=["STb"])
        Psb, Usb = tm["Psb"], tm["Usb"]
        bS = bank("state")
        for ch in range(nch):
            cs = CS(ch)
            VT = KBV[0:C, ch, 2, :]
            KH = KBV[0:C, ch, 0, :]
            BH = KBV[0:C, ch, 1, :]
            bP = bank("small")
            for h in range(2):
                hs = HS[h]
                rowsplit = (C == 64 and h == 1)
                S.pe_group([lambda e: e.matmul(PS(bP)[0:C, h * 64:h * 64 + 64], lhsT=ARc(ch)[hs, 0:C], rhs=STb[hs, :], start=True, stop=False)],
                           r=[K("AR"), "STb"], w=[("ps", bP)], pe_sync=(C == 64))
                S.pe_group([lambda e: e.matmul(PS(bP)[0:C, h * 64:h * 64 + 64], lhsT=S1[h][0:C, ch, 0, 0:C], rhs=VT[:, hs], start=False, stop=True)],
                           r=[("S1m", h), "KBV"], w=[("ps", bP)], pe_sync=(C == 64))
            S.op("act", lambda e: e.activation(out=Psb[0:C], in_=PS(bP)[0:C, 0:128].rearrange("p (a c) -> p a c", c=64), func=AF.Copy),
                 r=[("ps", bP)], w=["Psb"])
            bU = bank("small")
            for h in range(2):
                S.pe_group([lambda e: e.matmul(PS(bU)[0:C, h * 64:h * 64 + 64], lhsT=Xt[h][0:C, ch, 0:C], rhs=Psb[0:C, h, :], start=True, stop=True)],
                           r=[("X", h), "Psb"], w=[("ps", bU)], pe_sync=(C == 64))
            S.op("dve", lambda e: e.tensor_copy(out=Usb[0:C], in_=PS(bU)[0:C, 0:128].rearrange("p (a c) -> p a c", c=64)),
                 r=[("ps", bU)], w=["Usb"])
            for h in range(2):
                hs = HS[h]
                S.pe_group([lambda e: e.matmul(PS(bS)[hs, 0:64], lhsT=BH[:, hs], rhs=Usb[0:C, h, :], start=True, stop=False),
                            lambda e: e.matmul(PS(bS)[hs, 0:64], lhsT=KH[:, hs], rhs=VT[:, hs], start=False, stop=True)],
                           r=["KBV", "Usb"], w=[("ps", bS)], pe_sync=(C == 64))
            for h in range(2):
                hs = HS[h]
                S.pe_group([lambda e: e.matmul(PS(by)[hs, cs], lhsT=STb[hs, :], rhs=ARc(ch)[hs, C:2 * C], start=True, stop=False)],
                           r=["STb", K("AR")], w=[("ps", by)], pe_sync=(C == 64))
                S.pe_group([lambda e: e.matmul(PS(by)[hs, cs], lhsT=Usb[0:C, h, :], rhs=S2[h][0:C, ch, 1, 0:C], start=False, stop=False),
                            lambda e: e.matmul(PS(by)[hs, cs], lhsT=VT[:, hs], rhs=S1[h][0:C, ch, 1, 0:C], start=False, stop=True)],
                           r=["Usb", ("S2m", h), ("S1m", h), "KBV"], w=[("ps", by)], pe_sync=(C == 64))
            S.op("dve", lambda e, ch=ch: e.scalar_tensor_tensor(out=STf, in0=STf, scalar=gC[:, ch:ch + 1], in1=PS(bS)[:, 0:64],
                                                                op0=ALU.mult, op1=ALU.add),
                 r=[("ps", bS), ("st_S",) + sk, "gC"], w=[("st_S",) + sk])
            S.op("act", lambda e: e.activation(out=STb, in_=STf, func=AF.Copy), r=[("st_S",) + sk], w=["STb"])

        y = T("t7")
        S.op("act", lambda e: e.activation(out=y, in_=PS(by)[:, 0:W], func=AF.Copy), r=[("ps", by)], w=[K("t7")])
        bm = bank("small")
        S.pe_group([lambda e: e.matmul(PS(bm)[:, 0:W], lhsT=bones_f, rhs=y, start=True, stop=True)],
                   r=[K("t7")], w=[("ps", bm)])
        S.op("dve", lambda e: e.tensor_tensor(out=y, in0=y, in1=PS(bm)[:, 0:W], op=ALU.subtract),
             r=[("ps", bm), K("t7")], w=[K("t7")])
        sq = T("t8")
        S.op("act", lambda e: e.activation(out=sq, in_=y, func=AF.Square), r=[K("t7")], w=[K("t8")])
        bv2 = bank("small")
        S.pe_group([lambda e: e.matmul(PS(bv2)[:, 0:W], lhsT=bones_f, rhs=sq, start=True, stop=True)],
                   r=[K("t8")], w=[("ps", bv2)])
        rs = T("t9")
        S.op("act", lambda e: e.activation(out=rs, in_=PS(bv2)[:, 0:W], func=AF.Sqrt, bias=GN_EPS, scale=1.0),
             r=[("ps", bv2)], w=[K("t9")])
        S.op("dve", lambda e: e.reciprocal(out=rs, in_=rs), r=[K("t9")], w=[K("t9")])
        S.op("dve", lambda e: e.tensor_tensor(out=y, in0=y, in1=rs, op=ALU.mult), r=[K("t7"), K("t9")], w=[K("t7")])
        S.op("dve", lambda e: e.tensor_scalar(out=y, in0=y, scalar1=V(l, "gg", pr), scalar2=V(l, "gb", pr),
                                              op0=ALU.mult, op1=ALU.add), r=[K("t7")], w=[K("t7")])
        S.op("dve", lambda e: e.tensor_tensor(out=y, in0=y, in1=bonus, op=ALU.add), r=[K("t7"), K("t10")], w=[K("t7")])
        S.op("dve", lambda e: e.tensor_tensor(out=cat[:, 8 + pr, off:off + W], in0=y, in1=g_, op=ALU.mult),
             r=[K("t7"), K("t6")], w=["cat"])

    groups = []
    for g in range(npg):
        groups.append(dict(gw=512, parts=[dict(seq="P", off=0, W=512, C=128, first=(g == 0), last=(g == npg - 1),
                                               bi=0)],
                           src=xp[g * 512:(g + 1) * 512, :], dst=yp[g * 512:(g + 1) * 512, :]))
    if with_s:
      groups.append(dict(gw=128, parts=[dict(seq="S0", off=0, W=64, C=64, first=False, last=True, bi=0, sidx=0),
                                      dict(seq="S1", off=64, W=64, C=64, first=False, last=True, bi=1, sidx=1)],
                       src=xs[:, :], dst=ys[:, :]))
    for g in groups:
        for l in range(layers):
            wq["order"] += layer_order(l)

    dbg_out = {}

    def chk(name):
        if dbg == name:
            S.dead = True

    for gi_, G in enumerate(groups):
        gw = G["gw"]
        parts = G["parts"]
        S.flush()
        S.reorder = reorder and (parts[0]["seq"] == "P")
        ntb = gw // 128
        for tb in range(ntb):
            S.dma("sp", stage, G["src"][tb * 128:(tb + 1) * 128, :], w=["stage"])
            for k4 in range(4):
                b = bank("small")
                S.pe_group([lambda e, k=k: e.transpose(out=PS(b)[:, (k % 4) * 128:(k % 4 + 1) * 128],
                                                        in_=stage[:, k * 128:(k + 1) * 128], identity=ident_f)
                            for k in range(k4 * 4, k4 * 4 + 4)], r=["stage"], w=[("ps", b)])
                S.op("act" if k4 % 2 else "dve",
                     (lambda e, k4=k4, tb=tb, b=b: e.activation(
                         out=xT[:, k4 * 4:k4 * 4 + 4, tb * 128:(tb + 1) * 128],
                         in_=PS(b).rearrange("p (a c) -> p a c", c=128), func=AF.Copy)) if k4 % 2 else
                     (lambda e, k4=k4, tb=tb, b=b: e.tensor_copy(
                         out=xT[:, k4 * 4:k4 * 4 + 4, tb * 128:(tb + 1) * 128],
                         in_=PS(b).rearrange("p (a c) -> p a c", c=128))),
                     r=[("ps", b)], w=["xT"])

        for l in range(layers):
            LV = l * VL
            chk("A")
            S.dma("pool", wsmall, wblk[l, 0, :, :], w=["wsmall"], sem=wsem_small)
            S.dma("pool", wpool, wblk[l, 1, :, 0:512], w=["wpool"], sem=wsem_small)
            for p in parts:
                if p["seq"] == "P":
                    if p["first"]:
                        sd = st[("P", l)]
                        S.op("dve", lambda e, sd=sd: e.memset(sd["u"], 0.0), w=[("st_u", "P", l)])
                        S.op("dve", lambda e, sd=sd: e.memset(sd["p"], 0.0), w=[("st_p", "P", l)])
                        S.op("dve", lambda e, sd=sd: e.memset(sd["q"], 0.0), w=[("st_q", "P", l)])
                        S.op("dve", lambda e, sd=sd: e.memset(sd["S"], 0.0), w=[("st_S", "P", l)])
                    continue
                sq, si = p["seq"], p["sidx"]
                sd = st[(sq, l)]
                S.dma("sp", stage2[0:30, 0:512], cconv[l, si, :, :], w=["stage"])
                b = bank("small")
                S.pe_group([lambda e, c=c: e.transpose(out=PS(b)[:, c * 32:c * 32 + 30],
                                                        in_=stage2[0:30, c * 128:(c + 1) * 128],
                                                        identity=ident_f[0:30, 0:30]) for c in range(4)],
                           r=["stage"], w=[("ps", b)])
                S.op("dve", lambda e, sd=sd, b=b: e.tensor_copy(
                    out=sd["u"], in_=PS(b)[:, 0:128].rearrange("p (a c) -> p a c", c=32)[:, :, 0:30]),
                    r=[("ps", b)], w=[("st_u", sq, l)])
                S.dma("sp", stage2[0:15, 0:512], cpool[l, si, :, :], w=["stage"])
                b = bank("small")
                S.pe_group([lambda e, c=c: e.transpose(out=PS(b)[:, c * 16:c * 16 + 15],
                                                        in_=stage2[0:15, c * 128:(c + 1) * 128],
                                                        identity=ident_f[0:15, 0:15]) for c in range(4)],
                           r=["stage"], w=[("ps", b)])
                S.op("dve", lambda e, sd=sd, b=b: e.tensor_copy(
                    out=sd["p"], in_=PS(b)[:, 0:64].rearrange("p (a c) -> p a c", c=16)[:, :, 0:15]),
                    r=[("ps", b)], w=[("st_p", sq, l)])
                S.dma("sp", stage2[0:NQ, 0:128], cshift[l, si, :, :], w=["stage"])
                b = bank("small")
                S.pe_group([lambda e: e.transpose(out=PS(b)[:, 0:NQ], in_=stage2[0:NQ, 0:128],
                                                  identity=ident_f[0:NQ, 0:NQ])], r=["stage"], w=[("ps", b)])
                S.op("dve", lambda e, sd=sd, b=b: e.tensor_copy(out=sd["q"], in_=PS(b)[:, 0:NQ]),
                     r=[("ps", b)], w=[("st_q", sq, l)])
                S.dma("sp", stage2[0:64, :].rearrange("p (h j) -> p h j", j=64),
                      cwkv[l, si].rearrange("h i j -> i h j"), w=["stage"])
                for half in range(2):
                    b = bank("small")
                    S.pe_group([lambda e, pr=pr: e.transpose(
                        out=PS(b)[:, (pr % 4) * 64:(pr % 4) * 64 + 64], in_=stage2[0:64, pr * 128:(pr + 1) * 128],
                        identity=ident_f[0:64, 0:64]) for pr in range(half * 4, half * 4 + 4)],
                        r=["stage"], w=[("ps", b)])
                    S.op("dve", lambda e, sd=sd, b=b, half=half: e.tensor_copy(
                        out=sd["S"][:, half * 4:half * 4 + 4, :],
                        in_=PS(b)[:, 0:256].rearrange("p (a c) -> p a c", c=64)),
                        r=[("ps", b)], w=[("st_S", sq, l)])

            chk("A2")
            rmsnorm_to(lambda k: hT[:, k, 0:gw], gw, 0, "hT", LV + VO["nm"])
            chk("B")

            for c in range(4):
                bg = proj(w_next(), hT, gw, "hT")
                bv = proj(w_next(), hT, gw, "hT")
                for p in parts:
                    B = mb[p["bi"]]
                    W, off = p["W"], p["off"]
                    sk = (p["seq"], l)
                    t0 = B["t0"][:, 0:W]
                    if c == 0:
                        S.op("dve", lambda e, B=B, p=p: e.tensor_copy(out=B["ubuf"][:, :, 0:30],
                                                                     in_=st[(p["seq"], l)]["u"]),
                             r=[("st_u",) + sk], w=[("ubuf", p["bi"])])
                    S.op("act", lambda e, t0=t0, off=off, W=W, bg=bg: e.activation(
                        out=t0, in_=PS(bg)[:, off:off + W], func=AF.Sigmoid), r=[("ps", bg)], w=[("t0", p["bi"])])
                    S.op("dve", lambda e, B=B, t0=t0, off=off, W=W, bv=bv, c=c: e.tensor_tensor(
                        out=B["ubuf"][:, c, 30:30 + W], in0=PS(bv)[:, off:off + W], in1=t0, op=ALU.mult),
                        r=[("ps", bv), ("t0", p["bi"])], w=[("ubuf", p["bi"])])
            for p in parts:
                B = mb[p["bi"]]
                W, off, bi = p["W"], p["off"], p["bi"]
                sk = (p["seq"], l)
                S.op("act", lambda e, B=B, W=W: e.activation(out=B["ubf"][:, :, 0:30 + W], in_=B["ubuf"][:, :, 0:30 + W],
                                                            func=AF.Copy), r=[("ubuf", bi)], w=[("ubf", bi)])
                S.op("dve", lambda e, B=B, W=W, p=p: e.tensor_copy(out=st[(p["seq"], l)]["u"], in_=B["ubuf"][:, :, W:W + 30]),
                     r=[("ubuf", bi)], w=[("st_u",) + sk])
                pass
            for c in range(4):
                for j in range(31):
                    if j % 2 == 0:
                        S.op("act", lambda e, c=c, j=j: e.activation(out=diag[:, j, :], in_=ident_b, func=AF.Identity,
                                                                      scale=V(l, "cw", c * 31 + j)), r=[], w=[("diag", j)])
                    else:
                        S.op("dve", lambda e, c=c, j=j: e.tensor_scalar(
                            out=diag[:, j, :], in0=ident_b, scalar1=V(l, "cw", c * 31 + j), scalar2=None, op0=ALU.mult),
                            r=[], w=[("diag", j)])
                for p in parts:
                    B = mb[p["bi"]]
                    W, off, bi = p["W"], p["off"], p["bi"]
                    b = bank("small")
                    S.pe_group([lambda e, j=j, c=c, B=B, W=W, b=b: e.matmul(
                        PS(b)[:, 0:W], lhsT=diag[:, j, :], rhs=B["ubf"][:, c, j:j + W],
                        start=(j == 0), stop=(j == 30)) for j in range(31)],
                        r=[("diag", j) for j in range(31)] + [("ubf", bi)], w=[("ps", b)])
                    S.op("act", lambda e, B=B, W=W, b=b, c=c: e.activation(
                        out=B["hconv"][:, c, 0:W], in_=PS(b)[:, 0:W], func=AF.Identity,
                        bias=V(l, "cb", c), scale=1.0), r=[("ps", b)], w=[("hconv", bi)] + (["stage"] if bi == 0 else []))
            for p in parts:
                B = mb[p["bi"]]
                W, off, bi = p["W"], p["off"], p["bi"]
                sk = (p["seq"], l)
                bm = bank("small")
                S.pe_group([lambda e, c=c, B=B, W=W: e.matmul(PS(bm)[:, 0:W], lhsT=onesD_f, rhs=B["hconv"][:, c, 0:W],
                                                              start=(c == 0), stop=(c == 3)) for c in range(4)],
                           r=[("hconv", bi)], w=[("ps", bm)])
                mean = B["t1"][:, 0:W]
                S.op("act", lambda e, mean=mean, W=W: e.activation(out=mean, in_=PS(bm)[:, 0:W], func=AF.Copy),
                     r=[("ps", bm)], w=[("t1", bi)])
                for c in range(4):
                    S.op("dve", lambda e, c=c, B=B, W=W, mean=mean: e.tensor_tensor(
                        out=B["hconv"][:, c, 0:W], in0=B["hconv"][:, c, 0:W], in1=mean, op=ALU.subtract),
                        r=[("hconv", bi), ("t1", bi)], w=[("hconv", bi)])
                bvv = bank("small")
                for c in range(4):
                    S.op("act", lambda e, c=c, B=B, W=W: e.activation(out=B["t2"][:, 0:W], in_=B["hconv"][:, c, 0:W],
                                                                      func=AF.Square),
                         r=[("hconv", bi)], w=[("t2", bi)])
                    S.pe_group([lambda e, c=c, B=B, W=W: e.matmul(PS(bvv)[:, 0:W], lhsT=onesD_f, rhs=B["t2"][:, 0:W],
                                                                  start=(c == 0), stop=(c == 3))],
                               r=[("t2", bi)], w=[("ps", bvv)])
                rs = B["t3"][:, 0:W]
                S.op("act", lambda e, rs=rs, W=W: e.activation(out=rs, in_=PS(bvv)[:, 0:W], func=AF.Sqrt,
                                                              bias=LN_EPS, scale=1.0), r=[("ps", bvv)], w=[("t3", bi)])
                S.op("dve", lambda e, rs=rs: e.reciprocal(out=rs, in_=rs), r=[("t3", bi)], w=[("t3", bi)])
                for c in range(4):
                    S.op("dve", lambda e, c=c, B=B, W=W, rs=rs: e.tensor_tensor(
                        out=B["hconv"][:, c, 0:W], in0=B["hconv"][:, c, 0:W], in1=rs, op=ALU.mult),
                        r=[("hconv", bi), ("t3", bi)], w=[("hconv", bi)])
                    S.op("act", lambda e, c=c, B=B, W=W, off=off: e.activation(
                        out=cat[:, c, off:off + W], in_=B["hconv"][:, c, 0:W], func=AF.Silu,
                        bias=V(l, "lb", c), scale=V(l, "lg", c)), r=[("hconv", bi)], w=["cat"])

            chk("C")
            S.barrier()
            for c in range(4):
                bp = proj(w_next(), hT, gw, "hT")
                for p in parts:
                    B = mb[p["bi"]]
                    W, off, bi = p["W"], p["off"], p["bi"]
                    sk = (p["seq"], l)
                    if c == 0:
                        S.op("dve", lambda e, B=B, p=p: e.tensor_copy(out=B["pbuf"][:, :, 0:15],
                                                                     in_=st[(p["seq"], l)]["p"]),
                             r=[("st_p",) + sk], w=[("pbuf", bi)])
                    S.op("act", lambda e, B=B, W=W, off=off, bp=bp, c=c: e.activation(
                        out=B["pbuf"][:, c, 15:15 + W], in_=PS(bp)[:, off:off + W], func=AF.Copy),
                        r=[("ps", bp)], w=[("pbuf", bi)])
            for p in parts:
                B = mb[p["bi"]]
                W, off, bi = p["W"], p["off"], p["bi"]
                sk = (p["seq"], l)
                S.op("dve", lambda e, B=B, W=W, p=p: e.tensor_copy(out=st[(p["seq"], l)]["p"], in_=B["pbuf"][:, :, W:W + 15]),
                     r=[("pbuf", bi)], w=[("st_p",) + sk])
                for c, wdw in enumerate(POOL_WINDOWS):
                    src = B["pbuf"][:, c, :]
                    lo = 15
                    span = 1
                    ta, tb_ = B["t4"], B["t5"]
                    cur, cur_lo = src, 0
                    nsteps = {2: 1, 4: 2, 8: 3, 16: 4}[wdw]
                    for s_ in range(nsteps):
                        dst = ta if s_ % 2 == 0 else tb_
                        new_lo = cur_lo + span
                        n = 15 + W - new_lo
                        S.op("dve", lambda e, dst=dst, cur=cur, new_lo=new_lo, span=span, n=n: e.tensor_tensor(
                            out=dst[:, new_lo:new_lo + n], in0=cur[:, new_lo:new_lo + n],
                            in1=cur[:, new_lo - span:new_lo - span + n], op=ALU.add),
                            r=[("pbuf", bi), ("t4", bi), ("t5", bi)], w=[("t4" if s_ % 2 == 0 else "t5", bi)])
                        cur, cur_lo = dst, new_lo
                        span *= 2
                    S.op("dve", lambda e, cur=cur, W=W, c=c, B=B, wdw=wdw: e.scalar_tensor_tensor(
                        out=B["dpool"][:, c, 0:W], in0=cur[:, 15:15 + W], scalar=1.0 / wdw,
                        in1=B["pbuf"][:, c, 15:15 + W], op0=ALU.mult, op1=ALU.subtract),
                        r=[("t4", bi), ("t5", bi), ("pbuf", bi)], w=[("dpool", bi)])
                    if p["first"]:
                        S.op("dve", lambda e, cur=cur, c=c: e.tensor_tensor(
                            out=cur[:, 15:31], in0=cur[:, 15:31], in1=invc_first[:, c, :], op=ALU.mult),
                            r=[("t4", bi), ("t5", bi), ("dpool", bi)], w=[("t4", bi), ("t5", bi)])
                        S.op("dve", lambda e, cur=cur, c=c, B=B: e.tensor_tensor(
                            out=B["dpool"][:, c, 0:16], in0=cur[:, 15:31], in1=B["pbuf"][:, c, 15:31],
                            op=ALU.subtract), r=[("t4", bi), ("t5", bi), ("pbuf", bi)], w=[("dpool", bi)])
                    b = bank("small")
                    S.pe_group([lambda e, c=c, B=B, W=W, b=b: e.matmul(PS(b)[:, 0:W], lhsT=wpool[:, c * 128:(c + 1) * 128],
                                                                        rhs=B["dpool"][:, c, 0:W], start=True, stop=True)],
                               r=["wpool", ("dpool", bi)], w=[("ps", b)])
                    S.op("act", lambda e, c=c, W=W, off=off, b=b: e.activation(
                        out=cat[:, 4 + c, off:off + W], in_=PS(b)[:, 0:W], func=AF.Identity, scale=V(l, "psc", c)),
                        r=[("ps", b)], w=["cat"])


            chk("D")
            b24 = proj(w_next(), hT, gw, "hT")
            b25 = proj(w_next(), hT, gw, "hT")
            for p in parts:
                B = mb[p["bi"]]
                W, bi = p["W"], p["bi"]
                gl = B["gl"]
                shifted_from_psum(l, b24, 24, p, gl, ("t1", bi), B["t0"], ("t0", bi))
                S.op("act", lambda e, B=B, W=W, gl=gl: e.activation(out=B["lora"][0:64, 0, 0:W], in_=gl[0:64, 0:W],
                                                                    func=AF.Tanh), r=[("t1", bi)], w=[("lora", bi)])
                S.op("act", lambda e, B=B, W=W, gl=gl: e.activation(out=B["lora"][64:128, 0, 0:W], in_=gl[64:128, 0:W],
                                                                    func=AF.Copy), r=[("t1", bi)], w=[("lora", bi)])
                shifted_from_psum(l, b25, 25, p, gl, ("t1", bi), B["t0"], ("t0", bi))
                S.op("act", lambda e, B=B, W=W, gl=gl: e.activation(out=B["lora"][0:64, 1, 0:W], in_=gl[0:64, 0:W],
                                                                    func=AF.Sigmoid), r=[("t1", bi)], w=[("lora", bi)])

            chk("E")
            for pr in range(PAIRS):
                if pr == 1:
                    chk("F")
                br = proj(w_next(), hT, gw, "hT")
                bk = proj(w_next(), hT, gw, "hT")
                bv_ = proj(w_next(), hT, gw, "hT")
                for p in parts:
                    wkv_pair(l, pr, p, br, bk, bv_)

            chk("G")
            for n in range(16):
                bo = proj(w_next(), cat, gw, "cat")
                S.op("dve", lambda e, n=n, bo=bo: e.tensor_tensor(out=xT[:, n, 0:gw], in0=PS(bo)[:, 0:gw],
                                                                  in1=xT[:, n, 0:gw], op=ALU.add),
                     r=[("ps", bo), "xT"], w=["xT"])

            chk("H")
            rmsnorm_to(lambda k: hT[:, k, 0:gw], gw, 0, "hT", LV + VO["nf"])
            S.barrier()
            for f in range(FC if dbg != "outproj" else 0):
                bg = proj(w_next(), hT, gw, "hT")
                bu = proj(w_next(), hT, gw, "hT")
                ft = ftmp[f % 2]
                S.op("act", lambda e, ft=ft, bg=bg: e.activation(out=ft[:, 0:gw], in_=PS(bg)[:, 0:gw], func=AF.Silu),
                     r=[("ps", bg)], w=[("ftmp", f % 2)])
                S.op("dve", lambda e, ft=ft, bu=bu, f=f: e.tensor_tensor(out=act[:, f, 0:gw], in0=PS(bu)[:, 0:gw],
                                                                         in1=ft[:, 0:gw], op=ALU.mult),
                     r=[("ps", bu), ("ftmp", f % 2)], w=[("act", f)])
            for n in range(16 if dbg != "outproj" else 0):
                bd = bank("big")
                for j, nk in enumerate((16, 16, 12)):
                    sl = w_next()
                    wv = wring[:, sl, :].rearrange("p (k n) -> p k n", n=128)
                    fns = []
                    for k in range(nk):
                        f = j * 16 + k
                        fns.append(lambda e, wv=wv, k=k, f=f: e.matmul(PS(bd)[:, 0:gw], lhsT=wv[:, k, :],
                                                                        rhs=act[:, f, 0:gw], start=(f == 0),
                                                                        stop=(f == FC - 1)))
                    S.pe_group(fns, r=[("w", sl)] + [("act", f) for f in range(j * 16, j * 16 + nk)],
                               w=[("ps", bd)])
                S.op("dve", lambda e, n=n, bd=bd: e.tensor_tensor(out=xT[:, n, 0:gw], in0=PS(bd)[:, 0:gw],
                                                                  in1=xT[:, n, 0:gw], op=ALU.add),
                     r=[("ps", bd), "xT"], w=["xT"])
            S.barrier()

            chk("I")
            for p in parts:
                if not p["last"]:
                    continue
                sk = (p["seq"], l)
                sd = st[sk]
                oi = {"P": 0, "S0": 1, "S1": 2}[p["seq"]]
                b = bank("small")
                S.pe_group([lambda e, c=c: e.transpose(out=PS(b)[0:30, c * 128:(c + 1) * 128], in_=sd["u"][:, c, :],
                                                        identity=ident_f) for c in range(4)],
                           r=[("st_u",) + sk], w=[("ps", b)])
                S.op("act", lambda e, b=b: e.activation(out=stage2[0:30, 0:512], in_=PS(b)[0:30, 0:512], func=AF.Copy),
                     r=[("ps", b)], w=["stage"])
                S.dma("sp", nconv[l, oi, :, :], stage2[0:30, 0:512], r=["stage"], w=[("o_conv", l, oi)])
                b = bank("small")
                S.pe_group([lambda e, c=c: e.transpose(out=PS(b)[0:15, c * 128:(c + 1) * 128], in_=sd["p"][:, c, :],
                                                        identity=ident_f) for c in range(4)],
                           r=[("st_p",) + sk], w=[("ps", b)])
                S.op("act", lambda e, b=b: e.activation(out=stage2[0:15, 512:1024], in_=PS(b)[0:15, 0:512], func=AF.Copy),
                     r=[("ps", b)], w=["stage"])
                S.dma("sp", npool[l, oi, :, :], stage2[0:15, 512:1024], r=["stage"], w=[("o_pool", l, oi)])
                b = bank("small")
                S.pe_group([lambda e: e.transpose(out=PS(b)[0:NQ, 0:128], in_=sd["q"], identity=ident_f)],
                           r=[("st_q",) + sk], w=[("ps", b)])
                S.op("act", lambda e, b=b: e.activation(out=stage3[0:NQ, 512:640], in_=PS(b)[0:NQ, 0:128], func=AF.Copy),
                     r=[("ps", b)], w=["stage"])
                S.dma("sp", nshift[l, oi, :, :], stage3[0:NQ, 512:640], r=["stage"], w=[("o_shift", l, oi)])
                for half in range(2):
                    b = bank("small")
                    S.pe_group([lambda e, pr=pr: e.transpose(out=PS(b)[0:64, (pr % 4) * 128:(pr % 4 + 1) * 128],
                                                              in_=sd["S"][:, pr, :], identity=ident_f)
                                for pr in range(half * 4, half * 4 + 4)], r=[("st_S",) + sk], w=[("ps", b)])
                    S.op("act", lambda e, b=b: e.activation(out=stage3[0:64, 0:512], in_=PS(b)[0:64, 0:512],
                                                            func=AF.Copy), r=[("ps", b)], w=["stage"])
                    S.dma("sp", nwkv[l, oi, half * 8:half * 8 + 8].rearrange("h i j -> i h j"),
                          stage3[0:64, 0:512].rearrange("p (h j) -> p h j", j=64), r=["stage"],
                          w=[("o_wkv", l, oi, half)])

        S.dead = False
        if dbg is None:
            rmsnorm_to(lambda k: xT[:, k, 0:gw], gw, 0, "xT", DEPTH * VL)
        for tb in range(ntb):
            for k4 in range(4):
                b = bank("small")
                S.pe_group([lambda e, k=k: e.transpose(out=PS(b)[:, (k % 4) * 128:(k % 4 + 1) * 128],
                                                        in_=xT[:, k, tb * 128:(tb + 1) * 128], identity=ident_f)
                            for k in range(k4 * 4, k4 * 4 + 4)], r=["xT"], w=[("ps", b)])
                S.op("act" if k4 % 2 else "dve",
                     (lambda e, k4=k4, b=b: e.activation(out=stage[:, k4 * 512:(k4 + 1) * 512], in_=PS(b), func=AF.Copy))
                     if k4 % 2 else
                     (lambda e, k4=k4, b=b: e.tensor_copy(out=stage[:, k4 * 512:(k4 + 1) * 512], in_=PS(b))),
                     r=[("ps", b)], w=["stage"])
            S.dma("sp", G["dst"][tb * 128:(tb + 1) * 128, :], stage, r=["stage"], w=[("o_y", gi_, tb)])

    S.finish("sp")
    print("instructions emitted:", S.ninst)
    nc._arena_reg = A.reg
    return nc


def _colize(v):
    v = np.asarray(v, np.float32).reshape(-1)
    n = (v.size + 127) // 128
    out = np.zeros((n * 128,), np.float32)
    out[:v.size] = v
    return out.reshape(n, 128).T


def _prep_shared(inp):
    wblk = np.zeros((DEPTH, NBLK, 128, SLOT), np.float32)
    vecs = np.zeros((128, NVEC), np.float32)
    for l in range(DEPTH):
        wblk[l, 0, 0:64, 0:1024] = inp["decay_up"][l]
        wblk[l, 0, 64:128, 0:1024] = inp["iclr_up"][l]
        wblk[l, 0, 0:64, 1024:2048] = inp["gate_up"][l]
        wblk[l, 1, :, 0:512] = np.asarray(inp["pool_w"][l]).transpose(1, 0, 2).reshape(128, 512)
        win = np.zeros((D, 38 * 128), np.float32)
        win[:, :4800] = inp["w_in"][l]
        wblk[l, 2:40] = win.reshape(16, 128, 38, 128).transpose(2, 1, 0, 3).reshape(38, 128, SLOT)
        wblk[l, 40:56] = np.asarray(inp["w_out"][l]).reshape(16, 128, 16, 128).transpose(2, 1, 0, 3).reshape(16, 128, SLOT)
        g = np.asarray(inp["ffn_gate"][l]).reshape(16, 128, FC, 128).transpose(2, 1, 0, 3).reshape(FC, 128, SLOT)
        u = np.asarray(inp["ffn_up"][l]).reshape(16, 128, FC, 128).transpose(2, 1, 0, 3).reshape(FC, 128, SLOT)
        wblk[l, 56:144:2] = g
        wblk[l, 57:144:2] = u
        dn = np.zeros((48, 128, 16, 128), np.float32)
        dn[:FC] = np.asarray(inp["ffn_down"][l]).reshape(FC, 128, 16, 128)
        dn = dn.reshape(3, 16, 128, 16, 128).transpose(3, 0, 2, 1, 4).reshape(16, 3, 128, SLOT)
        wblk[l, 144:192] = dn.reshape(48, 128, SLOT)
        o = l * VL
        vecs[:, o + VO["nm"]:o + VO["nm"] + 16] = _colize(inp["norm_mix"][l])
        vecs[:, o + VO["nf"]:o + VO["nf"] + 16] = _colize(inp["norm_ffn"][l])
        vecs[:, o + VO["cb"]:o + VO["cb"] + 4] = _colize(inp["conv_b"][l])
        cw = np.asarray(inp["conv_w"][l])
        vecs[:, o + VO["cw"]:o + VO["cw"] + 124] = cw.reshape(31, 4, 128).transpose(2, 1, 0).reshape(128, 124)
        vecs[:, o + VO["lg"]:o + VO["lg"] + 4] = _colize(inp["conv_ln_g"][l])
        vecs[:, o + VO["lb"]:o + VO["lb"] + 4] = _colize(inp["conv_ln_b"][l])
        vecs[:, o + VO["psc"]:o + VO["psc"] + 4] = _colize(inp["pool_scale"][l])
        vecs[:, o + VO["mu"]:o + VO["mu"] + NQ] = _colize(inp["shift_mu"][l])
        vecs[:, o + VO["w0"]:o + VO["w0"] + 8] = _colize(inp["decay_w0"][l])
        vecs[:, o + VO["a0"]:o + VO["a0"] + 8] = _colize(inp["iclr_a0"][l])
        vecs[:, o + VO["kk"]:o + VO["kk"] + 8] = _colize(inp["k_k"][l])
        vecs[:, o + VO["ka"]:o + VO["ka"] + 8] = _colize(inp["k_a"][l])
        vecs[:, o + VO["rk"]:o + VO["rk"] + 8] = _colize(inp["r_k"][l])
        vecs[:, o + VO["gg"]:o + VO["gg"] + 8] = _colize(inp["gn_g"][l])
        vecs[:, o + VO["gb"]:o + VO["gb"] + 8] = _colize(inp["gn_b"][l])
    vecs[:, DEPTH * VL:DEPTH * VL + 16] = _colize(inp["norm_final"])
    return wblk, vecs


def _core_inputs(inp, c, shared, nseq_tok=SEQ):
    wblk, vecs = shared
    sh = np.zeros((DEPTH, 2, NQ * 128), np.float32)
    sh[:, :, :3264] = np.asarray(inp["state_shift"])[:, 2 * c:2 * c + 2, 0, :]
    return {
        "xp": np.ascontiguousarray(np.asarray(inp["x_prompt"])[c % 4, :nseq_tok]),
        "xs": np.ascontiguousarray(np.asarray(inp["x_sample"])[2 * c:2 * c + 2].reshape(2 * SLEN, D)),
        "cconv": np.ascontiguousarray(np.asarray(inp["cache_conv"])[:, 2 * c:2 * c + 2]),
        "cpool": np.ascontiguousarray(np.asarray(inp["cache_pool"])[:, 2 * c:2 * c + 2]),
        "cshift": sh.reshape(DEPTH, 2, NQ, 128),
        "cwkv": np.ascontiguousarray(np.asarray(inp["state_wkv"])[:, 2 * c:2 * c + 2]),
        "wblk": wblk,
        "vecs": vecs,
    }


_NC_CACHE = {}


def kernel(**inp):
    inp = {k: np.asarray(v) for k, v in inp.items()}
    shared = _prep_shared(inp)
    if "nc" not in _NC_CACHE:
        _NC_CACHE["nc"] = build_program()
    nc = _NC_CACHE["nc"]
    in_maps = [_core_inputs(inp, c, shared) for c in range(8)]
    res = run_bass_kernel_spmd(nc, in_maps, core_ids=list(range(8)))
    R = res.results
    y_prompt = np.stack([R[c]["yp"] for c in range(4)]).astype(np.float32)
    y_sample = np.concatenate([R[c]["ys"].reshape(2, SLEN, D) for c in range(8)]).astype(np.float32)

    def gather(name, tailshape, fix=None):
        pr = np.stack([R[c][name][:, 0] for c in range(4)], axis=1)
        sm = np.concatenate([R[c][name][:, 1:3] for c in range(8)], axis=1)
        if fix is not None:
            pr, sm = fix(pr), fix(sm)
        return pr.astype(np.float32), sm.astype(np.float32)

    p_conv, s_conv = gather("nconv", None)
    p_pool, s_pool = gather("npool", None)
    fixs = lambda a: a.reshape(a.shape[0], a.shape[1], 1, NQ * 128)[..., :3264]
    p_shift, s_shift = gather("nshift", None, fixs)
    p_wkv, s_wkv = gather("nwkv", None)
    return (y_prompt, y_sample, p_conv, p_pool, p_shift, p_wkv, s_conv, s_pool, s_shift, s_wkv)
```

```python
import numpy as np
import concourse.bass as bass
import concourse.mybir as mybir
from concourse.bass_utils import run_bass_kernel_spmd

F32 = mybir.dt.float32
BF16 = mybir.dt.bfloat16
AF = mybir.ActivationFunctionType
ALU = mybir.AluOpType

D = 2048
KC = 16
DFF = 5632
FC = 44
HEADS = 16
PAIRS = 8
NQ = 26
DEPTH = 4
SEQ = 2048
SLEN = 64
RMS_EPS = 1e-6
LN_EPS = 1e-5
GN_EPS = 64e-5
LW_SCALE = -float(np.exp(-0.5))
POOL_WINDOWS = (2, 4, 8, 16)

NBLK = 2 + 38 + 16 + 88 + 48
SLOT = 2048
NSLOT = 6

VO = {}
_o = 0
for _n, _w in (("nm", 16), ("nf", 16), ("cb", 4), ("cw", 124), ("lg", 4), ("lb", 4), ("psc", 4),
               ("mu", NQ), ("w0", 8), ("a0", 8), ("kk", 8), ("ka", 8), ("rk", 8), ("gg", 8), ("gb", 8)):
    VO[_n] = _o
    _o += _w
VL = _o
NVEC = DEPTH * VL + 16


class _Dummy:
    def then_inc(self, *a, **k):
        return self


class _Rec:
    def __init__(self):
        self.calls = []

    def __getattr__(self, name):
        def f(*args, **kw):
            self.calls.append((name, args, kw))
            return _Dummy()
        return f


def _free_size(ap):
    try:
        n = 1
        for s in tuple(ap.shape)[1:]:
            n *= int(s)
        return n
    except Exception:
        return 256


class Sched:
    LAT_X = 0.45
    LAT_S = 0.25

    def __init__(self, nc, reorder=True):
        self.nc = nc
        self.reorder = reorder
        self.eng = {"pe": nc.tensor, "act": nc.scalar, "dve": nc.vector, "pool": nc.gpsimd, "sp": nc.sync}
        self.semh = {}
        self.cnt = {}
        for e in self.eng:
            self.semh[e] = nc.alloc_semaphore("sem_" + e)
            self.cnt[e] = 0
        self.waited = {e: {} for e in self.eng}
        self.lastw = {}
        self.readers = {}
        self.dma_sems = []
        self.dma_rr = 0
        self.ninst = 0
        self.dead = False
        self.pending = []

    def new_dma_sem(self, name):
        self.semh[name] = self.nc.alloc_semaphore("sem_" + name)
        self.cnt[name] = 0
        return name

    def op(self, e, fn, r=(), w=()):
        if self.dead:
            return
        rec = _Rec()
        fn(rec)
        self.pending.append(dict(kind="op", eng=e, calls=rec.calls, r=tuple(r), w=tuple(w), sync=False))

    def pe_group(self, fns, r=(), w=(), pe_sync=False):
        if self.dead:
            return
        rec = _Rec()
        for fn in fns:
            fn(rec)
        self.pending.append(dict(kind="op", eng="pe", calls=rec.calls, r=tuple(r), w=tuple(w), sync=pe_sync))

    def dma(self, q, out, in_, r=(), w=(), sem=None):
        if self.dead:
            return
        self.pending.append(dict(kind="dma", eng=q, out=out, in_=in_, r=tuple(r), w=tuple(w), sem=sem))

    def barrier(self, engines=("pe", "act", "dve", "pool")):
        if self.dead:
            return
        self.flush()
        for e in engines:
            need = {}
            for o in engines:
                if o != e and self.cnt[o] > 0:
                    need[o] = self.cnt[o]
            self._wait(e, need)

    def finish(self, e="sp"):
        self.flush()
        need = {}
        for s, c in self.cnt.items():
            if c > 0 and s != e:
                need[s] = c
        self._wait(e, need)

    def _cost(self, o):
        if o["kind"] == "dma":
            return 0.7
        e = o["eng"]
        if e == "pe":
            t = 0.0
            for (name, args, kw) in o["calls"]:
                if name == "matmul":
                    n = _free_size(kw.get("rhs"))
                    f = 4.0 if str(getattr(kw.get("rhs"), "dtype", "")) .endswith("float32") else 1.0
                    t += f * max(n, 64) / 2400.0 + 0.012
                else:
                    t += 0.06
            return t
        f = _free_size(o["calls"][0][2].get("out")) if o["calls"] else 64
        if e == "act":
            return 0.2 + f / 1200.0
        if e == "dve":
            return 0.08 + f / 960.0
        return 0.3 + f / 500.0

    def flush(self):
        ops = self.pending
        self.pending = []
        n = len(ops)
        if n == 0:
            return
        if not self.reorder or n < 3:
            for o in ops:
                self._emit(o)
            return
        lastw, readers = {}, {}
        preds = [set() for _ in range(n)]
        for i, o in enumerate(ops):
            for k in o["r"]:
                j = lastw.get(k)
                if j is not None:
                    preds[i].add(j)
            for k in o["w"]:
                j = lastw.get(k)
                if j is not None:
                    preds[i].add(j)
                for j in readers.get(k, ()):
                    preds[i].add(j)
            for k in o["r"]:
                readers.setdefault(k, []).append(i)
            for k in o["w"]:
                lastw[k] = i
                readers[k] = []
            preds[i].discard(i)
        succs = [[] for _ in range(n)]
        for i in range(n):
            for j in preds[i]:
                succs[j].append(i)
        cost = [self._cost(o) for o in ops]
        lat_out = [2.5 if o["kind"] == "dma" else 0.0 for o in ops]
        prio = [0.0] * n
        for i in range(n - 1, -1, -1):
            m = 0.0
            for s in succs[i]:
                m = max(m, prio[s] + self.LAT_X)
            prio[i] = cost[i] + lat_out[i] + m
        npred = [len(p) for p in preds]
        ready = [i for i in range(n) if npred[i] == 0]
        fin = [0.0] * n
        efree = {}
        order = []
        engs = [o["eng"] for o in ops]
        while ready:
            best, best_key = None, None
            for i in ready:
                e = engs[i]
                t = efree.get(e, 0.0)
                for j in preds[i]:
                    tj = fin[j] + lat_out[j] + (self.LAT_S if engs[j] == e else self.LAT_X)
                    if tj > t:
                        t = tj
                key = (t, -prio[i], i)
                if best_key is None or key < best_key:
                    best, best_key = i, key
            i = best
            ready.remove(i)
            t = best_key[0]
            fin[i] = t + cost[i]
            efree[engs[i]] = fin[i]
            order.append(i)
            for s in succs[i]:
                npred[s] -= 1
                if npred[s] == 0:
                    ready.append(s)
        assert len(order) == n
        for i in order:
            self._emit(ops[i])

    def _deps(self, r, w):
        need = {}
        for k in r:
            t = self.lastw.get(k)
            if t is not None:
                need[t[0]] = max(need.get(t[0], 0), t[1])
        for k in w:
            t = self.lastw.get(k)
            if t is not None:
                need[t[0]] = max(need.get(t[0], 0), t[1])
            for t in self.readers.get(k, ()):
                need[t[0]] = max(need.get(t[0], 0), t[1])
        return need

    def _wait(self, e, need, skip_self=False):
        wd = self.waited[e]
        for s, v in need.items():
            if skip_self and s == e:
                continue
            if wd.get(s, 0) < v:
                self.eng[e].wait_ge(self.semh[s], v)
                wd[s] = v
                self.ninst += 1

    def _commit(self, tok, r, w):
        for k in r:
            lst = self.readers.setdefault(k, [])
            lst[:] = [t for t in lst if t[0] != tok[0]]
            lst.append(tok)
        for k in w:
            self.lastw[k] = tok
            self.readers[k] = []

    def _emit(self, o):
        if o["kind"] == "dma":
            return self._emit_dma(o)
        e = o["eng"]
        need = self._deps(o["r"], o["w"])
        self._wait(e, need, skip_self=(e == "pe" and not o["sync"]))
        inst = None
        for (name, args, kw) in o["calls"]:
            inst = getattr(self.eng[e], name)(*args, **kw)
            self.ninst += 1
        self.cnt[e] += 1
        inst.then_inc(self.semh[e], 1)
        self._commit((e, self.cnt[e]), o["r"], o["w"])

    def _emit_dma(self, o):
        q, sem = o["eng"], o["sem"]
        if sem is None:
            if len(self.dma_sems) < 24:
                sem = self.new_dma_sem("d%d" % len(self.dma_sems))
                self.dma_sems.append(sem)
            else:
                sem = self.dma_sems[self.dma_rr % len(self.dma_sems)]
                self.dma_rr += 1
        need = self._deps(o["r"], o["w"])
        if self.cnt[sem] > 0:
            need[sem] = max(need.get(sem, 0), self.cnt[sem])
        self._wait(q, need)
        self.eng[q].dma_start(out=o["out"], in_=o["in_"]).then_inc(self.semh[sem], 16)
        self.cnt[sem] += 16
        self.ninst += 1
        self._commit((sem, self.cnt[sem]), o["r"], o["w"])


class Arena:
    def __init__(self, nc, nbytes, name="arena"):
        assert nbytes % 4 == 0
        self.t = nc.alloc_sbuf_tensor(name, [128, nbytes // 4], F32)
        self.off = 0
        self.cap = nbytes

    def alloc(self, shape, dt, at=None, name=None):
        esz = 4 if dt == F32 else 2
        n = 1
        for s in shape[1:]:
            n *= s
        nb = (n * esz + 31) // 32 * 32
        if at is None:
            at = self.off
            self.off += nb
            assert self.off <= self.cap, ("arena overflow", self.off, self.cap)
        if not hasattr(self, "reg"):
            self.reg = {}
        self.reg[name if name is not None else "anon%d" % len(self.reg)] = (at, list(shape), "f32" if dt == F32 else "bf16")
        v = self.t[:, at // 4:(at + nb) // 4]
        if dt != F32:
            v = v.bitcast(dt)
        v = v[:, 0:n]
        if len(shape) == 3:
            v = v.rearrange("p (a b) -> p a b", b=shape[2])
        elif len(shape) == 4:
            v = v.rearrange("p (a b c) -> p a b c", b=shape[2], c=shape[3])
        return v


def build_program(layers=DEPTH, npg=4, dbg=None, with_s=True, reorder=True):
    nc = bass.Bass("TRN2", target_bir_lowering=False)
    S = Sched(nc, reorder=reorder)
    nseq_tok = 512 * npg

    xp = nc.dram_tensor("xp", [nseq_tok, D], F32, kind="ExternalInput").ap()
    xs = nc.dram_tensor("xs", [2 * SLEN, D], F32, kind="ExternalInput").ap()
    cconv = nc.dram_tensor("cconv", [DEPTH, 2, 30, 512], F32, kind="ExternalInput").ap()
    cpool = nc.dram_tensor("cpool", [DEPTH, 2, 15, 512], F32, kind="ExternalInput").ap()
    cshift = nc.dram_tensor("cshift", [DEPTH, 2, NQ, 128], F32, kind="ExternalInput").ap()
    cwkv = nc.dram_tensor("cwkv", [DEPTH, 2, HEADS, 64, 64], F32, kind="ExternalInput").ap()
    wblk = nc.dram_tensor("wblk", [layers, NBLK, 128, SLOT], F32, kind="ExternalInput").ap()
    vecs_d = nc.dram_tensor("vecs", [128, NVEC], F32, kind="ExternalInput").ap()
    yp = nc.dram_tensor("yp", [nseq_tok, D], F32, kind="ExternalOutput").ap()
    ys = nc.dram_tensor("ys", [2 * SLEN, D], F32, kind="ExternalOutput").ap()
    nconv = nc.dram_tensor("nconv", [DEPTH, 3, 30, 512], F32, kind="ExternalOutput").ap()
    npool = nc.dram_tensor("npool", [DEPTH, 3, 15, 512], F32, kind="ExternalOutput").ap()
    nshift = nc.dram_tensor("nshift", [DEPTH, 3, NQ, 128], F32, kind="ExternalOutput").ap()
    nwkv = nc.dram_tensor("nwkv", [DEPTH, 3, HEADS, 64, 64], F32, kind="ExternalOutput").ap()

    A = Arena(nc, 212736)
    vecs = A.alloc([128, NVEC], F32)
    omu = A.alloc([128, DEPTH, NQ], F32)
    ident_f = A.alloc([128, 128], F32)
    ident_b = A.alloc([128, 128], BF16)
    ones_b = A.alloc([128, 128], BF16)
    bones_b = A.alloc([128, 128], BF16)
    bones_f = A.alloc([128, 128], F32)
    onesD_f = A.alloc([128, 128], F32)
    m_su = A.alloc([128, 128], F32)
    m_ui = A.alloc([128, 128], F32)
    m_sl = A.alloc([128, 128], F32)
    cmask = {64: A.alloc([128, 64], BF16), 128: A.alloc([128, 512], BF16)}
    invc_first = A.alloc([128, 4, 16], F32)
    st = {}
    for l in range(DEPTH):
        st[("P", l)] = dict(u=A.alloc([128, 4, 30], F32), p=A.alloc([128, 4, 15], F32),
                            q=A.alloc([128, NQ], F32), S=A.alloc([128, PAIRS, 64], F32))
    for sq in ("S0", "S1"):
        d_ = dict(u=A.alloc([128, 4, 30], F32), p=A.alloc([128, 4, 15], F32),
                  q=A.alloc([128, NQ], F32), S=A.alloc([128, PAIRS, 64], F32))
        for l in range(DEPTH):
            st[(sq, l)] = d_
    xT = A.alloc([128, KC, 512], F32, name='xT')
    hT = A.alloc([128, KC, 512], BF16, name='hT')
    cat = A.alloc([128, KC, 512], BF16, name='cat')
    wring = A.alloc([128, NSLOT, SLOT], BF16)
    wsmall = A.alloc([128, SLOT], BF16)
    wpool = A.alloc([128, 512], BF16)
    rstd = A.alloc([128, 512], F32)
    sqb = A.alloc([128, 512], BF16)
    base_off = A.off

    def mixer_bufs(Wm):
        b = {}
        o0 = A.off
        b["ubuf"] = A.alloc([128, 4, 30 + Wm], F32, name="mb%d_ubuf" % Wm)
        b["ubf"] = A.alloc([128, 4, 30 + Wm], BF16)
        b["hconv"] = A.alloc([128, 4, Wm], F32)
        o1 = A.off
        b["pbuf"] = A.alloc([128, 4, 15 + Wm], F32, at=o0)
        b["dpool"] = A.alloc([128, 4, Wm], BF16, at=o0 + (4 * (15 + Wm) * 4 + 31) // 32 * 32)
        assert o0 + (4 * (15 + Wm) * 4 + 31) // 32 * 32 + 4 * Wm * 2 <= o1
        for n in ("t0", "t1", "t2", "t3", "t4", "t5", "t6", "t7", "t8", "t9", "t10", "t11"):
            b["off_" + n] = A.off
            b[n] = A.alloc([128, Wm + 16], F32, name="mb%d_%s" % (Wm, n))
        for n in ("b0", "b1", "b2", "b3", "b4", "b5"):
            b[n] = A.alloc([128, Wm], BF16, name="mb%d_%s" % (Wm, n))
        b["AR"] = A.alloc([128, 2 * Wm], BF16, name="mb%d_AR" % Wm)
        b["lora"] = A.alloc([128, 2, Wm], BF16, name="mb%d_lora" % Wm)
        b["gl"] = b["t1"]
        return b
    mb = [mixer_bufs(512), mixer_bufs(64)]
    diag = A.alloc([128, 31, 128], BF16, at=mb[0]['off_t8'])
    tm = {}
    for h in range(2):
        tm[("S1m", h)] = A.alloc([128, 4, 2, 128], BF16, name="tm_S1m%d" % h)
        tm[("S2m", h)] = A.alloc([128, 4, 2, 128], BF16, name="tm_S2m%d" % h)
        tm[("L", h)] = A.alloc([128, 4, 128], BF16, name="tm_L%d" % h)
        tm[("X", h)] = A.alloc([128, 4, 128], BF16, name="tm_X%d" % h)
    tm["KBV"] = A.alloc([128, 4, 3, 128], BF16, name="tm_KBV")
    tm["Psb"] = A.alloc([128, 2, 64], BF16, name="tm_Psb")
    tm["Usb"] = A.alloc([128, 2, 64], BF16, name="tm_Usb")
    tm["STb"] = A.alloc([128, 64], BF16, name="tm_STb")
    tm["gC"] = A.alloc([128, 8], F32, name="tm_gC")
    identb4 = A.alloc([128, 4, 128], BF16)
    mc = {64: A.alloc([128, 2, 64], F32), 128: A.alloc([128, 2, 128], F32)}
    stage = mb[0]["hconv"].rearrange("p a b -> p (a b)")
    stage2 = stage[:, 0:1024]
    stage3 = stage[:, 1024:1664]
    mix_end = A.off
    act = A.alloc([128, FC, 512], BF16, at=base_off)
    assert base_off + FC * 512 * 2 <= A.cap
    A.off = max(mix_end, base_off + FC * 512 * 2 + 4096)
    ftmp = [A.alloc([128, 512], F32, at=base_off + FC * 512 * 2), A.alloc([128, 512], F32, at=base_off + FC * 512 * 2 + 2048)]
    print("SBUF used", A.off, "of", A.cap)

    psb = [nc.alloc_psum_tensor("ps%d" % i, [128, 512], F32) for i in range(8)]
    bank_rr = {"big": 0, "small": 0}

    def bank(pool):
        if pool == "big":
            i = bank_rr["big"] % 3
            bank_rr["big"] += 1
            return i
        if pool == "p1":
            i = (3, 4, 5, 6, 7)[bank_rr.setdefault("p1", 0) % 5]
            bank_rr["p1"] += 1
            return i
        if pool == "y":
            return 3
        if pool == "state":
            return 7
        i = 4 + bank_rr["small"] % 3
        bank_rr["small"] += 1
        return i

    psap = [t[:, :] for t in psb]

    def PS(i):
        return psap[i]

    wsem = [S.new_dma_sem("w%d" % i) for i in range(NSLOT)]
    wsem_small = S.new_dma_sem("wsm")
    wq = {"next": 0, "issued": 0, "order": []}

    def w_issue_upto(n):
        while wq["issued"] < min(n, len(wq["order"])):
            i = wq["issued"]
            (l, b, ncols) = wq["order"][i]
            slot = i % NSLOT
            S.dma("pool", wring[:, slot, 0:ncols], wblk[l, b, :, 0:ncols], w=[("w", slot)], sem=wsem[slot])
            wq["issued"] += 1

    def w_next():
        i = wq["next"]
        wq["next"] += 1
        w_issue_upto(i + NSLOT - 1)
        return i % NSLOT

    def blk_in(cc):
        return 2 + cc
    def blk_out(n):
        return 2 + 38 + n
    def blk_gate(f):
        return 2 + 38 + 16 + 2 * f
    def blk_up(f):
        return 2 + 38 + 16 + 2 * f + 1
    def blk_down(n, j):
        return 2 + 38 + 16 + 88 + 3 * n + j
    IN_ORDER = [4, 0, 5, 1, 6, 2, 7, 3, 8, 9, 10, 11, 36, 37]
    for p_ in range(PAIRS):
        IN_ORDER += [12 + p_, 20 + p_, 28 + p_]

    def layer_order(l):
        o = [(l, blk_in(cc), SLOT) for cc in IN_ORDER]
        o += [(l, blk_out(n), SLOT) for n in range(16)]
        for f in range(FC):
            o += [(l, blk_gate(f), SLOT), (l, blk_up(f), SLOT)]
        for n in range(16):
            o += [(l, blk_down(n, 0), SLOT), (l, blk_down(n, 1), SLOT), (l, blk_down(n, 2), 12 * 128)]
        return o

    def pool_op(fn, r=(), w=()):
        return S.op("pool", fn, r, w)

    S.dma("sp", vecs, vecs_d[:, :], w=["vecs"])
    pool_op(lambda e: e.memset(ident_f, 1.0), w=["c_if"])
    pool_op(lambda e: e.affine_select(out=ident_f, in_=ident_f, pattern=[[-1, 128]], compare_op=ALU.is_equal,
                                      fill=0.0, base=0, channel_multiplier=1), r=["c_if"], w=["c_if"])
    S.op("dve", lambda e: e.tensor_copy(out=ident_b, in_=ident_f), r=["c_if"], w=["c_ib"])
    for i4 in range(4):
        S.op("dve", lambda e, i4=i4: e.tensor_copy(out=identb4[:, i4, :], in_=ident_f), r=["c_if"], w=["c_ib4"])
    S.op("dve", lambda e: e.memset(ones_b, 1.0), w=["c_ones"])
    S.op("dve", lambda e: e.memset(onesD_f, 1.0 / 512.0), w=["c_onesD"])
    S.op("dve", lambda e: e.memset(bones_b, 0.0), w=["c_bones"])
    S.op("dve", lambda e: e.memset(bones_b[0:64, 0:64], 1.0), w=["c_bones"])
    S.op("dve", lambda e: e.memset(bones_b[64:128, 64:128], 1.0), w=["c_bones"])
    S.op("dve", lambda e: e.memset(bones_f, 0.0), w=["c_bonesf"])
    S.op("dve", lambda e: e.memset(bones_f[0:64, 0:64], 1.0 / 64.0), w=["c_bonesf"])
    S.op("dve", lambda e: e.memset(bones_f[64:128, 64:128], 1.0 / 64.0), w=["c_bonesf"])
    for (m, base, cm, step) in ((m_su, -1, -1, 1), (m_ui, 0, -1, 1), (m_sl, -1, 1, -1)):
        pool_op(lambda e, m=m: e.memset(m, 1.0), w=["c_masks"])
        pool_op(lambda e, m=m, base=base, cm=cm, step=step: e.affine_select(
            out=m, in_=m, pattern=[[step, 128]], compare_op=ALU.is_ge, fill=0.0, base=base,
            channel_multiplier=cm), r=["c_masks"], w=["c_masks"])
    for C in (64, 128):
        S.op("dve", lambda e, C=C: e.tensor_copy(out=mc[C][:, 0, :], in_=m_su[:, 0:C]), r=["c_masks"], w=["c_mc"])
        S.op("dve", lambda e, C=C: e.tensor_copy(out=mc[C][:, 1, :], in_=m_ui[:, 0:C]), r=["c_masks"], w=["c_mc"])
        S.op("dve", lambda e, C=C: e.memset(cmask[C], 1.0), w=["c_cmask"])
        S.op("dve", lambda e, C=C: e.memset(cmask[C].rearrange("p (a b) -> p a b", b=C)[:, :, 0:1], 0.0),
             r=["c_cmask"], w=["c_cmask"])
    pool_op(lambda e: e.iota(out=invc_first[:, 0, :], pattern=[[1, 16]], base=1, channel_multiplier=0,
                             allow_small_or_imprecise_dtypes=True), w=["c_invc"])
    for gi, wdw in enumerate(POOL_WINDOWS):
        if gi > 0:
            S.op("dve", lambda e, gi=gi: e.tensor_copy(out=invc_first[:, gi, :], in_=invc_first[:, 0, :]),
                 r=["c_invc"], w=["c_invc%d" % gi])
    for gi, wdw in enumerate(POOL_WINDOWS):
        S.op("dve", lambda e, gi=gi, wdw=wdw: e.tensor_scalar(out=invc_first[:, gi, :], in0=invc_first[:, gi, :],
                                                              scalar1=float(wdw), scalar2=None, op0=ALU.min),
             r=["c_invc", "c_invc%d" % gi], w=["c_invc%d" % gi] + (["c_invc"] if gi == 0 else []))
        S.op("dve", lambda e, gi=gi: e.reciprocal(out=invc_first[:, gi, :], in_=invc_first[:, gi, :]),
             r=["c_invc%d" % gi], w=["c_invc%d" % gi] + (["c_invc"] if gi == 0 else []))
    for l in range(DEPTH):
        o = l * VL + VO["mu"]
        S.op("dve", lambda e, l=l, o=o: e.tensor_scalar(out=omu[:, l, :], in0=vecs[:, o:o + NQ], scalar1=-1.0,
                                                        scalar2=1.0, op0=ALU.mult, op1=ALU.add),
             r=["vecs"], w=["omu"])
    CONST_KEYS = ["vecs", "omu", "c_if", "c_ib", "c_ones", "c_onesD", "c_bones", "c_bonesf", "c_masks",
                  "c_cmask", "c_onesrow", "c_invc", "c_invc1", "c_invc2", "c_invc3"]
    S.barrier()

    def V(l, name, c0=0, n=1):
        o = l * VL + VO[name] + c0
        return vecs[:, o:o + n]

    def rmsnorm_to(dst_fn, gw, gcol, key_out, l_vec_off, dst_is_bf=True):
        b = bank("small")
        fns = []
        for k in range(KC):
            S.op("act", lambda e, k=k: e.activation(out=sqb[:, 0:gw], in_=xT[:, k, 0:gw], func=AF.Square),
                 r=["xT"], w=["sqb"])
            S.pe_group([lambda e, k=k: e.matmul(PS(b)[:, 0:gw], lhsT=ones_b, rhs=sqb[:, 0:gw],
                                                 start=(k == 0), stop=(k == KC - 1))],
                       r=["sqb"], w=[("ps", b)])
        S.op("act", lambda e: e.activation(out=rstd[:, 0:gw], in_=PS(b)[:, 0:gw], func=AF.Ln,
                                           bias=RMS_EPS, scale=1.0 / D), r=[("ps", b)], w=["rstd"])
        S.op("act", lambda e: e.activation(out=rstd[:, 0:gw], in_=rstd[:, 0:gw], func=AF.Exp, scale=-0.5),
             r=["rstd"], w=["rstd"])
        for k in range(KC):
            S.op("dve", lambda e, k=k: e.scalar_tensor_tensor(
                out=dst_fn(k), in0=xT[:, k, 0:gw], scalar=vecs[:, l_vec_off + k:l_vec_off + k + 1],
                in1=rstd[:, 0:gw], op0=ALU.mult, op1=ALU.mult), r=["xT", "rstd"], w=[key_out])

    def proj(slot, src, gw, key_src, kchunks=KC, b=None):
        if b is None:
            b = bank("big")
        wv = wring[:, slot, :].rearrange("p (k n) -> p k n", n=128)
        fns = [lambda e, k=k: e.matmul(PS(b)[:, 0:gw], lhsT=wv[:, k, :], rhs=src[:, k, 0:gw],
                                       start=(k == 0), stop=(k == kchunks - 1)) for k in range(kchunks)]
        S.pe_group(fns, r=[("w", slot), key_src], w=[("ps", b)])
        return b

    def shifted_from_psum(l, bq, qc, p, dst, key_dst, scratch, key_scr):
        W, off, bi = p["W"], p["off"], p["bi"]
        sk = (p["seq"], l)
        sd = st[sk]
        S.op("act", lambda e: e.activation(out=scratch[:, 0:W], in_=PS(bq)[:, off:off + W], func=AF.Identity,
                                           scale=omu[:, l, qc:qc + 1]), r=[("ps", bq)], w=[key_scr])
        S.op("dve", lambda e: e.scalar_tensor_tensor(
            out=dst[:, 1:W], in0=PS(bq)[:, off:off + W - 1], scalar=V(l, "mu", qc), in1=scratch[:, 1:W],
            op0=ALU.mult, op1=ALU.add), r=[("ps", bq), key_scr], w=[key_dst])
        S.op("dve", lambda e: e.scalar_tensor_tensor(
            out=dst[:, 0:1], in0=sd["q"][:, qc:qc + 1], scalar=V(l, "mu", qc), in1=scratch[:, 0:1],
            op0=ALU.mult, op1=ALU.add), r=[("st_q",) + sk, key_scr], w=[key_dst])
        S.op("act", lambda e: e.activation(out=sd["q"][:, qc:qc + 1], in_=PS(bq)[:, off + W - 1:off + W],
                                           func=AF.Copy), r=[("ps", bq), key_dst], w=[("st_q",) + sk])

    def wkv_pair(l, pr, p, br, bk, bv_):
        B = mb[p["bi"]]
        W, off, bi, C = p["W"], p["off"], p["bi"], p["C"]
        nch = W // C
        sk = (p["seq"], l)
        sd = st[sk]
        T = lambda n: B[n][:, 0:W]
        K = lambda n: (n, bi)
        c3 = lambda ap: ap.rearrange("p (a c) -> p a c", c=C)
        shifted_from_psum(l, br, pr, p, B["t1"], K("t1"), B["t0"], K("t0"))
        shifted_from_psum(l, bk, 8 + pr, p, B["t2"], K("t2"), B["t0"], K("t0"))
        shifted_from_psum(l, bv_, 16 + pr, p, B["t3"], K("t3"), B["t0"], K("t0"))
        r_, k_, v_ = T("t1"), T("t2"), T("t3")
        bw = bank("small")
        S.pe_group([lambda e: e.matmul(PS(bw)[:, 0:W], lhsT=wsmall[0:64, pr * 128:(pr + 1) * 128],
                                       rhs=B["lora"][0:64, 0, 0:W], start=True, stop=True)],
                   r=["wsmall", K("lora")], w=[("ps", bw)])
        lw = T("t4")
        S.op("act", lambda e: e.activation(out=lw, in_=PS(bw)[:, 0:W], func=AF.Sigmoid,
                                           bias=V(l, "w0", pr), scale=1.0), r=[("ps", bw)], w=[K("t4")])
        ba = bank("small")
        S.pe_group([lambda e: e.matmul(PS(ba)[:, 0:W], lhsT=wsmall[64:128, pr * 128:(pr + 1) * 128],
                                       rhs=B["lora"][64:128, 0, 0:W], start=True, stop=True)],
                   r=["wsmall", K("lora")], w=[("ps", ba)])
        a_ = T("t5")
        S.op("act", lambda e: e.activation(out=a_, in_=PS(ba)[:, 0:W], func=AF.Sigmoid,
                                           bias=V(l, "a0", pr), scale=1.0), r=[("ps", ba)], w=[K("t5")])
        bgp = bank("small")
        S.pe_group([lambda e: e.matmul(PS(bgp)[:, 0:W], lhsT=wsmall[0:64, 1024 + pr * 128:1024 + (pr + 1) * 128],
                                       rhs=B["lora"][0:64, 1, 0:W], start=True, stop=True)],
                   r=["wsmall", K("lora")], w=[("ps", bgp)])
        g_ = T("t6")
        S.op("act", lambda e: e.activation(out=g_, in_=PS(bgp)[:, 0:W], func=AF.Copy), r=[("ps", bgp)], w=[K("t6")])
        kk = T("t7")
        S.op("dve", lambda e: e.tensor_scalar(out=kk, in0=k_, scalar1=V(l, "kk", pr), scalar2=None, op0=ALU.mult),
             r=[K("t2")], w=[K("t7")])
        S.op("act", lambda e: e.activation(out=T("b0"), in_=kk, func=AF.Square), r=[K("t7")], w=[K("b0")])
        bs = bank("small")
        S.pe_group([lambda e: e.matmul(PS(bs)[:, 0:W], lhsT=bones_b, rhs=T("b0"), start=True, stop=True)],
                   r=[K("b0")], w=[("ps", bs)])
        nrm = T("t8")
        S.op("dve", lambda e: e.tensor_scalar(out=nrm, in0=PS(bs)[:, 0:W], scalar1=1e-24, scalar2=None, op0=ALU.max),
             r=[("ps", bs)], w=[K("t8")])
        S.op("act", lambda e: e.activation(out=nrm, in_=nrm, func=AF.Ln), r=[K("t8")], w=[K("t8")])
        S.op("act", lambda e: e.activation(out=nrm, in_=nrm, func=AF.Exp, scale=-0.5), r=[K("t8")], w=[K("t8")])
        S.op("dve", lambda e: e.tensor_tensor(out=kk, in0=kk, in1=nrm, op=ALU.mult), r=[K("t7"), K("t8")], w=[K("t7")])
        bvec = T("t8")
        S.op("pool", lambda e: e.tensor_tensor(out=bvec, in0=kk, in1=a_, op=ALU.mult), r=[K("t7"), K("t5")], w=[K("t8")])
        kp = T("t9")
        S.op("dve", lambda e: e.tensor_scalar(out=kp, in0=a_, scalar1=-1.0, scalar2=V(l, "ka", pr), op0=ALU.add,
                                              op1=ALU.mult), r=[K("t5")], w=[K("t9")])
        S.op("dve", lambda e: e.scalar_tensor_tensor(out=kp, in0=kp, scalar=1.0, in1=k_, op0=ALU.add, op1=ALU.mult),
             r=[K("t9"), K("t2")], w=[K("t9")])
        S.op("dve", lambda e: e.scalar_tensor_tensor(out=T("b0"), in0=r_, scalar=V(l, "rk", pr), in1=kp, op0=ALU.mult,
                                                     op1=ALU.mult), r=[K("t1"), K("t9")], w=[K("b0")])
        bb = bank("small")
        S.pe_group([lambda e: e.matmul(PS(bb)[:, 0:W], lhsT=bones_b, rhs=T("b0"), start=True, stop=True)],
                   r=[K("b0")], w=[("ps", bb)])
        bonus = T("t10")
        S.op("dve", lambda e: e.tensor_tensor(out=bonus, in0=PS(bb)[:, 0:W], in1=v_, op=ALU.mult),
             r=[("ps", bb), K("t3")], w=[K("t10")])
        S.op("dve", lambda e: e.tensor_scalar(out=lw, in0=lw, scalar1=LW_SCALE, scalar2=None, op0=ALU.mult),
             r=[K("t4")], w=[K("t4")])
        cl = T("t11")
        S.op("dve", lambda e: e.tensor_tensor_scan(out=cl, data0=cmask[C][:, 0:W], data1=lw, initial=0.0,
                                                   op0=ALU.mult, op1=ALU.add), r=[K("t4")], w=[K("t11")])
        gC = tm["gC"]
        S.op("act", lambda e: e.activation(out=gC[:, 0:nch], in_=c3(cl)[:, :, C - 1], func=AF.Exp),
             r=[K("t11")], w=["gC"])
        e_pos = T("t0")
        S.op("act", lambda e: e.activation(out=e_pos, in_=cl, func=AF.Exp), r=[K("t11")], w=[K("t0")])
        AR = B["AR"][:, 0:2 * W].rearrange("p (a two c) -> p a two c", two=2, c=C)
        S.op("dve", lambda e: e.tensor_tensor(out=AR[:, :, 1, :], in0=c3(r_), in1=c3(e_pos), op=ALU.mult),
             r=[K("t1"), K("t0")], w=[K("AR")])
        S.op("pool", lambda e: e.tensor_tensor(out=lw, in0=cl, in1=lw, op=ALU.subtract), r=[K("t11"), K("t4")],
             w=[K("t4")])
        S.op("act", lambda e: e.activation(out=lw, in_=lw, func=AF.Exp), r=[K("t4")], w=[K("t4")])
        S.op("dve", lambda e: e.scalar_tensor_tensor(out=AR[:, :, 0, :], in0=c3(kk), scalar=-1.0, in1=c3(lw),
                                                     op0=ALU.mult, op1=ALU.mult), r=[K("t7"), K("t4")], w=[K("AR")])
        e_neg = T("t0")
        S.op("act", lambda e: e.activation(out=e_neg, in_=cl, func=AF.Exp, scale=-1.0), r=[K("t11"), K("AR")],
             w=[K("t0")])
        kt, bt, kh, bh, vb = T("b1"), T("b2"), T("b3"), T("b4"), T("b5")
        S.op("pool", lambda e: e.tensor_tensor(out=kp, in0=kp, in1=e_neg, op=ALU.mult), r=[K("t9"), K("t0")], w=[K("t9")])
        S.op("act", lambda e: e.activation(out=kt, in_=kp, func=AF.Copy), r=[K("t9")], w=[K("b1")])
        S.op("dve", lambda e: e.tensor_tensor(out=bvec, in0=bvec, in1=e_neg, op=ALU.mult), r=[K("t8"), K("t0")],
             w=[K("t8")])
        S.op("act", lambda e: e.activation(out=bt, in_=bvec, func=AF.Copy), r=[K("t8")], w=[K("b2")])
        for ch in range(nch):
            cs = slice(ch * C, (ch + 1) * C)
            S.op("act", lambda e, cs=cs, ch=ch: e.activation(out=kh[:, cs], in_=kp[:, cs], func=AF.Identity,
                                                             scale=gC[:, ch:ch + 1]), r=[K("t9"), "gC"], w=[K("b3")])
            S.op("act", lambda e, cs=cs, ch=ch: e.activation(out=bh[:, cs], in_=bvec[:, cs], func=AF.Identity,
                                                             scale=gC[:, ch:ch + 1]), r=[K("t8"), "gC"], w=[K("b4")])
        S.op("act", lambda e: e.activation(out=vb, in_=v_, func=AF.Copy), r=[K("t3")], w=[K("b5")])

        by = bank("y")
        STf = sd["S"][:, pr, :]
        STb = tm["STb"]
        nupd = {64: 5, 128: 6}[C]
        HS = [slice(0, 64), slice(64, 128)]
        KBV = tm["KBV"]
        ARc = lambda ch: B["AR"][:, ch * 2 * C:(ch + 1) * 2 * C]
        CS = lambda ch: slice(ch * C, (ch + 1) * C)
        for ch in range(nch):
            btp = bank("p1")
            ptv = PS(btp)[:, 0:192].bitcast(BF16).rearrange("p (a c) -> p a c", c=128)
            S.pe_group([lambda e, src_=src_, i=i, ch=ch: e.transpose(out=ptv[0:C, i, :], in_=src_[:, CS(ch)], identity=ident_b)
                        for i, src_ in enumerate((kh, bh, vb))], r=[K("b3"), K("b4"), K("b5")], w=[("ps", btp)])
            S.op("act", lambda e, ch=ch: e.activation(out=KBV[0:C, ch], in_=ptv[0:C], func=AF.Copy),
                 r=[("ps", btp)], w=["KBV"])
        S1 = [tm[("S1m", h)] for h in range(2)]
        S2 = [tm[("S2m", h)] for h in range(2)]
        Lt = [tm[("L", h)] for h in range(2)]
        Xt = [tm[("X", h)] for h in range(2)]
        for c0 in range(0, nch, 2):
            cn = min(2, nch - c0)
            for h in range(2):
                hs = HS[h]
                for (lhs, dst, key, eng) in ((kt, S1[h], ("S1m", h), "dve"), (bt, S2[h], ("S2m", h), "dve")):
                    b1 = bank("p1")
                    S.pe_group([lambda e, ch=ch, j=j, lhs=lhs, b1=b1: e.matmul(PS(b1)[0:C, j * 2 * C:(j + 1) * 2 * C], lhsT=lhs[hs, CS(ch)],
                                                                               rhs=ARc(ch)[hs, :], start=True, stop=True)
                                for j, ch in enumerate(range(c0, c0 + cn))],
                               r=[K("b1"), K("b2"), K("AR")], w=[("ps", b1)])
                    for j, ch in enumerate(range(c0, c0 + cn)):
                        S.op(eng, lambda e, ch=ch, j=j, dst=dst, b1=b1: e.tensor_tensor(
                            out=dst[0:C, ch, :, 0:C], in0=PS(b1)[0:C, j * 2 * C:(j + 1) * 2 * C].rearrange("p (a c) -> p a c", c=C),
                            in1=mc[C][0:C], op=ALU.mult), r=[("ps", b1)], w=[key])
        for h in range(2):
            hs = HS[h]
            b3 = bank("p1")
            S.pe_group([lambda e, ch=ch, b3=b3: e.matmul(PS(b3)[0:C, ch * C:(ch + 1) * C], lhsT=ARc(ch)[hs, 0:C], rhs=bt[hs, CS(ch)],
                                                          start=True, stop=True) for ch in range(nch)],
                       r=[K("b2"), K("AR")], w=[("ps", b3)])
            for ch in range(nch):
                S.op("dve", lambda e, ch=ch, b3=b3, h=h: e.tensor_tensor(out=Lt[h][0:C, ch, 0:C], in0=PS(b3)[0:C, ch * C:(ch + 1) * C],
                                                                         in1=m_sl[0:C, 0:C], op=ALU.mult),
                     r=[("ps", b3)], w=[("L", h)])
            S.op("pool", lambda e, h=h: e.tensor_tensor(out=Xt[h][0:C, 0:nch, 0:C], in0=S2[h][0:C, 0:nch, 0, 0:C],
                                                        in1=identb4[0:C, 0:nch, 0:C], op=ALU.add),
                 r=[("S2m", h)], w=[("X", h)])
        c3v = lambda ap: ap[0:C, 0:nch * C].rearrange("p (a c) -> p a c", c=C)
        for u in range(nupd):
            last = (u == nupd - 1)
            bl, bn = {}, {}
            for h in range(2):
                bl[h] = bank("p1")
                S.pe_group([lambda e, ch=ch, h=h: e.matmul(PS(bl[h])[0:C, ch * C:(ch + 1) * C], lhsT=S2[h][0:C, ch, 0, 0:C],
                                                          rhs=Lt[h][0:C, ch, 0:C], start=True, stop=True) for ch in range(nch)],
                           r=[("L", h), ("S2m", h)], w=[("ps", bl[h])])
                if not last:
                    bn[h] = bank("p1")
                    S.pe_group([lambda e, ch=ch, h=h: e.matmul(PS(bn[h])[0:C, ch * C:(ch + 1) * C], lhsT=Lt[h][0:C, ch, 0:C],
                                                              rhs=S2[h][0:C, ch, 0, 0:C], start=True, stop=True) for ch in range(nch)],
                               r=[("L", h), ("S2m", h)], w=[("ps", bn[h])])
            for h in range(2):
                S.op("act", lambda e, h=h: e.activation(out=Lt[h][0:C, 0:nch, 0:C], in_=c3v(PS(bl[h])), func=AF.Copy),
                     r=[("ps", bl[h])], w=[("L", h)])
                if not last:
                    S.op("act", lambda e, h=h: e.activation(out=S2[h][0:C, 0:nch, 0, 0:C], in_=c3v(PS(bn[h])), func=AF.Copy),
                         r=[("ps", bn[h])], w=[("S2m", h)])
            bx = {}
            for h in range(2):
                bx[h] = bank("p1")
                S.pe_group([lambda e, ch=ch, h=h: e.matmul(PS(bx[h])[0:C, ch * C:(ch + 1) * C], lhsT=Lt[h][0:C, ch, 0:C],
                                                          rhs=Xt[h][0:C, ch, 0:C], start=True, stop=True) for ch in range(nch)],
                           r=[("L", h), ("X", h)], w=[("ps", bx[h])])
            for h in range(2):
                S.op("dve", lambda e, h=h: e.tensor_tensor(out=Xt[h][0:C, 0:nch, 0:C], in0=c3v(PS(bx[h])),
                                                           in1=Xt[h][0:C, 0:nch, 0:C], op=ALU.add),
                     r=[("ps", bx[h]), ("X", h)], w=[("X", h)])

        S.op("act", lambda e: e.activation(out=STb, in_=STf, func=AF.Copy), r=[("st_S",) + sk], w=["STb"])
        Psb, Usb = tm["Psb"], tm["Usb"]
        bS = bank("state")
        for ch in range(nch):
            cs = CS(ch)
            VT = KBV[0:C, ch, 2, :]
            KH = KBV[0:C, ch, 0, :]
            BH = KBV[0:C, ch, 1, :]
            bP = bank("small")
            for h in range(2):
                hs = HS[h]
                rowsplit = (C == 64 and h == 1)
                S.pe_group([lambda e: e.matmul(PS(bP)[0:C, h * 64:h * 64 + 64], lhsT=ARc(ch)[hs, 0:C], rhs=STb[hs, :], start=True, stop=False)],
                           r=[K("AR"), "STb"], w=[("ps", bP)], pe_sync=(C == 64))
                S.pe_group([lambda e: e.matmul(PS(bP)[0:C, h * 64:h * 64 + 64], lhsT=S1[h][0:C, ch, 0, 0:C], rhs=VT[:, hs], start=False, stop=True)],
                           r=[("S1m", h), "KBV"], w=[("ps", bP)], pe_sync=(C == 64))
            S.op("act", lambda e: e.activation(out=Psb[0:C], in_=PS(bP)[0:C, 0:128].rearrange("p (a c) -> p a c", c=64), func=AF.Copy),
                 r=[("ps", bP)], w=["Psb"])
            bU = bank("small")
            for h in range(2):
                S.pe_group([lambda e: e.matmul(PS(bU)[0:C, h * 64:h * 64 + 64], lhsT=Xt[h][0:C, ch, 0:C], rhs=Psb[0:C, h, :], start=True, stop=True)],
                           r=[("X", h), "Psb"], w=[("ps", bU)], pe_sync=(C == 64))
            S.op("dve", lambda e: e.tensor_copy(out=Usb[0:C], in_=PS(bU)[0:C, 0:128].rearrange("p (a c) -> p a c", c=64)),
                 r=[("ps", bU)], w=["Usb"])
            for h in range(2):
                hs = HS[h]
                S.pe_group([lambda e: e.matmul(PS(bS)[hs, 0:64], lhsT=BH[:, hs], rhs=Usb[0:C, h, :], start=True, stop=False),
                            lambda e: e.matmul(PS(bS)[hs, 0:64], lhsT=KH[:, hs], rhs=VT[:, hs], start=False, stop=True)],
                           r=["KBV", "Usb"], w=[("ps", bS)], pe_sync=(C == 64))
            for h in range(2):
                hs = HS[h]
                S.pe_group([lambda e: e.matmul(PS(by)[hs, cs], lhsT=STb[hs, :], rhs=ARc(ch)[hs, C:2 * C], start=True, stop=False)],
                           r=["STb", K("AR")], w=[("ps", by)], pe_sync=(C == 64))
                S.pe_group([lambda e: e.matmul(PS(by)[hs, cs], lhsT=Usb[0:C, h, :], rhs=S2[h][0:C, ch, 1, 0:C], start=False, stop=False),
                            lambda e: e.matmul(PS(by)[hs, cs], lhsT=VT[:, hs], rhs=S1[h][0:C, ch, 1, 0:C], start=False, stop=True)],
                           r=["Usb", ("S2m", h), ("S1m", h), "KBV"], w=[("ps", by)], pe_sync=(C == 64))
            S.op("dve", lambda e, ch=ch: e.scalar_tensor_tensor(out=STf, in0=STf, scalar=gC[:, ch:ch + 1], in1=PS(bS)[:, 0:64],
                                                                op0=ALU.mult, op1=ALU.add),
                 r=[("ps", bS), ("st_S",) + sk, "gC"], w=[("st_S",) + sk])
            S.op("act", lambda e: e.activation(out=STb, in_=STf, func=AF.Copy), r=[("st_S",) + sk], w=["STb"])

        y = T("t7")
        S.op("act", lambda e: e.activation(out=y, in_=PS(by)[:, 0:W], func=AF.Copy), r=[("ps", by)], w=[K("t7")])
        bm = bank("small")
        S.pe_group([lambda e: e.matmul(PS(bm)[:, 0:W], lhsT=bones_f, rhs=y, start=True, stop=True)],
                   r=[K("t7")], w=[("ps", bm)])
        S.op("dve", lambda e: e.tensor_tensor(out=y, in0=y, in1=PS(bm)[:, 0:W], op=ALU.subtract),
             r=[("ps", bm), K("t7")], w=[K("t7")])
        sq = T("t8")
        S.op("act", lambda e: e.activation(out=sq, in_=y, func=AF.Square), r=[K("t7")], w=[K("t8")])
        bv2 = bank("small")
        S.pe_group([lambda e: e.matmul(PS(bv2)[:, 0:W], lhsT=bones_f, rhs=sq, start=True, stop=True)],
                   r=[K("t8")], w=[("ps", bv2)])
        rs = T("t9")
        S.op("act", lambda e: e.activation(out=rs, in_=PS(bv2)[:, 0:W], func=AF.Ln, bias=GN_EPS, scale=1.0),
             r=[("ps", bv2)], w=[K("t9")])
        S.op("act", lambda e: e.activation(out=rs, in_=rs, func=AF.Exp, scale=-0.5), r=[K("t9")], w=[K("t9")])
        S.op("dve", lambda e: e.tensor_tensor(out=y, in0=y, in1=rs, op=ALU.mult), r=[K("t7"), K("t9")], w=[K("t7")])
        S.op("dve", lambda e: e.tensor_scalar(out=y, in0=y, scalar1=V(l, "gg", pr), scalar2=V(l, "gb", pr),
                                              op0=ALU.mult, op1=ALU.add), r=[K("t7")], w=[K("t7")])
        S.op("pool", lambda e: e.tensor_tensor(out=y, in0=y, in1=bonus, op=ALU.add), r=[K("t7"), K("t10")], w=[K("t7")])
        S.op("dve", lambda e: e.tensor_tensor(out=cat[:, 8 + pr, off:off + W], in0=y, in1=g_, op=ALU.mult),
             r=[K("t7"), K("t6")], w=["cat"])

    groups = []
    for g in range(npg):
        groups.append(dict(gw=512, parts=[dict(seq="P", off=0, W=512, C=128, first=(g == 0), last=(g == npg - 1),
                                               bi=0)],
                           src=xp[g * 512:(g + 1) * 512, :], dst=yp[g * 512:(g + 1) * 512, :]))
    if with_s:
      groups.append(dict(gw=128, parts=[dict(seq="S0", off=0, W=64, C=64, first=False, last=True, bi=0, sidx=0),
                                      dict(seq="S1", off=64, W=64, C=64, first=False, last=True, bi=1, sidx=1)],
                       src=xs[:, :], dst=ys[:, :]))
    for g in groups:
        for l in range(layers):
            wq["order"] += layer_order(l)

    dbg_out = {}

    def chk(name):
        if dbg == name:
            S.dead = True

    for gi_, G in enumerate(groups):
        gw = G["gw"]
        parts = G["parts"]
        S.flush()
        S.reorder = reorder and (parts[0]["seq"] == "P")
        ntb = gw // 128
        for tb in range(ntb):
            S.dma("sp", stage, G["src"][tb * 128:(tb + 1) * 128, :], w=["stage"])
            for k4 in range(4):
                b = bank("small")
                S.pe_group([lambda e, k=k: e.transpose(out=PS(b)[:, (k % 4) * 128:(k % 4 + 1) * 128],
                                                        in_=stage[:, k * 128:(k + 1) * 128], identity=ident_f)
                            for k in range(k4 * 4, k4 * 4 + 4)], r=["stage"], w=[("ps", b)])
                S.op("act" if k4 % 2 else "dve",
                     (lambda e, k4=k4, tb=tb, b=b: e.activation(
                         out=xT[:, k4 * 4:k4 * 4 + 4, tb * 128:(tb + 1) * 128],
                         in_=PS(b).rearrange("p (a c) -> p a c", c=128), func=AF.Copy)) if k4 % 2 else
                     (lambda e, k4=k4, tb=tb, b=b: e.tensor_copy(
                         out=xT[:, k4 * 4:k4 * 4 + 4, tb * 128:(tb + 1) * 128],
                         in_=PS(b).rearrange("p (a c) -> p a c", c=128))),
                     r=[("ps", b)], w=["xT"])

        for l in range(layers):
            LV = l * VL
            chk("A")
            S.dma("pool", wsmall, wblk[l, 0, :, :], w=["wsmall"], sem=wsem_small)
            S.dma("pool", wpool, wblk[l, 1, :, 0:512], w=["wpool"], sem=wsem_small)
            for p in parts:
                if p["seq"] == "P":
                    if p["first"]:
                        sd = st[("P", l)]
                        S.op("dve", lambda e, sd=sd: e.memset(sd["u"], 0.0), w=[("st_u", "P", l)])
                        S.op("dve", lambda e, sd=sd: e.memset(sd["p"], 0.0), w=[("st_p", "P", l)])
                        S.op("dve", lambda e, sd=sd: e.memset(sd["q"], 0.0), w=[("st_q", "P", l)])
                        S.op("dve", lambda e, sd=sd: e.memset(sd["S"], 0.0), w=[("st_S", "P", l)])
                    continue
                sq, si = p["seq"], p["sidx"]
                sd = st[(sq, l)]
                S.dma("sp", stage2[0:30, 0:512], cconv[l, si, :, :], w=["stage"])
                b = bank("small")
                S.pe_group([lambda e, c=c: e.transpose(out=PS(b)[:, c * 32:c * 32 + 30],
                                                        in_=stage2[0:30, c * 128:(c + 1) * 128],
                                                        identity=ident_f[0:30, 0:30]) for c in range(4)],
                           r=["stage"], w=[("ps", b)])
                S.op("dve", lambda e, sd=sd, b=b: e.tensor_copy(
                    out=sd["u"], in_=PS(b)[:, 0:128].rearrange("p (a c) -> p a c", c=32)[:, :, 0:30]),
                    r=[("ps", b)], w=[("st_u", sq, l)])
                S.dma("sp", stage2[0:15, 0:512], cpool[l, si, :, :], w=["stage"])
                b = bank("small")
                S.pe_group([lambda e, c=c: e.transpose(out=PS(b)[:, c * 16:c * 16 + 15],
                                                        in_=stage2[0:15, c * 128:(c + 1) * 128],
                                                        identity=ident_f[0:15, 0:15]) for c in range(4)],
                           r=["stage"], w=[("ps", b)])
                S.op("dve", lambda e, sd=sd, b=b: e.tensor_copy(
                    out=sd["p"], in_=PS(b)[:, 0:64].rearrange("p (a c) -> p a c", c=16)[:, :, 0:15]),
                    r=[("ps", b)], w=[("st_p", sq, l)])
                S.dma("sp", stage2[0:NQ, 0:128], cshift[l, si, :, :], w=["stage"])
                b = bank("small")
                S.pe_group([lambda e: e.transpose(out=PS(b)[:, 0:NQ], in_=stage2[0:NQ, 0:128],
                                                  identity=ident_f[0:NQ, 0:NQ])], r=["stage"], w=[("ps", b)])
                S.op("dve", lambda e, sd=sd, b=b: e.tensor_copy(out=sd["q"], in_=PS(b)[:, 0:NQ]),
                     r=[("ps", b)], w=[("st_q", sq, l)])
                S.dma("sp", stage2[0:64, :].rearrange("p (h j) -> p h j", j=64),
                      cwkv[l, si].rearrange("h i j -> i h j"), w=["stage"])
                for half in range(2):
                    b = bank("small")
                    S.pe_group([lambda e, pr=pr: e.transpose(
                        out=PS(b)[:, (pr % 4) * 64:(pr % 4) * 64 + 64], in_=stage2[0:64, pr * 128:(pr + 1) * 128],
                        identity=ident_f[0:64, 0:64]) for pr in range(half * 4, half * 4 + 4)],
                        r=["stage"], w=[("ps", b)])
                    S.op("dve", lambda e, sd=sd, b=b, half=half: e.tensor_copy(
                        out=sd["S"][:, half * 4:half * 4 + 4, :],
                        in_=PS(b)[:, 0:256].rearrange("p (a c) -> p a c", c=64)),
                        r=[("ps", b)], w=[("st_S", sq, l)])

            chk("A2")
            rmsnorm_to(lambda k: hT[:, k, 0:gw], gw, 0, "hT", LV + VO["nm"])
            chk("B")

            for c in range(4):
                bg = proj(w_next(), hT, gw, "hT")
                bv = proj(w_next(), hT, gw, "hT")
                for p in parts:
                    B = mb[p["bi"]]
                    W, off = p["W"], p["off"]
                    sk = (p["seq"], l)
                    t0 = B["t0"][:, 0:W]
                    if c == 0:
                        S.op("dve", lambda e, B=B, p=p: e.tensor_copy(out=B["ubuf"][:, :, 0:30],
                                                                     in_=st[(p["seq"], l)]["u"]),
                             r=[("st_u",) + sk], w=[("ubuf", p["bi"])])
                    S.op("act", lambda e, t0=t0, off=off, W=W, bg=bg: e.activation(
                        out=t0, in_=PS(bg)[:, off:off + W], func=AF.Sigmoid), r=[("ps", bg)], w=[("t0", p["bi"])])
                    S.op("dve", lambda e, B=B, t0=t0, off=off, W=W, bv=bv, c=c: e.tensor_tensor(
                        out=B["ubuf"][:, c, 30:30 + W], in0=PS(bv)[:, off:off + W], in1=t0, op=ALU.mult),
                        r=[("ps", bv), ("t0", p["bi"])], w=[("ubuf", p["bi"])])
            for p in parts:
                B = mb[p["bi"]]
                W, off, bi = p["W"], p["off"], p["bi"]
                sk = (p["seq"], l)
                S.op("act", lambda e, B=B, W=W: e.activation(out=B["ubf"][:, :, 0:30 + W], in_=B["ubuf"][:, :, 0:30 + W],
                                                            func=AF.Copy), r=[("ubuf", bi)], w=[("ubf", bi)])
                S.op("dve", lambda e, B=B, W=W, p=p: e.tensor_copy(out=st[(p["seq"], l)]["u"], in_=B["ubuf"][:, :, W:W + 30]),
                     r=[("ubuf", bi)], w=[("st_u",) + sk])
                pass
            for c in range(4):
                for j in range(31):
                    if j % 4 != 3:
                        S.op("act", lambda e, c=c, j=j: e.activation(out=diag[:, j, :], in_=ident_b, func=AF.Identity,
                                                                      scale=V(l, "cw", c * 31 + j)), r=[], w=[("diag", j)])
                    else:
                        S.op("dve", lambda e, c=c, j=j: e.tensor_scalar(
                            out=diag[:, j, :], in0=ident_b, scalar1=V(l, "cw", c * 31 + j), scalar2=None, op0=ALU.mult),
                            r=[], w=[("diag", j)])
                for p in parts:
                    B = mb[p["bi"]]
                    W, off, bi = p["W"], p["off"], p["bi"]
                    b = bank("small")
                    S.pe_group([lambda e, j=j, c=c, B=B, W=W, b=b: e.matmul(
                        PS(b)[:, 0:W], lhsT=diag[:, j, :], rhs=B["ubf"][:, c, j:j + W],
                        start=(j == 0), stop=(j == 30)) for j in range(31)],
                        r=[("diag", j) for j in range(31)] + [("ubf", bi)], w=[("ps", b)])
                    S.op("act", lambda e, B=B, W=W, b=b, c=c: e.activation(
                        out=B["hconv"][:, c, 0:W], in_=PS(b)[:, 0:W], func=AF.Identity,
                        bias=V(l, "cb", c), scale=1.0), r=[("ps", b)], w=[("hconv", bi)] + (["stage"] if bi == 0 else []))
            for p in parts:
                B = mb[p["bi"]]
                W, off, bi = p["W"], p["off"], p["bi"]
                sk = (p["seq"], l)
                bm = bank("small")
                S.pe_group([lambda e, c=c, B=B, W=W: e.matmul(PS(bm)[:, 0:W], lhsT=onesD_f, rhs=B["hconv"][:, c, 0:W],
                                                              start=(c == 0), stop=(c == 3)) for c in range(4)],
                           r=[("hconv", bi)], w=[("ps", bm)])
                mean = B["t1"][:, 0:W]
                S.op("act", lambda e, mean=mean, W=W: e.activation(out=mean, in_=PS(bm)[:, 0:W], func=AF.Copy),
                     r=[("ps", bm)], w=[("t1", bi)])
                for c in range(4):
                    S.op("dve", lambda e, c=c, B=B, W=W, mean=mean: e.tensor_tensor(
                        out=B["hconv"][:, c, 0:W], in0=B["hconv"][:, c, 0:W], in1=mean, op=ALU.subtract),
                        r=[("hconv", bi), ("t1", bi)], w=[("hconv", bi)])
                bvv = bank("small")
                for c in range(4):
                    S.op("act", lambda e, c=c, B=B, W=W: e.activation(out=B["t2"][:, 0:W], in_=B["hconv"][:, c, 0:W],
                                                                      func=AF.Square),
                         r=[("hconv", bi)], w=[("t2", bi)])
                    S.pe_group([lambda e, c=c, B=B, W=W: e.matmul(PS(bvv)[:, 0:W], lhsT=onesD_f, rhs=B["t2"][:, 0:W],
                                                                  start=(c == 0), stop=(c == 3))],
                               r=[("t2", bi)], w=[("ps", bvv)])
                rs = B["t3"][:, 0:W]
                S.op("act", lambda e, rs=rs, W=W: e.activation(out=rs, in_=PS(bvv)[:, 0:W], func=AF.Ln,
                                                              bias=LN_EPS, scale=1.0), r=[("ps", bvv)], w=[("t3", bi)])
                S.op("act", lambda e, rs=rs: e.activation(out=rs, in_=rs, func=AF.Exp, scale=-0.5),
                     r=[("t3", bi)], w=[("t3", bi)])
                for c in range(4):
                    S.op("dve", lambda e, c=c, B=B, W=W, rs=rs: e.tensor_tensor(
                        out=B["hconv"][:, c, 0:W], in0=B["hconv"][:, c, 0:W], in1=rs, op=ALU.mult),
                        r=[("hconv", bi), ("t3", bi)], w=[("hconv", bi)])
                    S.op("act", lambda e, c=c, B=B, W=W, off=off: e.activation(
                        out=cat[:, c, off:off + W], in_=B["hconv"][:, c, 0:W], func=AF.Silu,
                        bias=V(l, "lb", c), scale=V(l, "lg", c)), r=[("hconv", bi)], w=["cat"])

            chk("C")
            S.barrier()
            for c in range(4):
                bp = proj(w_next(), hT, gw, "hT")
                for p in parts:
                    B = mb[p["bi"]]
                    W, off, bi = p["W"], p["off"], p["bi"]
                    sk = (p["seq"], l)
                    if c == 0:
                        S.op("dve", lambda e, B=B, p=p: e.tensor_copy(out=B["pbuf"][:, :, 0:15],
                                                                     in_=st[(p["seq"], l)]["p"]),
                             r=[("st_p",) + sk], w=[("pbuf", bi)])
                    S.op("act", lambda e, B=B, W=W, off=off, bp=bp, c=c: e.activation(
                        out=B["pbuf"][:, c, 15:15 + W], in_=PS(bp)[:, off:off + W], func=AF.Copy),
                        r=[("ps", bp)], w=[("pbuf", bi)])
            for p in parts:
                B = mb[p["bi"]]
                W, off, bi = p["W"], p["off"], p["bi"]
                sk = (p["seq"], l)
                S.op("dve", lambda e, B=B, W=W, p=p: e.tensor_copy(out=st[(p["seq"], l)]["p"], in_=B["pbuf"][:, :, W:W + 15]),
                     r=[("pbuf", bi)], w=[("st_p",) + sk])
                for c, wdw in enumerate(POOL_WINDOWS):
                    src = B["pbuf"][:, c, :]
                    lo = 15
                    span = 1
                    ta, tb_ = B["t4"], B["t5"]
                    cur, cur_lo = src, 0
                    nsteps = {2: 1, 4: 2, 8: 3, 16: 4}[wdw]
                    for s_ in range(nsteps):
                        dst = ta if s_ % 2 == 0 else tb_
                        new_lo = cur_lo + span
                        n = 15 + W - new_lo
                        S.op("pool", lambda e, dst=dst, cur=cur, new_lo=new_lo, span=span, n=n: e.tensor_tensor(
                            out=dst[:, new_lo:new_lo + n], in0=cur[:, new_lo:new_lo + n],
                            in1=cur[:, new_lo - span:new_lo - span + n], op=ALU.add),
                            r=[("pbuf", bi), ("t4", bi), ("t5", bi)], w=[("t4" if s_ % 2 == 0 else "t5", bi)])
                        cur, cur_lo = dst, new_lo
                        span *= 2
                    S.op("dve", lambda e, cur=cur, W=W, c=c, B=B, wdw=wdw: e.scalar_tensor_tensor(
                        out=B["dpool"][:, c, 0:W], in0=cur[:, 15:15 + W], scalar=1.0 / wdw,
                        in1=B["pbuf"][:, c, 15:15 + W], op0=ALU.mult, op1=ALU.subtract),
                        r=[("t4", bi), ("t5", bi), ("pbuf", bi)], w=[("dpool", bi)])
                    if p["first"]:
                        S.op("dve", lambda e, cur=cur, c=c: e.tensor_tensor(
                            out=cur[:, 15:31], in0=cur[:, 15:31], in1=invc_first[:, c, :], op=ALU.mult),
                            r=[("t4", bi), ("t5", bi), ("dpool", bi)], w=[("t4", bi), ("t5", bi)])
                        S.op("dve", lambda e, cur=cur, c=c, B=B: e.tensor_tensor(
                            out=B["dpool"][:, c, 0:16], in0=cur[:, 15:31], in1=B["pbuf"][:, c, 15:31],
                            op=ALU.subtract), r=[("t4", bi), ("t5", bi), ("pbuf", bi)], w=[("dpool", bi)])
                    b = bank("small")
                    S.pe_group([lambda e, c=c, B=B, W=W, b=b: e.matmul(PS(b)[:, 0:W], lhsT=wpool[:, c * 128:(c + 1) * 128],
                                                                        rhs=B["dpool"][:, c, 0:W], start=True, stop=True)],
                               r=["wpool", ("dpool", bi)], w=[("ps", b)])
                    S.op("act", lambda e, c=c, W=W, off=off, b=b: e.activation(
                        out=cat[:, 4 + c, off:off + W], in_=PS(b)[:, 0:W], func=AF.Identity, scale=V(l, "psc", c)),
                        r=[("ps", b)], w=["cat"])


            chk("D")
            b24 = proj(w_next(), hT, gw, "hT")
            b25 = proj(w_next(), hT, gw, "hT")
            for p in parts:
                B = mb[p["bi"]]
                W, bi = p["W"], p["bi"]
                gl = B["gl"]
                shifted_from_psum(l, b24, 24, p, gl, ("t1", bi), B["t0"], ("t0", bi))
                S.op("act", lambda e, B=B, W=W, gl=gl: e.activation(out=B["lora"][0:64, 0, 0:W], in_=gl[0:64, 0:W],
                                                                    func=AF.Tanh), r=[("t1", bi)], w=[("lora", bi)])
                S.op("act", lambda e, B=B, W=W, gl=gl: e.activation(out=B["lora"][64:128, 0, 0:W], in_=gl[64:128, 0:W],
                                                                    func=AF.Copy), r=[("t1", bi)], w=[("lora", bi)])
                shifted_from_psum(l, b25, 25, p, gl, ("t1", bi), B["t0"], ("t0", bi))
                S.op("act", lambda e, B=B, W=W, gl=gl: e.activation(out=B["lora"][0:64, 1, 0:W], in_=gl[0:64, 0:W],
                                                                    func=AF.Sigmoid), r=[("t1", bi)], w=[("lora", bi)])

            chk("E")
            for pr in range(PAIRS):
                if pr == 1:
                    chk("F")
                br = proj(w_next(), hT, gw, "hT")
                bk = proj(w_next(), hT, gw, "hT")
                bv_ = proj(w_next(), hT, gw, "hT")
                for p in parts:
                    wkv_pair(l, pr, p, br, bk, bv_)

            chk("G")
            for n in range(16):
                bo = proj(w_next(), cat, gw, "cat")
                S.op("dve", lambda e, n=n, bo=bo: e.tensor_tensor(out=xT[:, n, 0:gw], in0=PS(bo)[:, 0:gw],
                                                                  in1=xT[:, n, 0:gw], op=ALU.add),
                     r=[("ps", bo), "xT"], w=["xT"])

            chk("H")
            rmsnorm_to(lambda k: hT[:, k, 0:gw], gw, 0, "hT", LV + VO["nf"])
            S.barrier()
            for f in range(FC if dbg != "outproj" else 0):
                bg = proj(w_next(), hT, gw, "hT")
                bu = proj(w_next(), hT, gw, "hT")
                ft = ftmp[f % 2]
                S.op("act", lambda e, ft=ft, bg=bg: e.activation(out=ft[:, 0:gw], in_=PS(bg)[:, 0:gw], func=AF.Silu),
                     r=[("ps", bg)], w=[("ftmp", f % 2)])
                S.op("dve", lambda e, ft=ft, bu=bu, f=f: e.tensor_tensor(out=act[:, f, 0:gw], in0=PS(bu)[:, 0:gw],
                                                                         in1=ft[:, 0:gw], op=ALU.mult),
                     r=[("ps", bu), ("ftmp", f % 2)], w=[("act", f)])
            for n in range(16 if dbg != "outproj" else 0):
                bd = bank("big")
                for j, nk in enumerate((16, 16, 12)):
                    sl = w_next()
                    wv = wring[:, sl, :].rearrange("p (k n) -> p k n", n=128)
                    fns = []
                    for k in range(nk):
                        f = j * 16 + k
                        fns.append(lambda e, wv=wv, k=k, f=f: e.matmul(PS(bd)[:, 0:gw], lhsT=wv[:, k, :],
                                                                        rhs=act[:, f, 0:gw], start=(f == 0),
                                                                        stop=(f == FC - 1)))
                    S.pe_group(fns, r=[("w", sl)] + [("act", f) for f in range(j * 16, j * 16 + nk)],
                               w=[("ps", bd)])
                S.op("dve", lambda e, n=n, bd=bd: e.tensor_tensor(out=xT[:, n, 0:gw], in0=PS(bd)[:, 0:gw],
                                                                  in1=xT[:, n, 0:gw], op=ALU.add),
                     r=[("ps", bd), "xT"], w=["xT"])
            S.barrier()

            chk("I")
            for p in parts:
                if not p["last"]:
                    continue
                sk = (p["seq"], l)
                sd = st[sk]
                oi = {"P": 0, "S0": 1, "S1": 2}[p["seq"]]
                b = bank("small")
                S.pe_group([lambda e, c=c: e.transpose(out=PS(b)[0:30, c * 128:(c + 1) * 128], in_=sd["u"][:, c, :],
                                                        identity=ident_f) for c in range(4)],
                           r=[("st_u",) + sk], w=[("ps", b)])
                S.op("act", lambda e, b=b: e.activation(out=stage2[0:30, 0:512], in_=PS(b)[0:30, 0:512], func=AF.Copy),
                     r=[("ps", b)], w=["stage"])
                S.dma("sp", nconv[l, oi, :, :], stage2[0:30, 0:512], r=["stage"], w=[("o_conv", l, oi)])
                b = bank("small")
                S.pe_group([lambda e, c=c: e.transpose(out=PS(b)[0:15, c * 128:(c + 1) * 128], in_=sd["p"][:, c, :],
                                                        identity=ident_f) for c in range(4)],
                           r=[("st_p",) + sk], w=[("ps", b)])
                S.op("act", lambda e, b=b: e.activation(out=stage2[0:15, 512:1024], in_=PS(b)[0:15, 0:512], func=AF.Copy),
                     r=[("ps", b)], w=["stage"])
                S.dma("sp", npool[l, oi, :, :], stage2[0:15, 512:1024], r=["stage"], w=[("o_pool", l, oi)])
                b = bank("small")
                S.pe_group([lambda e: e.transpose(out=PS(b)[0:NQ, 0:128], in_=sd["q"], identity=ident_f)],
                           r=[("st_q",) + sk], w=[("ps", b)])
                S.op("act", lambda e, b=b: e.activation(out=stage3[0:NQ, 512:640], in_=PS(b)[0:NQ, 0:128], func=AF.Copy),
                     r=[("ps", b)], w=["stage"])
                S.dma("sp", nshift[l, oi, :, :], stage3[0:NQ, 512:640], r=["stage"], w=[("o_shift", l, oi)])
                for half in range(2):
                    b = bank("small")
                    S.pe_group([lambda e, pr=pr: e.transpose(out=PS(b)[0:64, (pr % 4) * 128:(pr % 4 + 1) * 128],
                                                              in_=sd["S"][:, pr, :], identity=ident_f)
                                for pr in range(half * 4, half * 4 + 4)], r=[("st_S",) + sk], w=[("ps", b)])
                    S.op("act", lambda e, b=b: e.activation(out=stage3[0:64, 0:512], in_=PS(b)[0:64, 0:512],
                                                            func=AF.Copy), r=[("ps", b)], w=["stage"])
                    S.dma("sp", nwkv[l, oi, half * 8:half * 8 + 8].rearrange("h i j -> i h j"),
                          stage3[0:64, 0:512].rearrange("p (h j) -> p h j", j=64), r=["stage"],
                          w=[("o_wkv", l, oi, half)])

        S.dead = False
        if dbg is None:
            rmsnorm_to(lambda k: xT[:, k, 0:gw], gw, 0, "xT", DEPTH * VL)
        for tb in range(ntb):
            for k4 in range(4):
                b = bank("small")
                S.pe_group([lambda e, k=k: e.transpose(out=PS(b)[:, (k % 4) * 128:(k % 4 + 1) * 128],
                                                        in_=xT[:, k, tb * 128:(tb + 1) * 128], identity=ident_f)
                            for k in range(k4 * 4, k4 * 4 + 4)], r=["xT"], w=[("ps", b)])
                S.op("act" if k4 % 2 else "dve",
                     (lambda e, k4=k4, b=b: e.activation(out=stage[:, k4 * 512:(k4 + 1) * 512], in_=PS(b), func=AF.Copy))
                     if k4 % 2 else
                     (lambda e, k4=k4, b=b: e.tensor_copy(out=stage[:, k4 * 512:(k4 + 1) * 512], in_=PS(b))),
                     r=[("ps", b)], w=["stage"])
            S.dma("sp", G["dst"][tb * 128:(tb + 1) * 128, :], stage, r=["stage"], w=[("o_y", gi_, tb)])

    S.finish("sp")
    print("instructions emitted:", S.ninst)
    nc._arena_reg = A.reg
    return nc


def _colize(v):
    v = np.asarray(v, np.float32).reshape(-1)
    n = (v.size + 127) // 128
    out = np.zeros((n * 128,), np.float32)
    out[:v.size] = v
    return out.reshape(n, 128).T


def _prep_shared(inp):
    wblk = np.zeros((DEPTH, NBLK, 128, SLOT), np.float32)
    vecs = np.zeros((128, NVEC), np.float32)
    for l in range(DEPTH):
        wblk[l, 0, 0:64, 0:1024] = inp["decay_up"][l]
        wblk[l, 0, 64:128, 0:1024] = inp["iclr_up"][l]
        wblk[l, 0, 0:64, 1024:2048] = inp["gate_up"][l]
        wblk[l, 1, :, 0:512] = np.asarray(inp["pool_w"][l]).transpose(1, 0, 2).reshape(128, 512)
        win = np.zeros((D, 38 * 128), np.float32)
        win[:, :4800] = inp["w_in"][l]
        wblk[l, 2:40] = win.reshape(16, 128, 38, 128).transpose(2, 1, 0, 3).reshape(38, 128, SLOT)
        wblk[l, 40:56] = np.asarray(inp["w_out"][l]).reshape(16, 128, 16, 128).transpose(2, 1, 0, 3).reshape(16, 128, SLOT)
        g = np.asarray(inp["ffn_gate"][l]).reshape(16, 128, FC, 128).transpose(2, 1, 0, 3).reshape(FC, 128, SLOT)
        u = np.asarray(inp["ffn_up"][l]).reshape(16, 128, FC, 128).transpose(2, 1, 0, 3).reshape(FC, 128, SLOT)
        wblk[l, 56:144:2] = g
        wblk[l, 57:144:2] = u
        dn = np.zeros((48, 128, 16, 128), np.float32)
        dn[:FC] = np.asarray(inp["ffn_down"][l]).reshape(FC, 128, 16, 128)
        dn = dn.reshape(3, 16, 128, 16, 128).transpose(3, 0, 2, 1, 4).reshape(16, 3, 128, SLOT)
        wblk[l, 144:192] = dn.reshape(48, 128, SLOT)
        o = l * VL
        vecs[:, o + VO["nm"]:o + VO["nm"] + 16] = _colize(inp["norm_mix"][l])
        vecs[:, o + VO["nf"]:o + VO["nf"] + 16] = _colize(inp["norm_ffn"][l])
        vecs[:, o + VO["cb"]:o + VO["cb"] + 4] = _colize(inp["conv_b"][l])
        cw = np.asarray(inp["conv_w"][l])
        vecs[:, o + VO["cw"]:o + VO["cw"] + 124] = cw.reshape(31, 4, 128).transpose(2, 1, 0).reshape(128, 124)
        vecs[:, o + VO["lg"]:o + VO["lg"] + 4] = _colize(inp["conv_ln_g"][l])
        vecs[:, o + VO["lb"]:o + VO["lb"] + 4] = _colize(inp["conv_ln_b"][l])
        vecs[:, o + VO["psc"]:o + VO["psc"] + 4] = _colize(inp["pool_scale"][l])
        vecs[:, o + VO["mu"]:o + VO["mu"] + NQ] = _colize(inp["shift_mu"][l])
        vecs[:, o + VO["w0"]:o + VO["w0"] + 8] = _colize(inp["decay_w0"][l])
        vecs[:, o + VO["a0"]:o + VO["a0"] + 8] = _colize(inp["iclr_a0"][l])
        vecs[:, o + VO["kk"]:o + VO["kk"] + 8] = _colize(inp["k_k"][l])
        vecs[:, o + VO["ka"]:o + VO["ka"] + 8] = _colize(inp["k_a"][l])
        vecs[:, o + VO["rk"]:o + VO["rk"] + 8] = _colize(inp["r_k"][l])
        vecs[:, o + VO["gg"]:o + VO["gg"] + 8] = _colize(inp["gn_g"][l])
        vecs[:, o + VO["gb"]:o + VO["gb"] + 8] = _colize(inp["gn_b"][l])
    vecs[:, DEPTH * VL:DEPTH * VL + 16] = _colize(inp["norm_final"])
    return wblk, vecs


def _core_inputs(inp, c, shared, nseq_tok=SEQ):
    wblk, vecs = shared
    sh = np.zeros((DEPTH, 2, NQ * 128), np.float32)
    sh[:, :, :3264] = np.asarray(inp["state_shift"])[:, 2 * c:2 * c + 2, 0, :]
    return {
        "xp": np.ascontiguousarray(np.asarray(inp["x_prompt"])[c % 4, :nseq_tok]),
        "xs": np.ascontiguousarray(np.asarray(inp["x_sample"])[2 * c:2 * c + 2].reshape(2 * SLEN, D)),
        "cconv": np.ascontiguousarray(np.asarray(inp["cache_conv"])[:, 2 * c:2 * c + 2]),
        "cpool": np.ascontiguousarray(np.asarray(inp["cache_pool"])[:, 2 * c:2 * c + 2]),
        "cshift": sh.reshape(DEPTH, 2, NQ, 128),
        "cwkv": np.ascontiguousarray(np.asarray(inp["state_wkv"])[:, 2 * c:2 * c + 2]),
        "wblk": wblk,
        "vecs": vecs,
    }


_NC_CACHE = {}


def kernel(**inp):
    inp = {k: np.asarray(v) for k, v in inp.items()}
    shared = _prep_shared(inp)
    if "nc" not in _NC_CACHE:
        _NC_CACHE["nc"] = build_program()
    nc = _NC_CACHE["nc"]
    in_maps = [_core_inputs(inp, c, shared) for c in range(8)]
    res = run_bass_kernel_spmd(nc, in_maps, core_ids=list(range(8)))
    R = res.results
    y_prompt = np.stack([R[c]["yp"] for c in range(4)]).astype(np.float32)
    y_sample = np.concatenate([R[c]["ys"].reshape(2, SLEN, D) for c in range(8)]).astype(np.float32)

    def gather(name, tailshape, fix=None):
        pr = np.stack([R[c][name][:, 0] for c in range(4)], axis=1)
        sm = np.concatenate([R[c][name][:, 1:3] for c in range(8)], axis=1)
        if fix is not None:
            pr, sm = fix(pr), fix(sm)
        return pr.astype(np.float32), sm.astype(np.float32)

    p_conv, s_conv = gather("nconv", None)
    p_pool, s_pool = gather("npool", None)
    fixs = lambda a: a.reshape(a.shape[0], a.shape[1], 1, NQ * 128)[..., :3264]
    p_shift, s_shift = gather("nshift", None, fixs)
    p_wkv, s_wkv = gather("nwkv", None)
    return (y_prompt, y_sample, p_conv, p_pool, p_shift, p_wkv, s_conv, s_pool, s_shift, s_wkv)
```

```python
import numpy as np
import concourse.bass as bass
import concourse.mybir as mybir
from concourse.bass_utils import run_bass_kernel_spmd

F32 = mybir.dt.float32
BF16 = mybir.dt.bfloat16
AF = mybir.ActivationFunctionType
ALU = mybir.AluOpType

D = 2048
KC = 16
DFF = 5632
FC = 44
HEADS = 16
PAIRS = 8
NQ = 26
DEPTH = 4
SEQ = 2048
SLEN = 64
RMS_EPS = 1e-6
LN_EPS = 1e-5
GN_EPS = 64e-5
LW_SCALE = -float(np.exp(-0.5))
POOL_WINDOWS = (2, 4, 8, 16)

NBLK = 2 + 38 + 16 + 88 + 48
SLOT = 2048
NSLOT = 5

VO = {}
_o = 0
for _n, _w in (("nm", 16), ("nf", 16), ("cb", 4), ("cw", 124), ("lg", 4), ("lb", 4), ("psc", 4),
               ("mu", NQ), ("w0", 8), ("a0", 8), ("kk", 8), ("ka", 8), ("rk", 8), ("gg", 8), ("gb", 8)):
    VO[_n] = _o
    _o += _w
VL = _o
NVEC = DEPTH * VL + 16


class _Dummy:
    def then_inc(self, *a, **k):
        return self


class _Rec:
    def __init__(self):
        self.calls = []

    def __getattr__(self, name):
        def f(*args, **kw):
            self.calls.append((name, args, kw))
            return _Dummy()
        return f


def _free_size(ap):
    try:
        n = 1
        for s in tuple(ap.shape)[1:]:
            n *= int(s)
        return n
    except Exception:
        return 256


class Sched:
    LAT_X = 0.45
    LAT_S = 0.25

    def __init__(self, nc, reorder=True):
        self.nc = nc
        self.reorder = reorder
        self.eng = {"pe": nc.tensor, "act": nc.scalar, "dve": nc.vector, "pool": nc.gpsimd, "sp": nc.sync}
        self.semh = {}
        self.cnt = {}
        for e in self.eng:
            self.semh[e] = nc.alloc_semaphore("sem_" + e)
            self.cnt[e] = 0
        self.waited = {e: {} for e in self.eng}
        self.lastw = {}
        self.readers = {}
        self.dma_sems = []
        self.dma_rr = 0
        self.ninst = 0
        self.dead = False
        self.pending = []

    def new_dma_sem(self, name):
        self.semh[name] = self.nc.alloc_semaphore("sem_" + name)
        self.cnt[name] = 0
        return name

    def op(self, e, fn, r=(), w=()):
        if self.dead:
            return
        rec = _Rec()
        fn(rec)
        self.pending.append(dict(kind="op", eng=e, calls=rec.calls, r=tuple(r), w=tuple(w), sync=False))

    def pe_group(self, fns, r=(), w=(), pe_sync=False):
        if self.dead:
            return
        rec = _Rec()
        for fn in fns:
            fn(rec)
        self.pending.append(dict(kind="op", eng="pe", calls=rec.calls, r=tuple(r), w=tuple(w), sync=pe_sync))

    def dma(self, q, out, in_, r=(), w=(), sem=None):
        if self.dead:
            return
        self.pending.append(dict(kind="dma", eng=q, out=out, in_=in_, r=tuple(r), w=tuple(w), sem=sem))

    def barrier(self, engines=("pe", "act", "dve", "pool")):
        if self.dead:
            return
        self.flush()
        for e in engines:
            need = {}
            for o in engines:
                if o != e and self.cnt[o] > 0:
                    need[o] = self.cnt[o]
            self._wait(e, need)

    def finish(self, e="sp"):
        self.flush()
        need = {}
        for s, c in self.cnt.items():
            if c > 0 and s != e:
                need[s] = c
        self._wait(e, need)

    def _cost(self, o):
        if o["kind"] == "dma":
            return 0.7
        e = o["eng"]
        if e == "pe":
            t = 0.0
            for (name, args, kw) in o["calls"]:
                if name == "matmul":
                    n = _free_size(kw.get("rhs"))
                    f = 4.0 if str(getattr(kw.get("rhs"), "dtype", "")) .endswith("float32") else 1.0
                    t += f * max(n, 64) / 2400.0 + 0.07
                else:
                    t += 0.06
            return t
        f = _free_size(o["calls"][0][2].get("out")) if o["calls"] else 64
        if e == "act":
            return 0.2 + f / 1200.0
        if e == "dve":
            return 0.08 + f / 960.0
        return 0.3 + f / 500.0

    def flush(self):
        ops = self.pending
        self.pending = []
        n = len(ops)
        if n == 0:
            return
        if not self.reorder or n < 3:
            for o in ops:
                self._emit(o)
            return
        lastw, readers = {}, {}
        preds = [set() for _ in range(n)]
        for i, o in enumerate(ops):
            for k in o["r"]:
                j = lastw.get(k)
                if j is not None:
                    preds[i].add(j)
            for k in o["w"]:
                j = lastw.get(k)
                if j is not None:
                    preds[i].add(j)
                for j in readers.get(k, ()):
                    preds[i].add(j)
            for k in o["r"]:
                readers.setdefault(k, []).append(i)
            for k in o["w"]:
                lastw[k] = i
                readers[k] = []
            preds[i].discard(i)
        succs = [[] for _ in range(n)]
        for i in range(n):
            for j in preds[i]:
                succs[j].append(i)
        cost = [self._cost(o) for o in ops]
        lat_out = [2.5 if o["kind"] == "dma" else 0.0 for o in ops]
        prio = [0.0] * n
        for i in range(n - 1, -1, -1):
            m = 0.0
            for s in succs[i]:
                m = max(m, prio[s] + self.LAT_X)
            prio[i] = cost[i] + lat_out[i] + m
        npred = [len(p) for p in preds]
        ready = [i for i in range(n) if npred[i] == 0]
        fin = [0.0] * n
        efree = {}
        order = []
        engs = [o["eng"] for o in ops]
        while ready:
            best, best_key = None, None
            for i in ready:
                e = engs[i]
                t = efree.get(e, 0.0)
                for j in preds[i]:
                    tj = fin[j] + lat_out[j] + (self.LAT_S if engs[j] == e else self.LAT_X)
                    if tj > t:
                        t = tj
                key = (t, -prio[i], i)
                if best_key is None or key < best_key:
                    best, best_key = i, key
            i = best
            ready.remove(i)
            t = best_key[0]
            fin[i] = t + cost[i]
            efree[engs[i]] = fin[i]
            order.append(i)
            for s in succs[i]:
                npred[s] -= 1
                if npred[s] == 0:
                    ready.append(s)
        assert len(order) == n
        if getattr(self, "debug_sched", False) and n > 500:
            mk = max(fin)
            busy = {}
            for i in range(n):
                busy[engs[i]] = busy.get(engs[i], 0.0) + cost[i]
            print("WINDOW n=%d makespan=%.1f us busy=%s" % (n, mk, {k: round(v, 1) for k, v in busy.items()}))
            i = max(range(n), key=lambda k: fin[k])
            path = []
            while True:
                path.append(i)
                best, bt = None, -1.0
                for j in preds[i]:
                    tj = fin[j] + lat_out[j] + (self.LAT_S if engs[j] == engs[i] else self.LAT_X)
                    if tj > bt:
                        best, bt = j, tj
                start = fin[i] - cost[i]
                if best is None or bt < start - 1e-6:
                    prev = [k for k in order[:order.index(i)] if engs[k] == engs[i]]
                    if not prev:
                        break
                    i = prev[-1]
                    path.append(-1)
                else:
                    i = best
                if len(path) > 400:
                    break
            txt = []
            for k in reversed(path):
                if k == -1:
                    txt.append("|eng|")
                else:
                    txt.append("%s:%s@%.1f" % (engs[k], str(ops[k]["w"][:1]), fin[k]))
            print("CRIT:", " ".join(txt[:400]))
        for i in order:
            self._emit(ops[i])

    def _deps(self, r, w):
        need = {}
        for k in r:
            t = self.lastw.get(k)
            if t is not None:
                need[t[0]] = max(need.get(t[0], 0), t[1])
        for k in w:
            t = self.lastw.get(k)
            if t is not None:
                need[t[0]] = max(need.get(t[0], 0), t[1])
            for t in self.readers.get(k, ()):
                need[t[0]] = max(need.get(t[0], 0), t[1])
        return need

    def _wait(self, e, need, skip_self=False):
        wd = self.waited[e]
        for s, v in need.items():
            if skip_self and s == e:
                continue
            if wd.get(s, 0) < v:
                self.eng[e].wait_ge(self.semh[s], v)
                wd[s] = v
                self.ninst += 1

    def _commit(self, tok, r, w):
        for k in r:
            lst = self.readers.setdefault(k, [])
            lst[:] = [t for t in lst if t[0] != tok[0]]
            lst.append(tok)
        for k in w:
            self.lastw[k] = tok
            self.readers[k] = []

    def _emit(self, o):
        if o["kind"] == "dma":
            return self._emit_dma(o)
        e = o["eng"]
        need = self._deps(o["r"], o["w"])
        self._wait(e, need, skip_self=(e == "pe" and not o["sync"]))
        inst = None
        for (name, args, kw) in o["calls"]:
            inst = getattr(self.eng[e], name)(*args, **kw)
            self.ninst += 1
        self.cnt[e] += 1
        inst.then_inc(self.semh[e], 1)
        self._commit((e, self.cnt[e]), o["r"], o["w"])

    def _emit_dma(self, o):
        q, sem = o["eng"], o["sem"]
        if sem is None:
            if len(self.dma_sems) < 24:
                sem = self.new_dma_sem("d%d" % len(self.dma_sems))
                self.dma_sems.append(sem)
            else:
                sem = self.dma_sems[self.dma_rr % len(self.dma_sems)]
                self.dma_rr += 1
        need = self._deps(o["r"], o["w"])
        if self.cnt[sem] > 0:
            need[sem] = max(need.get(sem, 0), self.cnt[sem])
        self._wait(q, need)
        self.eng[q].dma_start(out=o["out"], in_=o["in_"]).then_inc(self.semh[sem], 16)
        self.cnt[sem] += 16
        self.ninst += 1
        self._commit((sem, self.cnt[sem]), o["r"], o["w"])


class Arena:
    def __init__(self, nc, nbytes, name="arena"):
        assert nbytes % 4 == 0
        self.t = nc.alloc_sbuf_tensor(name, [128, nbytes // 4], F32)
        self.off = 0
        self.cap = nbytes

    def alloc(self, shape, dt, at=None, name=None):
        esz = 4 if dt == F32 else 2
        n = 1
        for s in shape[1:]:
            n *= s
        nb = (n * esz + 31) // 32 * 32
        if at is None:
            at = self.off
            self.off += nb
            assert self.off <= self.cap, ("arena overflow", self.off, self.cap)
        if not hasattr(self, "reg"):
            self.reg = {}
        self.reg[name if name is not None else "anon%d" % len(self.reg)] = (at, list(shape), "f32" if dt == F32 else "bf16")
        v = self.t[:, at // 4:(at + nb) // 4]
        if dt != F32:
            v = v.bitcast(dt)
        v = v[:, 0:n]
        if len(shape) == 3:
            v = v.rearrange("p (a b) -> p a b", b=shape[2])
        elif len(shape) == 4:
            v = v.rearrange("p (a b c) -> p a b c", b=shape[2], c=shape[3])
        return v


def build_program(layers=DEPTH, npg=4, dbg=None, with_s=True, reorder=True):
    nc = bass.Bass("TRN2", target_bir_lowering=False)
    S = Sched(nc, reorder=reorder)
    nseq_tok = 512 * npg

    xp = nc.dram_tensor("xp", [nseq_tok, D], F32, kind="ExternalInput").ap()
    xs = nc.dram_tensor("xs", [2 * SLEN, D], F32, kind="ExternalInput").ap()
    cconv = nc.dram_tensor("cconv", [DEPTH, 2, 30, 512], F32, kind="ExternalInput").ap()
    cpool = nc.dram_tensor("cpool", [DEPTH, 2, 15, 512], F32, kind="ExternalInput").ap()
    cshift = nc.dram_tensor("cshift", [DEPTH, 2, NQ, 128], F32, kind="ExternalInput").ap()
    cwkv = nc.dram_tensor("cwkv", [DEPTH, 2, HEADS, 64, 64], F32, kind="ExternalInput").ap()
    wblk = nc.dram_tensor("wblk", [layers, NBLK, 128, SLOT], F32, kind="ExternalInput").ap()
    vecs_d = nc.dram_tensor("vecs", [128, NVEC], F32, kind="ExternalInput").ap()
    yp = nc.dram_tensor("yp", [nseq_tok, D], F32, kind="ExternalOutput").ap()
    ys = nc.dram_tensor("ys", [2 * SLEN, D], F32, kind="ExternalOutput").ap()
    nconv = nc.dram_tensor("nconv", [DEPTH, 3, 30, 512], F32, kind="ExternalOutput").ap()
    npool = nc.dram_tensor("npool", [DEPTH, 3, 15, 512], F32, kind="ExternalOutput").ap()
    nshift = nc.dram_tensor("nshift", [DEPTH, 3, NQ, 128], F32, kind="ExternalOutput").ap()
    nwkv = nc.dram_tensor("nwkv", [DEPTH, 3, HEADS, 64, 64], F32, kind="ExternalOutput").ap()

    A = Arena(nc, 212736)
    vecs = A.alloc([128, NVEC], F32)
    omu = A.alloc([128, DEPTH, NQ], F32)
    ident_f = A.alloc([128, 128], F32)
    ident_b = A.alloc([128, 128], BF16)
    ones_b = A.alloc([128, 128], BF16)
    bones_b = A.alloc([128, 128], BF16)
    bones_f = A.alloc([128, 128], F32)
    onesD_f = A.alloc([128, 128], F32)
    m_su = A.alloc([128, 128], F32)
    m_ui = A.alloc([128, 128], F32)
    m_sl = A.alloc([128, 128], F32)
    cmask = {64: A.alloc([128, 64], BF16), 128: A.alloc([128, 512], BF16)}
    invc_first = A.alloc([128, 4, 16], F32)
    st = {}
    for l in range(DEPTH):
        st[("P", l)] = dict(u=A.alloc([128, 4, 30], F32), p=A.alloc([128, 4, 15], F32),
                            q=A.alloc([128, NQ], F32), S=A.alloc([128, PAIRS, 64], F32))
    for sq in ("S0", "S1"):
        d_ = dict(u=A.alloc([128, 4, 30], F32), p=A.alloc([128, 4, 15], F32),
                  q=A.alloc([128, NQ], F32), S=A.alloc([128, PAIRS, 64], F32))
        for l in range(DEPTH):
            st[(sq, l)] = d_
    xT = A.alloc([128, KC, 512], F32, name='xT')
    hT = A.alloc([128, KC, 512], BF16, name='hT')
    cat = A.alloc([128, KC, 512], BF16, name='cat')
    wring = A.alloc([128, NSLOT, SLOT], BF16)
    wsmall = A.alloc([128, SLOT], BF16)
    wpool = A.alloc([128, 512], BF16)
    rstd = A.alloc([128, 512], F32)
    sqb2 = A.alloc([128, 2, 512], BF16)
    epi1 = A.alloc([128, 512], F32)
    epi2 = A.alloc([128, 512], F32)
    base_off = A.off

    def mixer_bufs(Wm):
        b = {}
        o0 = A.off
        b["ubuf"] = A.alloc([128, 4, 30 + Wm], F32, name="mb%d_ubuf" % Wm)
        b["ubf"] = A.alloc([128, 4, 30 + Wm], BF16)
        b["hconv"] = A.alloc([128, 4, Wm], F32)
        o1 = A.off
        b["pbuf"] = A.alloc([128, 4, 15 + Wm], F32, at=o0)
        b["dpool"] = A.alloc([128, 4, Wm], BF16, at=o0 + (4 * (15 + Wm) * 4 + 31) // 32 * 32)
        assert o0 + (4 * (15 + Wm) * 4 + 31) // 32 * 32 + 4 * Wm * 2 <= o1
        for n in ("t0", "t1", "t2", "t3", "t4", "t5", "t6", "t7", "t8", "t9", "t10", "t11"):
            b["off_" + n] = A.off
            b[n] = A.alloc([128, Wm + 16], F32, name="mb%d_%s" % (Wm, n))
        for n in ("b0", "b1", "b2", "b3", "b4", "b5"):
            b[n] = A.alloc([128, Wm], BF16, name="mb%d_%s" % (Wm, n))
        b["AR"] = A.alloc([128, 2 * Wm], BF16, name="mb%d_AR" % Wm)
        b["lora"] = A.alloc([128, 2, Wm], BF16, name="mb%d_lora" % Wm)
        b["gl"] = b["t1"]
        return b
    mb = [mixer_bufs(512), mixer_bufs(64)]
    diag = A.alloc([128, 31, 128], BF16, at=mb[0]['off_t8'])
    tm = {}
    for h in range(2):
        tm[("S1m", h)] = A.alloc([128, 4, 2, 128], BF16, name="tm_S1m%d" % h)
        tm[("S2m", h)] = A.alloc([128, 4, 2, 128], BF16, name="tm_S2m%d" % h)
        tm[("L", h)] = A.alloc([128, 4, 128], BF16, name="tm_L%d" % h)
        tm[("X", h)] = A.alloc([128, 4, 128], BF16, name="tm_X%d" % h)
    tm["KBV"] = A.alloc([128, 4, 3, 128], BF16, name="tm_KBV")
    tm["Psb"] = A.alloc([128, 2, 64], BF16, name="tm_Psb")
    tm["Usb"] = A.alloc([128, 2, 64], BF16, name="tm_Usb")
    tm["STb"] = A.alloc([128, 64], BF16, name="tm_STb")
    tm["gC"] = A.alloc([128, 8], F32, name="tm_gC")
    identb4 = A.alloc([128, 4, 128], BF16)
    mc2 = A.alloc([128, 2, 2, 128], BF16)
    m_sl4 = A.alloc([128, 4, 128], BF16)
    mc = {64: A.alloc([128, 2, 64], F32), 128: A.alloc([128, 2, 128], F32)}
    stage = mb[0]["hconv"].rearrange("p a b -> p (a b)")
    stage2 = stage[:, 0:1024]
    stage3 = stage[:, 1024:1664]
    mix_end = A.off
    act = A.alloc([128, FC, 512], BF16, at=base_off)
    assert base_off + FC * 512 * 2 <= A.cap
    A.off = max(mix_end, base_off + FC * 512 * 2 + 4096)
    ftmp = [A.alloc([128, 512], F32, at=base_off + FC * 512 * 2), A.alloc([128, 512], F32, at=base_off + FC * 512 * 2 + 2048)]
    print("SBUF used", A.off, "of", A.cap)

    psb = [nc.alloc_psum_tensor("ps%d" % i, [128, 512], F32) for i in range(8)]
    bank_rr = {"big": 0, "small": 0}

    def bank(pool):
        if pool == "big":
            i = bank_rr["big"] % 3
            bank_rr["big"] += 1
            return i
        if pool == "p1":
            i = (3, 4, 5, 6, 7)[bank_rr.setdefault("p1", 0) % 5]
            bank_rr["p1"] += 1
            return i
        if pool == "y":
            return 3
        if pool == "state":
            return 7
        i = 4 + bank_rr["small"] % 3
        bank_rr["small"] += 1
        return i

    psap = [t[:, :] for t in psb]

    def PS(i):
        return psap[i]

    wsem = [S.new_dma_sem("w%d" % i) for i in range(NSLOT)]
    wsem_small = S.new_dma_sem("wsm")
    wq = {"next": 0, "issued": 0, "order": []}

    def w_issue_upto(n):
        while wq["issued"] < min(n, len(wq["order"])):
            i = wq["issued"]
            (l, b, ncols) = wq["order"][i]
            slot = i % NSLOT
            S.dma("pool", wring[:, slot, 0:ncols], wblk[l, b, :, 0:ncols], w=[("w", slot)], sem=wsem[slot])
            wq["issued"] += 1

    def w_next():
        i = wq["next"]
        wq["next"] += 1
        w_issue_upto(i + NSLOT - 1)
        return i % NSLOT

    def blk_in(cc):
        return 2 + cc
    def blk_out(n):
        return 2 + 38 + n
    def blk_gate(f):
        return 2 + 38 + 16 + 2 * f
    def blk_up(f):
        return 2 + 38 + 16 + 2 * f + 1
    def blk_down(n, j):
        return 2 + 38 + 16 + 88 + 3 * n + j
    IN_ORDER = [4, 0, 5, 1, 6, 2, 7, 3, 8, 9, 10, 11, 36, 37]
    for p_ in range(PAIRS):
        IN_ORDER += [12 + p_, 20 + p_, 28 + p_]

    def layer_order(l):
        o = [(l, blk_in(cc), SLOT) for cc in IN_ORDER]
        o += [(l, blk_out(n), SLOT) for n in range(16)]
        for f in range(FC):
            o += [(l, blk_gate(f), SLOT), (l, blk_up(f), SLOT)]
        for n in range(16):
            o += [(l, blk_down(n, 0), SLOT), (l, blk_down(n, 1), SLOT), (l, blk_down(n, 2), 12 * 128)]
        return o

    def pool_op(fn, r=(), w=()):
        return S.op("pool", fn, r, w)

    S.dma("sp", vecs, vecs_d[:, :], w=["vecs"])
    pool_op(lambda e: e.memset(ident_f, 1.0), w=["c_if"])
    pool_op(lambda e: e.affine_select(out=ident_f, in_=ident_f, pattern=[[-1, 128]], compare_op=ALU.is_equal,
                                      fill=0.0, base=0, channel_multiplier=1), r=["c_if"], w=["c_if"])
    S.op("dve", lambda e: e.tensor_copy(out=ident_b, in_=ident_f), r=["c_if"], w=["c_ib"])
    for i4 in range(4):
        S.op("dve", lambda e, i4=i4: e.tensor_copy(out=identb4[:, i4, :], in_=ident_f), r=["c_if"], w=["c_ib4"])
    S.op("dve", lambda e: e.memset(ones_b, 1.0), w=["c_ones"])
    S.op("dve", lambda e: e.memset(onesD_f, 1.0 / 512.0), w=["c_onesD"])
    S.op("dve", lambda e: e.memset(bones_b, 0.0), w=["c_bones"])
    S.op("dve", lambda e: e.memset(bones_b[0:64, 0:64], 1.0), w=["c_bones"])
    S.op("dve", lambda e: e.memset(bones_b[64:128, 64:128], 1.0), w=["c_bones"])
    S.op("dve", lambda e: e.memset(bones_f, 0.0), w=["c_bonesf"])
    S.op("dve", lambda e: e.memset(bones_f[0:64, 0:64], 1.0 / 64.0), w=["c_bonesf"])
    S.op("dve", lambda e: e.memset(bones_f[64:128, 64:128], 1.0 / 64.0), w=["c_bonesf"])
    for (m, base, cm, step) in ((m_su, -1, -1, 1), (m_ui, 0, -1, 1), (m_sl, -1, 1, -1)):
        pool_op(lambda e, m=m: e.memset(m, 1.0), w=["c_masks"])
        pool_op(lambda e, m=m, base=base, cm=cm, step=step: e.affine_select(
            out=m, in_=m, pattern=[[step, 128]], compare_op=ALU.is_ge, fill=0.0, base=base,
            channel_multiplier=cm), r=["c_masks"], w=["c_masks"])
    for i2 in range(2):
        S.op("dve", lambda e, i2=i2: e.tensor_copy(out=mc2[:, i2, 0, :], in_=m_su), r=["c_masks"], w=["c_mc2"])
        S.op("dve", lambda e, i2=i2: e.tensor_copy(out=mc2[:, i2, 1, :], in_=m_ui), r=["c_masks"], w=["c_mc2"])
    for i4 in range(4):
        S.op("dve", lambda e, i4=i4: e.tensor_copy(out=m_sl4[:, i4, :], in_=m_sl), r=["c_masks"], w=["c_msl4"])
    for C in (64, 128):
        S.op("dve", lambda e, C=C: e.tensor_copy(out=mc[C][:, 0, :], in_=m_su[:, 0:C]), r=["c_masks"], w=["c_mc"])
        S.op("dve", lambda e, C=C: e.tensor_copy(out=mc[C][:, 1, :], in_=m_ui[:, 0:C]), r=["c_masks"], w=["c_mc"])
        S.op("dve", lambda e, C=C: e.memset(cmask[C], 1.0), w=["c_cmask"])
        S.op("dve", lambda e, C=C: e.memset(cmask[C].rearrange("p (a b) -> p a b", b=C)[:, :, 0:1], 0.0),
             r=["c_cmask"], w=["c_cmask"])
    pool_op(lambda e: e.iota(out=invc_first[:, 0, :], pattern=[[1, 16]], base=1, channel_multiplier=0,
                             allow_small_or_imprecise_dtypes=True), w=["c_invc"])
    for gi, wdw in enumerate(POOL_WINDOWS):
        if gi > 0:
            S.op("dve", lambda e, gi=gi: e.tensor_copy(out=invc_first[:, gi, :], in_=invc_first[:, 0, :]),
                 r=["c_invc"], w=["c_invc%d" % gi])
    for gi, wdw in enumerate(POOL_WINDOWS):
        S.op("dve", lambda e, gi=gi, wdw=wdw: e.tensor_scalar(out=invc_first[:, gi, :], in0=invc_first[:, gi, :],
                                                              scalar1=float(wdw), scalar2=None, op0=ALU.min),
             r=["c_invc", "c_invc%d" % gi], w=["c_invc%d" % gi] + (["c_invc"] if gi == 0 else []))
        S.op("dve", lambda e, gi=gi: e.reciprocal(out=invc_first[:, gi, :], in_=invc_first[:, gi, :]),
             r=["c_invc%d" % gi], w=["c_invc%d" % gi] + (["c_invc"] if gi == 0 else []))
    for l in range(DEPTH):
        o = l * VL + VO["mu"]
        S.op("dve", lambda e, l=l, o=o: e.tensor_scalar(out=omu[:, l, :], in0=vecs[:, o:o + NQ], scalar1=-1.0,
                                                        scalar2=1.0, op0=ALU.mult, op1=ALU.add),
             r=["vecs"], w=["omu"])
    CONST_KEYS = ["vecs", "omu", "c_if", "c_ib", "c_ones", "c_onesD", "c_bones", "c_bonesf", "c_masks",
                  "c_cmask", "c_onesrow", "c_invc", "c_invc1", "c_invc2", "c_invc3"]
    S.barrier()

    def V(l, name, c0=0, n=1):
        o = l * VL + VO[name] + c0
        return vecs[:, o:o + n]

    def rmsnorm_to(dst_fn, gw, gcol, key_out, l_vec_off, dst_is_bf=True):
        b = bank("small")
        fns = []
        for k in range(KC):
            S.op("act", lambda e, k=k: e.activation(out=sqb2[:, k % 2, 0:gw], in_=xT[:, k, 0:gw], func=AF.Square),
                 r=["xT"], w=[("sqb", k % 2)])
            S.pe_group([lambda e, k=k: e.matmul(PS(b)[:, 0:gw], lhsT=ones_b, rhs=sqb2[:, k % 2, 0:gw],
                                                 start=(k == 0), stop=(k == KC - 1))],
                       r=[("sqb", k % 2)], w=[("ps", b)])
        S.op("act", lambda e: e.activation(out=rstd[:, 0:gw], in_=PS(b)[:, 0:gw], func=AF.Ln,
                                           bias=RMS_EPS, scale=1.0 / D), r=[("ps", b)], w=["rstd"])
        S.op("act", lambda e: e.activation(out=rstd[:, 0:gw], in_=rstd[:, 0:gw], func=AF.Exp, scale=-0.5),
             r=["rstd"], w=["rstd"])
        for k in range(KC):
            S.op("dve", lambda e, k=k: e.scalar_tensor_tensor(
                out=dst_fn(k), in0=xT[:, k, 0:gw], scalar=vecs[:, l_vec_off + k:l_vec_off + k + 1],
                in1=rstd[:, 0:gw], op0=ALU.mult, op1=ALU.mult), r=["xT", "rstd"], w=[key_out])

    def proj(slot, src, gw, key_src, kchunks=KC, b=None):
        if b is None:
            b = bank("big")
        wv = wring[:, slot, :].rearrange("p (k n) -> p k n", n=128)
        fns = [lambda e, k=k: e.matmul(PS(b)[:, 0:gw], lhsT=wv[:, k, :], rhs=src[:, k, 0:gw],
                                       start=(k == 0), stop=(k == kchunks - 1)) for k in range(kchunks)]
        S.pe_group(fns, r=[("w", slot), key_src], w=[("ps", b)])
        return b

    def shifted_from_psum(l, bq, qc, p, dst, key_dst, scratch, key_scr):
        W, off, bi = p["W"], p["off"], p["bi"]
        sk = (p["seq"], l)
        sd = st[sk]
        S.op("act", lambda e: e.activation(out=scratch[:, 0:W], in_=PS(bq)[:, off:off + W], func=AF.Identity,
                                           scale=omu[:, l, qc:qc + 1]), r=[("ps", bq)], w=[key_scr])
        S.op("dve", lambda e: e.scalar_tensor_tensor(
            out=dst[:, 1:W], in0=PS(bq)[:, off:off + W - 1], scalar=V(l, "mu", qc), in1=scratch[:, 1:W],
            op0=ALU.mult, op1=ALU.add), r=[("ps", bq), key_scr], w=[key_dst])
        S.op("dve", lambda e: e.scalar_tensor_tensor(
            out=dst[:, 0:1], in0=sd["q"][:, qc:qc + 1], scalar=V(l, "mu", qc), in1=scratch[:, 0:1],
            op0=ALU.mult, op1=ALU.add), r=[("st_q",) + sk, key_scr], w=[key_dst])
        S.op("act", lambda e: e.activation(out=sd["q"][:, qc:qc + 1], in_=PS(bq)[:, off + W - 1:off + W],
                                           func=AF.Copy), r=[("ps", bq), key_dst], w=[("st_q",) + sk])

    def wkv_pair(l, pr, p, br, bk, bv_):
        B = mb[p["bi"]]
        W, off, bi, C = p["W"], p["off"], p["bi"], p["C"]
        nch = W // C
        sk = (p["seq"], l)
        sd = st[sk]
        T = lambda n: B[n][:, 0:W]
        K = lambda n: (n, bi)
        c3 = lambda ap: ap.rearrange("p (a c) -> p a c", c=C)
        shifted_from_psum(l, br, pr, p, B["t1"], K("t1"), B["t0"], K("t0"))
        shifted_from_psum(l, bk, 8 + pr, p, B["t2"], K("t2"), B["t0"], K("t0"))
        shifted_from_psum(l, bv_, 16 + pr, p, B["t3"], K("t3"), B["t0"], K("t0"))
        r_, k_, v_ = T("t1"), T("t2"), T("t3")
        bw = bank("small")
        S.pe_group([lambda e: e.matmul(PS(bw)[:, 0:W], lhsT=wsmall[0:64, pr * 128:(pr + 1) * 128],
                                       rhs=B["lora"][0:64, 0, 0:W], start=True, stop=True)],
                   r=["wsmall", K("lora")], w=[("ps", bw)])
        lw = T("t4")
        S.op("act", lambda e: e.activation(out=lw, in_=PS(bw)[:, 0:W], func=AF.Sigmoid,
                                           bias=V(l, "w0", pr), scale=1.0), r=[("ps", bw)], w=[K("t4")])
        ba = bank("small")
        S.pe_group([lambda e: e.matmul(PS(ba)[:, 0:W], lhsT=wsmall[64:128, pr * 128:(pr + 1) * 128],
                                       rhs=B["lora"][64:128, 0, 0:W], start=True, stop=True)],
                   r=["wsmall", K("lora")], w=[("ps", ba)])
        a_ = T("t5")
        S.op("act", lambda e: e.activation(out=a_, in_=PS(ba)[:, 0:W], func=AF.Sigmoid,
                                           bias=V(l, "a0", pr), scale=1.0), r=[("ps", ba)], w=[K("t5")])
        bgp = bank("small")
        S.pe_group([lambda e: e.matmul(PS(bgp)[:, 0:W], lhsT=wsmall[0:64, 1024 + pr * 128:1024 + (pr + 1) * 128],
                                       rhs=B["lora"][0:64, 1, 0:W], start=True, stop=True)],
                   r=["wsmall", K("lora")], w=[("ps", bgp)])
        g_ = T("t6")
        S.op("act", lambda e: e.activation(out=g_, in_=PS(bgp)[:, 0:W], func=AF.Copy), r=[("ps", bgp)], w=[K("t6")])
        kk = T("t7")
        S.op("dve", lambda e: e.tensor_scalar(out=kk, in0=k_, scalar1=V(l, "kk", pr), scalar2=None, op0=ALU.mult),
             r=[K("t2")], w=[K("t7")])
        S.op("act", lambda e: e.activation(out=T("b0"), in_=kk, func=AF.Square), r=[K("t7")], w=[K("b0")])
        bs = bank("small")
        S.pe_group([lambda e: e.matmul(PS(bs)[:, 0:W], lhsT=bones_b, rhs=T("b0"), start=True, stop=True)],
                   r=[K("b0")], w=[("ps", bs)])
        nrm = T("t8")
        S.op("dve", lambda e: e.tensor_scalar(out=nrm, in0=PS(bs)[:, 0:W], scalar1=1e-24, scalar2=None, op0=ALU.max),
             r=[("ps", bs)], w=[K("t8")])
        S.op("act", lambda e: e.activation(out=nrm, in_=nrm, func=AF.Ln), r=[K("t8")], w=[K("t8")])
        S.op("act", lambda e: e.activation(out=nrm, in_=nrm, func=AF.Exp, scale=-0.5), r=[K("t8")], w=[K("t8")])
        S.op("dve", lambda e: e.tensor_tensor(out=kk, in0=kk, in1=nrm, op=ALU.mult), r=[K("t7"), K("t8")], w=[K("t7")])
        bvec = T("t8")
        S.op("pool", lambda e: e.tensor_tensor(out=bvec, in0=kk, in1=a_, op=ALU.mult), r=[K("t7"), K("t5")], w=[K("t8")])
        kp = T("t9")
        S.op("dve", lambda e: e.tensor_scalar(out=kp, in0=a_, scalar1=-1.0, scalar2=V(l, "ka", pr), op0=ALU.add,
                                              op1=ALU.mult), r=[K("t5")], w=[K("t9")])
        S.op("dve", lambda e: e.scalar_tensor_tensor(out=kp, in0=kp, scalar=1.0, in1=k_, op0=ALU.add, op1=ALU.mult),
             r=[K("t9"), K("t2")], w=[K("t9")])
        S.op("dve", lambda e: e.scalar_tensor_tensor(out=T("b0"), in0=r_, scalar=V(l, "rk", pr), in1=kp, op0=ALU.mult,
                                                     op1=ALU.mult), r=[K("t1"), K("t9")], w=[K("b0")])
        bb = bank("small")
        S.pe_group([lambda e: e.matmul(PS(bb)[:, 0:W], lhsT=bones_b, rhs=T("b0"), start=True, stop=True)],
                   r=[K("b0")], w=[("ps", bb)])
        bonus = T("t10")
        S.op("dve", lambda e: e.tensor_tensor(out=bonus, in0=PS(bb)[:, 0:W], in1=v_, op=ALU.mult),
             r=[("ps", bb), K("t3")], w=[K("t10")])
        S.op("dve", lambda e: e.tensor_scalar(out=lw, in0=lw, scalar1=LW_SCALE, scalar2=None, op0=ALU.mult),
             r=[K("t4")], w=[K("t4")])
        cl = T("t11")
        S.op("dve", lambda e: e.tensor_tensor_scan(out=cl, data0=cmask[C][:, 0:W], data1=lw, initial=0.0,
                                                   op0=ALU.mult, op1=ALU.add), r=[K("t4")], w=[K("t11")])
        gC = tm["gC"]
        S.op("act", lambda e: e.activation(out=gC[:, 0:nch], in_=c3(cl)[:, :, C - 1], func=AF.Exp),
             r=[K("t11")], w=["gC"])
        e_pos = T("t0")
        S.op("act", lambda e: e.activation(out=e_pos, in_=cl, func=AF.Exp), r=[K("t11")], w=[K("t0")])
        AR = B["AR"][:, 0:2 * W].rearrange("p (a two c) -> p a two c", two=2, c=C)
        S.op("dve", lambda e: e.tensor_tensor(out=AR[:, :, 1, :], in0=c3(r_), in1=c3(e_pos), op=ALU.mult),
             r=[K("t1"), K("t0")], w=[K("AR")])
        S.op("pool", lambda e: e.tensor_tensor(out=lw, in0=cl, in1=lw, op=ALU.subtract), r=[K("t11"), K("t4")],
             w=[K("t4")])
        S.op("act", lambda e: e.activation(out=lw, in_=lw, func=AF.Exp), r=[K("t4")], w=[K("t4")])
        S.op("dve", lambda e: e.scalar_tensor_tensor(out=AR[:, :, 0, :], in0=c3(kk), scalar=-1.0, in1=c3(lw),
                                                     op0=ALU.mult, op1=ALU.mult), r=[K("t7"), K("t4")], w=[K("AR")])
        e_neg = T("t0")
        S.op("act", lambda e: e.activation(out=e_neg, in_=cl, func=AF.Exp, scale=-1.0), r=[K("t11"), K("AR")],
             w=[K("t0")])
        kt, bt, kh, bh, vb = T("b1"), T("b2"), T("b3"), T("b4"), T("b5")
        S.op("pool", lambda e: e.tensor_tensor(out=kp, in0=kp, in1=e_neg, op=ALU.mult), r=[K("t9"), K("t0")], w=[K("t9")])
        S.op("act", lambda e: e.activation(out=kt, in_=kp, func=AF.Copy), r=[K("t9")], w=[K("b1")])
        S.op("dve", lambda e: e.tensor_tensor(out=bvec, in0=bvec, in1=e_neg, op=ALU.mult), r=[K("t8"), K("t0")],
             w=[K("t8")])
        S.op("act", lambda e: e.activation(out=bt, in_=bvec, func=AF.Copy), r=[K("t8")], w=[K("b2")])
        for ch in range(nch):
            cs = slice(ch * C, (ch + 1) * C)
            S.op("act", lambda e, cs=cs, ch=ch: e.activation(out=kh[:, cs], in_=kp[:, cs], func=AF.Identity,
                                                             scale=gC[:, ch:ch + 1]), r=[K("t9"), "gC"], w=[K("b3")])
            S.op("act", lambda e, cs=cs, ch=ch: e.activation(out=bh[:, cs], in_=bvec[:, cs], func=AF.Identity,
                                                             scale=gC[:, ch:ch + 1]), r=[K("t8"), "gC"], w=[K("b4")])
        S.op("act", lambda e: e.activation(out=vb, in_=v_, func=AF.Copy), r=[K("t3")], w=[K("b5")])

        by = bank("y")
        STf = sd["S"][:, pr, :]
        STb = tm["STb"]
        nupd = {64: 5, 128: 6}[C]
        HS = [slice(0, 64), slice(64, 128)]
        KBV = tm["KBV"]
        ARc = lambda ch: B["AR"][:, ch * 2 * C:(ch + 1) * 2 * C]
        CS = lambda ch: slice(ch * C, (ch + 1) * C)
        for ch in range(nch):
            btp = bank("p1")
            ptv = PS(btp)[:, 0:192].bitcast(BF16).rearrange("p (a c) -> p a c", c=128)
            S.pe_group([lambda e, src_=src_, i=i, ch=ch: e.transpose(out=ptv[0:C, i, :], in_=src_[:, CS(ch)], identity=ident_b)
                        for i, src_ in enumerate((kh, bh, vb))], r=[K("b3"), K("b4"), K("b5")], w=[("ps", btp)])
            S.op("act", lambda e, ch=ch: e.activation(out=KBV[0:C, ch], in_=ptv[0:C], func=AF.Copy),
                 r=[("ps", btp)], w=["KBV"])
        S1 = [tm[("S1m", h)] for h in range(2)]
        S2 = [tm[("S2m", h)] for h in range(2)]
        Lt = [tm[("L", h)] for h in range(2)]
        Xt = [tm[("X", h)] for h in range(2)]
        for c0 in range(0, nch, 2):
            cn = min(2, nch - c0)
            for h in range(2):
                hs = HS[h]
                for (lhs, dst, key, eng) in ((kt, S1[h], ("S1m", h), "dve"), (bt, S2[h], ("S2m", h), "dve")):
                    b1 = bank("p1")
                    S.pe_group([lambda e, ch=ch, j=j, lhs=lhs, b1=b1: e.matmul(PS(b1)[0:C, j * 2 * C:(j + 1) * 2 * C], lhsT=lhs[hs, CS(ch)],
                                                                               rhs=ARc(ch)[hs, :], start=True, stop=True)
                                for j, ch in enumerate(range(c0, c0 + cn))],
                               r=[K("b1"), K("b2"), K("AR")], w=[("ps", b1)])
                    if C == 128 and cn == 2:
                        S.op(eng, lambda e, dst=dst, b1=b1, c0=c0: e.tensor_tensor(
                            out=dst[0:C, c0:c0 + 2, :, :],
                            in0=PS(b1)[0:C, 0:512].rearrange("p (j a c) -> p j a c", j=2, a=2),
                            in1=mc2[0:C], op=ALU.mult), r=[("ps", b1)], w=[key])
                    else:
                        for j, ch in enumerate(range(c0, c0 + cn)):
                            S.op(eng, lambda e, ch=ch, j=j, dst=dst, b1=b1: e.tensor_tensor(
                                out=dst[0:C, ch, :, 0:C], in0=PS(b1)[0:C, j * 2 * C:(j + 1) * 2 * C].rearrange("p (a c) -> p a c", c=C),
                                in1=mc[C][0:C], op=ALU.mult), r=[("ps", b1)], w=[key])
        for h in range(2):
            hs = HS[h]
            b3 = bank("p1")
            S.pe_group([lambda e, ch=ch, b3=b3: e.matmul(PS(b3)[0:C, ch * C:(ch + 1) * C], lhsT=ARc(ch)[hs, 0:C], rhs=bt[hs, CS(ch)],
                                                          start=True, stop=True) for ch in range(nch)],
                       r=[K("b2"), K("AR")], w=[("ps", b3)])
            if C == 128:
                S.op("dve", lambda e, b3=b3, h=h: e.tensor_tensor(out=Lt[h][0:C, 0:nch, :],
                                                                  in0=PS(b3)[0:C, 0:nch * C].rearrange("p (a c) -> p a c", c=C),
                                                                  in1=m_sl4[0:C, 0:nch, :], op=ALU.mult),
                     r=[("ps", b3)], w=[("L", h)])
            else:
                for ch in range(nch):
                    S.op("dve", lambda e, ch=ch, b3=b3, h=h: e.tensor_tensor(out=Lt[h][0:C, ch, 0:C], in0=PS(b3)[0:C, ch * C:(ch + 1) * C],
                                                                             in1=m_sl[0:C, 0:C], op=ALU.mult),
                         r=[("ps", b3)], w=[("L", h)])
            S.op("pool", lambda e, h=h: e.tensor_tensor(out=Xt[h][0:C, 0:nch, 0:C], in0=S2[h][0:C, 0:nch, 0, 0:C],
                                                        in1=identb4[0:C, 0:nch, 0:C], op=ALU.add),
                 r=[("S2m", h)], w=[("X", h)])
        c3v = lambda ap: ap[0:C, 0:nch * C].rearrange("p (a c) -> p a c", c=C)
        for u in range(nupd):
            last = (u == nupd - 1)
            bl, bn = {}, {}
            for h in range(2):
                bl[h] = bank("p1")
                S.pe_group([lambda e, ch=ch, h=h: e.matmul(PS(bl[h])[0:C, ch * C:(ch + 1) * C], lhsT=S2[h][0:C, ch, 0, 0:C],
                                                          rhs=Lt[h][0:C, ch, 0:C], start=True, stop=True) for ch in range(nch)],
                           r=[("L", h), ("S2m", h)], w=[("ps", bl[h])])
                if not last:
                    bn[h] = bank("p1")
                    S.pe_group([lambda e, ch=ch, h=h: e.matmul(PS(bn[h])[0:C, ch * C:(ch + 1) * C], lhsT=Lt[h][0:C, ch, 0:C],
                                                              rhs=S2[h][0:C, ch, 0, 0:C], start=True, stop=True) for ch in range(nch)],
                               r=[("L", h), ("S2m", h)], w=[("ps", bn[h])])
            for h in range(2):
                S.op("act", lambda e, h=h: e.activation(out=Lt[h][0:C, 0:nch, 0:C], in_=c3v(PS(bl[h])), func=AF.Copy),
                     r=[("ps", bl[h])], w=[("L", h)])
                if not last:
                    S.op("act", lambda e, h=h: e.activation(out=S2[h][0:C, 0:nch, 0, 0:C], in_=c3v(PS(bn[h])), func=AF.Copy),
                         r=[("ps", bn[h])], w=[("S2m", h)])
            bx = {}
            for h in range(2):
                bx[h] = bank("p1")
                S.pe_group([lambda e, ch=ch, h=h: e.matmul(PS(bx[h])[0:C, ch * C:(ch + 1) * C], lhsT=Lt[h][0:C, ch, 0:C],
                                                          rhs=Xt[h][0:C, ch, 0:C], start=True, stop=True) for ch in range(nch)],
                           r=[("L", h), ("X", h)], w=[("ps", bx[h])])
            for h in range(2):
                S.op("dve", lambda e, h=h: e.tensor_tensor(out=Xt[h][0:C, 0:nch, 0:C], in0=c3v(PS(bx[h])),
                                                           in1=Xt[h][0:C, 0:nch, 0:C], op=ALU.add),
                     r=[("ps", bx[h]), ("X", h)], w=[("X", h)])

        S.op("act", lambda e: e.activation(out=STb, in_=STf, func=AF.Copy), r=[("st_S",) + sk], w=["STb"])
        Psb, Usb = tm["Psb"], tm["Usb"]
        bS = bank("state")
        for ch in range(nch):
            cs = CS(ch)
            VT = KBV[0:C, ch, 2, :]
            KH = KBV[0:C, ch, 0, :]
            BH = KBV[0:C, ch, 1, :]
            bP = bank("small")
            for h in range(2):
                hs = HS[h]
                rowsplit = (C == 64 and h == 1)
                S.pe_group([lambda e: e.matmul(PS(bP)[0:C, h * 64:h * 64 + 64], lhsT=ARc(ch)[hs, 0:C], rhs=STb[hs, :], start=True, stop=False)],
                           r=[K("AR"), "STb"], w=[("ps", bP)], pe_sync=(C == 64))
                S.pe_group([lambda e: e.matmul(PS(bP)[0:C, h * 64:h * 64 + 64], lhsT=S1[h][0:C, ch, 0, 0:C], rhs=VT[:, hs], start=False, stop=True)],
                           r=[("S1m", h), "KBV"], w=[("ps", bP)], pe_sync=(C == 64))
            S.op("act", lambda e: e.activation(out=Psb[0:C], in_=PS(bP)[0:C, 0:128].rearrange("p (a c) -> p a c", c=64), func=AF.Copy),
                 r=[("ps", bP)], w=["Psb"])
            bU = bank("small")
            for h in range(2):
                S.pe_group([lambda e: e.matmul(PS(bU)[0:C, h * 64:h * 64 + 64], lhsT=Xt[h][0:C, ch, 0:C], rhs=Psb[0:C, h, :], start=True, stop=True)],
                           r=[("X", h), "Psb"], w=[("ps", bU)], pe_sync=(C == 64))
            S.op("dve", lambda e: e.tensor_copy(out=Usb[0:C], in_=PS(bU)[0:C, 0:128].rearrange("p (a c) -> p a c", c=64)),
                 r=[("ps", bU)], w=["Usb"])
            for h in range(2):
                hs = HS[h]
                S.pe_group([lambda e: e.matmul(PS(bS)[hs, 0:64], lhsT=BH[:, hs], rhs=Usb[0:C, h, :], start=True, stop=False),
                            lambda e: e.matmul(PS(bS)[hs, 0:64], lhsT=KH[:, hs], rhs=VT[:, hs], start=False, stop=True)],
                           r=["KBV", "Usb"], w=[("ps", bS)], pe_sync=(C == 64))
            for h in range(2):
                hs = HS[h]
                S.pe_group([lambda e: e.matmul(PS(by)[hs, cs], lhsT=STb[hs, :], rhs=ARc(ch)[hs, C:2 * C], start=True, stop=False)],
                           r=["STb", K("AR")], w=[("ps", by)], pe_sync=(C == 64))
                S.pe_group([lambda e: e.matmul(PS(by)[hs, cs], lhsT=Usb[0:C, h, :], rhs=S2[h][0:C, ch, 1, 0:C], start=False, stop=False),
                            lambda e: e.matmul(PS(by)[hs, cs], lhsT=VT[:, hs], rhs=S1[h][0:C, ch, 1, 0:C], start=False, stop=True)],
                           r=["Usb", ("S2m", h), ("S1m", h), "KBV"], w=[("ps", by)], pe_sync=(C == 64))
            if ch < nch - 1:
                S.op("dve", lambda e, ch=ch: e.scalar_tensor_tensor(out=STb, in0=STf, scalar=gC[:, ch:ch + 1], in1=PS(bS)[:, 0:64],
                                                                    op0=ALU.mult, op1=ALU.add),
                     r=[("ps", bS), ("st_S",) + sk, "gC"], w=["STb"])
            S.op("dve", lambda e, ch=ch: e.scalar_tensor_tensor(out=STf, in0=STf, scalar=gC[:, ch:ch + 1], in1=PS(bS)[:, 0:64],
                                                                op0=ALU.mult, op1=ALU.add),
                 r=[("ps", bS), ("st_S",) + sk, "gC"], w=[("st_S",) + sk])

        y = epi1[:, 0:W]
        S.op("act", lambda e: e.activation(out=y, in_=PS(by)[:, 0:W], func=AF.Copy), r=[("ps", by)], w=["epi1"])
        bm = bank("small")
        S.pe_group([lambda e: e.matmul(PS(bm)[:, 0:W], lhsT=bones_f, rhs=y, start=True, stop=True)],
                   r=["epi1"], w=[("ps", bm)])
        S.op("dve", lambda e: e.tensor_tensor(out=y, in0=y, in1=PS(bm)[:, 0:W], op=ALU.subtract),
             r=[("ps", bm), "epi1"], w=["epi1"])
        sq = epi2[:, 0:W]
        S.op("act", lambda e: e.activation(out=sq, in_=y, func=AF.Square), r=["epi1"], w=["epi2"])
        bv2 = bank("small")
        S.pe_group([lambda e: e.matmul(PS(bv2)[:, 0:W], lhsT=bones_f, rhs=sq, start=True, stop=True)],
                   r=["epi2"], w=[("ps", bv2)])
        rs = epi2[:, 0:W]
        S.op("act", lambda e: e.activation(out=rs, in_=PS(bv2)[:, 0:W], func=AF.Ln, bias=GN_EPS, scale=1.0),
             r=[("ps", bv2)], w=["epi2"])
        S.op("act", lambda e: e.activation(out=rs, in_=rs, func=AF.Exp, scale=-0.5), r=["epi2"], w=["epi2"])
        S.op("dve", lambda e: e.tensor_tensor(out=y, in0=y, in1=rs, op=ALU.mult), r=["epi1", "epi2"], w=["epi1"])
        S.op("dve", lambda e: e.tensor_scalar(out=y, in0=y, scalar1=V(l, "gg", pr), scalar2=V(l, "gb", pr),
                                              op0=ALU.mult, op1=ALU.add), r=["epi1"], w=["epi1"])
        S.op("pool", lambda e: e.tensor_tensor(out=y, in0=y, in1=bonus, op=ALU.add), r=["epi1", K("t10")], w=["epi1"])
        S.op("dve", lambda e: e.tensor_tensor(out=cat[:, 8 + pr, off:off + W], in0=y, in1=g_, op=ALU.mult),
             r=["epi1", K("t6")], w=["cat"])

    groups = []
    for g in range(npg):
        groups.append(dict(gw=512, parts=[dict(seq="P", off=0, W=512, C=128, first=(g == 0), last=(g == npg - 1),
                                               bi=0)],
                           src=xp[g * 512:(g + 1) * 512, :], dst=yp[g * 512:(g + 1) * 512, :]))
    if with_s:
      groups.append(dict(gw=128, parts=[dict(seq="S0", off=0, W=64, C=64, first=False, last=True, bi=0, sidx=0),
                                      dict(seq="S1", off=64, W=64, C=64, first=False, last=True, bi=1, sidx=1)],
                       src=xs[:, :], dst=ys[:, :]))
    for g in groups:
        for l in range(layers):
            wq["order"] += layer_order(l)

    dbg_out = {}

    def chk(name):
        if dbg == name:
            S.dead = True

    for gi_, G in enumerate(groups):
        gw = G["gw"]
        parts = G["parts"]
        S.flush()
        S.reorder = reorder
        ntb = gw // 128
        for tb in range(ntb):
            S.dma("sp", stage, G["src"][tb * 128:(tb + 1) * 128, :], w=["stage"])
            for k4 in range(4):
                b = bank("small")
                S.pe_group([lambda e, k=k: e.transpose(out=PS(b)[:, (k % 4) * 128:(k % 4 + 1) * 128],
                                                        in_=stage[:, k * 128:(k + 1) * 128], identity=ident_f)
                            for k in range(k4 * 4, k4 * 4 + 4)], r=["stage"], w=[("ps", b)])
                S.op("act" if k4 % 2 else "dve",
                     (lambda e, k4=k4, tb=tb, b=b: e.activation(
                         out=xT[:, k4 * 4:k4 * 4 + 4, tb * 128:(tb + 1) * 128],
                         in_=PS(b).rearrange("p (a c) -> p a c", c=128), func=AF.Copy)) if k4 % 2 else
                     (lambda e, k4=k4, tb=tb, b=b: e.tensor_copy(
                         out=xT[:, k4 * 4:k4 * 4 + 4, tb * 128:(tb + 1) * 128],
                         in_=PS(b).rearrange("p (a c) -> p a c", c=128))),
                     r=[("ps", b)], w=["xT"])

        for l in range(layers):
            LV = l * VL
            chk("A")
            S.dma("pool", wsmall, wblk[l, 0, :, :], w=["wsmall"], sem=wsem_small)
            S.dma("pool", wpool, wblk[l, 1, :, 0:512], w=["wpool"], sem=wsem_small)
            for p in parts:
                if p["seq"] == "P":
                    if p["first"]:
                        sd = st[("P", l)]
                        S.op("dve", lambda e, sd=sd: e.memset(sd["u"], 0.0), w=[("st_u", "P", l)])
                        S.op("dve", lambda e, sd=sd: e.memset(sd["p"], 0.0), w=[("st_p", "P", l)])
                        S.op("dve", lambda e, sd=sd: e.memset(sd["q"], 0.0), w=[("st_q", "P", l)])
                        S.op("dve", lambda e, sd=sd: e.memset(sd["S"], 0.0), w=[("st_S", "P", l)])
                    continue
                sq, si = p["seq"], p["sidx"]
                sd = st[(sq, l)]
                S.dma("sp", stage2[0:30, 0:512], cconv[l, si, :, :], w=["stage"])
                b = bank("small")
                S.pe_group([lambda e, c=c: e.transpose(out=PS(b)[:, c * 32:c * 32 + 30],
                                                        in_=stage2[0:30, c * 128:(c + 1) * 128],
                                                        identity=ident_f[0:30, 0:30]) for c in range(4)],
                           r=["stage"], w=[("ps", b)])
                S.op("dve", lambda e, sd=sd, b=b: e.tensor_copy(
                    out=sd["u"], in_=PS(b)[:, 0:128].rearrange("p (a c) -> p a c", c=32)[:, :, 0:30]),
                    r=[("ps", b)], w=[("st_u", sq, l)])
                S.dma("sp", stage2[0:15, 0:512], cpool[l, si, :, :], w=["stage"])
                b = bank("small")
                S.pe_group([lambda e, c=c: e.transpose(out=PS(b)[:, c * 16:c * 16 + 15],
                                                        in_=stage2[0:15, c * 128:(c + 1) * 128],
                                                        identity=ident_f[0:15, 0:15]) for c in range(4)],
                           r=["stage"], w=[("ps", b)])
                S.op("dve", lambda e, sd=sd, b=b: e.tensor_copy(
                    out=sd["p"], in_=PS(b)[:, 0:64].rearrange("p (a c) -> p a c", c=16)[:, :, 0:15]),
                    r=[("ps", b)], w=[("st_p", sq, l)])
                S.dma("sp", stage2[0:NQ, 0:128], cshift[l, si, :, :], w=["stage"])
                b = bank("small")
                S.pe_group([lambda e: e.transpose(out=PS(b)[:, 0:NQ], in_=stage2[0:NQ, 0:128],
                                                  identity=ident_f[0:NQ, 0:NQ])], r=["stage"], w=[("ps", b)])
                S.op("dve", lambda e, sd=sd, b=b: e.tensor_copy(out=sd["q"], in_=PS(b)[:, 0:NQ]),
                     r=[("ps", b)], w=[("st_q", sq, l)])
                S.dma("sp", stage2[0:64, :].rearrange("p (h j) -> p h j", j=64),
                      cwkv[l, si].rearrange("h i j -> i h j"), w=["stage"])
                for half in range(2):
                    b = bank("small")
                    S.pe_group([lambda e, pr=pr: e.transpose(
                        out=PS(b)[:, (pr % 4) * 64:(pr % 4) * 64 + 64], in_=stage2[0:64, pr * 128:(pr + 1) * 128],
                        identity=ident_f[0:64, 0:64]) for pr in range(half * 4, half * 4 + 4)],
                        r=["stage"], w=[("ps", b)])
                    S.op("dve", lambda e, sd=sd, b=b, half=half: e.tensor_copy(
                        out=sd["S"][:, half * 4:half * 4 + 4, :],
                        in_=PS(b)[:, 0:256].rearrange("p (a c) -> p a c", c=64)),
                        r=[("ps", b)], w=[("st_S", sq, l)])

            chk("A2")
            rmsnorm_to(lambda k: hT[:, k, 0:gw], gw, 0, "hT", LV + VO["nm"])
            chk("B")

            for c in range(4):
                bg = proj(w_next(), hT, gw, "hT")
                bv = proj(w_next(), hT, gw, "hT")
                for p in parts:
                    B = mb[p["bi"]]
                    W, off = p["W"], p["off"]
                    sk = (p["seq"], l)
                    t0 = B["t0"][:, 0:W]
                    if c == 0:
                        S.op("dve", lambda e, B=B, p=p: e.tensor_copy(out=B["ubuf"][:, :, 0:30],
                                                                     in_=st[(p["seq"], l)]["u"]),
                             r=[("st_u",) + sk], w=[("ubuf", p["bi"])])
                    S.op("act", lambda e, t0=t0, off=off, W=W, bg=bg: e.activation(
                        out=t0, in_=PS(bg)[:, off:off + W], func=AF.Sigmoid), r=[("ps", bg)], w=[("t0", p["bi"])])
                    S.op("dve", lambda e, B=B, t0=t0, off=off, W=W, bv=bv, c=c: e.tensor_tensor(
                        out=B["ubuf"][:, c, 30:30 + W], in0=PS(bv)[:, off:off + W], in1=t0, op=ALU.mult),
                        r=[("ps", bv), ("t0", p["bi"])], w=[("ubuf", p["bi"])])
            for p in parts:
                B = mb[p["bi"]]
                W, off, bi = p["W"], p["off"], p["bi"]
                sk = (p["seq"], l)
                S.op("act", lambda e, B=B, W=W: e.activation(out=B["ubf"][:, :, 0:30 + W], in_=B["ubuf"][:, :, 0:30 + W],
                                                            func=AF.Copy), r=[("ubuf", bi)], w=[("ubf", bi)])
                S.op("dve", lambda e, B=B, W=W, p=p: e.tensor_copy(out=st[(p["seq"], l)]["u"], in_=B["ubuf"][:, :, W:W + 30]),
                     r=[("ubuf", bi)], w=[("st_u",) + sk])
                pass
            for c in range(4):
                for j in range(31):
                    if j % 4 != 3:
                        S.op("act", lambda e, c=c, j=j: e.activation(out=diag[:, j, :], in_=ident_b, func=AF.Identity,
                                                                      scale=V(l, "cw", c * 31 + j)), r=[], w=[("diag", j)])
                    else:
                        S.op("dve", lambda e, c=c, j=j: e.tensor_scalar(
                            out=diag[:, j, :], in0=ident_b, scalar1=V(l, "cw", c * 31 + j), scalar2=None, op0=ALU.mult),
                            r=[], w=[("diag", j)])
                for p in parts:
                    B = mb[p["bi"]]
                    W, off, bi = p["W"], p["off"], p["bi"]
                    b = bank("small")
                    S.pe_group([lambda e, j=j, c=c, B=B, W=W, b=b: e.matmul(
                        PS(b)[:, 0:W], lhsT=diag[:, j, :], rhs=B["ubf"][:, c, j:j + W],
                        start=(j == 0), stop=(j == 30)) for j in range(31)],
                        r=[("diag", j) for j in range(31)] + [("ubf", bi)] + [(tn, 0) for tn in ("t8", "t9", "t10", "t11")],
                        w=[("ps", b)])
                    S.op("act", lambda e, B=B, W=W, b=b, c=c: e.activation(
                        out=B["hconv"][:, c, 0:W], in_=PS(b)[:, 0:W], func=AF.Identity,
                        bias=V(l, "cb", c), scale=1.0), r=[("ps", b)], w=[("hconv", bi)] + (["stage"] if bi == 0 else []))
            for p in parts:
                B = mb[p["bi"]]
                W, off, bi = p["W"], p["off"], p["bi"]
                sk = (p["seq"], l)
                bm = bank("small")
                S.pe_group([lambda e, c=c, B=B, W=W: e.matmul(PS(bm)[:, 0:W], lhsT=onesD_f, rhs=B["hconv"][:, c, 0:W],
                                                              start=(c == 0), stop=(c == 3)) for c in range(4)],
                           r=[("hconv", bi)], w=[("ps", bm)])
                mean = B["t1"][:, 0:W]
                S.op("act", lambda e, mean=mean, W=W: e.activation(out=mean, in_=PS(bm)[:, 0:W], func=AF.Copy),
                     r=[("ps", bm)], w=[("t1", bi)])
                for c in range(4):
                    S.op("dve", lambda e, c=c, B=B, W=W, mean=mean: e.tensor_tensor(
                        out=B["hconv"][:, c, 0:W], in0=B["hconv"][:, c, 0:W], in1=mean, op=ALU.subtract),
                        r=[("hconv", bi), ("t1", bi)], w=[("hconv", bi)])
                bvv = bank("small")
                for c in range(4):
                    S.op("act", lambda e, c=c, B=B, W=W: e.activation(out=B["t2"][:, 0:W], in_=B["hconv"][:, c, 0:W],
                                                                      func=AF.Square),
                         r=[("hconv", bi)], w=[("t2", bi)])
                    S.pe_group([lambda e, c=c, B=B, W=W: e.matmul(PS(bvv)[:, 0:W], lhsT=onesD_f, rhs=B["t2"][:, 0:W],
                                                                  start=(c == 0), stop=(c == 3))],
                               r=[("t2", bi)], w=[("ps", bvv)])
                rs = B["t3"][:, 0:W]
                S.op("act", lambda e, rs=rs, W=W: e.activation(out=rs, in_=PS(bvv)[:, 0:W], func=AF.Ln,
                                                              bias=LN_EPS, scale=1.0), r=[("ps", bvv)], w=[("t3", bi)])
                S.op("act", lambda e, rs=rs: e.activation(out=rs, in_=rs, func=AF.Exp, scale=-0.5),
                     r=[("t3", bi)], w=[("t3", bi)])
                for c in range(4):
                    S.op("dve", lambda e, c=c, B=B, W=W, rs=rs: e.tensor_tensor(
                        out=B["hconv"][:, c, 0:W], in0=B["hconv"][:, c, 0:W], in1=rs, op=ALU.mult),
                        r=[("hconv", bi), ("t3", bi)], w=[("hconv", bi)])
                    S.op("act", lambda e, c=c, B=B, W=W, off=off: e.activation(
                        out=cat[:, c, off:off + W], in_=B["hconv"][:, c, 0:W], func=AF.Silu,
                        bias=V(l, "lb", c), scale=V(l, "lg", c)), r=[("hconv", bi)], w=["cat"])

            chk("C")
            for c in range(4):
                bp = proj(w_next(), hT, gw, "hT")
                for p in parts:
                    B = mb[p["bi"]]
                    W, off, bi = p["W"], p["off"], p["bi"]
                    sk = (p["seq"], l)
                    if c == 0:
                        S.op("dve", lambda e, B=B, p=p: e.tensor_copy(out=B["pbuf"][:, :, 0:15],
                                                                     in_=st[(p["seq"], l)]["p"]),
                             r=[("st_p",) + sk], w=[("pbuf", bi), ("ubuf", bi)])
                    S.op("act", lambda e, B=B, W=W, off=off, bp=bp, c=c: e.activation(
                        out=B["pbuf"][:, c, 15:15 + W], in_=PS(bp)[:, off:off + W], func=AF.Copy),
                        r=[("ps", bp)], w=[("pbuf", bi), ("ubuf", bi)])
            for p in parts:
                B = mb[p["bi"]]
                W, off, bi = p["W"], p["off"], p["bi"]
                sk = (p["seq"], l)
                S.op("dve", lambda e, B=B, W=W, p=p: e.tensor_copy(out=st[(p["seq"], l)]["p"], in_=B["pbuf"][:, :, W:W + 15]),
                     r=[("pbuf", bi)], w=[("st_p",) + sk])
                for c, wdw in enumerate(POOL_WINDOWS):
                    src = B["pbuf"][:, c, :]
                    lo = 15
                    span = 1
                    ta, tb_ = B["t4"], B["t5"]
                    cur, cur_lo = src, 0
                    nsteps = {2: 1, 4: 2, 8: 3, 16: 4}[wdw]
                    for s_ in range(nsteps):
                        dst = ta if s_ % 2 == 0 else tb_
                        new_lo = cur_lo + span
                        n = 15 + W - new_lo
                        S.op("pool", lambda e, dst=dst, cur=cur, new_lo=new_lo, span=span, n=n: e.tensor_tensor(
                            out=dst[:, new_lo:new_lo + n], in0=cur[:, new_lo:new_lo + n],
                            in1=cur[:, new_lo - span:new_lo - span + n], op=ALU.add),
                            r=[("pbuf", bi), ("t4", bi), ("t5", bi)], w=[("t4" if s_ % 2 == 0 else "t5", bi)])
                        cur, cur_lo = dst, new_lo
                        span *= 2
                    S.op("dve", lambda e, cur=cur, W=W, c=c, B=B, wdw=wdw: e.scalar_tensor_tensor(
                        out=B["dpool"][:, c, 0:W], in0=cur[:, 15:15 + W], scalar=1.0 / wdw,
                        in1=B["pbuf"][:, c, 15:15 + W], op0=ALU.mult, op1=ALU.subtract),
                        r=[("t4", bi), ("t5", bi), ("pbuf", bi)], w=[("dpool", bi), ("ubuf", bi), ("ubf", bi)])
                    if p["first"]:
                        S.op("dve", lambda e, cur=cur, c=c: e.tensor_tensor(
                            out=cur[:, 15:31], in0=cur[:, 15:31], in1=invc_first[:, c, :], op=ALU.mult),
                            r=[("t4", bi), ("t5", bi), ("dpool", bi)], w=[("t4", bi), ("t5", bi)])
                        S.op("dve", lambda e, cur=cur, c=c, B=B: e.tensor_tensor(
                            out=B["dpool"][:, c, 0:16], in0=cur[:, 15:31], in1=B["pbuf"][:, c, 15:31],
                            op=ALU.subtract), r=[("t4", bi), ("t5", bi), ("pbuf", bi)], w=[("dpool", bi), ("ubuf", bi), ("ubf", bi)])
                    b = bank("small")
                    S.pe_group([lambda e, c=c, B=B, W=W, b=b: e.matmul(PS(b)[:, 0:W], lhsT=wpool[:, c * 128:(c + 1) * 128],
                                                                        rhs=B["dpool"][:, c, 0:W], start=True, stop=True)],
                               r=["wpool", ("dpool", bi)], w=[("ps", b)])
                    S.op("act", lambda e, c=c, W=W, off=off, b=b: e.activation(
                        out=cat[:, 4 + c, off:off + W], in_=PS(b)[:, 0:W], func=AF.Identity, scale=V(l, "psc", c)),
                        r=[("ps", b)], w=["cat"])


            chk("D")
            b24 = proj(w_next(), hT, gw, "hT")
            b25 = proj(w_next(), hT, gw, "hT")
            for p in parts:
                B = mb[p["bi"]]
                W, bi = p["W"], p["bi"]
                gl = B["gl"]
                shifted_from_psum(l, b24, 24, p, gl, ("t1", bi), B["t0"], ("t0", bi))
                S.op("act", lambda e, B=B, W=W, gl=gl: e.activation(out=B["lora"][0:64, 0, 0:W], in_=gl[0:64, 0:W],
                                                                    func=AF.Tanh), r=[("t1", bi)], w=[("lora", bi)])
                S.op("act", lambda e, B=B, W=W, gl=gl: e.activation(out=B["lora"][64:128, 0, 0:W], in_=gl[64:128, 0:W],
                                                                    func=AF.Copy), r=[("t1", bi)], w=[("lora", bi)])
                shifted_from_psum(l, b25, 25, p, gl, ("t1", bi), B["t0"], ("t0", bi))
                S.op("act", lambda e, B=B, W=W, gl=gl: e.activation(out=B["lora"][0:64, 1, 0:W], in_=gl[0:64, 0:W],
                                                                    func=AF.Sigmoid), r=[("t1", bi)], w=[("lora", bi)])

            chk("E")
            for pr in range(PAIRS):
                if pr == 1:
                    chk("F")
                br = proj(w_next(), hT, gw, "hT")
                bk = proj(w_next(), hT, gw, "hT")
                bv_ = proj(w_next(), hT, gw, "hT")
                for p in parts:
                    if p["C"] == 64:
                        S.flush()
                    wkv_pair(l, pr, p, br, bk, bv_)
                if parts[0]["C"] == 64:
                    S.flush()

            chk("G")
            for n in range(16):
                bo = proj(w_next(), cat, gw, "cat")
                S.op("dve", lambda e, n=n, bo=bo: e.tensor_tensor(out=xT[:, n, 0:gw], in0=PS(bo)[:, 0:gw],
                                                                  in1=xT[:, n, 0:gw], op=ALU.add),
                     r=[("ps", bo), "xT"], w=["xT"])

            chk("H")
            rmsnorm_to(lambda k: hT[:, k, 0:gw], gw, 0, "hT", LV + VO["nf"])
            S.barrier()
            for f in range(FC if dbg != "outproj" else 0):
                bg = proj(w_next(), hT, gw, "hT")
                bu = proj(w_next(), hT, gw, "hT")
                ft = ftmp[f % 2]
                S.op("act", lambda e, ft=ft, bg=bg: e.activation(out=ft[:, 0:gw], in_=PS(bg)[:, 0:gw], func=AF.Silu),
                     r=[("ps", bg)], w=[("ftmp", f % 2)])
                S.op("dve", lambda e, ft=ft, bu=bu, f=f: e.tensor_tensor(out=act[:, f, 0:gw], in0=PS(bu)[:, 0:gw],
                                                                         in1=ft[:, 0:gw], op=ALU.mult),
                     r=[("ps", bu), ("ftmp", f % 2)], w=[("act", f)])
            for n in range(16 if dbg != "outproj" else 0):
                bd = bank("big")
                for j, nk in enumerate((16, 16, 12)):
                    sl = w_next()
                    wv = wring[:, sl, :].rearrange("p (k n) -> p k n", n=128)
                    fns = []
                    for k in range(nk):
                        f = j * 16 + k
                        fns.append(lambda e, wv=wv, k=k, f=f: e.matmul(PS(bd)[:, 0:gw], lhsT=wv[:, k, :],
                                                                        rhs=act[:, f, 0:gw], start=(f == 0),
                                                                        stop=(f == FC - 1)))
                    S.pe_group(fns, r=[("w", sl)] + [("act", f) for f in range(j * 16, j * 16 + nk)],
                               w=[("ps", bd)])
                S.op("dve", lambda e, n=n, bd=bd: e.tensor_tensor(out=xT[:, n, 0:gw], in0=PS(bd)[:, 0:gw],
                                                                  in1=xT[:, n, 0:gw], op=ALU.add),
                     r=[("ps", bd), "xT"], w=["xT"])
            S.barrier()

            chk("I")
            for p in parts:
                if not p["last"]:
                    continue
                sk = (p["seq"], l)
                sd = st[sk]
                oi = {"P": 0, "S0": 1, "S1": 2}[p["seq"]]
                b = bank("small")
                S.pe_group([lambda e, c=c: e.transpose(out=PS(b)[0:30, c * 128:(c + 1) * 128], in_=sd["u"][:, c, :],
                                                        identity=ident_f) for c in range(4)],
                           r=[("st_u",) + sk], w=[("ps", b)])
                S.op("act", lambda e, b=b: e.activation(out=stage2[0:30, 0:512], in_=PS(b)[0:30, 0:512], func=AF.Copy),
                     r=[("ps", b)], w=["stage"])
                S.dma("sp", nconv[l, oi, :, :], stage2[0:30, 0:512], r=["stage"], w=[("o_conv", l, oi)])
                b = bank("small")
                S.pe_group([lambda e, c=c: e.transpose(out=PS(b)[0:15, c * 128:(c + 1) * 128], in_=sd["p"][:, c, :],
                                                        identity=ident_f) for c in range(4)],
                           r=[("st_p",) + sk], w=[("ps", b)])
                S.op("act", lambda e, b=b: e.activation(out=stage2[0:15, 512:1024], in_=PS(b)[0:15, 0:512], func=AF.Copy),
                     r=[("ps", b)], w=["stage"])
                S.dma("sp", npool[l, oi, :, :], stage2[0:15, 512:1024], r=["stage"], w=[("o_pool", l, oi)])
                b = bank("small")
                S.pe_group([lambda e: e.transpose(out=PS(b)[0:NQ, 0:128], in_=sd["q"], identity=ident_f)],
                           r=[("st_q",) + sk], w=[("ps", b)])
                S.op("act", lambda e, b=b: e.activation(out=stage3[0:NQ, 512:640], in_=PS(b)[0:NQ, 0:128], func=AF.Copy),
                     r=[("ps", b)], w=["stage"])
                S.dma("sp", nshift[l, oi, :, :], stage3[0:NQ, 512:640], r=["stage"], w=[("o_shift", l, oi)])
                for half in range(2):
                    b = bank("small")
                    S.pe_group([lambda e, pr=pr: e.transpose(out=PS(b)[0:64, (pr % 4) * 128:(pr % 4 + 1) * 128],
                                                              in_=sd["S"][:, pr, :], identity=ident_f)
                                for pr in range(half * 4, half * 4 + 4)], r=[("st_S",) + sk], w=[("ps", b)])
                    S.op("act", lambda e, b=b: e.activation(out=stage3[0:64, 0:512], in_=PS(b)[0:64, 0:512],
                                                            func=AF.Copy), r=[("ps", b)], w=["stage"])
                    S.dma("sp", nwkv[l, oi, half * 8:half * 8 + 8].rearrange("h i j -> i h j"),
                          stage3[0:64, 0:512].rearrange("p (h j) -> p h j", j=64), r=["stage"],
                          w=[("o_wkv", l, oi, half)])

        S.dead = False
        if dbg is None:
            rmsnorm_to(lambda k: xT[:, k, 0:gw], gw, 0, "xT", DEPTH * VL)
        for tb in range(ntb):
            for k4 in range(4):
                b = bank("small")
                S.pe_group([lambda e, k=k: e.transpose(out=PS(b)[:, (k % 4) * 128:(k % 4 + 1) * 128],
                                                        in_=xT[:, k, tb * 128:(tb + 1) * 128], identity=ident_f)
                            for k in range(k4 * 4, k4 * 4 + 4)], r=["xT"], w=[("ps", b)])
                S.op("act" if k4 % 2 else "dve",
                     (lambda e, k4=k4, b=b: e.activation(out=stage[:, k4 * 512:(k4 + 1) * 512], in_=PS(b), func=AF.Copy))
                     if k4 % 2 else
                     (lambda e, k4=k4, b=b: e.tensor_copy(out=stage[:, k4 * 512:(k4 + 1) * 512], in_=PS(b))),
                     r=[("ps", b)], w=["stage"])
            S.dma("sp", G["dst"][tb * 128:(tb + 1) * 128, :], stage, r=["stage"], w=[("o_y", gi_, tb)])

    S.finish("sp")
    print("instructions emitted:", S.ninst)
    nc._arena_reg = A.reg
    return nc


def _colize(v):
    v = np.asarray(v, np.float32).reshape(-1)
    n = (v.size + 127) // 128
    out = np.zeros((n * 128,), np.float32)
    out[:v.size] = v
    return out.reshape(n, 128).T


def _prep_shared(inp):
    wblk = np.zeros((DEPTH, NBLK, 128, SLOT), np.float32)
    vecs = np.zeros((128, NVEC), np.float32)
    for l in range(DEPTH):
        wblk[l, 0, 0:64, 0:1024] = inp["decay_up"][l]
        wblk[l, 0, 64:128, 0:1024] = inp["iclr_up"][l]
        wblk[l, 0, 0:64, 1024:2048] = inp["gate_up"][l]
        wblk[l, 1, :, 0:512] = np.asarray(inp["pool_w"][l]).transpose(1, 0, 2).reshape(128, 512)
        win = np.zeros((D, 38 * 128), np.float32)
        win[:, :4800] = inp["w_in"][l]
        wblk[l, 2:40] = win.reshape(16, 128, 38, 128).transpose(2, 1, 0, 3).reshape(38, 128, SLOT)
        wblk[l, 40:56] = np.asarray(inp["w_out"][l]).reshape(16, 128, 16, 128).transpose(2, 1, 0, 3).reshape(16, 128, SLOT)
        g = np.asarray(inp["ffn_gate"][l]).reshape(16, 128, FC, 128).transpose(2, 1, 0, 3).reshape(FC, 128, SLOT)
        u = np.asarray(inp["ffn_up"][l]).reshape(16, 128, FC, 128).transpose(2, 1, 0, 3).reshape(FC, 128, SLOT)
        wblk[l, 56:144:2] = g
        wblk[l, 57:144:2] = u
        dn = np.zeros((48, 128, 16, 128), np.float32)
        dn[:FC] = np.asarray(inp["ffn_down"][l]).reshape(FC, 128, 16, 128)
        dn = dn.reshape(3, 16, 128, 16, 128).transpose(3, 0, 2, 1, 4).reshape(16, 3, 128, SLOT)
        wblk[l, 144:192] = dn.reshape(48, 128, SLOT)
        o = l * VL
        vecs[:, o + VO["nm"]:o + VO["nm"] + 16] = _colize(inp["norm_mix"][l])
        vecs[:, o + VO["nf"]:o + VO["nf"] + 16] = _colize(inp["norm_ffn"][l])
        vecs[:, o + VO["cb"]:o + VO["cb"] + 4] = _colize(inp["conv_b"][l])
        cw = np.asarray(inp["conv_w"][l])
        vecs[:, o + VO["cw"]:o + VO["cw"] + 124] = cw.reshape(31, 4, 128).transpose(2, 1, 0).reshape(128, 124)
        vecs[:, o + VO["lg"]:o + VO["lg"] + 4] = _colize(inp["conv_ln_g"][l])
        vecs[:, o + VO["lb"]:o + VO["lb"] + 4] = _colize(inp["conv_ln_b"][l])
        vecs[:, o + VO["psc"]:o + VO["psc"] + 4] = _colize(inp["pool_scale"][l])
        vecs[:, o + VO["mu"]:o + VO["mu"] + NQ] = _colize(inp["shift_mu"][l])
        vecs[:, o + VO["w0"]:o + VO["w0"] + 8] = _colize(inp["decay_w0"][l])
        vecs[:, o + VO["a0"]:o + VO["a0"] + 8] = _colize(inp["iclr_a0"][l])
        vecs[:, o + VO["kk"]:o + VO["kk"] + 8] = _colize(inp["k_k"][l])
        vecs[:, o + VO["ka"]:o + VO["ka"] + 8] = _colize(inp["k_a"][l])
        vecs[:, o + VO["rk"]:o + VO["rk"] + 8] = _colize(inp["r_k"][l])
        vecs[:, o + VO["gg"]:o + VO["gg"] + 8] = _colize(inp["gn_g"][l])
        vecs[:, o + VO["gb"]:o + VO["gb"] + 8] = _colize(inp["gn_b"][l])
    vecs[:, DEPTH * VL:DEPTH * VL + 16] = _colize(inp["norm_final"])
    return wblk, vecs


def _core_inputs(inp, c, shared, nseq_tok=SEQ):
    wblk, vecs = shared
    sh = np.zeros((DEPTH, 2, NQ * 128), np.float32)
    sh[:, :, :3264] = np.asarray(inp["state_shift"])[:, 2 * c:2 * c + 2, 0, :]
    return {
        "xp": np.ascontiguousarray(np.asarray(inp["x_prompt"])[c % 4, :nseq_tok]),
        "xs": np.ascontiguousarray(np.asarray(inp["x_sample"])[2 * c:2 * c + 2].reshape(2 * SLEN, D)),
        "cconv": np.ascontiguousarray(np.asarray(inp["cache_conv"])[:, 2 * c:2 * c + 2]),
        "cpool": np.ascontiguousarray(np.asarray(inp["cache_pool"])[:, 2 * c:2 * c + 2]),
        "cshift": sh.reshape(DEPTH, 2, NQ, 128),
        "cwkv": np.ascontiguousarray(np.asarray(inp["state_wkv"])[:, 2 * c:2 * c + 2]),
        "wblk": wblk,
        "vecs": vecs,
    }


_NC_CACHE = {}


def kernel(**inp):
    inp = {k: np.asarray(v) for k, v in inp.items()}
    shared = _prep_shared(inp)
    if "nc" not in _NC_CACHE:
        _NC_CACHE["nc"] = build_program()
    nc = _NC_CACHE["nc"]
    in_maps = [_core_inputs(inp, c, shared) for c in range(8)]
    res = run_bass_kernel_spmd(nc, in_maps, core_ids=list(range(8)))
    R = res.results
    y_prompt = np.stack([R[c]["yp"] for c in range(4)]).astype(np.float32)
    y_sample = np.concatenate([R[c]["ys"].reshape(2, SLEN, D) for c in range(8)]).astype(np.float32)

    def gather(name, tailshape, fix=None):
        pr = np.stack([R[c][name][:, 0] for c in range(4)], axis=1)
        sm = np.concatenate([R[c][name][:, 1:3] for c in range(8)], axis=1)
        if fix is not None:
            pr, sm = fix(pr), fix(sm)
        return pr.astype(np.float32), sm.astype(np.float32)

    p_conv, s_conv = gather("nconv", None)
    p_pool, s_pool = gather("npool", None)
    fixs = lambda a: a.reshape(a.shape[0], a.shape[1], 1, NQ * 128)[..., :3264]
    p_shift, s_shift = gather("nshift", None, fixs)
    p_wkv, s_wkv = gather("nwkv", None)
    return (y_prompt, y_sample, p_conv, p_pool, p_shift, p_wkv, s_conv, s_pool, s_shift, s_wkv)
```

```python
import numpy as np
import concourse.bass as bass
import concourse.mybir as mybir
from concourse.bass_utils import run_bass_kernel_spmd

F32 = mybir.dt.float32
BF16 = mybir.dt.bfloat16
AF = mybir.ActivationFunctionType
ALU = mybir.AluOpType

D = 2048
KC = 16
DFF = 5632
FC = 44
HEADS = 16
PAIRS = 8
NQ = 26
DEPTH = 4
SEQ = 2048
SLEN = 64
RMS_EPS = 1e-6
LN_EPS = 1e-5
GN_EPS = 64e-5
LW_SCALE = -float(np.exp(-0.5))
POOL_WINDOWS = (2, 4, 8, 16)

NBLK = 2 + 38 + 16 + 88 + 48
SLOT = 2048
NSLOT = 5

VO = {}
_o = 0
for _n, _w in (("nm", 16), ("nf", 16), ("cb", 4), ("cw", 124), ("lg", 4), ("lb", 4), ("psc", 4),
               ("mu", NQ), ("w0", 8), ("a0", 8), ("kk", 8), ("ka", 8), ("rk", 8), ("gg", 8), ("gb", 8)):
    VO[_n] = _o
    _o += _w
VL = _o
NVEC = DEPTH * VL + 16


class _Dummy:
    def then_inc(self, *a, **k):
        return self


class _Rec:
    def __init__(self):
        self.calls = []

    def __getattr__(self, name):
        def f(*args, **kw):
            self.calls.append((name, args, kw))
            return _Dummy()
        return f


def _free_size(ap):
    try:
        n = 1
        for s in tuple(ap.shape)[1:]:
            n *= int(s)
        return n
    except Exception:
        return 256


class Sched:
    LAT_X = 0.45
    LAT_S = 0.25

    def __init__(self, nc, reorder=True):
        self.nc = nc
        self.reorder = reorder
        self.eng = {"pe": nc.tensor, "act": nc.scalar, "dve": nc.vector, "pool": nc.gpsimd, "sp": nc.sync}
        self.semh = {}
        self.cnt = {}
        for e in self.eng:
            self.semh[e] = nc.alloc_semaphore("sem_" + e)
            self.cnt[e] = 0
        self.waited = {e: {} for e in self.eng}
        self.lastw = {}
        self.readers = {}
        self.dma_sems = []
        self.dma_rr = 0
        self.ninst = 0
        self.dead = False
        self.pending = []

    def new_dma_sem(self, name):
        self.semh[name] = self.nc.alloc_semaphore("sem_" + name)
        self.cnt[name] = 0
        return name

    def op(self, e, fn, r=(), w=()):
        if self.dead:
            return
        rec = _Rec()
        fn(rec)
        self.pending.append(dict(kind="op", eng=e, calls=rec.calls, r=tuple(r), w=tuple(w), sync=False))

    def pe_group(self, fns, r=(), w=(), pe_sync=False):
        if self.dead:
            return
        rec = _Rec()
        for fn in fns:
            fn(rec)
        self.pending.append(dict(kind="op", eng="pe", calls=rec.calls, r=tuple(r), w=tuple(w), sync=pe_sync))

    def dma(self, q, out, in_, r=(), w=(), sem=None):
        if self.dead:
            return
        self.pending.append(dict(kind="dma", eng=q, out=out, in_=in_, r=tuple(r), w=tuple(w), sem=sem))

    def barrier(self, engines=("pe", "act", "dve", "pool")):
        if self.dead:
            return
        self.flush()
        for e in engines:
            need = {}
            for o in engines:
                if o != e and self.cnt[o] > 0:
                    need[o] = self.cnt[o]
            self._wait(e, need)

    def finish(self, e="sp"):
        self.flush()
        need = {}
        for s, c in self.cnt.items():
            if c > 0 and s != e:
                need[s] = c
        self._wait(e, need)

    def _cost(self, o):
        if o["kind"] == "dma":
            return 0.7
        e = o["eng"]
        if e == "pe":
            t = 0.0
            for (name, args, kw) in o["calls"]:
                if name == "matmul":
                    n = _free_size(kw.get("rhs"))
                    f = 4.0 if str(getattr(kw.get("rhs"), "dtype", "")) .endswith("float32") else 1.0
                    t += f * max(n, 64) / 2400.0 + 0.07
                else:
                    t += 0.06
            return t
        f = _free_size(o["calls"][0][2].get("out")) if o["calls"] else 64
        if e == "act":
            return 0.2 + f / 1200.0
        if e == "dve":
            return 0.08 + f / 960.0
        return 0.3 + f / 500.0

    def flush(self):
        ops = self.pending
        self.pending = []
        n = len(ops)
        if n == 0:
            return
        if not self.reorder or n < 3:
            for o in ops:
                self._emit(o)
            return
        lastw, readers = {}, {}
        preds = [set() for _ in range(n)]
        for i, o in enumerate(ops):
            for k in o["r"]:
                j = lastw.get(k)
                if j is not None:
                    preds[i].add(j)
            for k in o["w"]:
                j = lastw.get(k)
                if j is not None:
                    preds[i].add(j)
                for j in readers.get(k, ()):
                    preds[i].add(j)
            for k in o["r"]:
                readers.setdefault(k, []).append(i)
            for k in o["w"]:
                lastw[k] = i
                readers[k] = []
            preds[i].discard(i)
        succs = [[] for _ in range(n)]
        for i in range(n):
            for j in preds[i]:
                succs[j].append(i)
        cost = [self._cost(o) for o in ops]
        lat_out = [2.5 if o["kind"] == "dma" else 0.0 for o in ops]
        prio = [0.0] * n
        for i in range(n - 1, -1, -1):
            m = 0.0
            for s in succs[i]:
                m = max(m, prio[s] + self.LAT_X)
            prio[i] = cost[i] + lat_out[i] + m
        npred = [len(p) for p in preds]
        ready = [i for i in range(n) if npred[i] == 0]
        fin = [0.0] * n
        efree = {}
        order = []
        engs = [o["eng"] for o in ops]
        fam = [None] * n
        for i, o in enumerate(ops):
            if o["eng"] == "act" and o["kind"] == "op" and o["calls"]:
                fn_ = str(o["calls"][0][2].get("func", ""))
                if ("Exp" in fn_) or ("Ln" in fn_):
                    fam[i] = "E"
                elif ("Sigmoid" in fn_) or ("Tanh" in fn_) or ("Silu" in fn_):
                    fam[i] = "S"
        last_fam = [getattr(self, "_last_fam", None)]
        while ready:
            best, best_key = None, None
            for i in ready:
                e = engs[i]
                t = efree.get(e, 0.0)
                for j in preds[i]:
                    tj = fin[j] + lat_out[j] + (self.LAT_S if engs[j] == e else self.LAT_X)
                    if tj > t:
                        t = tj
                if fam[i] is not None and last_fam[0] is not None and fam[i] != last_fam[0]:
                    t += 1.3
                key = (t, -prio[i], i)
                if best_key is None or key < best_key:
                    best, best_key = i, key
            i = best
            ready.remove(i)
            t = best_key[0]
            fin[i] = t + cost[i]
            efree[engs[i]] = fin[i]
            if fam[i] is not None:
                last_fam[0] = fam[i]
                self._last_fam = fam[i]
            order.append(i)
            for s in succs[i]:
                npred[s] -= 1
                if npred[s] == 0:
                    ready.append(s)
        assert len(order) == n
        if getattr(self, "debug_sched", False) and n > 500:
            mk = max(fin)
            busy = {}
            for i in range(n):
                busy[engs[i]] = busy.get(engs[i], 0.0) + cost[i]
            print("WINDOW n=%d makespan=%.1f us busy=%s" % (n, mk, {k: round(v, 1) for k, v in busy.items()}))
            i = max(range(n), key=lambda k: fin[k])
            path = []
            while True:
                path.append(i)
                best, bt = None, -1.0
                for j in preds[i]:
                    tj = fin[j] + lat_out[j] + (self.LAT_S if engs[j] == engs[i] else self.LAT_X)
                    if tj > bt:
                        best, bt = j, tj
                start = fin[i] - cost[i]
                if best is None or bt < start - 1e-6:
                    prev = [k for k in order[:order.index(i)] if engs[k] == engs[i]]
                    if not prev:
                        break
                    i = prev[-1]
                    path.append(-1)
                else:
                    i = best
                if len(path) > 400:
                    break
            txt = []
            for k in reversed(path):
                if k == -1:
                    txt.append("|eng|")
                else:
                    txt.append("%s:%s@%.1f" % (engs[k], str(ops[k]["w"][:1]), fin[k]))
            print("CRIT:", " ".join(txt[:400]))
        for i in order:
            self._emit(ops[i])

    def _deps(self, r, w):
        need = {}
        for k in r:
            t = self.lastw.get(k)
            if t is not None:
                need[t[0]] = max(need.get(t[0], 0), t[1])
        for k in w:
            t = self.lastw.get(k)
            if t is not None:
                need[t[0]] = max(need.get(t[0], 0), t[1])
            for t in self.readers.get(k, ()):
                need[t[0]] = max(need.get(t[0], 0), t[1])
        return need

    def _wait(self, e, need, skip_self=False):
        wd = self.waited[e]
        for s, v in need.items():
            if skip_self and s == e:
                continue
            if wd.get(s, 0) < v:
                self.eng[e].wait_ge(self.semh[s], v)
                wd[s] = v
                self.ninst += 1

    def _commit(self, tok, r, w):
        for k in r:
            lst = self.readers.setdefault(k, [])
            lst[:] = [t for t in lst if t[0] != tok[0]]
            lst.append(tok)
        for k in w:
            self.lastw[k] = tok
            self.readers[k] = []

    def _emit(self, o):
        if o["kind"] == "dma":
            return self._emit_dma(o)
        e = o["eng"]
        need = self._deps(o["r"], o["w"])
        self._wait(e, need, skip_self=(e == "pe" and not o["sync"]))
        inst = None
        for (name, args, kw) in o["calls"]:
            inst = getattr(self.eng[e], name)(*args, **kw)
            self.ninst += 1
        self.cnt[e] += 1
        inst.then_inc(self.semh[e], 1)
        self._commit((e, self.cnt[e]), o["r"], o["w"])

    def _emit_dma(self, o):
        q, sem = o["eng"], o["sem"]
        if sem is None:
            if len(self.dma_sems) < 24:
                sem = self.new_dma_sem("d%d" % len(self.dma_sems))
                self.dma_sems.append(sem)
            else:
                sem = self.dma_sems[self.dma_rr % len(self.dma_sems)]
                self.dma_rr += 1
        need = self._deps(o["r"], o["w"])
        if self.cnt[sem] > 0:
            need[sem] = max(need.get(sem, 0), self.cnt[sem])
        self._wait(q, need)
        self.eng[q].dma_start(out=o["out"], in_=o["in_"]).then_inc(self.semh[sem], 16)
        self.cnt[sem] += 16
        self.ninst += 1
        self._commit((sem, self.cnt[sem]), o["r"], o["w"])


class Arena:
    def __init__(self, nc, nbytes, name="arena"):
        assert nbytes % 4 == 0
        self.t = nc.alloc_sbuf_tensor(name, [128, nbytes // 4], F32)
        self.off = 0
        self.cap = nbytes

    def alloc(self, shape, dt, at=None, name=None):
        esz = 4 if dt == F32 else 2
        n = 1
        for s in shape[1:]:
            n *= s
        nb = (n * esz + 31) // 32 * 32
        if at is None:
            at = self.off
            self.off += nb
            assert self.off <= self.cap, ("arena overflow", self.off, self.cap)
        if not hasattr(self, "reg"):
            self.reg = {}
        self.reg[name if name is not None else "anon%d" % len(self.reg)] = (at, list(shape), "f32" if dt == F32 else "bf16")
        v = self.t[:, at // 4:(at + nb) // 4]
        if dt != F32:
            v = v.bitcast(dt)
        v = v[:, 0:n]
        if len(shape) == 3:
            v = v.rearrange("p (a b) -> p a b", b=shape[2])
        elif len(shape) == 4:
            v = v.rearrange("p (a b c) -> p a b c", b=shape[2], c=shape[3])
        return v


def build_program(layers=DEPTH, npg=4, dbg=None, with_s=True, reorder=True):
    nc = bass.Bass("TRN2", target_bir_lowering=False)
    S = Sched(nc, reorder=reorder)
    nseq_tok = 512 * npg

    xp = nc.dram_tensor("xp", [nseq_tok, D], F32, kind="ExternalInput").ap()
    xs = nc.dram_tensor("xs", [2 * SLEN, D], F32, kind="ExternalInput").ap()
    cconv = nc.dram_tensor("cconv", [DEPTH, 2, 30, 512], F32, kind="ExternalInput").ap()
    cpool = nc.dram_tensor("cpool", [DEPTH, 2, 15, 512], F32, kind="ExternalInput").ap()
    cshift = nc.dram_tensor("cshift", [DEPTH, 2, NQ, 128], F32, kind="ExternalInput").ap()
    cwkv = nc.dram_tensor("cwkv", [DEPTH, 2, HEADS, 64, 64], F32, kind="ExternalInput").ap()
    wblk = nc.dram_tensor("wblk", [layers, NBLK, 128, SLOT], F32, kind="ExternalInput").ap()
    vecs_d = nc.dram_tensor("vecs", [128, NVEC], F32, kind="ExternalInput").ap()
    yp = nc.dram_tensor("yp", [nseq_tok, D], F32, kind="ExternalOutput").ap()
    ys = nc.dram_tensor("ys", [2 * SLEN, D], F32, kind="ExternalOutput").ap()
    nconv = nc.dram_tensor("nconv", [DEPTH, 3, 30, 512], F32, kind="ExternalOutput").ap()
    npool = nc.dram_tensor("npool", [DEPTH, 3, 15, 512], F32, kind="ExternalOutput").ap()
    nshift = nc.dram_tensor("nshift", [DEPTH, 3, NQ, 128], F32, kind="ExternalOutput").ap()
    nwkv = nc.dram_tensor("nwkv", [DEPTH, 3, HEADS, 64, 64], F32, kind="ExternalOutput").ap()

    A = Arena(nc, 212736)
    vecs = A.alloc([128, NVEC], F32)
    omu = A.alloc([128, DEPTH, NQ], F32)
    ident_f = A.alloc([128, 128], F32)
    ident_b = A.alloc([128, 128], BF16)
    ones_b = A.alloc([128, 128], BF16)
    bones_b = A.alloc([128, 128], BF16)
    bones_f = A.alloc([128, 128], F32)
    onesD_f = A.alloc([128, 128], F32)
    m_su = A.alloc([128, 128], F32)
    m_ui = A.alloc([128, 128], F32)
    m_sl = A.alloc([128, 128], F32)
    cmask = {64: A.alloc([128, 64], BF16), 128: A.alloc([128, 512], BF16)}
    invc_first = A.alloc([128, 4, 16], F32)
    st = {}
    for l in range(DEPTH):
        st[("P", l)] = dict(u=A.alloc([128, 4, 30], F32), p=A.alloc([128, 4, 15], F32),
                            q=A.alloc([128, NQ], F32), S=A.alloc([128, PAIRS, 64], F32))
    for sq in ("S0", "S1"):
        d_ = dict(u=A.alloc([128, 4, 30], F32), p=A.alloc([128, 4, 15], F32),
                  q=A.alloc([128, NQ], F32), S=A.alloc([128, PAIRS, 64], F32))
        for l in range(DEPTH):
            st[(sq, l)] = d_
    xT = A.alloc([128, KC, 512], F32, name='xT')
    hT = A.alloc([128, KC, 512], BF16, name='hT')
    cat = A.alloc([128, KC, 512], BF16, name='cat')
    wring = A.alloc([128, NSLOT, SLOT], BF16)
    wsmall = A.alloc([128, SLOT], BF16)
    wpool = A.alloc([128, 512], BF16)
    rstd = A.alloc([128, 512], F32)
    sqb2 = A.alloc([128, 2, 512], BF16)
    epi1 = A.alloc([128, 512], F32)
    epi2 = A.alloc([128, 512], F32)
    base_off = A.off

    def mixer_bufs(Wm):
        b = {}
        o0 = A.off
        b["ubuf"] = A.alloc([128, 4, 30 + Wm], F32, name="mb%d_ubuf" % Wm)
        b["ubf"] = A.alloc([128, 4, 30 + Wm], BF16)
        b["hconv"] = A.alloc([128, 4, Wm], F32)
        o1 = A.off
        b["pbuf"] = A.alloc([128, 4, 15 + Wm], F32, at=o0)
        b["dpool"] = A.alloc([128, 4, Wm], BF16, at=o0 + (4 * (15 + Wm) * 4 + 31) // 32 * 32)
        assert o0 + (4 * (15 + Wm) * 4 + 31) // 32 * 32 + 4 * Wm * 2 <= o1
        for n in ("t0", "t1", "t2", "t3", "t4", "t5", "t6", "t7", "t8", "t9", "t10", "t11"):
            b["off_" + n] = A.off
            b[n] = A.alloc([128, Wm + 16], F32, name="mb%d_%s" % (Wm, n))
        for n in ("b0", "b1", "b2", "b3", "b4", "b5"):
            b[n] = A.alloc([128, Wm], BF16, name="mb%d_%s" % (Wm, n))
        b["AR"] = A.alloc([128, 2 * Wm], BF16, name="mb%d_AR" % Wm)
        b["lora"] = A.alloc([128, 2, Wm], BF16, name="mb%d_lora" % Wm)
        b["gl"] = b["t1"]
        return b
    mb = [mixer_bufs(512), mixer_bufs(64)]
    diag = A.alloc([128, 31, 128], BF16, at=mb[0]['off_t8'])
    tm = {}
    for h in range(2):
        tm[("S1m", h)] = A.alloc([128, 4, 2, 128], BF16, name="tm_S1m%d" % h)
        tm[("S2m", h)] = A.alloc([128, 4, 2, 128], BF16, name="tm_S2m%d" % h)
        tm[("L", h)] = A.alloc([128, 4, 128], BF16, name="tm_L%d" % h)
        tm[("X", h)] = A.alloc([128, 4, 128], BF16, name="tm_X%d" % h)
    tm["KBV"] = A.alloc([128, 4, 3, 128], BF16, name="tm_KBV")
    tm["Psb"] = A.alloc([128, 2, 64], BF16, name="tm_Psb")
    tm["Usb"] = A.alloc([128, 2, 64], BF16, name="tm_Usb")
    tm["STb"] = A.alloc([128, 64], BF16, name="tm_STb")
    tm["gC"] = A.alloc([128, 8], F32, name="tm_gC")
    identb4 = A.alloc([128, 4, 128], BF16)
    mc2 = A.alloc([128, 2, 2, 128], BF16)
    m_sl4 = A.alloc([128, 4, 128], BF16)
    mc = {64: A.alloc([128, 2, 64], F32), 128: A.alloc([128, 2, 128], F32)}
    stage = mb[0]["hconv"].rearrange("p a b -> p (a b)")
    stage2 = stage[:, 0:1024]
    stage3 = stage[:, 1024:1664]
    mix_end = A.off
    act = A.alloc([128, FC, 512], BF16, at=base_off)
    assert base_off + FC * 512 * 2 <= A.cap
    A.off = max(mix_end, base_off + FC * 512 * 2 + 4096)
    ftmp = [A.alloc([128, 512], F32, at=base_off + FC * 512 * 2), A.alloc([128, 512], F32, at=base_off + FC * 512 * 2 + 2048)]
    print("SBUF used", A.off, "of", A.cap)

    psb = [nc.alloc_psum_tensor("ps%d" % i, [128, 512], F32) for i in range(8)]
    bank_rr = {"big": 0, "small": 0}

    def bank(pool):
        if pool == "big":
            i = bank_rr["big"] % 3
            bank_rr["big"] += 1
            return i
        if pool == "p1":
            i = (3, 4, 5, 6, 7)[bank_rr.setdefault("p1", 0) % 5]
            bank_rr["p1"] += 1
            return i
        if pool == "y":
            return 3
        if pool == "state":
            return 7
        i = 4 + bank_rr["small"] % 3
        bank_rr["small"] += 1
        return i

    psap = [t[:, :] for t in psb]

    def PS(i):
        return psap[i]

    wsem = [S.new_dma_sem("w%d" % i) for i in range(NSLOT)]
    wsem_small = S.new_dma_sem("wsm")
    wq = {"next": 0, "issued": 0, "order": []}

    def w_issue_upto(n):
        while wq["issued"] < min(n, len(wq["order"])):
            i = wq["issued"]
            (l, b, ncols) = wq["order"][i]
            slot = i % NSLOT
            S.dma("pool", wring[:, slot, 0:ncols], wblk[l, b, :, 0:ncols], w=[("w", slot)], sem=wsem[slot])
            wq["issued"] += 1

    def w_next():
        i = wq["next"]
        wq["next"] += 1
        w_issue_upto(i + NSLOT - 1)
        return i % NSLOT

    def blk_in(cc):
        return 2 + cc
    def blk_out(n):
        return 2 + 38 + n
    def blk_gate(f):
        return 2 + 38 + 16 + 2 * f
    def blk_up(f):
        return 2 + 38 + 16 + 2 * f + 1
    def blk_down(n, j):
        return 2 + 38 + 16 + 88 + 3 * n + j
    IN_ORDER = [4, 0, 5, 1, 6, 2, 7, 3, 8, 9, 10, 11, 36, 37]
    for p_ in range(PAIRS):
        IN_ORDER += [12 + p_, 20 + p_, 28 + p_]

    def layer_order(l):
        o = [(l, blk_in(cc), SLOT) for cc in IN_ORDER]
        o += [(l, blk_out(n), SLOT) for n in range(16)]
        for f in range(FC):
            o += [(l, blk_gate(f), SLOT), (l, blk_up(f), SLOT)]
        for n in range(16):
            o += [(l, blk_down(n, 0), SLOT), (l, blk_down(n, 1), SLOT), (l, blk_down(n, 2), 12 * 128)]
        return o

    def pool_op(fn, r=(), w=()):
        return S.op("pool", fn, r, w)

    S.dma("sp", vecs, vecs_d[:, :], w=["vecs"])
    pool_op(lambda e: e.memset(ident_f, 1.0), w=["c_if"])
    pool_op(lambda e: e.affine_select(out=ident_f, in_=ident_f, pattern=[[-1, 128]], compare_op=ALU.is_equal,
                                      fill=0.0, base=0, channel_multiplier=1), r=["c_if"], w=["c_if"])
    S.op("dve", lambda e: e.tensor_copy(out=ident_b, in_=ident_f), r=["c_if"], w=["c_ib"])
    for i4 in range(4):
        S.op("dve", lambda e, i4=i4: e.tensor_copy(out=identb4[:, i4, :], in_=ident_f), r=["c_if"], w=["c_ib4"])
    S.op("dve", lambda e: e.memset(ones_b, 1.0), w=["c_ones"])
    S.op("dve", lambda e: e.memset(onesD_f, 1.0 / 512.0), w=["c_onesD"])
    S.op("dve", lambda e: e.memset(bones_b, 0.0), w=["c_bones"])
    S.op("dve", lambda e: e.memset(bones_b[0:64, 0:64], 1.0), w=["c_bones"])
    S.op("dve", lambda e: e.memset(bones_b[64:128, 64:128], 1.0), w=["c_bones"])
    S.op("dve", lambda e: e.memset(bones_f, 0.0), w=["c_bonesf"])
    S.op("dve", lambda e: e.memset(bones_f[0:64, 0:64], 1.0 / 64.0), w=["c_bonesf"])
    S.op("dve", lambda e: e.memset(bones_f[64:128, 64:128], 1.0 / 64.0), w=["c_bonesf"])
    for (m, base, cm, step) in ((m_su, -1, -1, 1), (m_ui, 0, -1, 1), (m_sl, -1, 1, -1)):
        pool_op(lambda e, m=m: e.memset(m, 1.0), w=["c_masks"])
        pool_op(lambda e, m=m, base=base, cm=cm, step=step: e.affine_select(
            out=m, in_=m, pattern=[[step, 128]], compare_op=ALU.is_ge, fill=0.0, base=base,
            channel_multiplier=cm), r=["c_masks"], w=["c_masks"])
    for i2 in range(2):
        S.op("dve", lambda e, i2=i2: e.tensor_copy(out=mc2[:, i2, 0, :], in_=m_su), r=["c_masks"], w=["c_mc2"])
        S.op("dve", lambda e, i2=i2: e.tensor_copy(out=mc2[:, i2, 1, :], in_=m_ui), r=["c_masks"], w=["c_mc2"])
    for i4 in range(4):
        S.op("dve", lambda e, i4=i4: e.tensor_copy(out=m_sl4[:, i4, :], in_=m_sl), r=["c_masks"], w=["c_msl4"])
    for C in (64, 128):
        S.op("dve", lambda e, C=C: e.tensor_copy(out=mc[C][:, 0, :], in_=m_su[:, 0:C]), r=["c_masks"], w=["c_mc"])
        S.op("dve", lambda e, C=C: e.tensor_copy(out=mc[C][:, 1, :], in_=m_ui[:, 0:C]), r=["c_masks"], w=["c_mc"])
        S.op("dve", lambda e, C=C: e.memset(cmask[C], 1.0), w=["c_cmask"])
        S.op("dve", lambda e, C=C: e.memset(cmask[C].rearrange("p (a b) -> p a b", b=C)[:, :, 0:1], 0.0),
             r=["c_cmask"], w=["c_cmask"])
    pool_op(lambda e: e.iota(out=invc_first[:, 0, :], pattern=[[1, 16]], base=1, channel_multiplier=0,
                             allow_small_or_imprecise_dtypes=True), w=["c_invc"])
    for gi, wdw in enumerate(POOL_WINDOWS):
        if gi > 0:
            S.op("dve", lambda e, gi=gi: e.tensor_copy(out=invc_first[:, gi, :], in_=invc_first[:, 0, :]),
                 r=["c_invc"], w=["c_invc%d" % gi])
    for gi, wdw in enumerate(POOL_WINDOWS):
        S.op("dve", lambda e, gi=gi, wdw=wdw: e.tensor_scalar(out=invc_first[:, gi, :], in0=invc_first[:, gi, :],
                                                              scalar1=float(wdw), scalar2=None, op0=ALU.min),
             r=["c_invc", "c_invc%d" % gi], w=["c_invc%d" % gi] + (["c_invc"] if gi == 0 else []))
        S.op("dve", lambda e, gi=gi: e.reciprocal(out=invc_first[:, gi, :], in_=invc_first[:, gi, :]),
             r=["c_invc%d" % gi], w=["c_invc%d" % gi] + (["c_invc"] if gi == 0 else []))
    for l in range(DEPTH):
        o = l * VL + VO["mu"]
        S.op("dve", lambda e, l=l, o=o: e.tensor_scalar(out=omu[:, l, :], in0=vecs[:, o:o + NQ], scalar1=-1.0,
                                                        scalar2=1.0, op0=ALU.mult, op1=ALU.add),
             r=["vecs"], w=["omu"])
    CONST_KEYS = ["vecs", "omu", "c_if", "c_ib", "c_ones", "c_onesD", "c_bones", "c_bonesf", "c_masks",
                  "c_cmask", "c_onesrow", "c_invc", "c_invc1", "c_invc2", "c_invc3"]
    S.barrier()

    def V(l, name, c0=0, n=1):
        o = l * VL + VO[name] + c0
        return vecs[:, o:o + n]

    def rmsnorm_to(dst_fn, gw, gcol, key_out, l_vec_off, dst_is_bf=True):
        b = bank("small")
        fns = []
        for k in range(KC):
            S.op("act", lambda e, k=k: e.activation(out=sqb2[:, k % 2, 0:gw], in_=xT[:, k, 0:gw], func=AF.Square),
                 r=["xT"], w=[("sqb", k % 2)])
            S.pe_group([lambda e, k=k: e.matmul(PS(b)[:, 0:gw], lhsT=ones_b, rhs=sqb2[:, k % 2, 0:gw],
                                                 start=(k == 0), stop=(k == KC - 1))],
                       r=[("sqb", k % 2)], w=[("ps", b)])
        S.op("act", lambda e: e.activation(out=rstd[:, 0:gw], in_=PS(b)[:, 0:gw], func=AF.Ln,
                                           bias=RMS_EPS, scale=1.0 / D), r=[("ps", b)], w=["rstd"])
        S.op("act", lambda e: e.activation(out=rstd[:, 0:gw], in_=rstd[:, 0:gw], func=AF.Exp, scale=-0.5),
             r=["rstd"], w=["rstd"])
        for k in range(KC):
            S.op("dve", lambda e, k=k: e.scalar_tensor_tensor(
                out=dst_fn(k), in0=xT[:, k, 0:gw], scalar=vecs[:, l_vec_off + k:l_vec_off + k + 1],
                in1=rstd[:, 0:gw], op0=ALU.mult, op1=ALU.mult), r=["xT", "rstd"], w=[key_out])

    def proj(slot, src, gw, key_src, kchunks=KC, b=None):
        if b is None:
            b = bank("big")
        wv = wring[:, slot, :].rearrange("p (k n) -> p k n", n=128)
        fns = [lambda e, k=k: e.matmul(PS(b)[:, 0:gw], lhsT=wv[:, k, :], rhs=src[:, k, 0:gw],
                                       start=(k == 0), stop=(k == kchunks - 1)) for k in range(kchunks)]
        S.pe_group(fns, r=[("w", slot), key_src], w=[("ps", b)])
        return b

    def shifted_from_psum(l, bq, qc, p, dst, key_dst, scratch, key_scr):
        W, off, bi = p["W"], p["off"], p["bi"]
        sk = (p["seq"], l)
        sd = st[sk]
        S.op("act", lambda e: e.activation(out=scratch[:, 0:W], in_=PS(bq)[:, off:off + W], func=AF.Identity,
                                           scale=omu[:, l, qc:qc + 1]), r=[("ps", bq)], w=[key_scr])
        S.op("dve", lambda e: e.scalar_tensor_tensor(
            out=dst[:, 1:W], in0=PS(bq)[:, off:off + W - 1], scalar=V(l, "mu", qc), in1=scratch[:, 1:W],
            op0=ALU.mult, op1=ALU.add), r=[("ps", bq), key_scr], w=[key_dst])
        S.op("dve", lambda e: e.scalar_tensor_tensor(
            out=dst[:, 0:1], in0=sd["q"][:, qc:qc + 1], scalar=V(l, "mu", qc), in1=scratch[:, 0:1],
            op0=ALU.mult, op1=ALU.add), r=[("st_q",) + sk, key_scr], w=[key_dst])
        S.op("act", lambda e: e.activation(out=sd["q"][:, qc:qc + 1], in_=PS(bq)[:, off + W - 1:off + W],
                                           func=AF.Copy), r=[("ps", bq), key_dst], w=[("st_q",) + sk])

    def wkv_pair(l, pr, p, br, bk, bv_):
        B = mb[p["bi"]]
        W, off, bi, C = p["W"], p["off"], p["bi"], p["C"]
        nch = W // C
        sk = (p["seq"], l)
        sd = st[sk]
        T = lambda n: B[n][:, 0:W]
        K = lambda n: (n, bi)
        c3 = lambda ap: ap.rearrange("p (a c) -> p a c", c=C)
        shifted_from_psum(l, br, pr, p, B["t1"], K("t1"), B["t0"], K("t0"))
        shifted_from_psum(l, bk, 8 + pr, p, B["t2"], K("t2"), B["t0"], K("t0"))
        shifted_from_psum(l, bv_, 16 + pr, p, B["t3"], K("t3"), B["t0"], K("t0"))
        r_, k_, v_ = T("t1"), T("t2"), T("t3")
        bw = bank("small")
        S.pe_group([lambda e: e.matmul(PS(bw)[:, 0:W], lhsT=wsmall[0:64, pr * 128:(pr + 1) * 128],
                                       rhs=B["lora"][0:64, 0, 0:W], start=True, stop=True)],
                   r=["wsmall", K("lora")], w=[("ps", bw)])
        lw = T("t4")
        S.op("act", lambda e: e.activation(out=lw, in_=PS(bw)[:, 0:W], func=AF.Sigmoid,
                                           bias=V(l, "w0", pr), scale=1.0), r=[("ps", bw)], w=[K("t4")])
        ba = bank("small")
        S.pe_group([lambda e: e.matmul(PS(ba)[:, 0:W], lhsT=wsmall[64:128, pr * 128:(pr + 1) * 128],
                                       rhs=B["lora"][64:128, 0, 0:W], start=True, stop=True)],
                   r=["wsmall", K("lora")], w=[("ps", ba)])
        a_ = T("t5")
        S.op("act", lambda e: e.activation(out=a_, in_=PS(ba)[:, 0:W], func=AF.Sigmoid,
                                           bias=V(l, "a0", pr), scale=1.0), r=[("ps", ba)], w=[K("t5")])
        bgp = bank("small")
        S.pe_group([lambda e: e.matmul(PS(bgp)[:, 0:W], lhsT=wsmall[0:64, 1024 + pr * 128:1024 + (pr + 1) * 128],
                                       rhs=B["lora"][0:64, 1, 0:W], start=True, stop=True)],
                   r=["wsmall", K("lora")], w=[("ps", bgp)])
        g_ = T("t6")
        S.op("act", lambda e: e.activation(out=g_, in_=PS(bgp)[:, 0:W], func=AF.Copy), r=[("ps", bgp)], w=[K("t6")])
        kk = T("t7")
        S.op("dve", lambda e: e.tensor_scalar(out=kk, in0=k_, scalar1=V(l, "kk", pr), scalar2=None, op0=ALU.mult),
             r=[K("t2")], w=[K("t7")])
        S.op("act", lambda e: e.activation(out=T("b0"), in_=kk, func=AF.Square), r=[K("t7")], w=[K("b0")])
        bs = bank("small")
        S.pe_group([lambda e: e.matmul(PS(bs)[:, 0:W], lhsT=bones_b, rhs=T("b0"), start=True, stop=True)],
                   r=[K("b0")], w=[("ps", bs)])
        nrm = T("t8")
        S.op("dve", lambda e: e.tensor_scalar(out=nrm, in0=PS(bs)[:, 0:W], scalar1=1e-24, scalar2=None, op0=ALU.max),
             r=[("ps", bs)], w=[K("t8")])
        S.op("act", lambda e: e.activation(out=nrm, in_=nrm, func=AF.Ln), r=[K("t8")], w=[K("t8")])
        S.op("act", lambda e: e.activation(out=nrm, in_=nrm, func=AF.Exp, scale=-0.5), r=[K("t8")], w=[K("t8")])
        S.op("dve", lambda e: e.tensor_tensor(out=kk, in0=kk, in1=nrm, op=ALU.mult), r=[K("t7"), K("t8")], w=[K("t7")])
        bvec = T("t8")
        S.op("pool", lambda e: e.tensor_tensor(out=bvec, in0=kk, in1=a_, op=ALU.mult), r=[K("t7"), K("t5")], w=[K("t8")])
        kp = T("t9")
        S.op("dve", lambda e: e.tensor_scalar(out=kp, in0=a_, scalar1=-1.0, scalar2=V(l, "ka", pr), op0=ALU.add,
                                              op1=ALU.mult), r=[K("t5")], w=[K("t9")])
        S.op("dve", lambda e: e.scalar_tensor_tensor(out=kp, in0=kp, scalar=1.0, in1=k_, op0=ALU.add, op1=ALU.mult),
             r=[K("t9"), K("t2")], w=[K("t9")])
        S.op("dve", lambda e: e.scalar_tensor_tensor(out=T("b0"), in0=r_, scalar=V(l, "rk", pr), in1=kp, op0=ALU.mult,
                                                     op1=ALU.mult), r=[K("t1"), K("t9")], w=[K("b0")])
        bb = bank("small")
        S.pe_group([lambda e: e.matmul(PS(bb)[:, 0:W], lhsT=bones_b, rhs=T("b0"), start=True, stop=True)],
                   r=[K("b0")], w=[("ps", bb)])
        bonus = T("t10")
        S.op("dve", lambda e: e.tensor_tensor(out=bonus, in0=PS(bb)[:, 0:W], in1=v_, op=ALU.mult),
             r=[("ps", bb), K("t3")], w=[K("t10")])
        S.op("dve", lambda e: e.tensor_scalar(out=lw, in0=lw, scalar1=LW_SCALE, scalar2=None, op0=ALU.mult),
             r=[K("t4")], w=[K("t4")])
        cl = T("t11")
        S.op("dve", lambda e: e.tensor_tensor_scan(out=cl, data0=cmask[C][:, 0:W], data1=lw, initial=0.0,
                                                   op0=ALU.mult, op1=ALU.add), r=[K("t4")], w=[K("t11")])
        gC = tm["gC"]
        S.op("act", lambda e: e.activation(out=gC[:, 0:nch], in_=c3(cl)[:, :, C - 1], func=AF.Exp),
             r=[K("t11")], w=["gC"])
        e_pos = T("t0")
        S.op("act", lambda e: e.activation(out=e_pos, in_=cl, func=AF.Exp), r=[K("t11")], w=[K("t0")])
        AR = B["AR"][:, 0:2 * W].rearrange("p (a two c) -> p a two c", two=2, c=C)
        S.op("dve", lambda e: e.tensor_tensor(out=AR[:, :, 1, :], in0=c3(r_), in1=c3(e_pos), op=ALU.mult),
             r=[K("t1"), K("t0")], w=[K("AR")])
        S.op("pool", lambda e: e.tensor_tensor(out=lw, in0=cl, in1=lw, op=ALU.subtract), r=[K("t11"), K("t4")],
             w=[K("t4")])
        S.op("act", lambda e: e.activation(out=lw, in_=lw, func=AF.Exp), r=[K("t4")], w=[K("t4")])
        S.op("dve", lambda e: e.scalar_tensor_tensor(out=AR[:, :, 0, :], in0=c3(kk), scalar=-1.0, in1=c3(lw),
                                                     op0=ALU.mult, op1=ALU.mult), r=[K("t7"), K("t4")], w=[K("AR")])
        e_neg = T("t0")
        S.op("act", lambda e: e.activation(out=e_neg, in_=cl, func=AF.Exp, scale=-1.0), r=[K("t11"), K("AR")],
             w=[K("t0")])
        kt, bt, kh, bh, vb = T("b1"), T("b2"), T("b3"), T("b4"), T("b5")
        S.op("pool", lambda e: e.tensor_tensor(out=kp, in0=kp, in1=e_neg, op=ALU.mult), r=[K("t9"), K("t0")], w=[K("t9")])
        S.op("act", lambda e: e.activation(out=kt, in_=kp, func=AF.Copy), r=[K("t9")], w=[K("b1")])
        S.op("dve", lambda e: e.tensor_tensor(out=bvec, in0=bvec, in1=e_neg, op=ALU.mult), r=[K("t8"), K("t0")],
             w=[K("t8")])
        S.op("act", lambda e: e.activation(out=bt, in_=bvec, func=AF.Copy), r=[K("t8")], w=[K("b2")])
        for ch in range(nch):
            cs = slice(ch * C, (ch + 1) * C)
            S.op("act", lambda e, cs=cs, ch=ch: e.activation(out=kh[:, cs], in_=kp[:, cs], func=AF.Identity,
                                                             scale=gC[:, ch:ch + 1]), r=[K("t9"), "gC"], w=[K("b3")])
            S.op("act", lambda e, cs=cs, ch=ch: e.activation(out=bh[:, cs], in_=bvec[:, cs], func=AF.Identity,
                                                             scale=gC[:, ch:ch + 1]), r=[K("t8"), "gC"], w=[K("b4")])
        S.op("act", lambda e: e.activation(out=vb, in_=v_, func=AF.Copy), r=[K("t3")], w=[K("b5")])

        by = bank("y")
        STf = sd["S"][:, pr, :]
        STb = tm["STb"]
        nupd = {64: 5, 128: 6}[C]
        HS = [slice(0, 64), slice(64, 128)]
        KBV = tm["KBV"]
        ARc = lambda ch: B["AR"][:, ch * 2 * C:(ch + 1) * 2 * C]
        CS = lambda ch: slice(ch * C, (ch + 1) * C)
        for ch in range(nch):
            btp = bank("p1")
            ptv = PS(btp)[:, 0:192].bitcast(BF16).rearrange("p (a c) -> p a c", c=128)
            S.pe_group([lambda e, src_=src_, i=i, ch=ch: e.transpose(out=ptv[0:C, i, :], in_=src_[:, CS(ch)], identity=ident_b)
                        for i, src_ in enumerate((kh, bh, vb))], r=[K("b3"), K("b4"), K("b5")], w=[("ps", btp)])
            S.op("act", lambda e, ch=ch: e.activation(out=KBV[0:C, ch], in_=ptv[0:C], func=AF.Copy),
                 r=[("ps", btp)], w=["KBV"])
        S1 = [tm[("S1m", h)] for h in range(2)]
        S2 = [tm[("S2m", h)] for h in range(2)]
        Lt = [tm[("L", h)] for h in range(2)]
        Xt = [tm[("X", h)] for h in range(2)]
        for c0 in range(0, nch, 2):
            cn = min(2, nch - c0)
            for h in range(2):
                hs = HS[h]
                for (lhs, dst, key, eng) in ((kt, S1[h], ("S1m", h), "dve"), (bt, S2[h], ("S2m", h), "dve")):
                    b1 = bank("p1")
                    S.pe_group([lambda e, ch=ch, j=j, lhs=lhs, b1=b1: e.matmul(PS(b1)[0:C, j * 2 * C:(j + 1) * 2 * C], lhsT=lhs[hs, CS(ch)],
                                                                               rhs=ARc(ch)[hs, :], start=True, stop=True)
                                for j, ch in enumerate(range(c0, c0 + cn))],
                               r=[K("b1"), K("b2"), K("AR")], w=[("ps", b1)])
                    if C == 128 and cn == 2:
                        S.op(eng, lambda e, dst=dst, b1=b1, c0=c0: e.tensor_tensor(
                            out=dst[0:C, c0:c0 + 2, :, :],
                            in0=PS(b1)[0:C, 0:512].rearrange("p (j a c) -> p j a c", j=2, a=2),
                            in1=mc2[0:C], op=ALU.mult), r=[("ps", b1)], w=[key])
                    else:
                        for j, ch in enumerate(range(c0, c0 + cn)):
                            S.op(eng, lambda e, ch=ch, j=j, dst=dst, b1=b1: e.tensor_tensor(
                                out=dst[0:C, ch, :, 0:C], in0=PS(b1)[0:C, j * 2 * C:(j + 1) * 2 * C].rearrange("p (a c) -> p a c", c=C),
                                in1=mc[C][0:C], op=ALU.mult), r=[("ps", b1)], w=[key])
        for h in range(2):
            hs = HS[h]
            b3 = bank("p1")
            S.pe_group([lambda e, ch=ch, b3=b3: e.matmul(PS(b3)[0:C, ch * C:(ch + 1) * C], lhsT=ARc(ch)[hs, 0:C], rhs=bt[hs, CS(ch)],
                                                          start=True, stop=True) for ch in range(nch)],
                       r=[K("b2"), K("AR")], w=[("ps", b3)])
            if C == 128:
                S.op("dve", lambda e, b3=b3, h=h: e.tensor_tensor(out=Lt[h][0:C, 0:nch, :],
                                                                  in0=PS(b3)[0:C, 0:nch * C].rearrange("p (a c) -> p a c", c=C),
                                                                  in1=m_sl4[0:C, 0:nch, :], op=ALU.mult),
                     r=[("ps", b3)], w=[("L", h)])
            else:
                for ch in range(nch):
                    S.op("dve", lambda e, ch=ch, b3=b3, h=h: e.tensor_tensor(out=Lt[h][0:C, ch, 0:C], in0=PS(b3)[0:C, ch * C:(ch + 1) * C],
                                                                             in1=m_sl[0:C, 0:C], op=ALU.mult),
                         r=[("ps", b3)], w=[("L", h)])
            S.op("pool", lambda e, h=h: e.tensor_tensor(out=Xt[h][0:C, 0:nch, 0:C], in0=S2[h][0:C, 0:nch, 0, 0:C],
                                                        in1=identb4[0:C, 0:nch, 0:C], op=ALU.add),
                 r=[("S2m", h)], w=[("X", h)])
        c3v = lambda ap: ap[0:C, 0:nch * C].rearrange("p (a c) -> p a c", c=C)
        for u in range(nupd):
            last = (u == nupd - 1)
            bl, bn = {}, {}
            for h in range(2):
                bl[h] = bank("p1")
                S.pe_group([lambda e, ch=ch, h=h: e.matmul(PS(bl[h])[0:C, ch * C:(ch + 1) * C], lhsT=S2[h][0:C, ch, 0, 0:C],
                                                          rhs=Lt[h][0:C, ch, 0:C], start=True, stop=True) for ch in range(nch)],
                           r=[("L", h), ("S2m", h)], w=[("ps", bl[h])])
                if not last:
                    bn[h] = bank("p1")
                    S.pe_group([lambda e, ch=ch, h=h: e.matmul(PS(bn[h])[0:C, ch * C:(ch + 1) * C], lhsT=Lt[h][0:C, ch, 0:C],
                                                              rhs=S2[h][0:C, ch, 0, 0:C], start=True, stop=True) for ch in range(nch)],
                               r=[("L", h), ("S2m", h)], w=[("ps", bn[h])])
            for h in range(2):
                S.op("act", lambda e, h=h: e.activation(out=Lt[h][0:C, 0:nch, 0:C], in_=c3v(PS(bl[h])), func=AF.Copy),
                     r=[("ps", bl[h])], w=[("L", h)])
                if not last:
                    S.op("act", lambda e, h=h: e.activation(out=S2[h][0:C, 0:nch, 0, 0:C], in_=c3v(PS(bn[h])), func=AF.Copy),
                         r=[("ps", bn[h])], w=[("S2m", h)])
            bx = {}
            for h in range(2):
                bx[h] = bank("p1")
                S.pe_group([lambda e, ch=ch, h=h: e.matmul(PS(bx[h])[0:C, ch * C:(ch + 1) * C], lhsT=Lt[h][0:C, ch, 0:C],
                                                          rhs=Xt[h][0:C, ch, 0:C], start=True, stop=True) for ch in range(nch)],
                           r=[("L", h), ("X", h)], w=[("ps", bx[h])])
            for h in range(2):
                S.op("dve", lambda e, h=h: e.tensor_tensor(out=Xt[h][0:C, 0:nch, 0:C], in0=c3v(PS(bx[h])),
                                                           in1=Xt[h][0:C, 0:nch, 0:C], op=ALU.add),
                     r=[("ps", bx[h]), ("X", h)], w=[("X", h)])

        S.op("act", lambda e: e.activation(out=STb, in_=STf, func=AF.Copy), r=[("st_S",) + sk], w=["STb"])
        Psb, Usb = tm["Psb"], tm["Usb"]
        bS = bank("state")
        for ch in range(nch):
            cs = CS(ch)
            VT = KBV[0:C, ch, 2, :]
            KH = KBV[0:C, ch, 0, :]
            BH = KBV[0:C, ch, 1, :]
            bP = bank("small")
            for h in range(2):
                hs = HS[h]
                rowsplit = (C == 64 and h == 1)
                S.pe_group([lambda e: e.matmul(PS(bP)[0:C, h * 64:h * 64 + 64], lhsT=ARc(ch)[hs, 0:C], rhs=STb[hs, :], start=True, stop=False)],
                           r=[K("AR"), "STb"], w=[("ps", bP)], pe_sync=(C == 64))
                S.pe_group([lambda e: e.matmul(PS(bP)[0:C, h * 64:h * 64 + 64], lhsT=S1[h][0:C, ch, 0, 0:C], rhs=VT[:, hs], start=False, stop=True)],
                           r=[("S1m", h), "KBV"], w=[("ps", bP)], pe_sync=(C == 64))
            S.op("act", lambda e: e.activation(out=Psb[0:C], in_=PS(bP)[0:C, 0:128].rearrange("p (a c) -> p a c", c=64), func=AF.Copy),
                 r=[("ps", bP)], w=["Psb"])
            bU = bank("small")
            for h in range(2):
                S.pe_group([lambda e: e.matmul(PS(bU)[0:C, h * 64:h * 64 + 64], lhsT=Xt[h][0:C, ch, 0:C], rhs=Psb[0:C, h, :], start=True, stop=True)],
                           r=[("X", h), "Psb"], w=[("ps", bU)], pe_sync=(C == 64))
            S.op("dve", lambda e: e.tensor_copy(out=Usb[0:C], in_=PS(bU)[0:C, 0:128].rearrange("p (a c) -> p a c", c=64)),
                 r=[("ps", bU)], w=["Usb"])
            for h in range(2):
                hs = HS[h]
                S.pe_group([lambda e: e.matmul(PS(bS)[hs, 0:64], lhsT=BH[:, hs], rhs=Usb[0:C, h, :], start=True, stop=False),
                            lambda e: e.matmul(PS(bS)[hs, 0:64], lhsT=KH[:, hs], rhs=VT[:, hs], start=False, stop=True)],
                           r=["KBV", "Usb"], w=[("ps", bS)], pe_sync=(C == 64))
            for h in range(2):
                hs = HS[h]
                S.pe_group([lambda e: e.matmul(PS(by)[hs, cs], lhsT=STb[hs, :], rhs=ARc(ch)[hs, C:2 * C], start=True, stop=False)],
                           r=["STb", K("AR")], w=[("ps", by)], pe_sync=(C == 64))
                S.pe_group([lambda e: e.matmul(PS(by)[hs, cs], lhsT=Usb[0:C, h, :], rhs=S2[h][0:C, ch, 1, 0:C], start=False, stop=False),
                            lambda e: e.matmul(PS(by)[hs, cs], lhsT=VT[:, hs], rhs=S1[h][0:C, ch, 1, 0:C], start=False, stop=True)],
                           r=["Usb", ("S2m", h), ("S1m", h), "KBV"], w=[("ps", by)], pe_sync=(C == 64))
            if ch < nch - 1:
                S.op("dve", lambda e, ch=ch: e.scalar_tensor_tensor(out=STb, in0=STf, scalar=gC[:, ch:ch + 1], in1=PS(bS)[:, 0:64],
                                                                    op0=ALU.mult, op1=ALU.add),
                     r=[("ps", bS), ("st_S",) + sk, "gC"], w=["STb"])
            S.op("dve", lambda e, ch=ch: e.scalar_tensor_tensor(out=STf, in0=STf, scalar=gC[:, ch:ch + 1], in1=PS(bS)[:, 0:64],
                                                                op0=ALU.mult, op1=ALU.add),
                 r=[("ps", bS), ("st_S",) + sk, "gC"], w=[("st_S",) + sk])

        y = epi1[:, 0:W]
        S.op("act", lambda e: e.activation(out=y, in_=PS(by)[:, 0:W], func=AF.Copy), r=[("ps", by)], w=["epi1"])
        bm = bank("small")
        S.pe_group([lambda e: e.matmul(PS(bm)[:, 0:W], lhsT=bones_f, rhs=y, start=True, stop=True)],
                   r=["epi1"], w=[("ps", bm)])
        S.op("dve", lambda e: e.tensor_tensor(out=y, in0=y, in1=PS(bm)[:, 0:W], op=ALU.subtract),
             r=[("ps", bm), "epi1"], w=["epi1"])
        sq = epi2[:, 0:W]
        S.op("act", lambda e: e.activation(out=sq, in_=y, func=AF.Square), r=["epi1"], w=["epi2"])
        bv2 = bank("small")
        S.pe_group([lambda e: e.matmul(PS(bv2)[:, 0:W], lhsT=bones_f, rhs=sq, start=True, stop=True)],
                   r=["epi2"], w=[("ps", bv2)])
        rs = epi2[:, 0:W]
        S.op("act", lambda e: e.activation(out=rs, in_=PS(bv2)[:, 0:W], func=AF.Ln, bias=GN_EPS, scale=1.0),
             r=[("ps", bv2)], w=["epi2"])
        S.op("act", lambda e: e.activation(out=rs, in_=rs, func=AF.Exp, scale=-0.5), r=["epi2"], w=["epi2"])
        S.op("dve", lambda e: e.tensor_tensor(out=y, in0=y, in1=rs, op=ALU.mult), r=["epi1", "epi2"], w=["epi1"])
        S.op("dve", lambda e: e.tensor_scalar(out=y, in0=y, scalar1=V(l, "gg", pr), scalar2=V(l, "gb", pr),
                                              op0=ALU.mult, op1=ALU.add), r=["epi1"], w=["epi1"])
        S.op("pool", lambda e: e.tensor_tensor(out=y, in0=y, in1=bonus, op=ALU.add), r=["epi1", K("t10")], w=["epi1"])
        S.op("dve", lambda e: e.tensor_tensor(out=cat[:, 8 + pr, off:off + W], in0=y, in1=g_, op=ALU.mult),
             r=["epi1", K("t6")], w=["cat"])

    groups = []
    for g in range(npg):
        groups.append(dict(gw=512, parts=[dict(seq="P", off=0, W=512, C=128, first=(g == 0), last=(g == npg - 1),
                                               bi=0)],
                           src=xp[g * 512:(g + 1) * 512, :], dst=yp[g * 512:(g + 1) * 512, :]))
    if with_s:
      groups.append(dict(gw=128, parts=[dict(seq="S0", off=0, W=64, C=64, first=False, last=True, bi=0, sidx=0),
                                      dict(seq="S1", off=64, W=64, C=64, first=False, last=True, bi=1, sidx=1)],
                       src=xs[:, :], dst=ys[:, :]))
    for g in groups:
        for l in range(layers):
            wq["order"] += layer_order(l)

    dbg_out = {}

    def chk(name):
        if dbg == name:
            S.dead = True

    for gi_, G in enumerate(groups):
        gw = G["gw"]
        parts = G["parts"]
        S.flush()
        S.reorder = reorder
        ntb = gw // 128
        for tb in range(ntb):
            S.dma("sp", stage, G["src"][tb * 128:(tb + 1) * 128, :], w=["stage"])
            for k4 in range(4):
                b = bank("small")
                S.pe_group([lambda e, k=k: e.transpose(out=PS(b)[:, (k % 4) * 128:(k % 4 + 1) * 128],
                                                        in_=stage[:, k * 128:(k + 1) * 128], identity=ident_f)
                            for k in range(k4 * 4, k4 * 4 + 4)], r=["stage"], w=[("ps", b)])
                S.op("act" if k4 % 2 else "dve",
                     (lambda e, k4=k4, tb=tb, b=b: e.activation(
                         out=xT[:, k4 * 4:k4 * 4 + 4, tb * 128:(tb + 1) * 128],
                         in_=PS(b).rearrange("p (a c) -> p a c", c=128), func=AF.Copy)) if k4 % 2 else
                     (lambda e, k4=k4, tb=tb, b=b: e.tensor_copy(
                         out=xT[:, k4 * 4:k4 * 4 + 4, tb * 128:(tb + 1) * 128],
                         in_=PS(b).rearrange("p (a c) -> p a c", c=128))),
                     r=[("ps", b)], w=["xT"])

        for l in range(layers):
            LV = l * VL
            chk("A")
            S.dma("pool", wsmall, wblk[l, 0, :, :], w=["wsmall"], sem=wsem_small)
            S.dma("pool", wpool, wblk[l, 1, :, 0:512], w=["wpool"], sem=wsem_small)
            for p in parts:
                if p["seq"] == "P":
                    if p["first"]:
                        sd = st[("P", l)]
                        S.op("dve", lambda e, sd=sd: e.memset(sd["u"], 0.0), w=[("st_u", "P", l)])
                        S.op("dve", lambda e, sd=sd: e.memset(sd["p"], 0.0), w=[("st_p", "P", l)])
                        S.op("dve", lambda e, sd=sd: e.memset(sd["q"], 0.0), w=[("st_q", "P", l)])
                        S.op("dve", lambda e, sd=sd: e.memset(sd["S"], 0.0), w=[("st_S", "P", l)])
                    continue
                sq, si = p["seq"], p["sidx"]
                sd = st[(sq, l)]
                S.dma("sp", stage2[0:30, 0:512], cconv[l, si, :, :], w=["stage"])
                b = bank("small")
                S.pe_group([lambda e, c=c: e.transpose(out=PS(b)[:, c * 32:c * 32 + 30],
                                                        in_=stage2[0:30, c * 128:(c + 1) * 128],
                                                        identity=ident_f[0:30, 0:30]) for c in range(4)],
                           r=["stage"], w=[("ps", b)])
                S.op("dve", lambda e, sd=sd, b=b: e.tensor_copy(
                    out=sd["u"], in_=PS(b)[:, 0:128].rearrange("p (a c) -> p a c", c=32)[:, :, 0:30]),
                    r=[("ps", b)], w=[("st_u", sq, l)])
                S.dma("sp", stage2[0:15, 0:512], cpool[l, si, :, :], w=["stage"])
                b = bank("small")
                S.pe_group([lambda e, c=c: e.transpose(out=PS(b)[:, c * 16:c * 16 + 15],
                                                        in_=stage2[0:15, c * 128:(c + 1) * 128],
                                                        identity=ident_f[0:15, 0:15]) for c in range(4)],
                           r=["stage"], w=[("ps", b)])
                S.op("dve", lambda e, sd=sd, b=b: e.tensor_copy(
                    out=sd["p"], in_=PS(b)[:, 0:64].rearrange("p (a c) -> p a c", c=16)[:, :, 0:15]),
                    r=[("ps", b)], w=[("st_p", sq, l)])
                S.dma("sp", stage2[0:NQ, 0:128], cshift[l, si, :, :], w=["stage"])
                b = bank("small")
                S.pe_group([lambda e: e.transpose(out=PS(b)[:, 0:NQ], in_=stage2[0:NQ, 0:128],
                                                  identity=ident_f[0:NQ, 0:NQ])], r=["stage"], w=[("ps", b)])
                S.op("dve", lambda e, sd=sd, b=b: e.tensor_copy(out=sd["q"], in_=PS(b)[:, 0:NQ]),
                     r=[("ps", b)], w=[("st_q", sq, l)])
                S.dma("sp", stage2[0:64, :].rearrange("p (h j) -> p h j", j=64),
                      cwkv[l, si].rearrange("h i j -> i h j"), w=["stage"])
                for half in range(2):
                    b = bank("small")
                    S.pe_group([lambda e, pr=pr: e.transpose(
                        out=PS(b)[:, (pr % 4) * 64:(pr % 4) * 64 + 64], in_=stage2[0:64, pr * 128:(pr + 1) * 128],
                        identity=ident_f[0:64, 0:64]) for pr in range(half * 4, half * 4 + 4)],
                        r=["stage"], w=[("ps", b)])
                    S.op("dve", lambda e, sd=sd, b=b, half=half: e.tensor_copy(
                        out=sd["S"][:, half * 4:half * 4 + 4, :],
                        in_=PS(b)[:, 0:256].rearrange("p (a c) -> p a c", c=64)),
                        r=[("ps", b)], w=[("st_S", sq, l)])

            chk("A2")
            rmsnorm_to(lambda k: hT[:, k, 0:gw], gw, 0, "hT", LV + VO["nm"])
            chk("B")

            for c in range(4):
                bg = proj(w_next(), hT, gw, "hT")
                bv = proj(w_next(), hT, gw, "hT")
                for p in parts:
                    B = mb[p["bi"]]
                    W, off = p["W"], p["off"]
                    sk = (p["seq"], l)
                    t0 = B["t0"][:, 0:W]
                    if c == 0:
                        S.op("dve", lambda e, B=B, p=p: e.tensor_copy(out=B["ubuf"][:, :, 0:30],
                                                                     in_=st[(p["seq"], l)]["u"]),
                             r=[("st_u",) + sk], w=[("ubuf", p["bi"])])
                    S.op("act", lambda e, t0=t0, off=off, W=W, bg=bg: e.activation(
                        out=t0, in_=PS(bg)[:, off:off + W], func=AF.Sigmoid), r=[("ps", bg)], w=[("t0", p["bi"])])
                    S.op("dve", lambda e, B=B, t0=t0, off=off, W=W, bv=bv, c=c: e.tensor_tensor(
                        out=B["ubuf"][:, c, 30:30 + W], in0=PS(bv)[:, off:off + W], in1=t0, op=ALU.mult),
                        r=[("ps", bv), ("t0", p["bi"])], w=[("ubuf", p["bi"])])
            for p in parts:
                B = mb[p["bi"]]
                W, off, bi = p["W"], p["off"], p["bi"]
                sk = (p["seq"], l)
                S.op("act", lambda e, B=B, W=W: e.activation(out=B["ubf"][:, :, 0:30 + W], in_=B["ubuf"][:, :, 0:30 + W],
                                                            func=AF.Copy), r=[("ubuf", bi)], w=[("ubf", bi)])
                S.op("dve", lambda e, B=B, W=W, p=p: e.tensor_copy(out=st[(p["seq"], l)]["u"], in_=B["ubuf"][:, :, W:W + 30]),
                     r=[("ubuf", bi)], w=[("st_u",) + sk])
                pass
            for c in range(4):
                for j in range(31):
                    if j % 4 != 3:
                        S.op("act", lambda e, c=c, j=j: e.activation(out=diag[:, j, :], in_=ident_b, func=AF.Identity,
                                                                      scale=V(l, "cw", c * 31 + j)), r=[], w=[("diag", j)])
                    else:
                        S.op("dve", lambda e, c=c, j=j: e.tensor_scalar(
                            out=diag[:, j, :], in0=ident_b, scalar1=V(l, "cw", c * 31 + j), scalar2=None, op0=ALU.mult),
                            r=[], w=[("diag", j)])
                for p in parts:
                    B = mb[p["bi"]]
                    W, off, bi = p["W"], p["off"], p["bi"]
                    b = bank("small")
                    S.pe_group([lambda e, j=j, c=c, B=B, W=W, b=b: e.matmul(
                        PS(b)[:, 0:W], lhsT=diag[:, j, :], rhs=B["ubf"][:, c, j:j + W],
                        start=(j == 0), stop=(j == 30)) for j in range(31)],
                        r=[("diag", j) for j in range(31)] + [("ubf", bi)] + [(tn, 0) for tn in ("t8", "t9", "t10", "t11")],
                        w=[("ps", b)])
                    S.op("act", lambda e, B=B, W=W, b=b, c=c: e.activation(
                        out=B["hconv"][:, c, 0:W], in_=PS(b)[:, 0:W], func=AF.Identity,
                        bias=V(l, "cb", c), scale=1.0), r=[("ps", b)], w=[("hconv", bi)] + (["stage"] if bi == 0 else []))
            for p in parts:
                B = mb[p["bi"]]
                W, off, bi = p["W"], p["off"], p["bi"]
                sk = (p["seq"], l)
                bm = bank("small")
                S.pe_group([lambda e, c=c, B=B, W=W: e.matmul(PS(bm)[:, 0:W], lhsT=onesD_f, rhs=B["hconv"][:, c, 0:W],
                                                              start=(c == 0), stop=(c == 3)) for c in range(4)],
                           r=[("hconv", bi)], w=[("ps", bm)])
                mean = B["t1"][:, 0:W]
                S.op("act", lambda e, mean=mean, W=W: e.activation(out=mean, in_=PS(bm)[:, 0:W], func=AF.Copy),
                     r=[("ps", bm)], w=[("t1", bi)])
                for c in range(4):
                    S.op("dve", lambda e, c=c, B=B, W=W, mean=mean: e.tensor_tensor(
                        out=B["hconv"][:, c, 0:W], in0=B["hconv"][:, c, 0:W], in1=mean, op=ALU.subtract),
                        r=[("hconv", bi), ("t1", bi)], w=[("hconv", bi)])
                bvv = bank("small")
                for c in range(4):
                    S.op("act", lambda e, c=c, B=B, W=W: e.activation(out=B["t2"][:, 0:W], in_=B["hconv"][:, c, 0:W],
                                                                      func=AF.Square),
                         r=[("hconv", bi)], w=[("t2", bi)])
                    S.pe_group([lambda e, c=c, B=B, W=W: e.matmul(PS(bvv)[:, 0:W], lhsT=onesD_f, rhs=B["t2"][:, 0:W],
                                                                  start=(c == 0), stop=(c == 3))],
                               r=[("t2", bi)], w=[("ps", bvv)])
                rs = B["t3"][:, 0:W]
                S.op("act", lambda e, rs=rs, W=W: e.activation(out=rs, in_=PS(bvv)[:, 0:W], func=AF.Ln,
                                                              bias=LN_EPS, scale=1.0), r=[("ps", bvv)], w=[("t3", bi)])
                S.op("act", lambda e, rs=rs: e.activation(out=rs, in_=rs, func=AF.Exp, scale=-0.5),
                     r=[("t3", bi)], w=[("t3", bi)])
                for c in range(4):
                    S.op("dve", lambda e, c=c, B=B, W=W, rs=rs: e.tensor_tensor(
                        out=B["hconv"][:, c, 0:W], in0=B["hconv"][:, c, 0:W], in1=rs, op=ALU.mult),
                        r=[("hconv", bi), ("t3", bi)], w=[("hconv", bi)])
                    S.op("act", lambda e, c=c, B=B, W=W, off=off: e.activation(
                        out=cat[:, c, off:off + W], in_=B["hconv"][:, c, 0:W], func=AF.Silu,
                        bias=V(l, "lb", c), scale=V(l, "lg", c)), r=[("hconv", bi)], w=["cat"])

            chk("C")
            for c in range(4):
                bp = proj(w_next(), hT, gw, "hT")
                for p in parts:
                    B = mb[p["bi"]]
                    W, off, bi = p["W"], p["off"], p["bi"]
                    sk = (p["seq"], l)
                    if c == 0:
                        S.op("dve", lambda e, B=B, p=p: e.tensor_copy(out=B["pbuf"][:, :, 0:15],
                                                                     in_=st[(p["seq"], l)]["p"]),
                             r=[("st_p",) + sk], w=[("pbuf", bi), ("ubuf", bi)])
                    S.op("act", lambda e, B=B, W=W, off=off, bp=bp, c=c: e.activation(
                        out=B["pbuf"][:, c, 15:15 + W], in_=PS(bp)[:, off:off + W], func=AF.Copy),
                        r=[("ps", bp)], w=[("pbuf", bi), ("ubuf", bi)])
            for p in parts:
                B = mb[p["bi"]]
                W, off, bi = p["W"], p["off"], p["bi"]
                sk = (p["seq"], l)
                S.op("dve", lambda e, B=B, W=W, p=p: e.tensor_copy(out=st[(p["seq"], l)]["p"], in_=B["pbuf"][:, :, W:W + 15]),
                     r=[("pbuf", bi)], w=[("st_p",) + sk])
                for c, wdw in enumerate(POOL_WINDOWS):
                    src = B["pbuf"][:, c, :]
                    lo = 15
                    span = 1
                    ta, tb_ = B["t4"], B["t5"]
                    cur, cur_lo = src, 0
                    nsteps = {2: 1, 4: 2, 8: 3, 16: 4}[wdw]
                    for s_ in range(nsteps):
                        dst = ta if s_ % 2 == 0 else tb_
                        new_lo = cur_lo + span
                        n = 15 + W - new_lo
                        S.op("pool", lambda e, dst=dst, cur=cur, new_lo=new_lo, span=span, n=n: e.tensor_tensor(
                            out=dst[:, new_lo:new_lo + n], in0=cur[:, new_lo:new_lo + n],
                            in1=cur[:, new_lo - span:new_lo - span + n], op=ALU.add),
                            r=[("pbuf", bi), ("t4", bi), ("t5", bi)], w=[("t4" if s_ % 2 == 0 else "t5", bi)])
                        cur, cur_lo = dst, new_lo
                        span *= 2
                    S.op("dve", lambda e, cur=cur, W=W, c=c, B=B, wdw=wdw: e.scalar_tensor_tensor(
                        out=B["dpool"][:, c, 0:W], in0=cur[:, 15:15 + W], scalar=1.0 / wdw,
                        in1=B["pbuf"][:, c, 15:15 + W], op0=ALU.mult, op1=ALU.subtract),
                        r=[("t4", bi), ("t5", bi), ("pbuf", bi)], w=[("dpool", bi), ("ubuf", bi), ("ubf", bi)])
                    if p["first"]:
                        S.op("dve", lambda e, cur=cur, c=c: e.tensor_tensor(
                            out=cur[:, 15:31], in0=cur[:, 15:31], in1=invc_first[:, c, :], op=ALU.mult),
                            r=[("t4", bi), ("t5", bi), ("dpool", bi)], w=[("t4", bi), ("t5", bi)])
                        S.op("dve", lambda e, cur=cur, c=c, B=B: e.tensor_tensor(
                            out=B["dpool"][:, c, 0:16], in0=cur[:, 15:31], in1=B["pbuf"][:, c, 15:31],
                            op=ALU.subtract), r=[("t4", bi), ("t5", bi), ("pbuf", bi)], w=[("dpool", bi), ("ubuf", bi), ("ubf", bi)])
                    b = bank("small")
                    S.pe_group([lambda e, c=c, B=B, W=W, b=b: e.matmul(PS(b)[:, 0:W], lhsT=wpool[:, c * 128:(c + 1) * 128],
                                                                        rhs=B["dpool"][:, c, 0:W], start=True, stop=True)],
                               r=["wpool", ("dpool", bi)], w=[("ps", b)])
                    S.op("act", lambda e, c=c, W=W, off=off, b=b: e.activation(
                        out=cat[:, 4 + c, off:off + W], in_=PS(b)[:, 0:W], func=AF.Identity, scale=V(l, "psc", c)),
                        r=[("ps", b)], w=["cat"])


            chk("D")
            b24 = proj(w_next(), hT, gw, "hT")
            b25 = proj(w_next(), hT, gw, "hT")
            for p in parts:
                B = mb[p["bi"]]
                W, bi = p["W"], p["bi"]
                gl = B["gl"]
                shifted_from_psum(l, b24, 24, p, gl, ("t1", bi), B["t0"], ("t0", bi))
                S.op("act", lambda e, B=B, W=W, gl=gl: e.activation(out=B["lora"][0:64, 0, 0:W], in_=gl[0:64, 0:W],
                                                                    func=AF.Tanh), r=[("t1", bi)], w=[("lora", bi)])
                S.op("act", lambda e, B=B, W=W, gl=gl: e.activation(out=B["lora"][64:128, 0, 0:W], in_=gl[64:128, 0:W],
                                                                    func=AF.Copy), r=[("t1", bi)], w=[("lora", bi)])
                shifted_from_psum(l, b25, 25, p, gl, ("t1", bi), B["t0"], ("t0", bi))
                S.op("act", lambda e, B=B, W=W, gl=gl: e.activation(out=B["lora"][0:64, 1, 0:W], in_=gl[0:64, 0:W],
                                                                    func=AF.Sigmoid), r=[("t1", bi)], w=[("lora", bi)])

            chk("E")
            for pr in range(PAIRS):
                if pr == 1:
                    chk("F")
                br = proj(w_next(), hT, gw, "hT")
                bk = proj(w_next(), hT, gw, "hT")
                bv_ = proj(w_next(), hT, gw, "hT")
                for p in parts:
                    if p["C"] == 64:
                        S.flush()
                    wkv_pair(l, pr, p, br, bk, bv_)
                if parts[0]["C"] == 64:
                    S.flush()

            chk("G")
            for n in range(16):
                bo = proj(w_next(), cat, gw, "cat")
                S.op("dve", lambda e, n=n, bo=bo: e.tensor_tensor(out=xT[:, n, 0:gw], in0=PS(bo)[:, 0:gw],
                                                                  in1=xT[:, n, 0:gw], op=ALU.add),
                     r=[("ps", bo), "xT"], w=["xT"])

            chk("H")
            rmsnorm_to(lambda k: hT[:, k, 0:gw], gw, 0, "hT", LV + VO["nf"])
            S.barrier()
            for f in range(FC if dbg != "outproj" else 0):
                bg = proj(w_next(), hT, gw, "hT")
                bu = proj(w_next(), hT, gw, "hT")
                ft = ftmp[f % 2]
                S.op("act", lambda e, ft=ft, bg=bg: e.activation(out=ft[:, 0:gw], in_=PS(bg)[:, 0:gw], func=AF.Silu),
                     r=[("ps", bg)], w=[("ftmp", f % 2)])
                S.op("dve", lambda e, ft=ft, bu=bu, f=f: e.tensor_tensor(out=act[:, f, 0:gw], in0=PS(bu)[:, 0:gw],
                                                                         in1=ft[:, 0:gw], op=ALU.mult),
                     r=[("ps", bu), ("ftmp", f % 2)], w=[("act", f)])
            for n in range(16 if dbg != "outproj" else 0):
                bd = bank("big")
                for j, nk in enumerate((16, 16, 12)):
                    sl = w_next()
                    wv = wring[:, sl, :].rearrange("p (k n) -> p k n", n=128)
                    fns = []
                    for k in range(nk):
                        f = j * 16 + k
                        fns.append(lambda e, wv=wv, k=k, f=f: e.matmul(PS(bd)[:, 0:gw], lhsT=wv[:, k, :],
                                                                        rhs=act[:, f, 0:gw], start=(f == 0),
                                                                        stop=(f == FC - 1)))
                    S.pe_group(fns, r=[("w", sl)] + [("act", f) for f in range(j * 16, j * 16 + nk)],
                               w=[("ps", bd)])
                S.op("dve", lambda e, n=n, bd=bd: e.tensor_tensor(out=xT[:, n, 0:gw], in0=PS(bd)[:, 0:gw],
                                                                  in1=xT[:, n, 0:gw], op=ALU.add),
                     r=[("ps", bd), "xT"], w=["xT"])
            S.barrier()

            chk("I")
            for p in parts:
                if not p["last"]:
                    continue
                sk = (p["seq"], l)
                sd = st[sk]
                oi = {"P": 0, "S0": 1, "S1": 2}[p["seq"]]
                b = bank("small")
                S.pe_group([lambda e, c=c: e.transpose(out=PS(b)[0:30, c * 128:(c + 1) * 128], in_=sd["u"][:, c, :],
                                                        identity=ident_f) for c in range(4)],
                           r=[("st_u",) + sk], w=[("ps", b)])
                S.op("act", lambda e, b=b: e.activation(out=stage2[0:30, 0:512], in_=PS(b)[0:30, 0:512], func=AF.Copy),
                     r=[("ps", b)], w=["stage"])
                S.dma("sp", nconv[l, oi, :, :], stage2[0:30, 0:512], r=["stage"], w=[("o_conv", l, oi)])
                b = bank("small")
                S.pe_group([lambda e, c=c: e.transpose(out=PS(b)[0:15, c * 128:(c + 1) * 128], in_=sd["p"][:, c, :],
                                                        identity=ident_f) for c in range(4)],
                           r=[("st_p",) + sk], w=[("ps", b)])
                S.op("act", lambda e, b=b: e.activation(out=stage2[0:15, 512:1024], in_=PS(b)[0:15, 0:512], func=AF.Copy),
                     r=[("ps", b)], w=["stage"])
                S.dma("sp", npool[l, oi, :, :], stage2[0:15, 512:1024], r=["stage"], w=[("o_pool", l, oi)])
                b = bank("small")
                S.pe_group([lambda e: e.transpose(out=PS(b)[0:NQ, 0:128], in_=sd["q"], identity=ident_f)],
                           r=[("st_q",) + sk], w=[("ps", b)])
                S.op("act", lambda e, b=b: e.activation(out=stage3[0:NQ, 512:640], in_=PS(b)[0:NQ, 0:128], func=AF.Copy),
                     r=[("ps", b)], w=["stage"])
                S.dma("sp", nshift[l, oi, :, :], stage3[0:NQ, 512:640], r=["stage"], w=[("o_shift", l, oi)])
                for half in range(2):
                    b = bank("small")
                    S.pe_group([lambda e, pr=pr: e.transpose(out=PS(b)[0:64, (pr % 4) * 128:(pr % 4 + 1) * 128],
                                                              in_=sd["S"][:, pr, :], identity=ident_f)
                                for pr in range(half * 4, half * 4 + 4)], r=[("st_S",) + sk], w=[("ps", b)])
                    S.op("act", lambda e, b=b: e.activation(out=stage3[0:64, 0:512], in_=PS(b)[0:64, 0:512],
                                                            func=AF.Copy), r=[("ps", b)], w=["stage"])
                    S.dma("sp", nwkv[l, oi, half * 8:half * 8 + 8].rearrange("h i j -> i h j"),
                          stage3[0:64, 0:512].rearrange("p (h j) -> p h j", j=64), r=["stage"],
                          w=[("o_wkv", l, oi, half)])

        S.dead = False
        if dbg is None:
            rmsnorm_to(lambda k: xT[:, k, 0:gw], gw, 0, "xT", DEPTH * VL)
        for tb in range(ntb):
            for k4 in range(4):
                b = bank("small")
                S.pe_group([lambda e, k=k: e.transpose(out=PS(b)[:, (k % 4) * 128:(k % 4 + 1) * 128],
                                                        in_=xT[:, k, tb * 128:(tb + 1) * 128], identity=ident_f)
                            for k in range(k4 * 4, k4 * 4 + 4)], r=["xT"], w=[("ps", b)])
                S.op("act" if k4 % 2 else "dve",
                     (lambda e, k4=k4, b=b: e.activation(out=stage[:, k4 * 512:(k4 + 1) * 512], in_=PS(b), func=AF.Copy))
                     if k4 % 2 else
                     (lambda e, k4=k4, b=b: e.tensor_copy(out=stage[:, k4 * 512:(k4 + 1) * 512], in_=PS(b))),
                     r=[("ps", b)], w=["stage"])
            S.dma("sp", G["dst"][tb * 128:(tb + 1) * 128, :], stage, r=["stage"], w=[("o_y", gi_, tb)])

    S.finish("sp")
    print("instructions emitted:", S.ninst)
    nc._arena_reg = A.reg
    return nc


def _colize(v):
    v = np.asarray(v, np.float32).reshape(-1)
    n = (v.size + 127) // 128
    out = np.zeros((n * 128,), np.float32)
    out[:v.size] = v
    return out.reshape(n, 128).T


def _prep_shared(inp):
    wblk = np.zeros((DEPTH, NBLK, 128, SLOT), np.float32)
    vecs = np.zeros((128, NVEC), np.float32)
    for l in range(DEPTH):
        wblk[l, 0, 0:64, 0:1024] = inp["decay_up"][l]
        wblk[l, 0, 64:128, 0:1024] = inp["iclr_up"][l]
        wblk[l, 0, 0:64, 1024:2048] = inp["gate_up"][l]
        wblk[l, 1, :, 0:512] = np.asarray(inp["pool_w"][l]).transpose(1, 0, 2).reshape(128, 512)
        win = np.zeros((D, 38 * 128), np.float32)
        win[:, :4800] = inp["w_in"][l]
        wblk[l, 2:40] = win.reshape(16, 128, 38, 128).transpose(2, 1, 0, 3).reshape(38, 128, SLOT)
        wblk[l, 40:56] = np.asarray(inp["w_out"][l]).reshape(16, 128, 16, 128).transpose(2, 1, 0, 3).reshape(16, 128, SLOT)
        g = np.asarray(inp["ffn_gate"][l]).reshape(16, 128, FC, 128).transpose(2, 1, 0, 3).reshape(FC, 128, SLOT)
        u = np.asarray(inp["ffn_up"][l]).reshape(16, 128, FC, 128).transpose(2, 1, 0, 3).reshape(FC, 128, SLOT)
        wblk[l, 56:144:2] = g
        wblk[l, 57:144:2] = u
        dn = np.zeros((48, 128, 16, 128), np.float32)
        dn[:FC] = np.asarray(inp["ffn_down"][l]).reshape(FC, 128, 16, 128)
        dn = dn.reshape(3, 16, 128, 16, 128).transpose(3, 0, 2, 1, 4).reshape(16, 3, 128, SLOT)
        wblk[l, 144:192] = dn.reshape(48, 128, SLOT)
        o = l * VL
        vecs[:, o + VO["nm"]:o + VO["nm"] + 16] = _colize(inp["norm_mix"][l])
        vecs[:, o + VO["nf"]:o + VO["nf"] + 16] = _colize(inp["norm_ffn"][l])
        vecs[:, o + VO["cb"]:o + VO["cb"] + 4] = _colize(inp["conv_b"][l])
        cw = np.asarray(inp["conv_w"][l])
        vecs[:, o + VO["cw"]:o + VO["cw"] + 124] = cw.reshape(31, 4, 128).transpose(2, 1, 0).reshape(128, 124)
        vecs[:, o + VO["lg"]:o + VO["lg"] + 4] = _colize(inp["conv_ln_g"][l])
        vecs[:, o + VO["lb"]:o + VO["lb"] + 4] = _colize(inp["conv_ln_b"][l])
        vecs[:, o + VO["psc"]:o + VO["psc"] + 4] = _colize(inp["pool_scale"][l])
        vecs[:, o + VO["mu"]:o + VO["mu"] + NQ] = _colize(inp["shift_mu"][l])
        vecs[:, o + VO["w0"]:o + VO["w0"] + 8] = _colize(inp["decay_w0"][l])
        vecs[:, o + VO["a0"]:o + VO["a0"] + 8] = _colize(inp["iclr_a0"][l])
        vecs[:, o + VO["kk"]:o + VO["kk"] + 8] = _colize(inp["k_k"][l])
        vecs[:, o + VO["ka"]:o + VO["ka"] + 8] = _colize(inp["k_a"][l])
        vecs[:, o + VO["rk"]:o + VO["rk"] + 8] = _colize(inp["r_k"][l])
        vecs[:, o + VO["gg"]:o + VO["gg"] + 8] = _colize(inp["gn_g"][l])
        vecs[:, o + VO["gb"]:o + VO["gb"] + 8] = _colize(inp["gn_b"][l])
    vecs[:, DEPTH * VL:DEPTH * VL + 16] = _colize(inp["norm_final"])
    return wblk, vecs


def _core_inputs(inp, c, shared, nseq_tok=SEQ):
    wblk, vecs = shared
    sh = np.zeros((DEPTH, 2, NQ * 128), np.float32)
    sh[:, :, :3264] = np.asarray(inp["state_shift"])[:, 2 * c:2 * c + 2, 0, :]
    return {
        "xp": np.ascontiguousarray(np.asarray(inp["x_prompt"])[c % 4, :nseq_tok]),
        "xs": np.ascontiguousarray(np.asarray(inp["x_sample"])[2 * c:2 * c + 2].reshape(2 * SLEN, D)),
        "cconv": np.ascontiguousarray(np.asarray(inp["cache_conv"])[:, 2 * c:2 * c + 2]),
        "cpool": np.ascontiguousarray(np.asarray(inp["cache_pool"])[:, 2 * c:2 * c + 2]),
        "cshift": sh.reshape(DEPTH, 2, NQ, 128),
        "cwkv": np.ascontiguousarray(np.asarray(inp["state_wkv"])[:, 2 * c:2 * c + 2]),
        "wblk": wblk,
        "vecs": vecs,
    }


_NC_CACHE = {}


def kernel(**inp):
    inp = {k: np.asarray(v) for k, v in inp.items()}
    shared = _prep_shared(inp)
    if "nc" not in _NC_CACHE:
        _NC_CACHE["nc"] = build_program()
    nc = _NC_CACHE["nc"]
    in_maps = [_core_inputs(inp, c, shared) for c in range(8)]
    res = run_bass_kernel_spmd(nc, in_maps, core_ids=list(range(8)))
    R = res.results
    y_prompt = np.stack([R[c]["yp"] for c in range(4)]).astype(np.float32)
    y_sample = np.concatenate([R[c]["ys"].reshape(2, SLEN, D) for c in range(8)]).astype(np.float32)

    def gather(name, tailshape, fix=None):
        pr = np.stack([R[c][name][:, 0] for c in range(4)], axis=1)
        sm = np.concatenate([R[c][name][:, 1:3] for c in range(8)], axis=1)
        if fix is not None:
            pr, sm = fix(pr), fix(sm)
        return pr.astype(np.float32), sm.astype(np.float32)

    p_conv, s_conv = gather("nconv", None)
    p_pool, s_pool = gather("npool", None)
    fixs = lambda a: a.reshape(a.shape[0], a.shape[1], 1, NQ * 128)[..., :3264]
    p_shift, s_shift = gather("nshift", None, fixs)
    p_wkv, s_wkv = gather("nwkv", None)
    return (y_prompt, y_sample, p_conv, p_pool, p_shift, p_wkv, s_conv, s_pool, s_shift, s_wkv)
```
